# Optimizing a Trainium2 kernel written in Bass

```python
import math
import jax, jax.numpy as jnp
from jax import lax
import numpy as np

D_MODEL = 2048
BATCH = 4
SEQ = 4096
DEPTH = 2

NSA_HEADS = 16
NSA_KV_GROUPS = 4
NSA_HEAD_DIM = 64
NSA_CMP_LEN = 32
NSA_CMP_STRIDE = 16
NSA_CMP_HIDDEN = 256
NSA_SEL_LEN = 64
NSA_SEL_BLOCKS = 16
NSA_WINDOW = 512
DSA_HEADS = 16
DSA_HEAD_DIM = 64
DSA_KV_LATENT = 256
IDX_HEADS = 16
IDX_DIM = 32
DSA_TOPK_MAX = 256
INDEX_SCALE = (IDX_HEADS * IDX_DIM) ** -0.5
D_FF = 5632
REL_BUCKETS = 32
REL_MAX_DIST = 128
TOTAL_HEADS = NSA_HEADS + DSA_HEADS
Q_BLOCK = 128
N_SUBLAYERS = 3
ADA_INIT = 0.5
RMS_EPS = 1e-6
NEG_INF = -1e30
FORCE_BONUS = 1e4

NSA_KV_WIDTH = NSA_KV_GROUPS * NSA_HEAD_DIM
IN_SPLITS = (
    NSA_HEADS * NSA_HEAD_DIM,
    NSA_KV_WIDTH, NSA_KV_WIDTH,
    NSA_KV_WIDTH, NSA_KV_WIDTH,
    NSA_KV_WIDTH, NSA_KV_WIDTH,
    NSA_HEADS * 3,
    DSA_HEADS * DSA_HEAD_DIM,
    DSA_KV_LATENT,
    IDX_HEADS * IDX_DIM,
    IDX_DIM,
    IDX_HEADS,
    2 * D_MODEL,
)
N_IN = sum(IN_SPLITS)

kernel_name = "hybrid_nsa_dsa_macaron_adaln"


def _rmsnorm(x, g):
    xf = x.astype(jnp.float32)
    y = xf * lax.rsqrt(jnp.mean(xf * xf, axis=-1, keepdims=True) + RMS_EPS)
    return (y * g.astype(jnp.float32)).astype(x.dtype)


def _masked_softmax(s, mask):
    s = jnp.where(mask, s.astype(jnp.float32), NEG_INF)
    m = jnp.max(s, axis=-1, keepdims=True)
    p = jnp.where(mask, jnp.exp(s - m), 0.0)
    return p / jnp.maximum(jnp.sum(p, axis=-1, keepdims=True), 1e-30)


def _t5_bucket(dist):
    n = jnp.maximum(dist, 0)
    exact = REL_BUCKETS // 2
    nf = jnp.maximum(n, 1).astype(jnp.float32)
    large = exact + (jnp.log(nf / exact) / math.log(REL_MAX_DIST / exact)
                     * (REL_BUCKETS - exact)).astype(jnp.int32)
    large = jnp.minimum(large, REL_BUCKETS - 1)
    return jnp.where(n < exact, n, large)


def _to_blocks(a):
    b, s = a.shape[:2]
    return jnp.moveaxis(a.reshape(b, s // Q_BLOCK, Q_BLOCK, *a.shape[2:]), 1, 0)


def _from_blocks(o):
    nq, b, qb, f = o.shape
    return jnp.moveaxis(o, 0, 1).reshape(b, nq * qb, f)


def _swiglu(h, w_i, w_o):
    g, u = jnp.split(h @ w_i, 2, axis=-1)
    return (jax.nn.silu(g) * u) @ w_o


def _modulate(x, g, m):
    return _rmsnorm(x, g) * (1.0 + m[:, 1][:, None, :]) + m[:, 0][:, None, :]


def _nsa(q, kc, vc, ks, vs, kw, vw, gates, pe_k, pe_v, wk1, wk2, wv1, wv2, rel_bias):
    B, S = q.shape[:2]
    H, G, DK = NSA_HEADS, NSA_KV_GROUPS, NSA_HEAD_DIM
    HG = H // G
    scale = DK ** -0.5
    f32 = jnp.float32

    n_cmp = (S - NSA_CMP_LEN) // NSA_CMP_STRIDE + 1
    cmp_start = (np.arange(n_cmp) * NSA_CMP_STRIDE).astype(np.int32)
    blk_idx = cmp_start[:, None] + np.arange(NSA_CMP_LEN, dtype=np.int32)[None, :]
    cmp_end = jnp.asarray(cmp_start + NSA_CMP_LEN - 1)

    def compress(t, pe, w1, w2):
        blocks = jnp.moveaxis(t[:, blk_idx], 3, 2) + pe
        flat = blocks.reshape(B, n_cmp, G, NSA_CMP_LEN * DK)
        return jax.nn.silu(flat @ w1) @ w2

    k_cmp = compress(kc, pe_k, wk1, wk2)
    v_cmp = compress(vc, pe_v, wv1, wv2)

    n_blk = S // NSA_SEL_LEN
    n_sel = min(NSA_SEL_BLOCKS, n_blk)
    sel_start = np.arange(n_blk) * NSA_SEL_LEN
    overlap = jnp.asarray(((cmp_start[:, None] < sel_start[None, :] + NSA_SEL_LEN)
                           & (cmp_start[:, None] + NSA_CMP_LEN > sel_start[None, :])).astype(np.float32))
    k_sb = ks.reshape(B, n_blk, NSA_SEL_LEN, G, DK).transpose(0, 3, 1, 2, 4)
    v_sb = vs.reshape(B, n_blk, NSA_SEL_LEN, G, DK).transpose(0, 3, 1, 2, 4)

    pad = ((0, 0), (NSA_WINDOW, 0), (0, 0), (0, 0))
    kw_pad = jnp.pad(kw, pad)
    vw_pad = jnp.pad(vw, pad)
    table_n = rel_bias[:, :H]
    table_g = table_n.reshape(REL_BUCKETS, G, HG)
    wq = jnp.arange(Q_BLOCK)
    wk = jnp.arange(Q_BLOCK + NSA_WINDOW)
    off_w = wq[:, None] - wk[None, :] + NSA_WINDOW
    bias_w = table_n[_t5_bucket(off_w)].reshape(Q_BLOCK, Q_BLOCK + NSA_WINDOW, G, HG).transpose(0, 2, 3, 1)
    mask_w_rel = (off_w >= 0) & (off_w < NSA_WINDOW)
    bidx = jnp.arange(B)[:, None, None, None]
    gidx = jnp.arange(G)[None, None, :, None]
    blk = jnp.arange(n_blk)

    def block(args):
        i, qb, gb = args
        t = i * Q_BLOCK + wq
        qg = qb.reshape(B, Q_BLOCK, G, HG, DK)
        gg = gb.reshape(B, Q_BLOCK, G, HG, 3)

        dist_c = t[:, None] - cmp_end[None, :]
        bias_c = table_n[_t5_bucket(dist_c)].reshape(Q_BLOCK, n_cmp, G, HG).transpose(0, 2, 3, 1)
        s_c = jnp.einsum('btghd,bngd->btghn', qg, k_cmp).astype(f32) * scale + bias_c
        p_c = _masked_softmax(s_c, (dist_c >= 0)[:, None, None, :])
        o_c = jnp.einsum('btghn,bngd->btghd', p_c.astype(v_cmp.dtype), v_cmp)

        imp = jnp.einsum('btghn,nj->btgj', p_c, overlap)
        cur = t // NSA_SEL_LEN
        forced = (blk[None, :] == 0) | (blk[None, :] == cur[:, None]) | (blk[None, :] == cur[:, None] - 1)
        visible = blk[None, :] * NSA_SEL_LEN <= t[:, None]
        imp = jnp.where(visible[None, :, None, :],
                        imp + jnp.where(forced, FORCE_BONUS, 0.0)[None, :, None, :], NEG_INF)
        _, sel = lax.top_k(imp, n_sel)
        n_tok = n_sel * NSA_SEL_LEN
        k_g = k_sb[bidx, gidx, sel].reshape(B, Q_BLOCK, G, n_tok, DK)
        v_g = v_sb[bidx, gidx, sel].reshape(B, Q_BLOCK, G, n_tok, DK)
        pos = (sel[..., None] * NSA_SEL_LEN + jnp.arange(NSA_SEL_LEN)).reshape(B, Q_BLOCK, G, n_tok)
        dist_s = t[None, :, None, None] - pos
        bias_s = jnp.moveaxis(table_g[_t5_bucket(dist_s), gidx], -1, 3)
        s_s = jnp.einsum('btghd,btgkd->btghk', qg, k_g).astype(f32) * scale + bias_s
        p_s = _masked_softmax(s_s, (dist_s >= 0)[:, :, :, None, :])
        o_s = jnp.einsum('btghk,btgkd->btghd', p_s.astype(v_g.dtype), v_g)

        k_wb = lax.dynamic_slice_in_dim(kw_pad, i * Q_BLOCK, Q_BLOCK + NSA_WINDOW, axis=1)
        v_wb = lax.dynamic_slice_in_dim(vw_pad, i * Q_BLOCK, Q_BLOCK + NSA_WINDOW, axis=1)
        mask_w = mask_w_rel & ((i * Q_BLOCK - NSA_WINDOW + wk) >= 0)[None, :]
        s_w = jnp.einsum('btghd,bsgd->btghs', qg, k_wb).astype(f32) * scale + bias_w
        p_w = _masked_softmax(s_w, mask_w[:, None, None, :])
        o_w = jnp.einsum('btghs,bsgd->btghd', p_w.astype(v_wb.dtype), v_wb)

        o = gg[..., 0:1] * o_c + gg[..., 1:2] * o_s + gg[..., 2:3] * o_w
        return o.reshape(B, Q_BLOCK, H * DK)

    nq = S // Q_BLOCK
    out = lax.map(block, (jnp.arange(nq), _to_blocks(q), _to_blocks(gates)))
    return _from_blocks(out)


def _dsa(q, ckv, qi, ki, wi, w_uk, w_uv, rel_bias):
    B, S = q.shape[:2]
    k_top = min(DSA_TOPK_MAX, S // 4)
    scale = DSA_HEAD_DIM ** -0.5
    table_d = rel_bias[:, NSA_HEADS:]
    bidx = jnp.arange(B)[:, None, None]
    s_pos = jnp.arange(S)

    def block(args):
        i, qb, qib, wib = args
        t = i * Q_BLOCK + jnp.arange(Q_BLOCK)
        idx_logits = jnp.einsum('bthe,bse->bths', qib, ki)
        score = jnp.einsum('bths,bth->bts', jax.nn.relu(idx_logits), wib).astype(jnp.float32)
        score = jnp.where(s_pos[None, None, :] <= t[None, :, None], score, NEG_INF)
        _, idx = lax.top_k(score, k_top)
        c_g = ckv[bidx, idx]
        q_lat = jnp.einsum('bthd,chd->bthc', qb, w_uk)
        dist = t[None, :, None] - idx
        bias = jnp.moveaxis(table_d[_t5_bucket(dist)], -1, 2)
        s = jnp.einsum('bthc,btkc->bthk', q_lat, c_g).astype(jnp.float32) * scale + bias
        p = _masked_softmax(s, (dist >= 0)[:, :, None, :])
        o_lat = jnp.einsum('bthk,btkc->bthc', p.astype(c_g.dtype), c_g)
        o = jnp.einsum('bthc,chd->bthd', o_lat, w_uv)
        return o.reshape(B, Q_BLOCK, DSA_HEADS * DSA_HEAD_DIM)

    nq = S // Q_BLOCK
    out = lax.map(block, (jnp.arange(nq), _to_blocks(q), _to_blocks(qi), _to_blocks(wi)))
    return _from_blocks(out)


def _mixer(h, w_in, pe_k, pe_v, wk1, wk2, wv1, wv2, g_kv, w_uk, w_uv,
           w_up_nsa, w_up_dsa, w_out, rel_bias):
    B, S, D = h.shape
    cuts = [int(v) for v in np.cumsum(IN_SPLITS)[:-1]]
    (q_n, kc, vc, ks, vs, kw, vw, g_n, q_d, ckv, qi, ki, wi, a_m) = jnp.split(h @ w_in, cuts, axis=-1)
    kv_shape = (B, S, NSA_KV_GROUPS, NSA_HEAD_DIM)
    o_n = _nsa(q_n.reshape(B, S, NSA_HEADS, NSA_HEAD_DIM),
               kc.reshape(kv_shape), vc.reshape(kv_shape),
               ks.reshape(kv_shape), vs.reshape(kv_shape),
               kw.reshape(kv_shape), vw.reshape(kv_shape),
               jax.nn.sigmoid(g_n).reshape(B, S, NSA_HEADS, 3),
               pe_k, pe_v, wk1, wk2, wv1, wv2, rel_bias)
    o_d = _dsa(q_d.reshape(B, S, DSA_HEADS, DSA_HEAD_DIM),
               _rmsnorm(ckv, g_kv),
               qi.reshape(B, S, IDX_HEADS, IDX_DIM), ki, wi * INDEX_SCALE,
               w_uk, w_uv, rel_bias)
    a_m = jax.nn.sigmoid(a_m).reshape(B, S, 2, D)
    y = a_m[:, :, 0] * (o_n @ w_up_nsa) + a_m[:, :, 1] * (o_d @ w_up_dsa)
    return y @ w_out


def setup_inputs(seed: int = 0) -> dict:
    key = jax.random.key(seed)
    k = jax.random.split(key, 24)
    f32 = jnp.float32
    D = D_MODEL

    def nrm(kk, shape, s):
        return jax.random.normal(kk, shape, f32) * s

    cmp_in = NSA_CMP_LEN * NSA_HEAD_DIM
    return {
        "x": nrm(k[0], (BATCH, SEQ, D), 1.0),
        "c": nrm(k[1], (BATCH, D), 1.0),
        "w_ada": nrm(k[2], (DEPTH, D, N_SUBLAYERS * 3 * D), ADA_INIT * D ** -0.5),
        "b_ada": nrm(k[3], (DEPTH, N_SUBLAYERS * 3 * D), 0.02),
        "g_norm": 1.0 + nrm(k[4], (DEPTH, N_SUBLAYERS, D), 0.02),
        "w_ffn_in": nrm(k[5], (DEPTH, 2, D, 2 * D_FF), D ** -0.5),
        "w_ffn_out": nrm(k[6], (DEPTH, 2, D_FF, D), D_FF ** -0.5),
        "w_in": nrm(k[7], (DEPTH, D, N_IN), D ** -0.5),
        "nsa_pe_k": nrm(k[8], (DEPTH, NSA_CMP_LEN, NSA_HEAD_DIM), 0.5),
        "nsa_pe_v": nrm(k[9], (DEPTH, NSA_CMP_LEN, NSA_HEAD_DIM), 0.5),
        "nsa_cmp_k1": nrm(k[10], (DEPTH, cmp_in, NSA_CMP_HIDDEN), cmp_in ** -0.5),
        "nsa_cmp_k2": nrm(k[11], (DEPTH, NSA_CMP_HIDDEN, NSA_HEAD_DIM), NSA_CMP_HIDDEN ** -0.5),
        "nsa_cmp_v1": nrm(k[12], (DEPTH, cmp_in, NSA_CMP_HIDDEN), cmp_in ** -0.5),
        "nsa_cmp_v2": nrm(k[13], (DEPTH, NSA_CMP_HIDDEN, NSA_HEAD_DIM), NSA_CMP_HIDDEN ** -0.5),
        "dsa_g_kv": 1.0 + nrm(k[14], (DEPTH, DSA_KV_LATENT), 0.02),
        "dsa_w_uk": nrm(k[15], (DEPTH, DSA_KV_LATENT, DSA_HEADS, DSA_HEAD_DIM), DSA_KV_LATENT ** -0.5),
        "dsa_w_uv": nrm(k[16], (DEPTH, DSA_KV_LATENT, DSA_HEADS, DSA_HEAD_DIM), DSA_KV_LATENT ** -0.5),
        "w_up_nsa": nrm(k[17], (DEPTH, NSA_HEADS * NSA_HEAD_DIM, D), (NSA_HEADS * NSA_HEAD_DIM) ** -0.5),
        "w_up_dsa": nrm(k[18], (DEPTH, DSA_HEADS * DSA_HEAD_DIM, D), (DSA_HEADS * DSA_HEAD_DIM) ** -0.5),
        "w_out": nrm(k[19], (DEPTH, D, D), D ** -0.5),
        "rel_bias": nrm(k[20], (REL_BUCKETS, TOTAL_HEADS), 0.5),
        "g_final": 1.0 + nrm(k[21], (D,), 0.02),
    }


def reference(x, c, w_ada, b_ada, g_norm, w_ffn_in, w_ffn_out, w_in,
              nsa_pe_k, nsa_pe_v, nsa_cmp_k1, nsa_cmp_k2, nsa_cmp_v1, nsa_cmp_v2,
              dsa_g_kv, dsa_w_uk, dsa_w_uv, w_up_nsa, w_up_dsa, w_out, rel_bias, g_final):
    B = x.shape[0]
    c_act = jax.nn.silu(c)
    for l in range(DEPTH):
        mod = (c_act @ w_ada[l] + b_ada[l]).reshape(B, N_SUBLAYERS, 3, D_MODEL)
        h = _modulate(x, g_norm[l, 0], mod[:, 0])
        x = x + 0.5 * mod[:, 0, 2][:, None, :] * _swiglu(h, w_ffn_in[l, 0], w_ffn_out[l, 0])
        h = _modulate(x, g_norm[l, 1], mod[:, 1])
        y = _mixer(h, w_in[l], nsa_pe_k[l], nsa_pe_v[l], nsa_cmp_k1[l], nsa_cmp_k2[l],
                   nsa_cmp_v1[l], nsa_cmp_v2[l], dsa_g_kv[l], dsa_w_uk[l], dsa_w_uv[l],
                   w_up_nsa[l], w_up_dsa[l], w_out[l], rel_bias)
        x = x + mod[:, 1, 2][:, None, :] * y
        h = _modulate(x, g_norm[l, 2], mod[:, 2])
        x = x + 0.5 * mod[:, 2, 2][:, None, :] * _swiglu(h, w_ffn_in[l, 1], w_ffn_out[l, 1])
    return _rmsnorm(x, g_final)
```

```python
import numpy as np
from contextlib import ExitStack
import concourse.bass as bass
import concourse.mybir as mybir
from concourse.bass_utils import run_bass_kernel_spmd

F32 = mybir.dt.float32
BF16 = mybir.dt.bfloat16
AF = mybir.ActivationFunctionType
ALU = mybir.AluOpType
AX = mybir.AxisListType

D = 2048
KC = 16
FF = 5632
FC = 44
NTOK = 2048
TT = 512
NTT = NTOK // TT
S = 4096
N_IN = 8544
EPS = 1e-6
NCORES = 8
INDEX_SCALE = 512 ** -0.5
WST_ELEMS = 11264

C_QN, C_KC, C_VC, C_KS, C_VS, C_KW, C_VW, C_GN = 0, 1024, 1280, 1536, 1792, 2048, 2304, 2560
C_QD, C_CKV, C_QI, C_KI, C_WI, C_AM = 2608, 3632, 3888, 4400, 4432, 4448


class Eng:
    def __init__(self, name, eng, sem):
        self.name, self.eng, self.sem = name, eng, sem
        self.cnt = 0
        self.seen = {}


class Buf:
    def __init__(self, ap):
        self.ap = ap
        self.wr = None
        self.rd = {}

    def wdeps(self):
        d = list(self.rd.values())
        if self.wr is not None:
            d.append(self.wr)
        return d

    def rdeps(self):
        return [self.wr] if self.wr is not None else []

    def wrote(self, h):
        self.wr = h
        self.rd = {}

    def read(self, h):
        k = h[0]
        if k not in self.rd or self.rd[k][2] < h[2]:
            self.rd[k] = h


class KB:
    def __init__(self, nc, st):
        self.nc, self.st = nc, st
        self.pe = Eng("pe", nc.tensor, st.enter_context(nc.semaphore("s_pe")))
        self.act = Eng("act", nc.scalar, st.enter_context(nc.semaphore("s_act")))
        self.dve = Eng("dve", nc.vector, st.enter_context(nc.semaphore("s_dve")))
        self.pool = Eng("pool", nc.gpsimd, st.enter_context(nc.semaphore("s_pool")))
        self.sp = Eng("sp", nc.sync, st.enter_context(nc.semaphore("s_sp")))
        self.nslot = 0
        self.slots = []
        self.prefix = ""
        self.pools = None
        self.banks = None
        self.nbank = 0
        self.out_handles = []
        self.uid = 0

    def sbuf(self, name, shape, dt):
        return self.st.enter_context(self.nc.sbuf_tensor(self.prefix + name, shape, dt))

    def psum(self, name, shape, dt=F32):
        if self.banks is not None:
            b = self.banks[self.nbank % 8]
            self.nbank += 1
            return b
        return self.st.enter_context(self.nc.psum_tensor(self.prefix + name, shape, dt))

    def setup_fused(self, n_sp=30, n_pool=12):
        self.banks = [self.st.enter_context(self.nc.psum_tensor("bank%d" % i, [128, 512], F32)) for i in range(8)]
        self.pools = {"sp": [], "pool": []}
        for kind, n in (("sp", n_sp), ("pool", n_pool)):
            for i in range(n):
                sem = self.st.enter_context(self.nc.semaphore("s_%s%d" % (kind, i)))
                self.pools[kind].append({"key": "%s%d" % (kind, i), "sem": sem, "cnt": 0, "kind": kind})

    def begin_phase(self, prefix, stack):
        self.prefix = prefix
        self.main_st = self.st
        self.st = stack
        self.nbank = 0
        self.phase_recs = []

    def end_phase(self):
        self.barrier()
        for r in self.phase_recs:
            self.pools[r["kind"]].append(r)
        self.phase_recs = []
        self.slots = []
        self.st = self.main_st
        self.prefix = ""

    def slot(self):
        if self.pools is not None:
            return {"lazy": True}
        self.nslot += 1
        sem = self.st.enter_context(self.nc.semaphore("s_dma%d" % self.nslot))
        sl = {"key": "dma%d" % self.nslot, "sem": sem, "cnt": 0}
        self.slots.append(sl)
        return sl

    def barrier(self):
        engs = [self.pe, self.act, self.dve, self.pool, self.sp]
        for e in engs:
            for e2 in engs:
                if e2.cnt > 0:
                    self.wait(e, (e2.name, e2.sem, e2.cnt))
            for sl in self.slots:
                if sl["cnt"] > 0:
                    self.wait(e, (sl["key"], sl["sem"], sl["cnt"], sl))

    def wait(self, e, h):
        if h is None:
            return
        key, sem, cnt = h[0], h[1], h[2]
        if len(h) > 3:
            cnt = max(cnt, h[3]["cnt"])
            h[3]["waited"] = max(h[3].get("waited", 0), cnt)
        if e.seen.get(key, 0) >= cnt:
            return
        e.eng.wait_ge(sem, cnt)
        e.seen[key] = cnt

    def op(self, e, fn, deps=()):
        for d in deps:
            self.wait(e, d)
        ins = fn()
        e.cnt += 1
        ins.then_inc(e.sem, 1)
        return (e.name, e.sem, e.cnt)

    def dma(self, q, out, in_, slot, deps=()):
        if slot.get("lazy"):
            if "rec" not in slot:
                slot["rec"] = self.pools["pool" if q is self.pool else "sp"].pop()
                self.phase_recs.append(slot["rec"])
                self.slots.append(slot["rec"])
            assert slot["rec"]["kind"] == ("pool" if q is self.pool else "sp"), "slot used from both DMA queue kinds"
            slot = slot["rec"]
        for d in deps:
            self.wait(q, d)
        W = slot.get("waited", 0)
        if W > q.seen.get(slot["key"], 0):
            q.eng.wait_ge(slot["sem"], W)
            q.seen[slot["key"]] = W
        q.eng.dma_start(out=out, in_=in_).then_inc(slot["sem"], 16)
        slot["cnt"] += 16
        return (slot["key"], slot["sem"], slot["cnt"], slot)

    def mm(self, outb, out_ap, lhsT, rhs, start, stop, rd=(), extra=()):
        deps = list(extra)
        if start:
            deps += outb.wdeps()
        for b in rd:
            deps += b.rdeps()
        deps = [d for d in deps if d is not None and d[0] != "pe"]
        h = self.op(self.pe, lambda: self.nc.tensor.matmul(out_ap, lhsT=lhsT, rhs=rhs, start=start, stop=stop,
                                                           skip_group_check=True), deps)
        for b in rd:
            b.read(h)
        outb.wr = h
        if start:
            outb.rd = {}
        return h

    def mm_t(self, outb, out_ap, in_ap, ident_ap, first, rd=()):
        deps = []
        if first:
            deps += outb.wdeps()
        for b in rd:
            deps += b.rdeps()
        deps = [d for d in deps if d is not None and d[0] != "pe"]
        h = self.op(self.pe, lambda: self.nc.tensor.transpose(out_ap, in_ap, ident_ap), deps)
        for b in rd:
            b.read(h)
        outb.wr = h
        if first:
            outb.rd = {}
        return h

    def ew(self, e, fn, wr=(), rd=(), extra=()):
        deps = list(extra)
        for b in wr:
            deps += b.wdeps()
        for b in rd:
            deps += b.rdeps()
        h = self.op(e, fn, deps)
        for b in rd:
            b.read(h)
        for b in wr:
            b.wrote(h)
        return h

    def load(self, q, buf, out_ap, in_ap, slot, extra=()):
        h = self.dma(q, out_ap, in_ap, slot, list(extra) + buf.wdeps())
        buf.wrote(h)
        return h

    def store(self, q, dram_ap, buf, in_ap, slot, extra=(), final=False):
        h = self.dma(q, dram_ap, in_ap, slot, list(extra) + buf.rdeps())
        buf.read(h)
        if final:
            self.out_handles.append(h)
        return h

    def finish(self):
        last = {}
        for h in self.out_handles:
            if h[0] not in last or last[h[0]][2] < h[2]:
                last[h[0]] = h
        for h in last.values():
            self.wait(self.sp, h)


class Ring:
    def __init__(self, bufs):
        self.bufs = bufs
        self.i = 0

    def next(self):
        b = self.bufs[self.i % len(self.bufs)]
        self.i += 1
        return b


class Dense:
    def __init__(self, kb):
        self.kb = kb
        nc = kb.nc
        xres_t = kb.sbuf("xres", [128, KC, TT], F32)
        h_t = kb.sbuf("hmod", [128, KC, TT], BF16)
        hid_t = kb.sbuf("hid", [128, FC, TT], BF16)
        self.xres = [Buf(xres_t[:, k, :]) for k in range(KC)]
        self.h = [Buf(h_t[:, k, :]) for k in range(KC)]
        self.hid = [Buf(hid_t[:, k, :]) for k in range(FC)]
        self.hid_t = hid_t
        self.wst = Ring([Buf(kb.sbuf("wst%d" % i, [128, WST_ELEMS], BF16)[:]) for i in range(3)])
        self.wslots = [kb.slot() for _ in range(3)]
        self.ones = Buf(kb.sbuf("ones32", [128, 128], F32)[:])
        self.epsb = Buf(kb.sbuf("epsb", [128, 1], F32)[:])
        self.sq = Ring([Buf(kb.sbuf("sq%d" % i, [128, 512], F32)[:]) for i in range(2)])
        self.rstd = Buf(kb.sbuf("rstd", [128, TT], F32)[:])
        self.tmp = Ring([Buf(kb.sbuf("tmpf%d" % i, [128, 512], F32)[:]) for i in range(3)])
        self.ps = Ring([Buf(kb.psum("psg%d" % i, [128, 512])[:]) for i in range(6)])
        self.ps_stat = Buf(kb.psum("ps_stat", [128, 512])[:])
        self.ps_misc = Buf(kb.psum("ps_misc", [128, 512])[:])
        self.xslot = kb.slot()
        self.cslot = kb.slot()
        self.cslot2 = kb.slot()
        self.oslot = kb.slot()
        self.mod = Buf(kb.sbuf("modt", [128, 144], F32)[:])
        self.gn = Buf(kb.sbuf("gn", [128, 48], F32)[:])
        self.A = Buf(kb.sbuf("coefA", [128, 48], F32)[:])
        self.GT = Buf(kb.sbuf("coefG", [128, 48], F32)[:])
        kb.ew(kb.dve, lambda: nc.vector.memset(self.ones.ap, 1.0), wr=[self.ones])
        kb.ew(kb.dve, lambda: nc.vector.memset(self.epsb.ap, EPS), wr=[self.epsb])

    def wstage(self):
        i = self.wst.i % 3
        return self.wst.next(), self.wslots[i]


def emit_modprep(kb, dn, cT_d, wada_d, badaT_d, gnT_d, col_lo, col_hi):
    nc = kb.nc
    if True:
        cact = Buf(kb.sbuf("cact", [128, KC, 2], F32)[:])
        craw = Buf(kb.sbuf("craw", [128, KC], F32)[:])
        bada = Buf(kb.sbuf("bada", [128, 144], F32)[:])
        mod, gn = dn.mod, dn.gn
        kb.load(kb.sp, craw, craw.ap, cT_d, dn.cslot)
        kb.load(kb.sp, bada, bada.ap, badaT_d, dn.cslot)
        kb.load(kb.sp, gn, gn.ap, gnT_d, dn.cslot)
        for r in range(2):
            kb.ew(kb.act, lambda r=r: nc.scalar.activation(out=cact.ap[:, :, r], in_=craw.ap, func=AF.Silu),
                  wr=[cact], rd=[craw])
        psm = dn.ps_misc
        stg = [Buf(kb.sbuf("wada_stg%d" % i, [128, KC, 128], F32)[:]) for i in range(2)]
        sslots = [kb.slot(), kb.slot()]
        cbs = list(range(col_lo, col_hi))

        def issue(i):
            cb = cbs[i]
            b = stg[i % 2]
            src = wada_d[:, cb * 128:(cb + 1) * 128].rearrange("(k p) c -> p k c", p=128)
            kb.load(kb.sp, b, b.ap[:, 0:8, :], src[:, 0:8, :], sslots[i % 2])
            b.wr = kb.dma(kb.sp, b.ap[:, 8:16, :], src[:, 8:16, :], sslots[i % 2])

        issue(0)
        first = True
        for i, col in enumerate(cbs):
            if i + 1 < len(cbs):
                issue(i + 1)
            b = stg[i % 2]
            for kc in range(KC):
                kb.mm(psm, psm.ap[:, 2 * col:2 * col + 2], lhsT=b.ap[:, kc, :], rhs=cact.ap[:, kc, :],
                      start=first, stop=(kc == KC - 1), rd=[b, cact])
                first = False
        pv = psm.ap[:, 2 * col_lo:2 * col_hi].rearrange("p (c two) -> p c two", two=2)[:, :, 0]
        kb.ew(kb.dve, lambda: nc.vector.tensor_tensor(out=mod.ap[:, col_lo:col_hi], in0=pv, in1=bada.ap[:, col_lo:col_hi], op=ALU.add),
              wr=[mod], rd=[psm, bada])
        A, GT = dn.A, dn.GT
        for sub in range(3):
            if col_lo <= sub * 48 + 16 and sub * 48 + 32 <= col_hi:
                kb.ew(kb.dve, lambda sub=sub: nc.vector.scalar_tensor_tensor(
                    out=A.ap[:, sub * 16:(sub + 1) * 16], in0=mod.ap[:, sub * 48 + 16:sub * 48 + 32], scalar=1.0,
                    in1=gn.ap[:, sub * 16:(sub + 1) * 16], op0=ALU.add, op1=ALU.mult), wr=[A], rd=[mod, gn])
            if col_lo <= sub * 48 + 32 and sub * 48 + 48 <= col_hi:
                kb.ew(kb.dve, lambda sub=sub: nc.vector.tensor_scalar(
                    out=GT.ap[:, sub * 16:(sub + 1) * 16], in0=mod.ap[:, sub * 48 + 32:sub * 48 + 48],
                    scalar1=(1.0 if sub == 1 else 0.5), scalar2=None, op0=ALU.mult), wr=[GT], rd=[mod])


def emit_norm_mod(kb, dn, sub):
    nc = kb.nc
    xres, h = dn.xres, dn.h
    for st_ in range(TT // 512):
        sl = slice(st_ * 512, (st_ + 1) * 512)
        for kc in range(KC):
            sq = dn.sq.next()
            kb.ew(kb.act, lambda kc=kc, sq=sq: nc.scalar.activation(out=sq.ap, in_=xres[kc].ap[:, sl], func=AF.Square),
                  wr=[sq], rd=[xres[kc]])
            kb.mm(dn.ps_stat, dn.ps_stat.ap, lhsT=dn.ones.ap, rhs=sq.ap, start=(kc == 0), stop=(kc == KC - 1),
                  rd=[dn.ones, sq])
        t = dn.tmp.next()
        kb.ew(kb.act, lambda t=t: nc.scalar.activation(out=t.ap, in_=dn.ps_stat.ap, func=AF.Sqrt, scale=1.0 / D, bias=dn.epsb.ap[:, 0:1]),
              wr=[t], rd=[dn.ps_stat, dn.epsb])
        kb.ew(kb.dve, lambda t=t, sl=sl: nc.vector.reciprocal(out=dn.rstd.ap[:, sl], in_=t.ap), wr=[dn.rstd], rd=[t])
    for kc in range(KC):
        for st_ in range(TT // 512):
            sl = slice(st_ * 512, (st_ + 1) * 512)
            t = dn.tmp.next()
            kb.ew(kb.dve, lambda kc=kc, t=t, sl=sl: nc.vector.scalar_tensor_tensor(
                out=t.ap, in0=xres[kc].ap[:, sl], scalar=dn.A.ap[:, sub * 16 + kc:sub * 16 + kc + 1], in1=dn.rstd.ap[:, sl],
                op0=ALU.mult, op1=ALU.mult), wr=[t], rd=[xres[kc], dn.A, dn.rstd])
            kb.ew(kb.act, lambda kc=kc, t=t, sl=sl: nc.scalar.activation(
                out=h[kc].ap[:, sl], in_=t.ap, func=AF.Identity, bias=dn.mod.ap[:, sub * 48 + kc:sub * 48 + kc + 1], scale=1.0),
                wr=[h[kc]], rd=[t, dn.mod])


def emit_gemm_fm(kb, dn, w_d, kchunks, pieces, src, epilogue, cwcap=None):
    nc = kb.nc
    cwmax = (WST_ELEMS // kchunks) // 64 * 64
    if cwcap:
        cwmax = min(cwmax, cwcap)
    groups, cur = [], []
    for pi, (c0, wd) in enumerate(pieces):
        if cur and (cur[-1][1] + cur[-1][2] == c0) and (c0 + wd - cur[0][1] <= cwmax):
            cur.append((pi, c0, wd))
        else:
            if cur:
                groups.append(cur)
            cur = [(pi, c0, wd)]
    if cur:
        groups.append(cur)
    loaded = {}

    def issue(gi):
        g = groups[gi]
        c0 = g[0][1]
        wtot = g[-1][1] + g[-1][2] - c0
        wb, ws = dn.wstage()
        wv = wb.ap[:, 0:kchunks * wtot].rearrange("p (k c) -> p k c", c=wtot)
        kb.load(kb.pool, wb, wv, w_d[:, c0:c0 + wtot].rearrange("(k p) c -> p k c", p=128), ws)
        loaded[gi] = (wb, wv, c0)

    issue(0)
    if len(groups) > 1:
        issue(1)
    for gi, g in enumerate(groups):
        if gi + 2 < len(groups):
            issue(gi + 2)
        wb, wv, gc0 = loaded.pop(gi)
        for (pi, c0, wd) in g:
            for st_ in range(TT // 512):
                psb = dn.ps.next()
                for kc in range(kchunks):
                    sap, sbufs = src(kc, st_)
                    kb.mm(psb, psb.ap[0:wd, :], lhsT=wv[:, kc, c0 - gc0:c0 - gc0 + wd], rhs=sap,
                          start=(kc == 0), stop=(kc == kchunks - 1), rd=[wb] + sbufs)
                epilogue(pi, st_, psb)


def emit_ffn(kb, dn, sub, win_d, wout_d, xT_out_d, n0, final_out=False):
    nc = kb.nc
    emit_norm_mod(kb, dn, sub)
    hsrc = lambda kc, st_: (dn.h[kc].ap[:, st_ * 512:(st_ + 1) * 512], [dn.h[kc]])
    pieces = []
    for J in range(FF // 256):
        pieces += [(J * 256, 128), (J * 256 + 128, 128), (FF + J * 256, 128), (FF + J * 256 + 128, 128)]
    hold = {}

    def ep_in(pi, st_, psb):
        J, r = pi // 4, pi % 4
        if r < 2:
            hold[(2 * J + r, st_)] = psb
            return
        j = 2 * J + r - 2
        psg = hold.pop((j, st_))
        t = dn.tmp.next()
        sl = slice(st_ * 512, (st_ + 1) * 512)
        kb.ew(kb.act, lambda: nc.scalar.activation(out=t.ap, in_=psg.ap, func=AF.Silu), wr=[t], rd=[psg])
        kb.ew(kb.dve, lambda: nc.vector.tensor_tensor(out=dn.hid[j].ap[:, sl], in0=t.ap, in1=psb.ap, op=ALU.mult),
              wr=[dn.hid[j]], rd=[t, psb])

    emit_gemm_fm(kb, dn, win_d, KC, pieces, hsrc, ep_in, cwcap=256)

    def ep_out(pi, st_, psb):
        sl = slice(st_ * 512, (st_ + 1) * 512)
        xb = dn.xres[pi]
        kb.ew(kb.dve, lambda: nc.vector.scalar_tensor_tensor(
            out=xb.ap[:, sl], in0=psb.ap, scalar=dn.GT.ap[:, sub * 16 + pi:sub * 16 + pi + 1], in1=xb.ap[:, sl],
            op0=ALU.mult, op1=ALU.add), wr=[xb], rd=[psb, dn.GT])
        if xT_out_d is not None:
            kb.store(kb.sp, xT_out_d[pi * 128:(pi + 1) * 128, n0 + st_ * 512:n0 + (st_ + 1) * 512], xb, xb.ap[:, sl],
                     dn.oslot, final=final_out)

    hidsrc = lambda kc, st_: (dn.hid[kc].ap[:, st_ * 512:(st_ + 1) * 512], [dn.hid[kc]])
    emit_gemm_fm(kb, dn, wout_d, FC, [(m * 128, 128) for m in range(KC)], hidsrc, ep_out)


def load_xtile(kb, dn, xT_d, n0):
    for kc in range(KC):
        kb.load(kb.sp, dn.xres[kc], dn.xres[kc].ap, xT_d[kc * 128:(kc + 1) * 128, n0:n0 + TT], dn.xslot)


def _env(env, prefix):
    if env is None:
        nc = bass.Bass("TRN2", target_bir_lowering=False)
        gi = lambda name, shape, d=F32: nc.dram_tensor(name, shape, d, kind="ExternalInput").ap()
        go = lambda name, shape, d=F32: nc.dram_tensor(name, shape, d, kind="ExternalOutput").ap()
        return nc, None, gi, go
    nc = env["nc"]
    g = lambda name, shape, d=F32: env["T"][name]
    return nc, env["kb"], g, g

class OutStage:
    def __init__(self, kb, n=3):
        self.kb = kb
        self.f = [Buf(kb.sbuf("ostgf%d" % i, [128, 512], F32)[:]) for i in range(n)]
        self.b = [Buf(kb.sbuf("ostgb%d" % i, [128, 512], BF16)[:]) for i in range(n)]
        self.fs = [kb.slot() for _ in range(n)]
        self.bs = [kb.slot() for _ in range(n)]
        self.fi = 0
        self.bi = 0
        self.flip = 0

    def next(self, dt=F32):
        if dt == F32:
            k = self.fi % len(self.f)
            self.fi += 1
            return self.f[k], self.fs[k]
        k = self.bi % len(self.b)
        self.bi += 1
        return self.b[k], self.bs[k]


def emit_gemm_tm(kb, dn, w_d, kchunks, segs, src_h, epilogue):
    nc = kb.nc
    for si, (c0, wd) in enumerate(segs):
        wb, ws = dn.wstage()
        wv = wb.ap[:, 0:kchunks * wd].rearrange("p (k c) -> p k c", c=wd)
        kb.load(kb.pool, wb, wv, w_d[:, c0:c0 + wd].rearrange("(k p) c -> p k c", p=128), ws)
        for ts in range(TT // 128):
            psb = dn.ps.next()
            for kc in range(kchunks):
                sap, sbufs = src_h(kc, ts)
                kb.mm(psb, psb.ap[:, 0:wd], lhsT=sap, rhs=wv[:, kc, :], start=(kc == 0), stop=(kc == kchunks - 1), rd=[wb] + sbufs)
            epilogue(si, ts, psb)


def emit_proj(kb, dn, os_, win_d, gkv, o, n0, ghalf=None):
    nc = kb.nc
    emit_norm_mod(kb, dn, 1)
    hsrc = lambda kc, st_: (dn.h[kc].ap[:, st_ * 512:(st_ + 1) * 512], [dn.h[kc]])
    pieces, kinds = [], []

    def add(c0, wd, kind, idx):
        pieces.append((c0, wd))
        kinds.append((kind, idx))

    for c in range(8):
        add(C_QN + c * 128, 128, "qn", c)
    for nm, cc in (("kc", C_KC), ("vc", C_VC), ("ks", C_KS), ("kw", C_KW)):
        for c in range(2):
            add(cc + c * 128, 128, nm, c)
    for c in range(8):
        add(C_QD + c * 128, 128, "qd", c)
    for i in range(2):
        add(C_CKV + i * 128, 128, "ckv", i)
    for c in range(4):
        add(C_QI + c * 128, 128, "qi", c)
    add(C_KI, 128, "ki", 0)
    for i in range(32):
        add(C_AM + i * 128, 128, "am", i)
    sl_tok = slice(n0, n0 + TT)
    tt_ = n0 // TT

    def kdst(ap2):
        if ghalf is None:
            return ap2[:, sl_tok], None
        return ap2.rearrange("p (j hf t) -> p j hf t", hf=2, t=128)[:, 4 * tt_:4 * tt_ + 4, ghalf, :], "p (j t) -> p j t"

    cbuf = [Buf(dn.cbuf_t[:, i, :]) for i in range(2)]

    def evac_copy(psb, wd, dst_ap, dt, rr=None):
        ob, oslot = os_.next(dt)
        oap = ob.ap[0:wd, :]
        os_.flip ^= 1
        if os_.flip:
            kb.ew(kb.act, lambda: nc.scalar.copy(out=oap, in_=psb.ap[0:wd, :]), wr=[ob], rd=[psb])
        else:
            kb.ew(kb.dve, lambda: nc.vector.tensor_copy(out=oap, in_=psb.ap[0:wd, :]), wr=[ob], rd=[psb])
        kb.store(kb.sp, dst_ap, ob, oap if rr is None else oap.rearrange(rr, t=128), oslot, final=(ghalf is None))

    def ep(pi, st_, psb):
        kind, idx = kinds[pi]
        wd = pieces[pi][1]
        if kind in ("qn", "qd"):
            evac_copy(psb, 128, o[kind][2 * idx:2 * idx + 2, :, sl_tok].rearrange("h p n -> (h p) n"), BF16)
        elif kind == "qi":
            evac_copy(psb, 128, o[kind][4 * idx:4 * idx + 4, :, sl_tok].rearrange("h p n -> (h p) n"), BF16)
        elif kind in ("kc", "vc", "ks", "kw"):
            dst, rr = kdst(o[kind][2 * idx:2 * idx + 2].rearrange("h p n -> (h p) n"))
            evac_copy(psb, 128, dst, BF16, rr)
        elif kind == "ki":
            dst, rr = kdst(o["ki"])
            evac_copy(psb, 32, dst, BF16, rr)
        elif kind == "am":
            ob, oslot = os_.next()
            kb.ew(kb.act, lambda: nc.scalar.activation(out=ob.ap, in_=psb.ap, func=AF.Sigmoid), wr=[ob], rd=[psb])
            kb.store(kb.sp, o["am"][idx * 128:(idx + 1) * 128, sl_tok], ob, ob.ap, oslot, final=(ghalf is None))
        elif kind == "ckv":
            kb.ew(kb.dve, lambda: nc.vector.tensor_copy(out=cbuf[idx].ap, in_=psb.ap), wr=[cbuf[idx]], rd=[psb])
            if idx == 1:
                for i in range(2):
                    sq = dn.sq.next()
                    kb.ew(kb.act, lambda i=i, sq=sq: nc.scalar.activation(out=sq.ap, in_=cbuf[i].ap, func=AF.Square), wr=[sq], rd=[cbuf[i]])
                    kb.mm(dn.ps_stat, dn.ps_stat.ap, lhsT=dn.ones.ap, rhs=sq.ap, start=(i == 0), stop=(i == 1), rd=[dn.ones, sq])
                t = dn.tmp.next()
                kb.ew(kb.act, lambda: nc.scalar.activation(out=t.ap, in_=dn.ps_stat.ap, func=AF.Sqrt, scale=1.0 / 256, bias=dn.epsb.ap[:, 0:1]),
                      wr=[t], rd=[dn.ps_stat, dn.epsb])
                t2 = dn.tmp.next()
                kb.ew(kb.dve, lambda: nc.vector.reciprocal(out=t2.ap, in_=t.ap), wr=[t2], rd=[t])
                for i in range(2):
                    ob, oslot = os_.next(BF16)
                    oap = ob.ap
                    kb.ew(kb.dve, lambda i=i, oap=oap: nc.vector.scalar_tensor_tensor(
                        out=oap, in0=cbuf[i].ap, scalar=gkv.ap[:, i:i + 1], in1=t2.ap, op0=ALU.mult, op1=ALU.mult),
                        wr=[ob], rd=[cbuf[i], gkv, t2])
                    dst, rr = kdst(o["cn"][i * 128:(i + 1) * 128, :])
                    kb.store(kb.sp, dst, ob, oap if rr is None else oap.rearrange(rr, t=128), oslot, final=(ghalf is None))

    if dn.pflags & 1:
        emit_gemm_fm(kb, dn, win_d, KC, pieces, hsrc, ep, cwcap=512)

    segs = [(C_VS, 256), (C_VW, 256), (C_GN, 128), (C_KI, 128)]
    hsrc_t = lambda kc, ts: (dn.h[kc].ap[:, ts * 128:(ts + 1) * 128], [dn.h[kc]])

    def ep_t(si, ts, psb):
        tsl = slice(n0 + ts * 128, n0 + (ts + 1) * 128)
        gt_ = 2 * (4 * tt_ + ts) + (ghalf or 0)
        gsl = tsl if ghalf is None else slice(gt_ * 128, (gt_ + 1) * 128)
        if si == 0:
            ob, oslot = os_.next(BF16)
            oap = ob.ap[:, 0:256]
            kb.ew(kb.dve, lambda: nc.vector.tensor_copy(out=oap, in_=psb.ap[:, 0:256]), wr=[ob], rd=[psb])
            kb.store(kb.sp, o["vs"][gsl, :], ob, oap, oslot, final=(ghalf is None))
        elif si == 1:
            ob, oslot = os_.next(BF16)
            oap = ob.ap[:, 0:256]
            kb.ew(kb.dve, lambda: nc.vector.tensor_copy(out=oap, in_=psb.ap[:, 0:256]), wr=[ob], rd=[psb])
            kb.store(kb.sp, o["vw"][gsl, :], ob, oap, oslot, final=(ghalf is None))
        elif si == 2:
            ob2, oslot2 = os_.next()
            kb.ew(kb.act, lambda: nc.scalar.activation(out=ob2.ap[:, 0:48], in_=psb.ap[:, 0:48], func=AF.Sigmoid), wr=[ob2], rd=[psb])
            kb.store(kb.sp, o["gn"][tsl, :], ob2, ob2.ap[:, 0:48], oslot2, final=(ghalf is None))
        else:
            ob, oslot = os_.next()
            kb.ew(kb.dve, lambda: nc.vector.tensor_scalar(out=ob.ap[:, 0:16], in0=psb.ap[:, 32:48], scalar1=INDEX_SCALE, scalar2=None, op0=ALU.mult),
                  wr=[ob], rd=[psb])
            kb.store(kb.sp, o["wi"][tsl, :], ob, ob.ap[:, 0:16], oslot, final=(ghalf is None))

    if dn.pflags & 2:
        if dn.pflags > 3:
            keep = [i for i in range(4) if dn.pflags & (4 << i)]
            segs2 = [segs[i] for i in keep]
            emit_gemm_tm(kb, dn, win_d, KC, segs2, hsrc_t, lambda si, ts, psb: ep_t(keep[si], ts, psb))
        else:
            emit_gemm_tm(kb, dn, win_d, KC, segs, hsrc_t, ep_t)


PROJ_OUTS = [("qn", [16, 64, NTOK], BF16), ("kc", [4, 64, NTOK], BF16), ("vc", [4, 64, NTOK], BF16),
             ("ks", [4, 64, NTOK], BF16), ("kw", [4, 64, NTOK], BF16), ("qd", [16, 64, NTOK], BF16),
             ("cn", [256, NTOK], BF16), ("qi", [16, 32, NTOK], BF16), ("ki", [32, NTOK], BF16),
             ("am", [4096, NTOK], F32), ("vs", [NTOK, 256], BF16), ("vw", [NTOK, 256], BF16),
             ("gn", [NTOK, 48], F32), ("wi", [NTOK, 16], F32)]


def build_A(ntt=NTT, do_proj=True, do_ffn=True, do_mod=True, pflags=3, env=None, prefix="", ghalf=None, load_tabs=False):
    nc, ekb, gi, go = _env(env, prefix)
    dt = lambda name, shape, d=F32, kind="ExternalInput": (gi if kind == "ExternalInput" else go)(name, shape, d)
    xT = dt("xT", [D, NTOK])
    cT = dt("cT", [128, KC])
    if do_mod and not load_tabs:
        wada = dt("wada", [D, 9 * D])
        badaT = dt("badaT", [128, 144])
        gnT = dt("gnT", [128, 48])
    if do_ffn:
        wfi = dt("wfi", [D, 2 * FF])
        wfo = dt("wfo", [FF, D])
    win = dt("win", [D, N_IN])
    gkvT = dt("gkvT", [128, 2])
    x1T = dt("x1T", [D, NTOK], kind="ExternalOutput")
    o = {nm: dt("o_" + nm, shp, d, kind="ExternalOutput") for nm, shp, d in PROJ_OUTS}
    tabs = {"o_mod": dt("o_mod", [128, 144], kind="ExternalOutput"), "o_A": dt("o_A", [128, 48], kind="ExternalOutput"),
            "o_GT": dt("o_GT", [128, 48], kind="ExternalOutput")}
    with ExitStack() as st:
        if ekb is None:
            kb = KB(nc, st)
        else:
            kb = ekb
            kb.begin_phase(prefix, st)
        dn = Dense(kb)
        os_ = OutStage(kb)
        dn.cbuf_t = kb.sbuf("cbuf", [128, 2, 512], F32)
        gkv = Buf(kb.sbuf("gkv", [128, 2], F32)[:])
        kb.load(kb.sp, gkv, gkv.ap, gkvT, dn.cslot)
        if load_tabs:
            kb.load(kb.sp, dn.mod, dn.mod.ap, tabs["o_mod"], dn.cslot)
            kb.load(kb.sp, dn.A, dn.A.ap, tabs["o_A"], dn.cslot)
            kb.load(kb.sp, dn.GT, dn.GT.ap, tabs["o_GT"], dn.cslot)
        elif do_mod:
            emit_modprep(kb, dn, cT, wada, badaT, gnT, 0, 144)
            for nm_, bf_ in (("o_mod", dn.mod), ("o_A", dn.A), ("o_GT", dn.GT)):
                kb.store(kb.sp, tabs[nm_], bf_, bf_.ap, dn.cslot2, final=(ekb is None))
        else:
            kb.ew(kb.dve, lambda: nc.vector.memset(dn.mod.ap, 0.0), wr=[dn.mod])
            kb.ew(kb.dve, lambda: nc.vector.memset(dn.A.ap, 1.0), wr=[dn.A])
            kb.ew(kb.dve, lambda: nc.vector.memset(dn.GT.ap, 1.0), wr=[dn.GT])
        dn.pflags = pflags
        for tt in range(ntt):
            n0 = tt * TT
            load_xtile(kb, dn, xT, n0)
            if do_ffn:
                emit_ffn(kb, dn, 0, wfi, wfo, x1T, n0, final_out=(ekb is None))
            if do_proj:
                emit_proj(kb, dn, os_, win, gkv, o, n0, ghalf=ghalf)
        if ekb is None:
            kb.finish()
        else:
            kb.end_phase()
    return nc


BIG = 30000.0
SCALE = 0.125
NU = 256
NPAT_N = 9
GC_W = 576


def _t5_bucket_np(n):
    n = np.maximum(n, 0)
    nf = np.maximum(n, 1).astype(np.float32)
    large = 16 + (np.log(nf / 16) / np.log(128 / 16) * 16).astype(np.int32)
    large = np.minimum(large, 31)
    return np.where(n < 16, n, large)


def _pattern_onehot(dist, valid):
    n = dist.shape[0]
    oh = np.zeros((33, n), np.float32)
    b = _t5_bucket_np(dist)
    idx = np.arange(n)
    v = valid.astype(bool)
    oh[b[v], idx[v]] += 8.0
    oh[31, idx[v]] -= 8.0
    oh[32, idx[~v]] = 1.0
    return oh


def host_consts_nsa(half):
    u = np.arange(NU) - 127
    pats = []
    for r in (-1, 0, 1):
        d = (half - r) * 128 + u
        pats.append(_pattern_onehot(d, d >= 0))
    for r in (-4, -3, -2, -1, 0, 1):
        d = (half - r) * 128 + u
        pats.append(_pattern_onehot(d, (d >= 0) & (d < 512)))
    dc = np.arange(GC_W) - 300
    pats.append(_pattern_onehot(dc, dc >= 0))
    ohg = np.concatenate(pats, axis=1)
    selc = np.zeros((17, 16, 2, 128), np.float32)
    for j in range(16):
        i = 2 * j + half
        for nt in range(2):
            n = nt * 128 + np.arange(128)
            m = n - 8 * i + 9
            m = np.where((m >= 16) | (n >= 255), 16, m)
            ok = m >= 0
            selc[16 - m[ok], j, nt, np.arange(128)[ok]] = 1.0
    cs = np.arange(256) * 16
    ss = np.arange(64) * 64
    ov = ((cs[:, None] < ss[None, :] + 64) & (cs[:, None] + 32 > ss[None, :])).astype(np.float32)
    ov[255] = 0
    ov = ov.reshape(2, 128, 64).transpose(1, 0, 2)
    visb = np.zeros((128, 16, 64), np.float32)
    blk = np.arange(64)
    for j in range(16):
        t = (2 * j + half) * 128 + np.arange(128)
        cur = t // 64
        forced = (blk[None, :] == 0) | (blk[None, :] == cur[:, None]) | (blk[None, :] == cur[:, None] - 1)
        vis = blk[None, :] * 64 <= t[:, None]
        visb[:, j, :] = np.where(vis, np.where(forced, 1e4, 0.0), -1e30)
    bidx = np.zeros((128, 2, 128), np.float32)
    for ri, r in enumerate((0, 1)):
        d = (half - r) * 128 + np.arange(128)[:, None] - np.arange(128)[None, :]
        bidx[:, ri, :] = np.where(d >= 0, 0.0, -1e30)
    return {"ohg": ohg, "selc": selc.reshape(17, 16 * 2 * 128), "ov": ov, "visb": visb, "bidx": bidx}


def host_consts_dsa(half):
    u = np.arange(NU) - 127
    pats = []
    for r in (-1, 0, 1):
        d = (half - r) * 128 + u
        pats.append(_pattern_onehot(d, d >= 0))
    return {"ohg_d": np.concatenate(pats, axis=1)}


class Attn:
    def __init__(self, kb, ident_d, rev_d):
        self.kb = kb
        nc = kb.nc
        self.ident = Buf(kb.sbuf("identb", [128, 128], BF16)[:])
        self.i4 = Buf(kb.sbuf("i4", [128, 4, 128], BF16)[:])
        identf = Buf(kb.sbuf("identf", [128, 128], F32)[:])
        self.identf = identf
        isl = kb.slot()
        kb.load(kb.sp, identf, identf.ap, ident_d, isl)
        kb.ew(kb.dve, lambda: nc.vector.tensor_copy(out=self.ident.ap, in_=identf.ap), wr=[self.ident], rd=[identf])
        revf = Buf(kb.sbuf("revf", [128, 128], F32)[:])
        self.rev = Buf(kb.sbuf("revb", [128, 128], BF16)[:])
        kb.load(kb.sp, revf, revf.ap, rev_d, isl)
        kb.ew(kb.dve, lambda: nc.vector.tensor_copy(out=self.rev.ap, in_=revf.ap), wr=[self.rev], rd=[revf])
        for h in range(4):
            kb.ew(kb.dve, lambda h=h: nc.vector.tensor_copy(out=self.i4.ap[:, h, :], in_=identf.ap), wr=[self.i4], rd=[identf])
        self.ps_s = Ring([Buf(kb.psum("ps_s%d" % i, [128, 512])[:]) for i in range(2)])
        self.ps_o = Ring([Buf(kb.psum("ps_o%d" % i, [128, 512])[:]) for i in range(3)])
        self.E = Ring([Buf(kb.sbuf("Ebuf%d" % i, [128, 512], BF16)[:]) for i in range(3)])
        self.small = Ring([Buf(kb.sbuf("small%d" % i, [128, 16], F32)[:]) for i in range(6)])

    def build_bias(self, relb_d, ohg_d, width, scratch_d):
        kb, nc = self.kb, self.kb.nc
        tab = Buf(kb.sbuf("tabaug", [33, 32], F32)[:])
        sl = kb.slot()
        kb.ew(kb.dve, lambda: nc.vector.memset(tab.ap, -BIG), wr=[tab])
        kb.load(kb.sp, tab, tab.ap[0:32, :], relb_d, sl)
        CH = 512
        stg = Buf(kb.sbuf("ohg_stg", [33, CH], F32)[:])
        gout = Buf(kb.sbuf("g_out", [32, CH], F32)[:])
        sl2, sl3 = kb.slot(), kb.slot()
        psb = self.ps_s.bufs[0]
        for c0 in range(0, width, CH):
            w = min(CH, width - c0)
            kb.load(kb.sp, stg, stg.ap[:, 0:w], ohg_d[:, c0:c0 + w], sl2)
            kb.mm(psb, psb.ap[0:32, 0:w], lhsT=tab.ap, rhs=stg.ap[:, 0:w], start=True, stop=True, rd=[tab, stg])
            kb.ew(kb.dve, lambda w=w: nc.vector.tensor_copy(out=gout.ap[:, 0:w], in_=psb.ap[0:32, 0:w]), wr=[gout], rd=[psb])
            self.g_store = kb.store(kb.sp, scratch_d[:, c0:c0 + w], gout, gout.ap[:, 0:w], sl3)

    def load_skew(self, dst_buf, dst_ap, scratch_t, col0, h0, nh, pstride, slot):
        kb = self.kb
        npart = dst_ap.shape[0]
        src = bass.AP(scratch_t, h0 * scratch_t.shape[1] + col0 - pstride * (npart - 1), ((pstride, npart), (scratch_t.shape[1], nh), (1, 128)))
        return kb.load(kb.pool, dst_buf, dst_ap, src, slot, extra=[self.g_store])

    def attend(self, kt_list, score_mms, o_buf, v_of, nheads=4, vw=65, extra_o=None):
        kb, nc = self.kb, self.kb.nc
        first = True
        for kt in kt_list:
            ps = self.ps_s.next()
            score_mms(kt, ps)
            e = self.E.next()
            kb.ew(kb.act, lambda ps=ps, e=e: nc.scalar.activation(out=e.ap[:, 0:nheads * 128], in_=ps.ap[:, 0:nheads * 128], func=AF.Exp, scale=SCALE),
                  wr=[e], rd=[ps])
            for h in range(nheads):
                vap, vb = v_of(kt, h)
                kb.mm(o_buf, o_buf.ap[:, h * vw:(h + 1) * vw], lhsT=e.ap[:, h * 128:(h + 1) * 128], rhs=vap,
                      start=first, stop=False, rd=[e] + vb)
                first = False
                if extra_o is not None:
                    extra_o(kt, h, e, first)
        return o_buf


def build_B(env=None, prefix=""):
    nc, ekb, dti, dto = _env(env, prefix)
    qn_d = dti("qn", [16, 64, NTOK], BF16)
    qi_d = dti("qi", [16, 32, NTOK], BF16)
    gn_d = dti("gn", [NTOK, 48])
    wi_d = dti("wi", [NTOK, 16])
    ksT_d = dti("ksT", [4, 64, S], BF16)
    kwT_d = dti("kwT", [4, 64, S], BF16)
    kcT_d = dti("kcT", [4, 64, S], BF16)
    vcT_d = dti("vcT", [4, 64, S], BF16)
    vs_d = dti("vs", [S, 256], BF16)
    vw_d = dti("vw", [S, 256], BF16)
    kiT_d = dti("kiT", [32, S], BF16)
    wk1_d, wv1_d = dti("wk1", [2048, 256]), dti("wv1", [2048, 256])
    wk2_d, wv2_d = dti("wk2", [256, 64]), dti("wv2", [256, 64])
    pek_d, pev_d = dti("pekT", [64, 32]), dti("pevT", [64, 32])
    relb_d = dti("relb", [32, 32])
    ohg_d = dti("ohg", [33, NPAT_N * NU + GC_W])
    selc_d = dti("selc", [17, 16 * 2 * 128])
    ov_d = dti("ov", [128, 2, 64])
    visb_d = dti("visb", [128, 16, 64])
    bidx_d = dti("bidx", [128, 2, 128])
    ident_d = dti("ident", [128, 128])
    rev_d = dti("rev", [128, 128])
    onT_d = dto("onT", [1024, NTOK])
    nm_d = dto("negmask", [16, 128, S], BF16)
    gsc_t = nc.dram_tensor(prefix + "gscratch", [32, NPAT_N * NU + GC_W], F32, kind="Internal")
    with ExitStack() as st:
        if ekb is None:
            kb = KB(nc, st)
        else:
            kb = ekb
            kb.begin_phase(prefix, st)
        at = Attn(kb, ident_d, rev_d)
        ld = kb.slot()
        at.build_bias(relb_d, ohg_d, NPAT_N * NU + GC_W, gsc_t.ap())
        ksT = Buf(kb.sbuf("sb_ksT", [64, 4, S], BF16)[:])
        vsA = Buf(kb.sbuf("sb_vsA", [128, 32, 4, 65], BF16)[:])
        kiT = Buf(kb.sbuf("sb_kiT", [32, S], BF16)[:])
        kcmpT = Buf(kb.sbuf("sb_kcmpT", [64, 4, 256], BF16)[:])
        vcA = Buf(kb.sbuf("sb_vcA", [128, 2, 4, 65], BF16)[:])
        ovb = Buf(kb.sbuf("sb_ovb", [128, 2, 64], BF16)[:])
        bsel = Buf(kb.sbuf("sb_bsel", [128, 3, 16, 128], BF16)[:])
        bwin = Buf(kb.sbuf("sb_bwin", [128, 6, 16, 128], BF16)[:])
        pc = Buf(kb.sbuf("sb_pc", [17, 16, 128], BF16)[:])
        selc = Buf(kb.sbuf("sb_selc", [17, 16 * 2 * 128], BF16)[:])
        visb = Buf(kb.sbuf("sb_visb", [128, 16, 64], F32)[:])
        bidx = Buf(kb.sbuf("sb_bidx", [128, 2, 128], F32)[:])
        halfc = Buf(kb.sbuf("sb_halfc", [128, 1], F32)[:])
        kb.ew(kb.dve, lambda: nc.vector.memset(halfc.ap, 0.5), wr=[halfc])
        for g in range(4):
            kb.load(kb.sp, ksT, ksT.ap[:, g, :], ksT_d[g], ld)
        kb.ew(kb.pool, lambda: nc.gpsimd.memset(vsA.ap, 1.0), wr=[vsA])
        kb.ew(kb.pool, lambda: nc.gpsimd.memset(vcA.ap, 0.0), wr=[vcA])
        kb.ew(kb.pool, lambda: nc.gpsimd.memset(vcA.ap[:, :, :, 64:65], 1.0), wr=[vcA])
        for kt in range(32):
            kb.load(kb.sp, vsA, vsA.ap[:, kt, :, 0:64], vs_d[kt * 128:(kt + 1) * 128, :].rearrange("p (g d) -> p g d", d=64), ld)
        kb.load(kb.sp, kiT, kiT.ap, kiT_d, ld)
        kb.load(kb.sp, visb, visb.ap, visb_d, ld)
        kb.load(kb.sp, bidx, bidx.ap, bidx_d, ld)
        ldp = kb.slot()
        kb.load(kb.pool, selc, selc.ap, selc_d, ldp)
        kb.load(kb.pool, ovb, ovb.ap, ov_d, ldp)
        for r in range(3):
            at.load_skew(bsel, bsel.ap[:, r, :, :], gsc_t, r * NU + 127, 0, 16, 1, ldp)
        for r in range(6):
            at.load_skew(bwin, bwin.ap[:, r, :, :], gsc_t, (3 + r) * NU + 127, 0, 16, 1, ldp)
        at.load_skew(pc, pc.ap, gsc_t, NPAT_N * NU + 413, 0, 16, 16, ldp)

        with ExitStack() as cst:
            w1 = Buf(cst.enter_context(nc.sbuf_tensor(kb.prefix + "sb_w1c", [64, 32, 256], BF16))[:])
            w2 = Buf(cst.enter_context(nc.sbuf_tensor(kb.prefix + "sb_w2c", [128, 2, 64], BF16))[:])
            pe2 = Buf(cst.enter_context(nc.sbuf_tensor(kb.prefix + "sb_pe2", [64, 32, 2], BF16))[:])
            pe1 = Buf(cst.enter_context(nc.sbuf_tensor(kb.prefix + "sb_pe1", [64, 32], F32))[:])
            src = Buf(cst.enter_context(nc.sbuf_tensor(kb.prefix + "sb_csrc", [64, S], BF16))[:])
            hb = Buf(cst.enter_context(nc.sbuf_tensor(kb.prefix + "sb_hbias", [128, 2], F32))[:])
            hid = Buf(cst.enter_context(nc.sbuf_tensor(kb.prefix + "sb_chid", [128, 2, 256], BF16))[:])
            lw = kb.slot()
            lsrc = kb.slot()
            psh = at.ps_o.bufs[0]
            psb2 = at.ps_o.bufs[1]
            for which, (w1_d, w2_d, pe_d, srcT_d) in enumerate(((wk1_d, wk2_d, pek_d, kcT_d), (wv1_d, wv2_d, pev_d, vcT_d))):
                kb.load(kb.pool, w1, w1.ap, w1_d.rearrange("(l d) c -> d l c", d=64), lw)
                kb.load(kb.pool, w2, w2.ap, w2_d.rearrange("(k p) c -> p k c", p=128), lw)
                kb.load(kb.sp, pe1, pe1.ap, pe_d, lsrc)
                for r_ in range(2):
                    kb.ew(kb.dve, lambda r_=r_: nc.vector.tensor_copy(out=pe2.ap[:, :, r_], in_=pe1.ap), wr=[pe2], rd=[pe1])
                for c in range(2):
                    for l in range(32):
                        kb.mm(psb2, psb2.ap[:, 2 * c:2 * c + 2], lhsT=w1.ap[:, l, c * 128:(c + 1) * 128], rhs=pe2.ap[:, l, :],
                              start=(c == 0 and l == 0), stop=(l == 31), rd=[w1, pe2])
                kb.ew(kb.dve, lambda: nc.vector.tensor_copy(out=hb.ap, in_=psb2.ap[:, 0:4].rearrange("p (c two) -> p c two", two=2)[:, :, 0]),
                      wr=[hb], rd=[psb2])
                for g in range(4):
                    kb.load(kb.sp, src, src.ap, srcT_d[g], lsrc)
                    for c in range(2):
                        for l in range(32):
                            kb.mm(psh, psh.ap[:, 0:255], lhsT=w1.ap[:, l, c * 128:(c + 1) * 128], rhs=src.ap[:, l:l + 16 * 254 + 1:16],
                                  start=(l == 0), stop=(l == 31), rd=[w1, src])
                        kb.ew(kb.act, lambda c=c: nc.scalar.activation(out=hid.ap[:, c, 0:255], in_=psh.ap[:, 0:255], func=AF.Silu, bias=hb.ap[:, c:c + 1]),
                              wr=[hid], rd=[psh, hb])
                    if which == 0:
                        for c in range(2):
                            kb.mm(psb2, psb2.ap[0:64, 0:255], lhsT=w2.ap[:, c, :], rhs=hid.ap[:, c, 0:255], start=(c == 0), stop=(c == 1), rd=[w2, hid])
                        kb.ew(kb.dve, lambda g=g: nc.vector.memset(kcmpT.ap[:, g, 255:256], 0.0), wr=[kcmpT])
                        kb.ew(kb.dve, lambda g=g: nc.vector.tensor_copy(out=kcmpT.ap[:, g, 0:255], in_=psb2.ap[0:64, 0:255]), wr=[kcmpT], rd=[psb2])
                    else:
                        for nt in range(2):
                            m = 128 if nt == 0 else 127
                            for c in range(2):
                                kb.mm(psb2, psb2.ap[0:m, 0:64], lhsT=hid.ap[:, c, nt * 128:nt * 128 + m], rhs=w2.ap[:, c, :], start=(c == 0), stop=(c == 1),
                                      rd=[w2, hid])
                            kb.ew(kb.dve, lambda g=g, nt=nt, m=m: nc.vector.tensor_copy(out=vcA.ap[0:m, nt, g, 0:64], in_=psb2.ap[0:m, 0:64]), wr=[vcA], rd=[psb2])
            kb.barrier()
        guard = []

        qn = Buf(kb.sbuf("sb_qn_sb", [64, 16, 128], BF16)[:])
        qi = Buf(kb.sbuf("sb_qi_sb", [32, 16, 128], BF16)[:])
        gnb = Buf(kb.sbuf("sb_gn_sb", [128, 48], F32)[:])
        wib = Buf(kb.sbuf("sb_wi_sb", [128, 16], F32)[:])
        kwT = Buf(kb.sbuf("sb_kw_sb", [64, 4, 768], BF16)[:])
        vwA = Buf(kb.sbuf("sb_vw_sb", [128, 6, 4, 65], BF16)[:])
        onacc = Buf(kb.sbuf("sb_onacc", [128, 16, 64], F32)[:])
        imp = Buf(kb.sbuf("sb_imp", [128, 4, 64], F32)[:])
        imp2 = Buf(kb.sbuf("sb_imp2", [128, 4, 64], F32)[:])
        negsel = Buf(kb.sbuf("sb_negsel", [128, 4, 64], BF16)[:])
        nsx = Ring([Buf(kb.sbuf("sb_nsx%d" % i, [128, S], BF16)[:]) for i in range(2)])
        score = Buf(kb.sbuf("sb_score", [128, S], F32)[:])
        nmask = Buf(kb.sbuf("sb_nmask", [128, S], BF16)[:])
        junk = nmask
        rt = Ring([Buf(kb.sbuf("sb_rt%d" % i, [128, 512], F32)[:]) for i in range(2)])
        m8 = Buf(kb.sbuf("sb_m8", [128, 4, 8], F32)[:])
        m8b = Buf(kb.sbuf("sb_m8b", [128, 4, 8], F32)[:])
        bis = Buf(kb.sbuf("sb_bis", [128, 8], F32)[:])
        kb.ew(kb.pool, lambda: nc.gpsimd.memset(vwA.ap, 1.0), wr=[vwA], extra=guard)
        qs, ws_, os1, os2 = kb.slot(), kb.slot(), kb.slot(), kb.slot()
        onst = Ring([Buf(kb.sbuf("sb_onst%d" % i, [128, 512], F32)[:]) for i in range(2)])

        def combine(o_buf, br, g, first_branch):
            sm = at.small.next()
            ov_ = o_buf.ap[:, 0:260].rearrange("p (h c) -> p h c", c=65)
            kb.ew(kb.dve, lambda: nc.vector.tensor_scalar(out=sm.ap[:, 8:12], in0=ov_[:, :, 64], scalar1=1e-30, scalar2=None, op0=ALU.max), wr=[sm], rd=[o_buf])
            kb.ew(kb.dve, lambda: nc.vector.reciprocal(out=sm.ap[:, 0:4], in_=sm.ap[:, 8:12]), wr=[sm])
            kb.ew(kb.dve, lambda: nc.vector.tensor_tensor(out=sm.ap[:, 4:8], in0=sm.ap[:, 0:4], in1=gnb.ap[:, 12 * g + br:12 * g + br + 10:3], op=ALU.mult),
                  wr=[sm], rd=[gnb])
            for h in range(4):
                if first_branch:
                    kb.ew(kb.dve, lambda h=h: nc.vector.tensor_scalar(out=onacc.ap[:, 4 * g + h, :], in0=ov_[:, h, 0:64], scalar1=sm.ap[:, 4 + h:5 + h], scalar2=None,
                                                                       op0=ALU.mult), wr=[onacc], rd=[o_buf, sm])
                else:
                    kb.ew(kb.dve, lambda h=h: nc.vector.scalar_tensor_tensor(out=onacc.ap[:, 4 * g + h, :], in0=ov_[:, h, 0:64], scalar=sm.ap[:, 4 + h:5 + h],
                                                                              in1=onacc.ap[:, 4 * g + h, :], op0=ALU.mult, op1=ALU.add), wr=[onacc], rd=[o_buf, sm])
            return sm

        for j in range(16):
            tsl = slice(j * 128, (j + 1) * 128)
            nkt = 2 * j + 2
            N = nkt * 128
            kb.load(kb.sp, qn, qn.ap, qn_d[:, :, tsl].rearrange("h d t -> d h t"), qs)
            kb.load(kb.sp, qi, qi.ap, qi_d[:, :, tsl].rearrange("h d t -> d h t"), qs)
            kb.load(kb.sp, gnb, gnb.ap, gn_d[tsl, :], qs)
            kb.load(kb.sp, wib, wib.ap, wi_d[tsl, :], qs)
            wk0 = max(0, 2 * j - 4)
            nwk = 2 * j + 2 - wk0
            woff = 6 - nwk
            for g in range(4):
                kb.load(kb.sp, kwT, kwT.ap[:, g, woff * 128:768], kwT_d[g][:, wk0 * 128:(2 * j + 2) * 128], ws_)
            for kl in range(woff, 6):
                kt_ = wk0 + kl - woff
                kb.load(kb.sp, vwA, vwA.ap[:, kl, :, 0:64], vw_d[kt_ * 128:(kt_ + 1) * 128, :].rearrange("p (g d) -> p g d", d=64), ws_)

            for g in range(4):
                o_c = at.ps_o.next()
                o_i = at.ps_o.next()
                first_i = [True]

                def sc_cmp(nt, ps, g=g):
                    kb.mm(ps, ps.ap, lhsT=kcmpT.ap[:, g, nt * 128:(nt + 1) * 128], rhs=qn.ap[:, 4 * g:4 * g + 4, :], start=True, stop=False, rd=[kcmpT, qn])
                    kb.mm(ps, ps.ap, lhsT=selc.ap[:, (j * 2 + nt) * 128:(j * 2 + nt + 1) * 128], rhs=pc.ap[:, 4 * g:4 * g + 4, :], start=False, stop=True,
                          rd=[selc, pc])

                def imp_mm(nt, h, e, _f, o_i=o_i, first_i=first_i):
                    kb.mm(o_i, o_i.ap[:, h * 64:(h + 1) * 64], lhsT=e.ap[:, h * 128:(h + 1) * 128], rhs=ovb.ap[:, nt, :], start=first_i[0], stop=False, rd=[e, ovb])
                    first_i[0] = False

                at.attend([0, 1], sc_cmp, o_c, lambda nt, h, g=g: (vcA.ap[:, nt, g, :], [vcA]), extra_o=imp_mm)
                sm = combine(o_c, 0, g, True)
                for h in range(4):
                    if h == 0:
                        kb.ew(kb.dve, lambda h=h: nc.vector.tensor_scalar(out=imp.ap[:, g, :], in0=o_i.ap[:, 0:64], scalar1=sm.ap[:, 0:1], scalar2=None, op0=ALU.mult),
                              wr=[imp], rd=[o_i, sm])
                    else:
                        kb.ew(kb.dve, lambda h=h: nc.vector.scalar_tensor_tensor(out=imp.ap[:, g, :], in0=o_i.ap[:, h * 64:(h + 1) * 64], scalar=sm.ap[:, h:h + 1],
                                                                                  in1=imp.ap[:, g, :], op0=ALU.mult, op1=ALU.add), wr=[imp], rd=[o_i, sm])
            for g in range(4):
                kb.ew(kb.dve, lambda g=g: nc.vector.tensor_tensor(out=imp.ap[:, g, :], in0=imp.ap[:, g, :], in1=visb.ap[:, j, :], op=ALU.add), wr=[imp], rd=[visb])
                kb.ew(kb.dve, lambda g=g: nc.vector.max(out=m8.ap[:, g, :], in_=imp.ap[:, g, :]), wr=[m8], rd=[imp])
                kb.ew(kb.dve, lambda g=g: nc.vector.match_replace(out=imp2.ap[:, g, :], in_to_replace=m8.ap[:, g, :], in_values=imp.ap[:, g, :], imm_value=-3e38),
                      wr=[imp2], rd=[imp, m8])
                kb.ew(kb.dve, lambda g=g: nc.vector.max(out=m8b.ap[:, g, :], in_=imp2.ap[:, g, :]), wr=[m8b], rd=[imp2])
                kb.ew(kb.dve, lambda g=g: nc.vector.tensor_scalar(out=negsel.ap[:, g, :], in0=imp.ap[:, g, :], scalar1=m8b.ap[:, g, 7:8], scalar2=-BIG,
                                                                   op0=ALU.is_lt, op1=ALU.mult), wr=[negsel], rd=[imp, m8b])

            for g in range(4):
                nx = nsx.next()
                kb.ew(kb.pool, lambda g=g, nx=nx: nc.gpsimd.tensor_copy(
                    out=nx.ap[:, 0:N].rearrange("p (b k) -> p b k", k=64),
                    in_=negsel.ap[:, g, 0:N // 64].unsqueeze(2).broadcast_to([128, N // 64, 64])), wr=[nx], rd=[negsel])
                o_s = at.ps_o.next()

                def sc_sel(kt, ps, g=g, nx=nx):
                    r = kt - 2 * j
                    kb.mm(ps, ps.ap, lhsT=ksT.ap[:, g, kt * 128:(kt + 1) * 128], rhs=qn.ap[:, 4 * g:4 * g + 4, :], start=True, stop=False, rd=[ksT, qn])
                    near = r >= -1
                    kb.mm(ps, ps.ap, lhsT=nx.ap[:, kt * 128:(kt + 1) * 128], rhs=at.i4.ap, start=False, stop=not near, rd=[nx, at.i4])
                    if near:
                        kb.mm(ps, ps.ap, lhsT=at.rev.ap, rhs=bsel.ap[:, r + 1, 4 * g:4 * g + 4, :], start=False, stop=True, rd=[at.rev, bsel])

                at.attend(list(range(nkt)), sc_sel, o_s, lambda kt, h, g=g: (vsA.ap[:, kt, g, :], [vsA]))
                combine(o_s, 1, g, False)
                o_w = at.ps_o.next()

                def sc_win(kl, ps, g=g):
                    kb.mm(ps, ps.ap, lhsT=kwT.ap[:, g, kl * 128:(kl + 1) * 128], rhs=qn.ap[:, 4 * g:4 * g + 4, :], start=True, stop=False, rd=[kwT, qn])
                    kb.mm(ps, ps.ap, lhsT=at.rev.ap, rhs=bwin.ap[:, kl, 4 * g:4 * g + 4, :], start=False, stop=True, rd=[at.rev, bwin])

                at.attend(list(range(woff, 6)), sc_win, o_w, lambda kl, h, g=g: (vwA.ap[:, kl, g, :], [vwA]))
                combine(o_w, 2, g, False)
            for c4 in range(2):
                pst = at.ps_s.next()
                for cc in range(4):
                    c = c4 * 4 + cc
                    kb.mm_t(pst, pst.ap[:, cc * 128:(cc + 1) * 128], onacc.ap[:, 2 * c:2 * c + 2, :].rearrange("p h d -> p (h d)"), at.identf.ap, first=(cc == 0),
                            rd=[onacc, at.identf])
                tb = onst.next()
                kb.ew(kb.act, lambda pst=pst, tb=tb: nc.scalar.copy(out=tb.ap, in_=pst.ap), wr=[tb], rd=[pst])
                kb.store(kb.sp, onT_d[c4 * 512:(c4 + 1) * 512, tsl].rearrange("(c p) t -> p c t", p=128), tb, tb.ap.rearrange("p (c t) -> p c t", t=128), os1,
                         final=(ekb is None))

            nch = (N + 511) // 512
            for h in range(16):
                for c in range(nch):
                    w = min(512, N - c * 512)
                    ps = at.ps_s.next()
                    kb.mm(ps, ps.ap[:, 0:w], lhsT=qi.ap[:, h, :], rhs=kiT.ap[:, c * 512:c * 512 + w], start=True, stop=True, rd=[qi, kiT])
                    t = rt.next()
                    kb.ew(kb.act, lambda ps=ps, t=t, w=w: nc.scalar.activation(out=t.ap[:, 0:w], in_=ps.ap[:, 0:w], func=AF.Relu), wr=[t], rd=[ps])
                    if h == 0:
                        kb.ew(kb.dve, lambda t=t, c=c, w=w: nc.vector.tensor_scalar(out=score.ap[:, c * 512:c * 512 + w], in0=t.ap[:, 0:w], scalar1=wib.ap[:, 0:1],
                                                                                    scalar2=None, op0=ALU.mult), wr=[score], rd=[t, wib])
                    else:
                        kb.ew(kb.dve, lambda t=t, c=c, w=w, h=h: nc.vector.scalar_tensor_tensor(
                            out=score.ap[:, c * 512:c * 512 + w], in0=t.ap[:, 0:w], scalar=wib.ap[:, h:h + 1], in1=score.ap[:, c * 512:c * 512 + w],
                            op0=ALU.mult, op1=ALU.add), wr=[score], rd=[t, wib])
            kb.ew(kb.dve, lambda: nc.vector.tensor_reduce(out=bis.ap[:, 1:2], in_=score.ap[:, 0:N], axis=AX.X, op=ALU.max, apply_absolute_value=True),
                  wr=[bis], rd=[score])
            kb.ew(kb.dve, lambda: nc.vector.tensor_scalar(out=bis.ap[:, 1:2], in0=bis.ap[:, 1:2], scalar1=1.001, scalar2=1e-6, op0=ALU.mult, op1=ALU.add), wr=[bis])
            kb.ew(kb.dve, lambda: nc.vector.tensor_scalar(out=bis.ap[:, 0:1], in0=bis.ap[:, 1:2], scalar1=-1.0, scalar2=None, op0=ALU.mult), wr=[bis])
            kb.ew(kb.dve, lambda: nc.vector.memset(bis.ap[:, 2:3], 0.0), wr=[bis])
            kb.ew(kb.dve, lambda: nc.vector.tensor_tensor(out=score.ap[:, N - 256:N], in0=score.ap[:, N - 256:N], in1=bidx.ap.rearrange("p r k -> p (r k)"), op=ALU.add),
                  wr=[score], rd=[bidx])
            for it in range(22):
                kb.ew(kb.dve, lambda: nc.vector.tensor_scalar(out=junk.ap[:, 0:N], in0=score.ap[:, 0:N], scalar1=bis.ap[:, 2:3], scalar2=None, op0=ALU.is_ge,
                                                               op1=ALU.add, accum_out=bis.ap[:, 3:4]), wr=[junk, bis], rd=[score])
                kb.ew(kb.dve, lambda: nc.vector.tensor_scalar(out=bis.ap[:, 4:5], in0=bis.ap[:, 3:4], scalar1=255.5, scalar2=None, op0=ALU.is_ge), wr=[bis])
                kb.ew(kb.dve, lambda: nc.vector.tensor_tensor(out=bis.ap[:, 5:6], in0=bis.ap[:, 2:3], in1=bis.ap[:, 0:1], op=ALU.subtract), wr=[bis])
                kb.ew(kb.dve, lambda: nc.vector.tensor_tensor(out=bis.ap[:, 6:7], in0=bis.ap[:, 1:2], in1=bis.ap[:, 2:3], op=ALU.subtract), wr=[bis])
                kb.ew(kb.dve, lambda: nc.vector.scalar_tensor_tensor(out=bis.ap[:, 0:1], in0=bis.ap[:, 5:6], scalar=bis.ap[:, 4:5], in1=bis.ap[:, 0:1],
                                                                      op0=ALU.mult, op1=ALU.add), wr=[bis])
                kb.ew(kb.dve, lambda: nc.vector.scalar_tensor_tensor(out=bis.ap[:, 1:2], in0=bis.ap[:, 6:7], scalar=bis.ap[:, 4:5], in1=bis.ap[:, 2:3],
                                                                      op0=ALU.mult, op1=ALU.add), wr=[bis])
                kb.ew(kb.dve, lambda: nc.vector.scalar_tensor_tensor(out=bis.ap[:, 2:3], in0=bis.ap[:, 0:1], scalar=bis.ap[:, 1:2], in1=halfc.ap[:, 0:1],
                                                                      op0=ALU.add, op1=ALU.mult), wr=[bis], rd=[halfc])
            kb.ew(kb.dve, lambda: nc.vector.tensor_scalar(out=nmask.ap[:, 0:N], in0=score.ap[:, 0:N], scalar1=bis.ap[:, 0:1], scalar2=-BIG, op0=ALU.is_lt, op1=ALU.mult),
                  wr=[nmask], rd=[score, bis])
            kb.store(kb.sp, nm_d[j, :, 0:N], nmask, nmask.ap[:, 0:N], os2, final=(ekb is None))
        if ekb is None:
            kb.finish()
        else:
            kb.end_phase()
    return nc


def build_C(env=None, prefix=""):
    nc, ekb, dti, dto = _env(env, prefix)
    qd_d = dti("qd", [16, 64, NTOK], BF16)
    cnT_d = dti("cnT", [256, S], BF16)
    wuk_d = dti("wuk", [256, 1024])
    wuv_d = dti("wuv", [256, 1024])
    nm_d = dti("negmask", [16, 128, S], BF16)
    relb_d = dti("relb", [32, 32])
    ohg_d = dti("ohg_d", [33, 3 * NU])
    ident_d = dti("ident", [128, 128])
    rev_d = dti("rev", [128, 128])
    odT_d = dto("odT", [1024, NTOK])
    gsc_t = nc.dram_tensor(prefix + "gscratch_d", [32, 3 * NU], F32, kind="Internal")
    with ExitStack() as st:
        if ekb is None:
            kb = KB(nc, st)
        else:
            kb = ekb
            kb.begin_phase(prefix, st)
        at = Attn(kb, ident_d, rev_d)
        at.build_bias(relb_d, ohg_d, 3 * NU, gsc_t.ap())
        ld, ldp, qs, ms, os1 = kb.slot(), kb.slot(), kb.slot(), kb.slot(), kb.slot()
        cnT = Buf(kb.sbuf("sc_cnT", [128, 2, S], BF16)[:])
        wuk = Buf(kb.sbuf("sc_wuk", [128, 2, 1024], BF16)[:])
        wuv = Buf(kb.sbuf("sc_wuv", [128, 2, 1024], BF16)[:])
        bd = Buf(kb.sbuf("sc_bd", [128, 3, 16, 128], BF16)[:])
        kdT = Buf(kb.sbuf("sc_kdT", [64, 4, S], BF16)[:])
        vdA = Buf(kb.sbuf("sc_vdA", [128, 32, 4, 65], BF16)[:])
        qd = Buf(kb.sbuf("sc_qd", [64, 4, 128], BF16)[:])
        nmk = Ring([Buf(kb.sbuf("sc_nm%d" % i, [128, S], BF16)[:]) for i in range(2)])
        odb = Ring([Buf(kb.sbuf("sc_od%d" % i, [128, 4, 64], F32)[:]) for i in range(2)])
        odst = Ring([Buf(kb.sbuf("sc_odst%d" % i, [128, 256], F32)[:]) for i in range(2)])
        for c in range(2):
            kb.load(kb.sp, cnT, cnT.ap[:, c, :], cnT_d[c * 128:(c + 1) * 128, :], ld)
        kb.load(kb.pool, wuk, wuk.ap, wuk_d.rearrange("(k p) c -> p k c", p=128), ldp)
        kb.load(kb.pool, wuv, wuv.ap, wuv_d.rearrange("(k p) c -> p k c", p=128), ldp)
        for r in range(3):
            at.load_skew(bd, bd.ap[:, r, :, :], gsc_t, r * NU + 127, 16, 16, 1, ldp)
        kb.ew(kb.pool, lambda: nc.gpsimd.memset(vdA.ap, 1.0), wr=[vdA])
        flip = [0]
        for hg in range(4):
            for h in range(4):
                hh = hg * 4 + h
                for c8 in range(S // 512):
                    ps = at.ps_s.next()
                    for c in range(2):
                        kb.mm(ps, ps.ap[0:64, :], lhsT=wuk.ap[:, c, hh * 64:(hh + 1) * 64], rhs=cnT.ap[:, c, c8 * 512:(c8 + 1) * 512],
                              start=(c == 0), stop=(c == 1), rd=[wuk, cnT])
                    flip[0] ^= 1
                    if flip[0]:
                        kb.ew(kb.act, lambda ps=ps, h=h, c8=c8: nc.scalar.copy(out=kdT.ap[:, h, c8 * 512:(c8 + 1) * 512], in_=ps.ap[0:64, :]), wr=[kdT], rd=[ps])
                    else:
                        kb.ew(kb.dve, lambda ps=ps, h=h, c8=c8: nc.vector.tensor_copy(out=kdT.ap[:, h, c8 * 512:(c8 + 1) * 512], in_=ps.ap[0:64, :]), wr=[kdT], rd=[ps])
            for kt in range(32):
                ps = at.ps_s.next()
                for c in range(2):
                    kb.mm(ps, ps.ap[:, 0:256], lhsT=cnT.ap[:, c, kt * 128:(kt + 1) * 128], rhs=wuv.ap[:, c, hg * 256:(hg + 1) * 256],
                          start=(c == 0), stop=(c == 1), rd=[wuv, cnT])
                kb.ew(kb.dve, lambda ps=ps, kt=kt: nc.vector.tensor_copy(out=vdA.ap[:, kt, :, 0:64], in_=ps.ap[:, 0:256].rearrange("p (h d) -> p h d", d=64)),
                      wr=[vdA], rd=[ps])
            for j in range(16):
                tsl = slice(j * 128, (j + 1) * 128)
                nkt = 2 * j + 2
                N = nkt * 128
                kb.load(kb.sp, qd, qd.ap, qd_d[hg * 4:(hg + 1) * 4, :, tsl].rearrange("h d t -> d h t"), qs)
                nm = nmk.next()
                kb.load(kb.sp, nm, nm.ap[:, 0:N], nm_d[j, :, 0:N], ms)
                o_d = at.ps_o.next()

                def sc_d(kt, ps, nm=nm, j=j):
                    r = kt - 2 * j
                    for h in range(4):
                        kb.mm(ps, ps.ap[:, h * 128:(h + 1) * 128], lhsT=kdT.ap[:, h, kt * 128:(kt + 1) * 128], rhs=qd.ap[:, h, :], start=(h == 0), stop=False,
                              rd=[kdT, qd])
                    near = r >= -1
                    kb.mm(ps, ps.ap, lhsT=nm.ap[:, kt * 128:(kt + 1) * 128], rhs=at.i4.ap, start=False, stop=not near, rd=[nm, at.i4])
                    if near:
                        kb.mm(ps, ps.ap, lhsT=at.rev.ap, rhs=bd.ap[:, r + 1, hg * 4:(hg + 1) * 4, :], start=False, stop=True, rd=[at.rev, bd])

                at.attend(list(range(nkt)), sc_d, o_d, lambda kt, h: (vdA.ap[:, kt, h, :], [vdA]))
                sm = at.small.next()
                ob = odb.next()
                ov_ = o_d.ap[:, 0:260].rearrange("p (h c) -> p h c", c=65)
                kb.ew(kb.dve, lambda: nc.vector.reciprocal(out=sm.ap[:, 0:4], in_=ov_[:, :, 64]), wr=[sm], rd=[o_d])
                for h in range(4):
                    kb.ew(kb.dve, lambda h=h: nc.vector.tensor_scalar(out=ob.ap[:, h, :], in0=ov_[:, h, 0:64], scalar1=sm.ap[:, h:h + 1], scalar2=None, op0=ALU.mult),
                          wr=[ob], rd=[o_d, sm])
                pst = at.ps_s.next()
                for cc in range(2):
                    kb.mm_t(pst, pst.ap[:, cc * 128:(cc + 1) * 128], ob.ap[:, 2 * cc:2 * cc + 2, :].rearrange("p h d -> p (h d)"), at.identf.ap, first=(cc == 0),
                            rd=[ob, at.identf])
                tb = odst.next()
                kb.ew(kb.act, lambda pst=pst, tb=tb: nc.scalar.copy(out=tb.ap, in_=pst.ap[:, 0:256]), wr=[tb], rd=[pst])
                kb.store(kb.sp, odT_d[hg * 256:(hg + 1) * 256, tsl].rearrange("(c p) t -> p c t", p=128), tb, tb.ap.rearrange("p (c t) -> p c t", t=128), os1,
                         final=(ekb is None))
        if ekb is None:
            kb.finish()
        else:
            kb.end_phase()
    return nc


def build_D(last=False, env=None, prefix=""):
    nc, ekb, dti, dto = _env(env, prefix)
    x1T = dti("x1T", [D, NTOK])
    onT = dti("onT", [1024, NTOK])
    odT = dti("odT", [1024, NTOK])
    amT = dti("amT", [4096, NTOK])
    mod_d = dti("modt_in", [128, 144])
    A_d = dti("A_in", [128, 48])
    GT_d = dti("GT_in", [128, 48])
    wun = dti("wun", [1024, D])
    wud = dti("wud", [1024, D])
    wo = dti("wo", [D, D])
    wfi = dti("wfi", [D, 2 * FF])
    wfo = dti("wfo", [FF, D])
    if last:
        gfT = dti("gfT", [128, KC])
    x3T = dto("x3T", [D, NTOK])
    with ExitStack() as st:
        if ekb is None:
            kb = KB(nc, st)
        else:
            kb = ekb
            kb.begin_phase(prefix, st)
        dn = Dense(kb)
        kb.load(kb.sp, dn.mod, dn.mod.ap, mod_d, dn.cslot)
        kb.load(kb.sp, dn.A, dn.A.ap, A_d, dn.cslot)
        kb.load(kb.sp, dn.GT, dn.GT.ap, GT_d, dn.cslot)
        a_st = Ring([Buf(kb.sbuf("a_st%d" % i, [128, 512], F32)[:]) for i in range(4)])
        a_sl = [kb.slot() for _ in range(4)]
        osl = kb.slot()
        if last:
            gf = Buf(kb.sbuf("gf", [128, KC], F32)[:])
            kb.load(kb.sp, gf, gf.ap, gfT, dn.cslot)
        for tt in range(NTT):
            n0 = tt * TT
            sl_tok = slice(n0, n0 + TT)
            load_xtile(kb, dn, x1T, n0)
            for k in range(8):
                kb.load(kb.pool, dn.hid[k], dn.hid[k].ap, onT[k * 128:(k + 1) * 128, sl_tok], osl)
                kb.load(kb.pool, dn.hid[8 + k], dn.hid[8 + k].ap, odT[k * 128:(k + 1) * 128, sl_tok], osl)
            hold = {}

            def ep_up(pi, st_, psb):
                which, m = pi % 2, pi // 2
                if which == 0:
                    hold[m] = psb
                    return
                psn = hold.pop(m)
                a0, a1 = a_st.next(), a_st.next()
                i0 = (a_st.i - 2) % 4
                i1 = (a_st.i - 1) % 4
                kb.load(kb.sp, a0, a0.ap, amT[m * 128:(m + 1) * 128, sl_tok], a_sl[i0])
                kb.load(kb.sp, a1, a1.ap, amT[2048 + m * 128:2048 + (m + 1) * 128, sl_tok], a_sl[i1])
                kb.ew(kb.dve, lambda: nc.vector.tensor_tensor(out=a0.ap, in0=a0.ap, in1=psn.ap, op=ALU.mult), wr=[a0], rd=[psn])
                kb.ew(kb.dve, lambda: nc.vector.tensor_tensor(out=a1.ap, in0=a1.ap, in1=psb.ap, op=ALU.mult), wr=[a1], rd=[psb])
                kb.ew(kb.dve, lambda: nc.vector.tensor_tensor(out=dn.h[m].ap, in0=a0.ap, in1=a1.ap, op=ALU.add), wr=[dn.h[m]], rd=[a0, a1])

            for m0 in range(0, KC, 4):
                srcn = lambda kc, st_: (dn.hid[kc].ap, [dn.hid[kc]])
                srcd = lambda kc, st_: (dn.hid[8 + kc].ap, [dn.hid[8 + kc]])
                emit_gemm_fm(kb, dn, wun, 8, [((m0 + i) * 128, 128) for i in range(4)], srcn, lambda pi, st_, psb, m0=m0: ep_up(2 * (m0 + pi), st_, psb))
                emit_gemm_fm(kb, dn, wud, 8, [((m0 + i) * 128, 128) for i in range(4)], srcd, lambda pi, st_, psb, m0=m0: ep_up(2 * (m0 + pi) + 1, st_, psb))

            def ep_o(pi, st_, psb):
                xb = dn.xres[pi]
                kb.ew(kb.dve, lambda: nc.vector.scalar_tensor_tensor(out=xb.ap, in0=psb.ap, scalar=dn.GT.ap[:, 16 + pi:17 + pi], in1=xb.ap, op0=ALU.mult, op1=ALU.add),
                      wr=[xb], rd=[psb, dn.GT])

            ysrc = lambda kc, st_: (dn.h[kc].ap, [dn.h[kc]])
            emit_gemm_fm(kb, dn, wo, KC, [(m * 128, 128) for m in range(KC)], ysrc, ep_o)
            emit_ffn(kb, dn, 2, wfi, wfo, None if last else x3T, n0, final_out=(ekb is None))
            if last:
                for kc in range(KC):
                    sq = dn.sq.next()
                    kb.ew(kb.act, lambda kc=kc, sq=sq: nc.scalar.activation(out=sq.ap, in_=dn.xres[kc].ap, func=AF.Square), wr=[sq], rd=[dn.xres[kc]])
                    kb.mm(dn.ps_stat, dn.ps_stat.ap, lhsT=dn.ones.ap, rhs=sq.ap, start=(kc == 0), stop=(kc == KC - 1), rd=[dn.ones, sq])
                t = dn.tmp.next()
                kb.ew(kb.act, lambda: nc.scalar.activation(out=t.ap, in_=dn.ps_stat.ap, func=AF.Sqrt, scale=1.0 / D, bias=dn.epsb.ap[:, 0:1]), wr=[t], rd=[dn.ps_stat, dn.epsb])
                kb.ew(kb.dve, lambda: nc.vector.reciprocal(out=dn.rstd.ap, in_=t.ap), wr=[dn.rstd], rd=[t])
                for kc in range(KC):
                    xb = dn.xres[kc]
                    kb.ew(kb.dve, lambda kc=kc, xb=xb: nc.vector.scalar_tensor_tensor(out=xb.ap, in0=xb.ap, scalar=gf.ap[:, kc:kc + 1], in1=dn.rstd.ap,
                                                                                      op0=ALU.mult, op1=ALU.mult), wr=[xb], rd=[gf, dn.rstd])
                    kb.store(kb.sp, x3T[kc * 128:(kc + 1) * 128, sl_tok], xb, xb.ap, dn.oslot, final=True)
        if ekb is None:
            kb.finish()
        else:
            kb.end_phase()
    return nc


DEPTH = 2


def build_fused(depth=DEPTH):
    nc = bass.Bass("TRN2", target_bir_lowering=False)
    gi = lambda name, shape, d=F32: nc.dram_tensor(name, shape, d, kind="ExternalInput").ap()
    sc = lambda name, shape, d=F32: nc.dram_tensor(name, shape, d, kind="Internal").ap()
    I = {}
    for h in range(2):
        I["xT%d" % h] = gi("xT%d" % h, [D, NTOK])
        for nm, shp in (("ohg", [33, NPAT_N * NU + GC_W]), ("selc", [17, 16 * 2 * 128]), ("visb", [128, 16, 64]), ("bidx", [128, 2, 128]), ("ohg_d", [33, 3 * NU])):
            I["%s%d" % (nm, h)] = gi("%s%d" % (nm, h), shp)
    I["ov"] = gi("ov", [128, 2, 64])
    I["ident"] = gi("ident", [128, 128])
    I["rev"] = gi("rev", [128, 128])
    I["cT"] = gi("cT", [128, KC])
    I["relb"] = gi("relb", [32, 32])
    I["gfT"] = gi("gfT", [128, KC])
    for nm, shp in (("w_ada", [D, 9 * D]), ("badaT", [128, 144]), ("gnT", [128, 48]), ("w_in", [D, N_IN]), ("gkvT", [128, 2]),
                    ("wk1", [2048, 256]), ("wv1", [2048, 256]), ("wk2", [256, 64]), ("wv2", [256, 64]), ("pekT", [64, 32]), ("pevT", [64, 32]),
                    ("wuk", [256, 1024]), ("wuv", [256, 1024]), ("wun", [1024, D]), ("wud", [1024, D]), ("wo", [D, D])):
        I[nm] = gi(nm, [depth] + shp)
    I["w_ffn_in"] = gi("w_ffn_in", [depth, 2, D, 2 * FF])
    I["w_ffn_out"] = gi("w_ffn_out", [depth, 2, FF, D])
    outs = [nc.dram_tensor("out%d" % h, [D, NTOK], F32, kind="ExternalOutput").ap() for h in range(2)]
    Sx = {}
    for h in range(2):
        for nm, shp, d in (("x1T", [D, NTOK], F32), ("xn", [D, NTOK], F32), ("qn", [16, 64, NTOK], BF16), ("qi", [16, 32, NTOK], BF16),
                           ("qd", [16, 64, NTOK], BF16), ("gn", [NTOK, 48], F32), ("wi", [NTOK, 16], F32), ("am", [4096, NTOK], F32),
                           ("onT", [1024, NTOK], F32), ("odT", [1024, NTOK], F32), ("negmask", [16, 128, S], BF16)):
            Sx["%s%d" % (nm, h)] = sc("s_%s%d" % (nm, h), shp, d)
    for nm, shp, d in (("ks", [4, 64, S], BF16), ("kw", [4, 64, S], BF16), ("kc", [4, 64, S], BF16), ("vc", [4, 64, S], BF16),
                       ("vs", [S, 256], BF16), ("vw", [S, 256], BF16), ("ki", [32, S], BF16), ("cn", [256, S], BF16),
                       ("mod", [128, 144], F32), ("A", [128, 48], F32), ("GT", [128, 48], F32)):
        Sx[nm] = sc("s_" + nm, shp, d)
    with ExitStack() as st:
        kb = KB(nc, st)
        kb.setup_fused()
        for l in range(depth):
            last = (l == depth - 1)
            for h in range(2):
                T = {"xT": I["xT%d" % h] if l == 0 else Sx["xn%d" % h], "cT": I["cT"], "wada": I["w_ada"][l], "badaT": I["badaT"][l], "gnT": I["gnT"][l],
                     "wfi": I["w_ffn_in"][l, 0], "wfo": I["w_ffn_out"][l, 0], "win": I["w_in"][l], "gkvT": I["gkvT"][l], "x1T": Sx["x1T%d" % h],
                     "o_qn": Sx["qn%d" % h], "o_qi": Sx["qi%d" % h], "o_qd": Sx["qd%d" % h], "o_gn": Sx["gn%d" % h], "o_wi": Sx["wi%d" % h], "o_am": Sx["am%d" % h],
                     "o_kc": Sx["kc"], "o_vc": Sx["vc"], "o_ks": Sx["ks"], "o_kw": Sx["kw"], "o_ki": Sx["ki"], "o_cn": Sx["cn"], "o_vs": Sx["vs"], "o_vw": Sx["vw"],
                     "o_mod": Sx["mod"], "o_A": Sx["A"], "o_GT": Sx["GT"]}
                build_A(env={"nc": nc, "kb": kb, "T": T}, prefix="L%dA%d_" % (l, h), ghalf=h, load_tabs=(h == 1))
            for h in range(2):
                T = {"qn": Sx["qn%d" % h], "qi": Sx["qi%d" % h], "gn": Sx["gn%d" % h], "wi": Sx["wi%d" % h], "ksT": Sx["ks"], "kwT": Sx["kw"], "kcT": Sx["kc"],
                     "vcT": Sx["vc"], "vs": Sx["vs"], "vw": Sx["vw"], "kiT": Sx["ki"], "wk1": I["wk1"][l], "wv1": I["wv1"][l], "wk2": I["wk2"][l], "wv2": I["wv2"][l],
                     "pekT": I["pekT"][l], "pevT": I["pevT"][l], "relb": I["relb"], "ohg": I["ohg%d" % h], "selc": I["selc%d" % h], "ov": I["ov"],
                     "visb": I["visb%d" % h], "bidx": I["bidx%d" % h], "ident": I["ident"], "rev": I["rev"], "onT": Sx["onT%d" % h], "negmask": Sx["negmask%d" % h]}
                build_B(env={"nc": nc, "kb": kb, "T": T}, prefix="L%dB%d_" % (l, h))
            for h in range(2):
                T = {"qd": Sx["qd%d" % h], "cnT": Sx["cn"], "wuk": I["wuk"][l], "wuv": I["wuv"][l], "negmask": Sx["negmask%d" % h], "relb": I["relb"],
                     "ohg_d": I["ohg_d%d" % h], "ident": I["ident"], "rev": I["rev"], "odT": Sx["odT%d" % h]}
                build_C(env={"nc": nc, "kb": kb, "T": T}, prefix="L%dC%d_" % (l, h))
            for h in range(2):
                T = {"x1T": Sx["x1T%d" % h], "onT": Sx["onT%d" % h], "odT": Sx["odT%d" % h], "amT": Sx["am%d" % h], "modt_in": Sx["mod"], "A_in": Sx["A"],
                     "GT_in": Sx["GT"], "wun": I["wun"][l], "wud": I["wud"][l], "wo": I["wo"][l], "wfi": I["w_ffn_in"][l, 1], "wfo": I["w_ffn_out"][l, 1],
                     "gfT": I["gfT"], "x3T": outs[h] if last else Sx["xn%d" % h]}
                build_D(last, env={"nc": nc, "kb": kb, "T": T}, prefix="L%dD%d_" % (l, h))
        kb.finish()
    return nc


_PROG = []


def _core_tokens(a, b, half):
    t = a[b].reshape(32, 128, *a.shape[2:])[half::2]
    return t.reshape(NTOK, *a.shape[2:])


def kernel(x, c, w_ada, b_ada, g_norm, w_ffn_in, w_ffn_out, w_in, nsa_pe_k, nsa_pe_v, nsa_cmp_k1, nsa_cmp_k2,
           nsa_cmp_v1, nsa_cmp_v2, dsa_g_kv, dsa_w_uk, dsa_w_uv, w_up_nsa, w_up_dsa, w_out, rel_bias, g_final):
    f32 = np.float32
    A = lambda v: np.ascontiguousarray(np.asarray(v, f32))
    x = A(x)
    c = A(c)
    depth = np.asarray(w_ada).shape[0]
    if not _PROG:
        _PROG.append(build_fused(depth))
    nc = _PROG[0]
    ident = np.eye(128, dtype=f32)
    shared = {"ident": ident, "rev": np.ascontiguousarray(ident[::-1]), "relb": A(rel_bias), "gfT": A(np.asarray(g_final).reshape(16, 128).T),
              "w_ada": A(w_ada), "badaT": A(np.asarray(b_ada).reshape(depth, 144, 128).transpose(0, 2, 1)),
              "gnT": A(np.asarray(g_norm).reshape(depth, 48, 128).transpose(0, 2, 1)), "w_in": A(w_in),
              "gkvT": A(np.asarray(dsa_g_kv).reshape(depth, 2, 128).transpose(0, 2, 1)),
              "wk1": A(nsa_cmp_k1), "wv1": A(nsa_cmp_v1), "wk2": A(nsa_cmp_k2), "wv2": A(nsa_cmp_v2),
              "pekT": A(np.asarray(nsa_pe_k).transpose(0, 2, 1)), "pevT": A(np.asarray(nsa_pe_v).transpose(0, 2, 1)),
              "wuk": A(np.asarray(dsa_w_uk).reshape(depth, 256, 1024)), "wuv": A(np.asarray(dsa_w_uv).reshape(depth, 256, 1024)),
              "wun": A(w_up_nsa), "wud": A(w_up_dsa), "wo": A(w_out), "w_ffn_in": A(w_ffn_in), "w_ffn_out": A(w_ffn_out)}
    for h in range(2):
        cn_, cd_ = host_consts_nsa(h), host_consts_dsa(h)
        shared["ohg%d" % h] = cn_["ohg"]
        shared["selc%d" % h] = cn_["selc"]
        shared["visb%d" % h] = cn_["visb"]
        shared["bidx%d" % h] = cn_["bidx"]
        shared["ohg_d%d" % h] = cd_["ohg_d"]
        shared["ov"] = cn_["ov"]
    in_maps = []
    for k in range(NCORES):
        b = k % 4
        m = dict(shared)
        m["xT0"] = np.ascontiguousarray(_core_tokens(x, b, 0).T)
        m["xT1"] = np.ascontiguousarray(_core_tokens(x, b, 1).T)
        m["cT"] = np.ascontiguousarray(c[b].reshape(16, 128).T)
        in_maps.append(m)
    res = run_bass_kernel_spmd(nc, in_maps, core_ids=list(range(NCORES))).results
    out = np.zeros(x.shape, f32)
    for b in range(4):
        for h in range(2):
            out[b].reshape(32, 128, D)[h::2] = np.ascontiguousarray(res[b]["out%d" % h].T).reshape(16, 128, D)
    return out
```

```python
import numpy as np
from contextlib import ExitStack
import concourse.bass as bass
import concourse.mybir as mybir
from concourse.bass_utils import run_bass_kernel_spmd

F32 = mybir.dt.float32
BF16 = mybir.dt.bfloat16
AF = mybir.ActivationFunctionType
ALU = mybir.AluOpType
AX = mybir.AxisListType

D = 2048
KC = 16
FF = 5632
FC = 44
NTOK = 2048
TT = 512
NTT = NTOK // TT
S = 4096
N_IN = 8544
EPS = 1e-6
NCORES = 8
INDEX_SCALE = 512 ** -0.5
WST_ELEMS = 11264

C_QN, C_KC, C_VC, C_KS, C_VS, C_KW, C_VW, C_GN = 0, 1024, 1280, 1536, 1792, 2048, 2304, 2560
C_QD, C_CKV, C_QI, C_KI, C_WI, C_AM = 2608, 3632, 3888, 4400, 4432, 4448


class Eng:
    def __init__(self, name, eng, sem):
        self.name, self.eng, self.sem = name, eng, sem
        self.cnt = 0
        self.seen = {}


class Buf:
    def __init__(self, ap):
        self.ap = ap
        self.wr = None
        self.rd = {}

    def wdeps(self):
        d = list(self.rd.values())
        if self.wr is not None:
            d.append(self.wr)
        return d

    def rdeps(self):
        return [self.wr] if self.wr is not None else []

    def wrote(self, h):
        self.wr = h
        self.rd = {}

    def read(self, h):
        k = h[0]
        if k not in self.rd or self.rd[k][2] < h[2]:
            self.rd[k] = h


class KB:
    def __init__(self, nc, st):
        self.nc, self.st = nc, st
        self.pe = Eng("pe", nc.tensor, st.enter_context(nc.semaphore("s_pe")))
        self.act = Eng("act", nc.scalar, st.enter_context(nc.semaphore("s_act")))
        self.dve = Eng("dve", nc.vector, st.enter_context(nc.semaphore("s_dve")))
        self.pool = Eng("pool", nc.gpsimd, st.enter_context(nc.semaphore("s_pool")))
        self.sp = Eng("sp", nc.sync, st.enter_context(nc.semaphore("s_sp")))
        self.nslot = 0
        self.slots = []
        self.prefix = ""
        self.pools = None
        self.banks = None
        self.nbank = 0
        self.out_handles = []
        self.uid = 0

    def sbuf(self, name, shape, dt):
        return self.st.enter_context(self.nc.sbuf_tensor(self.prefix + name, shape, dt))

    def psum(self, name, shape, dt=F32):
        if self.banks is not None:
            b = self.banks[self.nbank % 8]
            self.nbank += 1
            return b
        return self.st.enter_context(self.nc.psum_tensor(self.prefix + name, shape, dt))

    def setup_fused(self, n_sp=30, n_pool=12):
        self.banks = [self.st.enter_context(self.nc.psum_tensor("bank%d" % i, [128, 512], F32)) for i in range(8)]
        self.pools = {"sp": [], "pool": []}
        for kind, n in (("sp", n_sp), ("pool", n_pool)):
            for i in range(n):
                sem = self.st.enter_context(self.nc.semaphore("s_%s%d" % (kind, i)))
                self.pools[kind].append({"key": "%s%d" % (kind, i), "sem": sem, "cnt": 0, "kind": kind})

    def begin_phase(self, prefix, stack):
        self.prefix = prefix
        self.main_st = self.st
        self.st = stack
        self.nbank = 0
        self.phase_recs = []

    def end_phase(self):
        self.barrier()
        for r in self.phase_recs:
            self.pools[r["kind"]].append(r)
        self.phase_recs = []
        self.slots = []
        self.st = self.main_st
        self.prefix = ""

    def slot(self):
        if self.pools is not None:
            return {"lazy": True}
        self.nslot += 1
        sem = self.st.enter_context(self.nc.semaphore("s_dma%d" % self.nslot))
        sl = {"key": "dma%d" % self.nslot, "sem": sem, "cnt": 0}
        self.slots.append(sl)
        return sl

    def barrier(self):
        engs = [self.pe, self.act, self.dve, self.pool, self.sp]
        for e in engs:
            for e2 in engs:
                if e2.cnt > 0:
                    self.wait(e, (e2.name, e2.sem, e2.cnt))
            for sl in self.slots:
                if sl["cnt"] > 0:
                    self.wait(e, (sl["key"], sl["sem"], sl["cnt"], sl))

    def wait(self, e, h):
        if h is None:
            return
        key, sem, cnt = h[0], h[1], h[2]
        if len(h) > 3:
            cnt = max(cnt, h[3]["cnt"])
            h[3]["waited"] = max(h[3].get("waited", 0), cnt)
        if e.seen.get(key, 0) >= cnt:
            return
        e.eng.wait_ge(sem, cnt)
        e.seen[key] = cnt

    def op(self, e, fn, deps=()):
        for d in deps:
            self.wait(e, d)
        ins = fn()
        e.cnt += 1
        ins.then_inc(e.sem, 1)
        return (e.name, e.sem, e.cnt)

    def dma(self, q, out, in_, slot, deps=()):
        if slot.get("lazy"):
            if "rec" not in slot:
                slot["rec"] = self.pools["pool" if q is self.pool else "sp"].pop()
                self.phase_recs.append(slot["rec"])
                self.slots.append(slot["rec"])
            assert slot["rec"]["kind"] == ("pool" if q is self.pool else "sp"), "slot used from both DMA queue kinds"
            slot = slot["rec"]
        for d in deps:
            self.wait(q, d)
        W = slot.get("waited", 0)
        if W > q.seen.get(slot["key"], 0):
            q.eng.wait_ge(slot["sem"], W)
            q.seen[slot["key"]] = W
        q.eng.dma_start(out=out, in_=in_).then_inc(slot["sem"], 16)
        slot["cnt"] += 16
        return (slot["key"], slot["sem"], slot["cnt"], slot)

    def mm(self, outb, out_ap, lhsT, rhs, start, stop, rd=(), extra=()):
        deps = list(extra)
        if start:
            deps += outb.wdeps()
        for b in rd:
            deps += b.rdeps()
        deps = [d for d in deps if d is not None and d[0] != "pe"]
        h = self.op(self.pe, lambda: self.nc.tensor.matmul(out_ap, lhsT=lhsT, rhs=rhs, start=start, stop=stop,
                                                           skip_group_check=True), deps)
        for b in rd:
            b.read(h)
        outb.wr = h
        if start:
            outb.rd = {}
        return h

    def mm_t(self, outb, out_ap, in_ap, ident_ap, first, rd=()):
        deps = []
        if first:
            deps += outb.wdeps()
        for b in rd:
            deps += b.rdeps()
        deps = [d for d in deps if d is not None and d[0] != "pe"]
        h = self.op(self.pe, lambda: self.nc.tensor.transpose(out_ap, in_ap, ident_ap), deps)
        for b in rd:
            b.read(h)
        outb.wr = h
        if first:
            outb.rd = {}
        return h

    def ew(self, e, fn, wr=(), rd=(), extra=()):
        deps = list(extra)
        for b in wr:
            deps += b.wdeps()
        for b in rd:
            deps += b.rdeps()
        h = self.op(e, fn, deps)
        for b in rd:
            b.read(h)
        for b in wr:
            b.wrote(h)
        return h

    def load(self, q, buf, out_ap, in_ap, slot, extra=()):
        h = self.dma(q, out_ap, in_ap, slot, list(extra) + buf.wdeps())
        buf.wrote(h)
        return h

    def store(self, q, dram_ap, buf, in_ap, slot, extra=(), final=False):
        h = self.dma(q, dram_ap, in_ap, slot, list(extra) + buf.rdeps())
        buf.read(h)
        if final:
            self.out_handles.append(h)
        return h

    def finish(self):
        last = {}
        for h in self.out_handles:
            if h[0] not in last or last[h[0]][2] < h[2]:
                last[h[0]] = h
        for h in last.values():
            self.wait(self.sp, h)


class Ring:
    def __init__(self, bufs):
        self.bufs = bufs
        self.i = 0

    def next(self):
        b = self.bufs[self.i % len(self.bufs)]
        self.i += 1
        return b


class Dense:
    def __init__(self, kb):
        self.kb = kb
        nc = kb.nc
        xres_t = kb.sbuf("xres", [128, KC, TT], F32)
        h_t = kb.sbuf("hmod", [128, KC, TT], BF16)
        hid_t = kb.sbuf("hid", [128, FC, TT], BF16)
        self.xres = [Buf(xres_t[:, k, :]) for k in range(KC)]
        self.h = [Buf(h_t[:, k, :]) for k in range(KC)]
        self.hid = [Buf(hid_t[:, k, :]) for k in range(FC)]
        self.hid_t = hid_t
        self.wst = Ring([Buf(kb.sbuf("wst%d" % i, [128, WST_ELEMS], BF16)[:]) for i in range(3)])
        self.wslots = [kb.slot() for _ in range(3)]
        self.ones = Buf(kb.sbuf("ones32", [128, 128], F32)[:])
        self.epsb = Buf(kb.sbuf("epsb", [128, 1], F32)[:])
        self.sq = Ring([Buf(kb.sbuf("sq%d" % i, [128, 512], F32)[:]) for i in range(2)])
        self.rstd = Buf(kb.sbuf("rstd", [128, TT], F32)[:])
        self.tmp = Ring([Buf(kb.sbuf("tmpf%d" % i, [128, 512], F32)[:]) for i in range(3)])
        self.ps = Ring([Buf(kb.psum("psg%d" % i, [128, 512])[:]) for i in range(6)])
        self.ps_stat = Buf(kb.psum("ps_stat", [128, 512])[:])
        self.ps_misc = Buf(kb.psum("ps_misc", [128, 512])[:])
        self.xslot = kb.slot()
        self.cslot = kb.slot()
        self.cslot2 = kb.slot()
        self.oslot = kb.slot()
        self.mod = Buf(kb.sbuf("modt", [128, 144], F32)[:])
        self.gn = Buf(kb.sbuf("gn", [128, 48], F32)[:])
        self.A = Buf(kb.sbuf("coefA", [128, 48], F32)[:])
        self.GT = Buf(kb.sbuf("coefG", [128, 48], F32)[:])
        kb.ew(kb.dve, lambda: nc.vector.memset(self.ones.ap, 1.0), wr=[self.ones])
        kb.ew(kb.dve, lambda: nc.vector.memset(self.epsb.ap, EPS), wr=[self.epsb])

    def wstage(self):
        i = self.wst.i % 3
        return self.wst.next(), self.wslots[i]


def emit_modprep(kb, dn, cT_d, wada_d, badaT_d, gnT_d, col_lo, col_hi):
    nc = kb.nc
    if True:
        cact = Buf(kb.sbuf("cact", [128, KC, 2], F32)[:])
        craw = Buf(kb.sbuf("craw", [128, KC], F32)[:])
        bada = Buf(kb.sbuf("bada", [128, 144], F32)[:])
        mod, gn = dn.mod, dn.gn
        kb.load(kb.sp, craw, craw.ap, cT_d, dn.cslot)
        kb.load(kb.sp, bada, bada.ap, badaT_d, dn.cslot)
        kb.load(kb.sp, gn, gn.ap, gnT_d, dn.cslot)
        for r in range(2):
            kb.ew(kb.act, lambda r=r: nc.scalar.activation(out=cact.ap[:, :, r], in_=craw.ap, func=AF.Silu),
                  wr=[cact], rd=[craw])
        psm = dn.ps_misc
        stg = [Buf(kb.sbuf("wada_stg%d" % i, [128, KC, 128], F32)[:]) for i in range(2)]
        sslots = [kb.slot(), kb.slot()]
        cbs = list(range(col_lo, col_hi))

        def issue(i):
            cb = cbs[i]
            b = stg[i % 2]
            src = wada_d[:, cb * 128:(cb + 1) * 128].rearrange("(k p) c -> p k c", p=128)
            kb.load(kb.sp, b, b.ap[:, 0:8, :], src[:, 0:8, :], sslots[i % 2])
            b.wr = kb.dma(kb.sp, b.ap[:, 8:16, :], src[:, 8:16, :], sslots[i % 2])

        issue(0)
        first = True
        for i, col in enumerate(cbs):
            if i + 1 < len(cbs):
                issue(i + 1)
            b = stg[i % 2]
            for kc in range(KC):
                kb.mm(psm, psm.ap[:, 2 * col:2 * col + 2], lhsT=b.ap[:, kc, :], rhs=cact.ap[:, kc, :],
                      start=first, stop=(kc == KC - 1), rd=[b, cact])
                first = False
        pv = psm.ap[:, 2 * col_lo:2 * col_hi].rearrange("p (c two) -> p c two", two=2)[:, :, 0]
        kb.ew(kb.dve, lambda: nc.vector.tensor_tensor(out=mod.ap[:, col_lo:col_hi], in0=pv, in1=bada.ap[:, col_lo:col_hi], op=ALU.add),
              wr=[mod], rd=[psm, bada])
        A, GT = dn.A, dn.GT
        for sub in range(3):
            if col_lo <= sub * 48 + 16 and sub * 48 + 32 <= col_hi:
                kb.ew(kb.dve, lambda sub=sub: nc.vector.scalar_tensor_tensor(
                    out=A.ap[:, sub * 16:(sub + 1) * 16], in0=mod.ap[:, sub * 48 + 16:sub * 48 + 32], scalar=1.0,
                    in1=gn.ap[:, sub * 16:(sub + 1) * 16], op0=ALU.add, op1=ALU.mult), wr=[A], rd=[mod, gn])
            if col_lo <= sub * 48 + 32 and sub * 48 + 48 <= col_hi:
                kb.ew(kb.dve, lambda sub=sub: nc.vector.tensor_scalar(
                    out=GT.ap[:, sub * 16:(sub + 1) * 16], in0=mod.ap[:, sub * 48 + 32:sub * 48 + 48],
                    scalar1=(1.0 if sub == 1 else 0.5), scalar2=None, op0=ALU.mult), wr=[GT], rd=[mod])


def emit_norm_mod(kb, dn, sub):
    nc = kb.nc
    xres, h = dn.xres, dn.h
    for st_ in range(TT // 512):
        sl = slice(st_ * 512, (st_ + 1) * 512)
        for kc in range(KC):
            sq = dn.sq.next()
            kb.ew(kb.act, lambda kc=kc, sq=sq: nc.scalar.activation(out=sq.ap, in_=xres[kc].ap[:, sl], func=AF.Square),
                  wr=[sq], rd=[xres[kc]])
            kb.mm(dn.ps_stat, dn.ps_stat.ap, lhsT=dn.ones.ap, rhs=sq.ap, start=(kc == 0), stop=(kc == KC - 1),
                  rd=[dn.ones, sq])
        t = dn.tmp.next()
        kb.ew(kb.act, lambda t=t: nc.scalar.activation(out=t.ap, in_=dn.ps_stat.ap, func=AF.Sqrt, scale=1.0 / D, bias=dn.epsb.ap[:, 0:1]),
              wr=[t], rd=[dn.ps_stat, dn.epsb])
        kb.ew(kb.dve, lambda t=t, sl=sl: nc.vector.reciprocal(out=dn.rstd.ap[:, sl], in_=t.ap), wr=[dn.rstd], rd=[t])
    for kc in range(KC):
        for st_ in range(TT // 512):
            sl = slice(st_ * 512, (st_ + 1) * 512)
            t = dn.tmp.next()
            kb.ew(kb.dve, lambda kc=kc, t=t, sl=sl: nc.vector.scalar_tensor_tensor(
                out=t.ap, in0=xres[kc].ap[:, sl], scalar=dn.A.ap[:, sub * 16 + kc:sub * 16 + kc + 1], in1=dn.rstd.ap[:, sl],
                op0=ALU.mult, op1=ALU.mult), wr=[t], rd=[xres[kc], dn.A, dn.rstd])
            kb.ew(kb.act, lambda kc=kc, t=t, sl=sl: nc.scalar.activation(
                out=h[kc].ap[:, sl], in_=t.ap, func=AF.Identity, bias=dn.mod.ap[:, sub * 48 + kc:sub * 48 + kc + 1], scale=1.0),
                wr=[h[kc]], rd=[t, dn.mod])


def emit_gemm_fm(kb, dn, w_d, kchunks, pieces, src, epilogue, cwcap=None):
    nc = kb.nc
    cwmax = (WST_ELEMS // kchunks) // 64 * 64
    if cwcap:
        cwmax = min(cwmax, cwcap)
    groups, cur = [], []
    for pi, (c0, wd) in enumerate(pieces):
        if cur and (cur[-1][1] + cur[-1][2] == c0) and (c0 + wd - cur[0][1] <= cwmax):
            cur.append((pi, c0, wd))
        else:
            if cur:
                groups.append(cur)
            cur = [(pi, c0, wd)]
    if cur:
        groups.append(cur)
    loaded = {}

    def issue(gi):
        g = groups[gi]
        c0 = g[0][1]
        wtot = g[-1][1] + g[-1][2] - c0
        wb, ws = dn.wstage()
        wv = wb.ap[:, 0:kchunks * wtot].rearrange("p (k c) -> p k c", c=wtot)
        kb.load(kb.pool, wb, wv, w_d[:, c0:c0 + wtot].rearrange("(k p) c -> p k c", p=128), ws)
        loaded[gi] = (wb, wv, c0)

    issue(0)
    if len(groups) > 1:
        issue(1)
    for gi, g in enumerate(groups):
        if gi + 2 < len(groups):
            issue(gi + 2)
        wb, wv, gc0 = loaded.pop(gi)
        for (pi, c0, wd) in g:
            for st_ in range(TT // 512):
                psb = dn.ps.next()
                for kc in range(kchunks):
                    sap, sbufs = src(kc, st_)
                    kb.mm(psb, psb.ap[0:wd, :], lhsT=wv[:, kc, c0 - gc0:c0 - gc0 + wd], rhs=sap,
                          start=(kc == 0), stop=(kc == kchunks - 1), rd=[wb] + sbufs)
                epilogue(pi, st_, psb)


def emit_ffn(kb, dn, sub, win_d, wout_d, xT_out_d, n0, final_out=False):
    nc = kb.nc
    emit_norm_mod(kb, dn, sub)
    hsrc = lambda kc, st_: (dn.h[kc].ap[:, st_ * 512:(st_ + 1) * 512], [dn.h[kc]])
    pieces = []
    for J in range(FF // 256):
        pieces += [(J * 256, 128), (J * 256 + 128, 128), (FF + J * 256, 128), (FF + J * 256 + 128, 128)]
    hold = {}

    def ep_in(pi, st_, psb):
        J, r = pi // 4, pi % 4
        if r < 2:
            hold[(2 * J + r, st_)] = psb
            return
        j = 2 * J + r - 2
        psg = hold.pop((j, st_))
        t = dn.tmp.next()
        sl = slice(st_ * 512, (st_ + 1) * 512)
        kb.ew(kb.act, lambda: nc.scalar.activation(out=t.ap, in_=psg.ap, func=AF.Silu), wr=[t], rd=[psg])
        kb.ew(kb.dve, lambda: nc.vector.tensor_tensor(out=dn.hid[j].ap[:, sl], in0=t.ap, in1=psb.ap, op=ALU.mult),
              wr=[dn.hid[j]], rd=[t, psb])

    emit_gemm_fm(kb, dn, win_d, KC, pieces, hsrc, ep_in, cwcap=256)

    def ep_out(pi, st_, psb):
        sl = slice(st_ * 512, (st_ + 1) * 512)
        xb = dn.xres[pi]
        kb.ew(kb.dve, lambda: nc.vector.scalar_tensor_tensor(
            out=xb.ap[:, sl], in0=psb.ap, scalar=dn.GT.ap[:, sub * 16 + pi:sub * 16 + pi + 1], in1=xb.ap[:, sl],
            op0=ALU.mult, op1=ALU.add), wr=[xb], rd=[psb, dn.GT])
        if xT_out_d is not None:
            kb.store(kb.sp, xT_out_d[pi * 128:(pi + 1) * 128, n0 + st_ * 512:n0 + (st_ + 1) * 512], xb, xb.ap[:, sl],
                     dn.oslot, final=final_out)

    hidsrc = lambda kc, st_: (dn.hid[kc].ap[:, st_ * 512:(st_ + 1) * 512], [dn.hid[kc]])
    emit_gemm_fm(kb, dn, wout_d, FC, [(m * 128, 128) for m in range(KC)], hidsrc, ep_out)


def load_xtile(kb, dn, xT_d, n0):
    for kc in range(KC):
        kb.load(kb.sp, dn.xres[kc], dn.xres[kc].ap, xT_d[kc * 128:(kc + 1) * 128, n0:n0 + TT], dn.xslot)


def _env(env, prefix):
    if env is None:
        nc = bass.Bass("TRN2", target_bir_lowering=False)
        gi = lambda name, shape, d=F32: nc.dram_tensor(name, shape, d, kind="ExternalInput").ap()
        go = lambda name, shape, d=F32: nc.dram_tensor(name, shape, d, kind="ExternalOutput").ap()
        return nc, None, gi, go
    nc = env["nc"]
    g = lambda name, shape, d=F32: env["T"][name]
    return nc, env["kb"], g, g

class OutStage:
    def __init__(self, kb, n=3):
        self.kb = kb
        self.f = [Buf(kb.sbuf("ostgf%d" % i, [128, 512], F32)[:]) for i in range(n)]
        self.b = [Buf(kb.sbuf("ostgb%d" % i, [128, 512], BF16)[:]) for i in range(n)]
        self.fs = [kb.slot() for _ in range(n)]
        self.bs = [kb.slot() for _ in range(n)]
        self.fi = 0
        self.bi = 0
        self.flip = 0

    def next(self, dt=F32):
        if dt == F32:
            k = self.fi % len(self.f)
            self.fi += 1
            return self.f[k], self.fs[k]
        k = self.bi % len(self.b)
        self.bi += 1
        return self.b[k], self.bs[k]


def emit_gemm_tm(kb, dn, w_d, kchunks, segs, src_h, epilogue):
    nc = kb.nc
    for si, (c0, wd) in enumerate(segs):
        wb, ws = dn.wstage()
        wv = wb.ap[:, 0:kchunks * wd].rearrange("p (k c) -> p k c", c=wd)
        kb.load(kb.pool, wb, wv, w_d[:, c0:c0 + wd].rearrange("(k p) c -> p k c", p=128), ws)
        for ts in range(TT // 128):
            psb = dn.ps.next()
            for kc in range(kchunks):
                sap, sbufs = src_h(kc, ts)
                kb.mm(psb, psb.ap[:, 0:wd], lhsT=sap, rhs=wv[:, kc, :], start=(kc == 0), stop=(kc == kchunks - 1), rd=[wb] + sbufs)
            epilogue(si, ts, psb)


def emit_proj(kb, dn, os_, win_d, gkv, o, n0, ghalf=None):
    nc = kb.nc
    emit_norm_mod(kb, dn, 1)
    hsrc = lambda kc, st_: (dn.h[kc].ap[:, st_ * 512:(st_ + 1) * 512], [dn.h[kc]])
    pieces, kinds = [], []

    def add(c0, wd, kind, idx):
        pieces.append((c0, wd))
        kinds.append((kind, idx))

    for c in range(8):
        add(C_QN + c * 128, 128, "qn", c)
    for nm, cc in (("kc", C_KC), ("vc", C_VC), ("ks", C_KS), ("kw", C_KW)):
        for c in range(2):
            add(cc + c * 128, 128, nm, c)
    for c in range(8):
        add(C_QD + c * 128, 128, "qd", c)
    for i in range(2):
        add(C_CKV + i * 128, 128, "ckv", i)
    for c in range(4):
        add(C_QI + c * 128, 128, "qi", c)
    add(C_KI, 128, "ki", 0)
    for i in range(32):
        add(C_AM + i * 128, 128, "am", i)
    sl_tok = slice(n0, n0 + TT)
    tt_ = n0 // TT

    def kdst(ap2):
        if ghalf is None:
            return ap2[:, sl_tok], None
        return ap2.rearrange("p (j hf t) -> p j hf t", hf=2, t=128)[:, 4 * tt_:4 * tt_ + 4, ghalf, :], "p (j t) -> p j t"

    cbuf = [Buf(dn.cbuf_t[:, i, :]) for i in range(2)]

    def evac_copy(psb, wd, dst_ap, dt, rr=None):
        ob, oslot = os_.next(dt)
        oap = ob.ap[0:wd, :]
        os_.flip ^= 1
        if os_.flip:
            kb.ew(kb.act, lambda: nc.scalar.copy(out=oap, in_=psb.ap[0:wd, :]), wr=[ob], rd=[psb])
        else:
            kb.ew(kb.dve, lambda: nc.vector.tensor_copy(out=oap, in_=psb.ap[0:wd, :]), wr=[ob], rd=[psb])
        kb.store(kb.sp, dst_ap, ob, oap if rr is None else oap.rearrange(rr, t=128), oslot, final=(ghalf is None))

    def ep(pi, st_, psb):
        kind, idx = kinds[pi]
        wd = pieces[pi][1]
        if kind in ("qn", "qd"):
            evac_copy(psb, 128, o[kind][2 * idx:2 * idx + 2, :, sl_tok].rearrange("h p n -> (h p) n"), BF16)
        elif kind == "qi":
            evac_copy(psb, 128, o[kind][4 * idx:4 * idx + 4, :, sl_tok].rearrange("h p n -> (h p) n"), BF16)
        elif kind in ("kc", "vc", "ks", "kw"):
            dst, rr = kdst(o[kind][2 * idx:2 * idx + 2].rearrange("h p n -> (h p) n"))
            evac_copy(psb, 128, dst, BF16, rr)
        elif kind == "ki":
            dst, rr = kdst(o["ki"])
            evac_copy(psb, 32, dst, BF16, rr)
        elif kind == "am":
            ob, oslot = os_.next()
            kb.ew(kb.act, lambda: nc.scalar.activation(out=ob.ap, in_=psb.ap, func=AF.Sigmoid), wr=[ob], rd=[psb])
            kb.store(kb.sp, o["am"][idx * 128:(idx + 1) * 128, sl_tok], ob, ob.ap, oslot, final=(ghalf is None))
        elif kind == "ckv":
            kb.ew(kb.dve, lambda: nc.vector.tensor_copy(out=cbuf[idx].ap, in_=psb.ap), wr=[cbuf[idx]], rd=[psb])
            if idx == 1:
                for i in range(2):
                    sq = dn.sq.next()
                    kb.ew(kb.act, lambda i=i, sq=sq: nc.scalar.activation(out=sq.ap, in_=cbuf[i].ap, func=AF.Square), wr=[sq], rd=[cbuf[i]])
                    kb.mm(dn.ps_stat, dn.ps_stat.ap, lhsT=dn.ones.ap, rhs=sq.ap, start=(i == 0), stop=(i == 1), rd=[dn.ones, sq])
                t = dn.tmp.next()
                kb.ew(kb.act, lambda: nc.scalar.activation(out=t.ap, in_=dn.ps_stat.ap, func=AF.Sqrt, scale=1.0 / 256, bias=dn.epsb.ap[:, 0:1]),
                      wr=[t], rd=[dn.ps_stat, dn.epsb])
                t2 = dn.tmp.next()
                kb.ew(kb.dve, lambda: nc.vector.reciprocal(out=t2.ap, in_=t.ap), wr=[t2], rd=[t])
                for i in range(2):
                    ob, oslot = os_.next(BF16)
                    oap = ob.ap
                    kb.ew(kb.dve, lambda i=i, oap=oap: nc.vector.scalar_tensor_tensor(
                        out=oap, in0=cbuf[i].ap, scalar=gkv.ap[:, i:i + 1], in1=t2.ap, op0=ALU.mult, op1=ALU.mult),
                        wr=[ob], rd=[cbuf[i], gkv, t2])
                    dst, rr = kdst(o["cn"][i * 128:(i + 1) * 128, :])
                    kb.store(kb.sp, dst, ob, oap if rr is None else oap.rearrange(rr, t=128), oslot, final=(ghalf is None))

    if dn.pflags & 1:
        emit_gemm_fm(kb, dn, win_d, KC, pieces, hsrc, ep, cwcap=512)

    segs = [(C_VS, 256), (C_VW, 256), (C_GN, 128), (C_KI, 128)]
    hsrc_t = lambda kc, ts: (dn.h[kc].ap[:, ts * 128:(ts + 1) * 128], [dn.h[kc]])

    def ep_t(si, ts, psb):
        tsl = slice(n0 + ts * 128, n0 + (ts + 1) * 128)
        gt_ = 2 * (4 * tt_ + ts) + (ghalf or 0)
        gsl = tsl if ghalf is None else slice(gt_ * 128, (gt_ + 1) * 128)
        if si == 0:
            ob, oslot = os_.next(BF16)
            oap = ob.ap[:, 0:256]
            kb.ew(kb.dve, lambda: nc.vector.tensor_copy(out=oap, in_=psb.ap[:, 0:256]), wr=[ob], rd=[psb])
            kb.store(kb.sp, o["vs"][gsl, :], ob, oap, oslot, final=(ghalf is None))
        elif si == 1:
            ob, oslot = os_.next(BF16)
            oap = ob.ap[:, 0:256]
            kb.ew(kb.dve, lambda: nc.vector.tensor_copy(out=oap, in_=psb.ap[:, 0:256]), wr=[ob], rd=[psb])
            kb.store(kb.sp, o["vw"][gsl, :], ob, oap, oslot, final=(ghalf is None))
        elif si == 2:
            ob2, oslot2 = os_.next()
            kb.ew(kb.act, lambda: nc.scalar.activation(out=ob2.ap[:, 0:48], in_=psb.ap[:, 0:48], func=AF.Sigmoid), wr=[ob2], rd=[psb])
            kb.store(kb.sp, o["gn"][tsl, :], ob2, ob2.ap[:, 0:48], oslot2, final=(ghalf is None))
        else:
            ob, oslot = os_.next()
            kb.ew(kb.dve, lambda: nc.vector.tensor_scalar(out=ob.ap[:, 0:16], in0=psb.ap[:, 32:48], scalar1=INDEX_SCALE, scalar2=None, op0=ALU.mult),
                  wr=[ob], rd=[psb])
            kb.store(kb.sp, o["wi"][tsl, :], ob, ob.ap[:, 0:16], oslot, final=(ghalf is None))

    if dn.pflags & 2:
        if dn.pflags > 3:
            keep = [i for i in range(4) if dn.pflags & (4 << i)]
            segs2 = [segs[i] for i in keep]
            emit_gemm_tm(kb, dn, win_d, KC, segs2, hsrc_t, lambda si, ts, psb: ep_t(keep[si], ts, psb))
        else:
            emit_gemm_tm(kb, dn, win_d, KC, segs, hsrc_t, ep_t)


PROJ_OUTS = [("qn", [16, 64, NTOK], BF16), ("kc", [4, 64, NTOK], BF16), ("vc", [4, 64, NTOK], BF16),
             ("ks", [4, 64, NTOK], BF16), ("kw", [4, 64, NTOK], BF16), ("qd", [16, 64, NTOK], BF16),
             ("cn", [256, NTOK], BF16), ("qi", [16, 32, NTOK], BF16), ("ki", [32, NTOK], BF16),
             ("am", [4096, NTOK], F32), ("vs", [NTOK, 256], BF16), ("vw", [NTOK, 256], BF16),
             ("gn", [NTOK, 48], F32), ("wi", [NTOK, 16], F32)]


def build_A(ntt=NTT, do_proj=True, do_ffn=True, do_mod=True, pflags=3, env=None, prefix="", ghalf=None, load_tabs=False):
    nc, ekb, gi, go = _env(env, prefix)
    dt = lambda name, shape, d=F32, kind="ExternalInput": (gi if kind == "ExternalInput" else go)(name, shape, d)
    xT = dt("xT", [D, NTOK])
    cT = dt("cT", [128, KC])
    if do_mod and not load_tabs:
        wada = dt("wada", [D, 9 * D])
        badaT = dt("badaT", [128, 144])
        gnT = dt("gnT", [128, 48])
    if do_ffn:
        wfi = dt("wfi", [D, 2 * FF])
        wfo = dt("wfo", [FF, D])
    win = dt("win", [D, N_IN])
    gkvT = dt("gkvT", [128, 2])
    x1T = dt("x1T", [D, NTOK], kind="ExternalOutput")
    o = {nm: dt("o_" + nm, shp, d, kind="ExternalOutput") for nm, shp, d in PROJ_OUTS}
    tabs = {"o_mod": dt("o_mod", [128, 144], kind="ExternalOutput"), "o_A": dt("o_A", [128, 48], kind="ExternalOutput"),
            "o_GT": dt("o_GT", [128, 48], kind="ExternalOutput")}
    with ExitStack() as st:
        if ekb is None:
            kb = KB(nc, st)
        else:
            kb = ekb
            kb.begin_phase(prefix, st)
        dn = Dense(kb)
        os_ = OutStage(kb)
        dn.cbuf_t = kb.sbuf("cbuf", [128, 2, 512], F32)
        gkv = Buf(kb.sbuf("gkv", [128, 2], F32)[:])
        kb.load(kb.sp, gkv, gkv.ap, gkvT, dn.cslot)
        if load_tabs:
            kb.load(kb.sp, dn.mod, dn.mod.ap, tabs["o_mod"], dn.cslot)
            kb.load(kb.sp, dn.A, dn.A.ap, tabs["o_A"], dn.cslot)
            kb.load(kb.sp, dn.GT, dn.GT.ap, tabs["o_GT"], dn.cslot)
        elif do_mod:
            emit_modprep(kb, dn, cT, wada, badaT, gnT, 0, 144)
            for nm_, bf_ in (("o_mod", dn.mod), ("o_A", dn.A), ("o_GT", dn.GT)):
                kb.store(kb.sp, tabs[nm_], bf_, bf_.ap, dn.cslot2, final=(ekb is None))
        else:
            kb.ew(kb.dve, lambda: nc.vector.memset(dn.mod.ap, 0.0), wr=[dn.mod])
            kb.ew(kb.dve, lambda: nc.vector.memset(dn.A.ap, 1.0), wr=[dn.A])
            kb.ew(kb.dve, lambda: nc.vector.memset(dn.GT.ap, 1.0), wr=[dn.GT])
        dn.pflags = pflags
        for tt in range(ntt):
            n0 = tt * TT
            load_xtile(kb, dn, xT, n0)
            if do_ffn:
                emit_ffn(kb, dn, 0, wfi, wfo, x1T, n0, final_out=(ekb is None))
            if do_proj:
                emit_proj(kb, dn, os_, win, gkv, o, n0, ghalf=ghalf)
        if ekb is None:
            kb.finish()
        else:
            kb.end_phase()
    return nc


BIG = 30000.0
SCALE = 0.125
NU = 256
NPAT_N = 9
GC_W = 576


def _t5_bucket_np(n):
    n = np.maximum(n, 0)
    nf = np.maximum(n, 1).astype(np.float32)
    large = 16 + (np.log(nf / 16) / np.log(128 / 16) * 16).astype(np.int32)
    large = np.minimum(large, 31)
    return np.where(n < 16, n, large)


def _pattern_onehot(dist, valid):
    n = dist.shape[0]
    oh = np.zeros((33, n), np.float32)
    b = _t5_bucket_np(dist)
    idx = np.arange(n)
    v = valid.astype(bool)
    oh[b[v], idx[v]] += 8.0
    oh[31, idx[v]] -= 8.0
    oh[32, idx[~v]] = 1.0
    return oh


def host_consts_nsa(half):
    u = np.arange(NU) - 127
    pats = []
    for r in (-1, 0, 1):
        d = (half - r) * 128 + u
        pats.append(_pattern_onehot(d, d >= 0))
    for r in (-4, -3, -2, -1, 0, 1):
        d = (half - r) * 128 + u
        pats.append(_pattern_onehot(d, (d >= 0) & (d < 512)))
    dc = np.arange(GC_W) - 300
    pats.append(_pattern_onehot(dc, dc >= 0))
    ohg = np.concatenate(pats, axis=1)
    selc = np.zeros((17, 16, 2, 128), np.float32)
    for j in range(16):
        i = 2 * j + half
        for nt in range(2):
            n = nt * 128 + np.arange(128)
            m = n - 8 * i + 9
            m = np.where((m >= 16) | (n >= 255), 16, m)
            ok = m >= 0
            selc[16 - m[ok], j, nt, np.arange(128)[ok]] = 1.0
    cs = np.arange(256) * 16
    ss = np.arange(64) * 64
    ov = ((cs[:, None] < ss[None, :] + 64) & (cs[:, None] + 32 > ss[None, :])).astype(np.float32)
    ov[255] = 0
    ov = ov.reshape(2, 128, 64).transpose(1, 0, 2)
    visb = np.zeros((128, 16, 64), np.float32)
    blk = np.arange(64)
    for j in range(16):
        t = (2 * j + half) * 128 + np.arange(128)
        cur = t // 64
        forced = (blk[None, :] == 0) | (blk[None, :] == cur[:, None]) | (blk[None, :] == cur[:, None] - 1)
        vis = blk[None, :] * 64 <= t[:, None]
        visb[:, j, :] = np.where(vis, np.where(forced, 1e4, 0.0), -1e30)
    bidx = np.zeros((128, 2, 128), np.float32)
    for ri, r in enumerate((0, 1)):
        d = (half - r) * 128 + np.arange(128)[:, None] - np.arange(128)[None, :]
        bidx[:, ri, :] = np.where(d >= 0, 0.0, -1e30)
    return {"ohg": ohg, "selc": selc.reshape(17, 16 * 2 * 128), "ov": ov, "visb": visb, "bidx": bidx}


def host_consts_dsa(half):
    u = np.arange(NU) - 127
    pats = []
    for r in (-1, 0, 1):
        d = (half - r) * 128 + u
        pats.append(_pattern_onehot(d, d >= 0))
    return {"ohg_d": np.concatenate(pats, axis=1)}


class Attn:
    def __init__(self, kb, ident_d, rev_d):
        self.kb = kb
        nc = kb.nc
        self.ident = Buf(kb.sbuf("identb", [128, 128], BF16)[:])
        self.i4 = Buf(kb.sbuf("i4", [128, 4, 128], BF16)[:])
        identf = Buf(kb.sbuf("identf", [128, 128], F32)[:])
        self.identf = identf
        isl = kb.slot()
        kb.load(kb.sp, identf, identf.ap, ident_d, isl)
        kb.ew(kb.dve, lambda: nc.vector.tensor_copy(out=self.ident.ap, in_=identf.ap), wr=[self.ident], rd=[identf])
        revf = Buf(kb.sbuf("revf", [128, 128], F32)[:])
        self.rev = Buf(kb.sbuf("revb", [128, 128], BF16)[:])
        kb.load(kb.sp, revf, revf.ap, rev_d, isl)
        kb.ew(kb.dve, lambda: nc.vector.tensor_copy(out=self.rev.ap, in_=revf.ap), wr=[self.rev], rd=[revf])
        for h in range(4):
            kb.ew(kb.dve, lambda h=h: nc.vector.tensor_copy(out=self.i4.ap[:, h, :], in_=identf.ap), wr=[self.i4], rd=[identf])
        self.ps_s = Ring([Buf(kb.psum("ps_s%d" % i, [128, 512])[:]) for i in range(2)])
        self.ps_o = Ring([Buf(kb.psum("ps_o%d" % i, [128, 512])[:]) for i in range(3)])
        self.E = Ring([Buf(kb.sbuf("Ebuf%d" % i, [128, 512], BF16)[:]) for i in range(3)])
        self.small = Ring([Buf(kb.sbuf("small%d" % i, [128, 16], F32)[:]) for i in range(6)])

    def build_bias(self, relb_d, ohg_d, width, scratch_d):
        kb, nc = self.kb, self.kb.nc
        tab = Buf(kb.sbuf("tabaug", [33, 32], F32)[:])
        sl = kb.slot()
        kb.ew(kb.dve, lambda: nc.vector.memset(tab.ap, -BIG), wr=[tab])
        kb.load(kb.sp, tab, tab.ap[0:32, :], relb_d, sl)
        CH = 512
        stg = Buf(kb.sbuf("ohg_stg", [33, CH], F32)[:])
        gout = Buf(kb.sbuf("g_out", [32, CH], F32)[:])
        sl2, sl3 = kb.slot(), kb.slot()
        psb = self.ps_s.bufs[0]
        for c0 in range(0, width, CH):
            w = min(CH, width - c0)
            kb.load(kb.sp, stg, stg.ap[:, 0:w], ohg_d[:, c0:c0 + w], sl2)
            kb.mm(psb, psb.ap[0:32, 0:w], lhsT=tab.ap, rhs=stg.ap[:, 0:w], start=True, stop=True, rd=[tab, stg])
            kb.ew(kb.dve, lambda w=w: nc.vector.tensor_copy(out=gout.ap[:, 0:w], in_=psb.ap[0:32, 0:w]), wr=[gout], rd=[psb])
            self.g_store = kb.store(kb.sp, scratch_d[:, c0:c0 + w], gout, gout.ap[:, 0:w], sl3)

    def load_skew(self, dst_buf, dst_ap, scratch_t, col0, h0, nh, pstride, slot):
        kb = self.kb
        npart = dst_ap.shape[0]
        src = bass.AP(scratch_t, h0 * scratch_t.shape[1] + col0 - pstride * (npart - 1), ((pstride, npart), (scratch_t.shape[1], nh), (1, 128)))
        return kb.load(kb.pool, dst_buf, dst_ap, src, slot, extra=[self.g_store])

    def attend(self, kt_list, score_mms, o_buf, v_of, nheads=4, vw=65, extra_o=None):
        kb, nc = self.kb, self.kb.nc
        st_ = {"first": True}

        def finish(kt, ps):
            e = self.E.next()
            kb.ew(kb.act, lambda ps=ps, e=e: nc.scalar.activation(out=e.ap[:, 0:nheads * 128], in_=ps.ap[:, 0:nheads * 128], func=AF.Exp, scale=SCALE),
                  wr=[e], rd=[ps])
            for h in range(nheads):
                vap, vb = v_of(kt, h)
                kb.mm(o_buf, o_buf.ap[:, h * vw:(h + 1) * vw], lhsT=e.ap[:, h * 128:(h + 1) * 128], rhs=vap,
                      start=st_["first"], stop=False, rd=[e] + vb)
                st_["first"] = False
                if extra_o is not None:
                    extra_o(kt, h, e, False)

        pending = []
        for kt in kt_list:
            ps = self.ps_s.next()
            score_mms(kt, ps)
            pending.append((kt, ps))
            if len(pending) == 2:
                finish(*pending.pop(0))
        while pending:
            finish(*pending.pop(0))
        return o_buf


def build_B(env=None, prefix=""):
    nc, ekb, dti, dto = _env(env, prefix)
    qn_d = dti("qn", [16, 64, NTOK], BF16)
    qi_d = dti("qi", [16, 32, NTOK], BF16)
    gn_d = dti("gn", [NTOK, 48])
    wi_d = dti("wi", [NTOK, 16])
    ksT_d = dti("ksT", [4, 64, S], BF16)
    kwT_d = dti("kwT", [4, 64, S], BF16)
    kcT_d = dti("kcT", [4, 64, S], BF16)
    vcT_d = dti("vcT", [4, 64, S], BF16)
    vs_d = dti("vs", [S, 256], BF16)
    vw_d = dti("vw", [S, 256], BF16)
    kiT_d = dti("kiT", [32, S], BF16)
    wk1_d, wv1_d = dti("wk1", [2048, 256]), dti("wv1", [2048, 256])
    wk2_d, wv2_d = dti("wk2", [256, 64]), dti("wv2", [256, 64])
    pek_d, pev_d = dti("pekT", [64, 32]), dti("pevT", [64, 32])
    relb_d = dti("relb", [32, 32])
    ohg_d = dti("ohg", [33, NPAT_N * NU + GC_W])
    selc_d = dti("selc", [17, 16 * 2 * 128])
    ov_d = dti("ov", [128, 2, 64])
    visb_d = dti("visb", [128, 16, 64])
    bidx_d = dti("bidx", [128, 2, 128])
    ident_d = dti("ident", [128, 128])
    rev_d = dti("rev", [128, 128])
    onT_d = dto("onT", [1024, NTOK])
    nm_d = dto("negmask", [16, 128, S], BF16)
    gsc_t = nc.dram_tensor(prefix + "gscratch", [32, NPAT_N * NU + GC_W], F32, kind="Internal")
    with ExitStack() as st:
        if ekb is None:
            kb = KB(nc, st)
        else:
            kb = ekb
            kb.begin_phase(prefix, st)
        at = Attn(kb, ident_d, rev_d)
        ld = kb.slot()
        at.build_bias(relb_d, ohg_d, NPAT_N * NU + GC_W, gsc_t.ap())
        ksT = Buf(kb.sbuf("sb_ksT", [64, 4, S], BF16)[:])
        vsA = Buf(kb.sbuf("sb_vsA", [128, 32, 4, 65], BF16)[:])
        kiT = Buf(kb.sbuf("sb_kiT", [32, S], BF16)[:])
        kcmpT = Buf(kb.sbuf("sb_kcmpT", [64, 4, 256], BF16)[:])
        vcA = Buf(kb.sbuf("sb_vcA", [128, 2, 4, 65], BF16)[:])
        ovb = Buf(kb.sbuf("sb_ovb", [128, 2, 64], BF16)[:])
        bsel = Buf(kb.sbuf("sb_bsel", [128, 3, 16, 128], BF16)[:])
        bwin = Buf(kb.sbuf("sb_bwin", [128, 6, 16, 128], BF16)[:])
        pc = Buf(kb.sbuf("sb_pc", [17, 16, 128], BF16)[:])
        selc = Buf(kb.sbuf("sb_selc", [17, 16 * 2 * 128], BF16)[:])
        visb = Buf(kb.sbuf("sb_visb", [128, 16, 64], F32)[:])
        bidx = Buf(kb.sbuf("sb_bidx", [128, 2, 128], F32)[:])
        halfc = Buf(kb.sbuf("sb_halfc", [128, 1], F32)[:])
        kb.ew(kb.dve, lambda: nc.vector.memset(halfc.ap, 0.5), wr=[halfc])
        for g in range(4):
            kb.load(kb.sp, ksT, ksT.ap[:, g, :], ksT_d[g], ld)
        kb.ew(kb.pool, lambda: nc.gpsimd.memset(vsA.ap, 1.0), wr=[vsA])
        kb.ew(kb.pool, lambda: nc.gpsimd.memset(vcA.ap, 0.0), wr=[vcA])
        kb.ew(kb.pool, lambda: nc.gpsimd.memset(vcA.ap[:, :, :, 64:65], 1.0), wr=[vcA])
        for kt in range(32):
            kb.load(kb.sp, vsA, vsA.ap[:, kt, :, 0:64], vs_d[kt * 128:(kt + 1) * 128, :].rearrange("p (g d) -> p g d", d=64), ld)
        kb.load(kb.sp, kiT, kiT.ap, kiT_d, ld)
        kb.load(kb.sp, visb, visb.ap, visb_d, ld)
        kb.load(kb.sp, bidx, bidx.ap, bidx_d, ld)
        ldp = kb.slot()
        kb.load(kb.pool, selc, selc.ap, selc_d, ldp)
        kb.load(kb.pool, ovb, ovb.ap, ov_d, ldp)
        for r in range(3):
            at.load_skew(bsel, bsel.ap[:, r, :, :], gsc_t, r * NU + 127, 0, 16, 1, ldp)
        for r in range(6):
            at.load_skew(bwin, bwin.ap[:, r, :, :], gsc_t, (3 + r) * NU + 127, 0, 16, 1, ldp)
        at.load_skew(pc, pc.ap, gsc_t, NPAT_N * NU + 413, 0, 16, 16, ldp)

        with ExitStack() as cst:
            w1 = Buf(cst.enter_context(nc.sbuf_tensor(kb.prefix + "sb_w1c", [64, 32, 256], BF16))[:])
            w2 = Buf(cst.enter_context(nc.sbuf_tensor(kb.prefix + "sb_w2c", [128, 2, 64], BF16))[:])
            pe2 = Buf(cst.enter_context(nc.sbuf_tensor(kb.prefix + "sb_pe2", [64, 32, 2], BF16))[:])
            pe1 = Buf(cst.enter_context(nc.sbuf_tensor(kb.prefix + "sb_pe1", [64, 32], F32))[:])
            src = Buf(cst.enter_context(nc.sbuf_tensor(kb.prefix + "sb_csrc", [64, S], BF16))[:])
            hb = Buf(cst.enter_context(nc.sbuf_tensor(kb.prefix + "sb_hbias", [128, 2], F32))[:])
            hid = Buf(cst.enter_context(nc.sbuf_tensor(kb.prefix + "sb_chid", [128, 2, 256], BF16))[:])
            lw = kb.slot()
            lsrc = kb.slot()
            psh = at.ps_o.bufs[0]
            psb2 = at.ps_o.bufs[1]
            for which, (w1_d, w2_d, pe_d, srcT_d) in enumerate(((wk1_d, wk2_d, pek_d, kcT_d), (wv1_d, wv2_d, pev_d, vcT_d))):
                kb.load(kb.pool, w1, w1.ap, w1_d.rearrange("(l d) c -> d l c", d=64), lw)
                kb.load(kb.pool, w2, w2.ap, w2_d.rearrange("(k p) c -> p k c", p=128), lw)
                kb.load(kb.sp, pe1, pe1.ap, pe_d, lsrc)
                for r_ in range(2):
                    kb.ew(kb.dve, lambda r_=r_: nc.vector.tensor_copy(out=pe2.ap[:, :, r_], in_=pe1.ap), wr=[pe2], rd=[pe1])
                for c in range(2):
                    for l in range(32):
                        kb.mm(psb2, psb2.ap[:, 2 * c:2 * c + 2], lhsT=w1.ap[:, l, c * 128:(c + 1) * 128], rhs=pe2.ap[:, l, :],
                              start=(c == 0 and l == 0), stop=(l == 31), rd=[w1, pe2])
                kb.ew(kb.dve, lambda: nc.vector.tensor_copy(out=hb.ap, in_=psb2.ap[:, 0:4].rearrange("p (c two) -> p c two", two=2)[:, :, 0]),
                      wr=[hb], rd=[psb2])
                for g in range(4):
                    kb.load(kb.sp, src, src.ap, srcT_d[g], lsrc)
                    for c in range(2):
                        for l in range(32):
                            kb.mm(psh, psh.ap[:, 0:255], lhsT=w1.ap[:, l, c * 128:(c + 1) * 128], rhs=src.ap[:, l:l + 16 * 254 + 1:16],
                                  start=(l == 0), stop=(l == 31), rd=[w1, src])
                        kb.ew(kb.act, lambda c=c: nc.scalar.activation(out=hid.ap[:, c, 0:255], in_=psh.ap[:, 0:255], func=AF.Silu, bias=hb.ap[:, c:c + 1]),
                              wr=[hid], rd=[psh, hb])
                    if which == 0:
                        for c in range(2):
                            kb.mm(psb2, psb2.ap[0:64, 0:255], lhsT=w2.ap[:, c, :], rhs=hid.ap[:, c, 0:255], start=(c == 0), stop=(c == 1), rd=[w2, hid])
                        kb.ew(kb.dve, lambda g=g: nc.vector.memset(kcmpT.ap[:, g, 255:256], 0.0), wr=[kcmpT])
                        kb.ew(kb.dve, lambda g=g: nc.vector.tensor_copy(out=kcmpT.ap[:, g, 0:255], in_=psb2.ap[0:64, 0:255]), wr=[kcmpT], rd=[psb2])
                    else:
                        for nt in range(2):
                            m = 128 if nt == 0 else 127
                            for c in range(2):
                                kb.mm(psb2, psb2.ap[0:m, 0:64], lhsT=hid.ap[:, c, nt * 128:nt * 128 + m], rhs=w2.ap[:, c, :], start=(c == 0), stop=(c == 1),
                                      rd=[w2, hid])
                            kb.ew(kb.dve, lambda g=g, nt=nt, m=m: nc.vector.tensor_copy(out=vcA.ap[0:m, nt, g, 0:64], in_=psb2.ap[0:m, 0:64]), wr=[vcA], rd=[psb2])
            kb.barrier()
        guard = []

        qn = Buf(kb.sbuf("sb_qn_sb", [64, 16, 128], BF16)[:])
        qi = Buf(kb.sbuf("sb_qi_sb", [32, 16, 128], BF16)[:])
        gnb = Buf(kb.sbuf("sb_gn_sb", [128, 48], F32)[:])
        wib = Buf(kb.sbuf("sb_wi_sb", [128, 16], F32)[:])
        kwT = Buf(kb.sbuf("sb_kw_sb", [64, 4, 768], BF16)[:])
        vwA = Buf(kb.sbuf("sb_vw_sb", [128, 6, 4, 65], BF16)[:])
        onacc = Buf(kb.sbuf("sb_onacc", [128, 16, 64], F32)[:])
        imp = Buf(kb.sbuf("sb_imp", [128, 4, 64], F32)[:])
        imp2 = Buf(kb.sbuf("sb_imp2", [128, 4, 64], F32)[:])
        negsel = Buf(kb.sbuf("sb_negsel", [128, 4, 64], BF16)[:])
        nsx = Ring([Buf(kb.sbuf("sb_nsx%d" % i, [128, S], BF16)[:]) for i in range(2)])
        score = Buf(kb.sbuf("sb_score", [128, S], F32)[:])
        nmask = Buf(kb.sbuf("sb_nmask", [128, S], BF16)[:])
        junk = nmask
        rt = Ring([Buf(kb.sbuf("sb_rt%d" % i, [128, 512], F32)[:]) for i in range(2)])
        m8 = Buf(kb.sbuf("sb_m8", [128, 4, 8], F32)[:])
        m8b = Buf(kb.sbuf("sb_m8b", [128, 4, 8], F32)[:])
        bis = Buf(kb.sbuf("sb_bis", [128, 8], F32)[:])
        kb.ew(kb.pool, lambda: nc.gpsimd.memset(vwA.ap, 1.0), wr=[vwA], extra=guard)
        qs, ws_, os1, os2 = kb.slot(), kb.slot(), kb.slot(), kb.slot()
        onst = Ring([Buf(kb.sbuf("sb_onst%d" % i, [128, 512], F32)[:]) for i in range(2)])

        def combine(o_buf, br, g, first_branch):
            sm = at.small.next()
            ov_ = o_buf.ap[:, 0:260].rearrange("p (h c) -> p h c", c=65)
            kb.ew(kb.dve, lambda: nc.vector.tensor_scalar(out=sm.ap[:, 8:12], in0=ov_[:, :, 64], scalar1=1e-30, scalar2=None, op0=ALU.max), wr=[sm], rd=[o_buf])
            kb.ew(kb.dve, lambda: nc.vector.reciprocal(out=sm.ap[:, 0:4], in_=sm.ap[:, 8:12]), wr=[sm])
            kb.ew(kb.dve, lambda: nc.vector.tensor_tensor(out=sm.ap[:, 4:8], in0=sm.ap[:, 0:4], in1=gnb.ap[:, 12 * g + br:12 * g + br + 10:3], op=ALU.mult),
                  wr=[sm], rd=[gnb])
            for h in range(4):
                if first_branch:
                    kb.ew(kb.dve, lambda h=h: nc.vector.tensor_scalar(out=onacc.ap[:, 4 * g + h, :], in0=ov_[:, h, 0:64], scalar1=sm.ap[:, 4 + h:5 + h], scalar2=None,
                                                                       op0=ALU.mult), wr=[onacc], rd=[o_buf, sm])
                else:
                    kb.ew(kb.dve, lambda h=h: nc.vector.scalar_tensor_tensor(out=onacc.ap[:, 4 * g + h, :], in0=ov_[:, h, 0:64], scalar=sm.ap[:, 4 + h:5 + h],
                                                                              in1=onacc.ap[:, 4 * g + h, :], op0=ALU.mult, op1=ALU.add), wr=[onacc], rd=[o_buf, sm])
            return sm

        for j in range(16):
            tsl = slice(j * 128, (j + 1) * 128)
            nkt = 2 * j + 2
            N = nkt * 128
            kb.load(kb.sp, qn, qn.ap, qn_d[:, :, tsl].rearrange("h d t -> d h t"), qs)
            kb.load(kb.sp, qi, qi.ap, qi_d[:, :, tsl].rearrange("h d t -> d h t"), qs)
            kb.load(kb.sp, gnb, gnb.ap, gn_d[tsl, :], qs)
            kb.load(kb.sp, wib, wib.ap, wi_d[tsl, :], qs)
            wk0 = max(0, 2 * j - 4)
            nwk = 2 * j + 2 - wk0
            woff = 6 - nwk
            for g in range(4):
                kb.load(kb.sp, kwT, kwT.ap[:, g, woff * 128:768], kwT_d[g][:, wk0 * 128:(2 * j + 2) * 128], ws_)
            for kl in range(woff, 6):
                kt_ = wk0 + kl - woff
                kb.load(kb.sp, vwA, vwA.ap[:, kl, :, 0:64], vw_d[kt_ * 128:(kt_ + 1) * 128, :].rearrange("p (g d) -> p g d", d=64), ws_)

            for g in range(4):
                o_c = at.ps_o.next()
                o_i = at.ps_o.next()
                first_i = [True]

                def sc_cmp(nt, ps, g=g):
                    kb.mm(ps, ps.ap, lhsT=kcmpT.ap[:, g, nt * 128:(nt + 1) * 128], rhs=qn.ap[:, 4 * g:4 * g + 4, :], start=True, stop=False, rd=[kcmpT, qn])
                    kb.mm(ps, ps.ap, lhsT=selc.ap[:, (j * 2 + nt) * 128:(j * 2 + nt + 1) * 128], rhs=pc.ap[:, 4 * g:4 * g + 4, :], start=False, stop=True,
                          rd=[selc, pc])

                def imp_mm(nt, h, e, _f, o_i=o_i, first_i=first_i):
                    kb.mm(o_i, o_i.ap[:, h * 64:(h + 1) * 64], lhsT=e.ap[:, h * 128:(h + 1) * 128], rhs=ovb.ap[:, nt, :], start=first_i[0], stop=False, rd=[e, ovb])
                    first_i[0] = False

                at.attend([0, 1], sc_cmp, o_c, lambda nt, h, g=g: (vcA.ap[:, nt, g, :], [vcA]), extra_o=imp_mm)
                sm = combine(o_c, 0, g, True)
                for h in range(4):
                    if h == 0:
                        kb.ew(kb.dve, lambda h=h: nc.vector.tensor_scalar(out=imp.ap[:, g, :], in0=o_i.ap[:, 0:64], scalar1=sm.ap[:, 0:1], scalar2=None, op0=ALU.mult),
                              wr=[imp], rd=[o_i, sm])
                    else:
                        kb.ew(kb.dve, lambda h=h: nc.vector.scalar_tensor_tensor(out=imp.ap[:, g, :], in0=o_i.ap[:, h * 64:(h + 1) * 64], scalar=sm.ap[:, h:h + 1],
                                                                                  in1=imp.ap[:, g, :], op0=ALU.mult, op1=ALU.add), wr=[imp], rd=[o_i, sm])
            for g in range(4):
                kb.ew(kb.dve, lambda g=g: nc.vector.tensor_tensor(out=imp.ap[:, g, :], in0=imp.ap[:, g, :], in1=visb.ap[:, j, :], op=ALU.add), wr=[imp], rd=[visb])
                kb.ew(kb.dve, lambda g=g: nc.vector.max(out=m8.ap[:, g, :], in_=imp.ap[:, g, :]), wr=[m8], rd=[imp])
                kb.ew(kb.dve, lambda g=g: nc.vector.match_replace(out=imp2.ap[:, g, :], in_to_replace=m8.ap[:, g, :], in_values=imp.ap[:, g, :], imm_value=-3e38),
                      wr=[imp2], rd=[imp, m8])
                kb.ew(kb.dve, lambda g=g: nc.vector.max(out=m8b.ap[:, g, :], in_=imp2.ap[:, g, :]), wr=[m8b], rd=[imp2])
                kb.ew(kb.dve, lambda g=g: nc.vector.tensor_scalar(out=negsel.ap[:, g, :], in0=imp.ap[:, g, :], scalar1=m8b.ap[:, g, 7:8], scalar2=-BIG,
                                                                   op0=ALU.is_lt, op1=ALU.mult), wr=[negsel], rd=[imp, m8b])

            for g in range(4):
                nx = nsx.next()
                kb.ew(kb.pool, lambda g=g, nx=nx: nc.gpsimd.tensor_copy(
                    out=nx.ap[:, 0:N].rearrange("p (b k) -> p b k", k=64),
                    in_=negsel.ap[:, g, 0:N // 64].unsqueeze(2).broadcast_to([128, N // 64, 64])), wr=[nx], rd=[negsel])
                o_s = at.ps_o.next()

                def sc_sel(kt, ps, g=g, nx=nx):
                    r = kt - 2 * j
                    kb.mm(ps, ps.ap, lhsT=ksT.ap[:, g, kt * 128:(kt + 1) * 128], rhs=qn.ap[:, 4 * g:4 * g + 4, :], start=True, stop=False, rd=[ksT, qn])
                    near = r >= -1
                    kb.mm(ps, ps.ap, lhsT=nx.ap[:, kt * 128:(kt + 1) * 128], rhs=at.i4.ap, start=False, stop=not near, rd=[nx, at.i4])
                    if near:
                        kb.mm(ps, ps.ap, lhsT=at.rev.ap, rhs=bsel.ap[:, r + 1, 4 * g:4 * g + 4, :], start=False, stop=True, rd=[at.rev, bsel])

                at.attend(list(range(nkt)), sc_sel, o_s, lambda kt, h, g=g: (vsA.ap[:, kt, g, :], [vsA]))
                combine(o_s, 1, g, False)
                o_w = at.ps_o.next()

                def sc_win(kl, ps, g=g):
                    kb.mm(ps, ps.ap, lhsT=kwT.ap[:, g, kl * 128:(kl + 1) * 128], rhs=qn.ap[:, 4 * g:4 * g + 4, :], start=True, stop=False, rd=[kwT, qn])
                    kb.mm(ps, ps.ap, lhsT=at.rev.ap, rhs=bwin.ap[:, kl, 4 * g:4 * g + 4, :], start=False, stop=True, rd=[at.rev, bwin])

                at.attend(list(range(woff, 6)), sc_win, o_w, lambda kl, h, g=g: (vwA.ap[:, kl, g, :], [vwA]))
                combine(o_w, 2, g, False)
            for c4 in range(2):
                pst = at.ps_s.next()
                for cc in range(4):
                    c = c4 * 4 + cc
                    kb.mm_t(pst, pst.ap[:, cc * 128:(cc + 1) * 128], onacc.ap[:, 2 * c:2 * c + 2, :].rearrange("p h d -> p (h d)"), at.identf.ap, first=(cc == 0),
                            rd=[onacc, at.identf])
                tb = onst.next()
                kb.ew(kb.act, lambda pst=pst, tb=tb: nc.scalar.copy(out=tb.ap, in_=pst.ap), wr=[tb], rd=[pst])
                kb.store(kb.sp, onT_d[c4 * 512:(c4 + 1) * 512, tsl].rearrange("(c p) t -> p c t", p=128), tb, tb.ap.rearrange("p (c t) -> p c t", t=128), os1,
                         final=(ekb is None))

            nch = (N + 511) // 512
            for h in range(16):
                for c in range(nch):
                    w = min(512, N - c * 512)
                    ps = at.ps_s.next()
                    kb.mm(ps, ps.ap[:, 0:w], lhsT=qi.ap[:, h, :], rhs=kiT.ap[:, c * 512:c * 512 + w], start=True, stop=True, rd=[qi, kiT])
                    t = rt.next()
                    kb.ew(kb.act, lambda ps=ps, t=t, w=w: nc.scalar.activation(out=t.ap[:, 0:w], in_=ps.ap[:, 0:w], func=AF.Relu), wr=[t], rd=[ps])
                    if h == 0:
                        kb.ew(kb.dve, lambda t=t, c=c, w=w: nc.vector.tensor_scalar(out=score.ap[:, c * 512:c * 512 + w], in0=t.ap[:, 0:w], scalar1=wib.ap[:, 0:1],
                                                                                    scalar2=None, op0=ALU.mult), wr=[score], rd=[t, wib])
                    else:
                        kb.ew(kb.dve, lambda t=t, c=c, w=w, h=h: nc.vector.scalar_tensor_tensor(
                            out=score.ap[:, c * 512:c * 512 + w], in0=t.ap[:, 0:w], scalar=wib.ap[:, h:h + 1], in1=score.ap[:, c * 512:c * 512 + w],
                            op0=ALU.mult, op1=ALU.add), wr=[score], rd=[t, wib])
            kb.ew(kb.dve, lambda: nc.vector.tensor_reduce(out=bis.ap[:, 1:2], in_=score.ap[:, 0:N], axis=AX.X, op=ALU.max, apply_absolute_value=True),
                  wr=[bis], rd=[score])
            kb.ew(kb.dve, lambda: nc.vector.tensor_scalar(out=bis.ap[:, 1:2], in0=bis.ap[:, 1:2], scalar1=1.001, scalar2=1e-6, op0=ALU.mult, op1=ALU.add), wr=[bis])
            kb.ew(kb.dve, lambda: nc.vector.tensor_scalar(out=bis.ap[:, 0:1], in0=bis.ap[:, 1:2], scalar1=-1.0, scalar2=None, op0=ALU.mult), wr=[bis])
            kb.ew(kb.dve, lambda: nc.vector.memset(bis.ap[:, 2:3], 0.0), wr=[bis])
            kb.ew(kb.dve, lambda: nc.vector.tensor_tensor(out=score.ap[:, N - 256:N], in0=score.ap[:, N - 256:N], in1=bidx.ap.rearrange("p r k -> p (r k)"), op=ALU.add),
                  wr=[score], rd=[bidx])
            for it in range(17):
                kb.ew(kb.dve, lambda: nc.vector.tensor_scalar(out=junk.ap[:, 0:N], in0=score.ap[:, 0:N], scalar1=bis.ap[:, 2:3], scalar2=None, op0=ALU.is_ge,
                                                               op1=ALU.add, accum_out=bis.ap[:, 3:4]), wr=[junk, bis], rd=[score])
                kb.ew(kb.dve, lambda: nc.vector.tensor_scalar(out=bis.ap[:, 4:5], in0=bis.ap[:, 3:4], scalar1=255.5, scalar2=None, op0=ALU.is_ge), wr=[bis])
                kb.ew(kb.dve, lambda: nc.vector.tensor_tensor(out=bis.ap[:, 5:6], in0=bis.ap[:, 2:3], in1=bis.ap[:, 0:1], op=ALU.subtract), wr=[bis])
                kb.ew(kb.dve, lambda: nc.vector.tensor_tensor(out=bis.ap[:, 6:7], in0=bis.ap[:, 1:2], in1=bis.ap[:, 2:3], op=ALU.subtract), wr=[bis])
                kb.ew(kb.dve, lambda: nc.vector.scalar_tensor_tensor(out=bis.ap[:, 0:1], in0=bis.ap[:, 5:6], scalar=bis.ap[:, 4:5], in1=bis.ap[:, 0:1],
                                                                      op0=ALU.mult, op1=ALU.add), wr=[bis])
                kb.ew(kb.dve, lambda: nc.vector.scalar_tensor_tensor(out=bis.ap[:, 1:2], in0=bis.ap[:, 6:7], scalar=bis.ap[:, 4:5], in1=bis.ap[:, 2:3],
                                                                      op0=ALU.mult, op1=ALU.add), wr=[bis])
                kb.ew(kb.dve, lambda: nc.vector.scalar_tensor_tensor(out=bis.ap[:, 2:3], in0=bis.ap[:, 0:1], scalar=bis.ap[:, 1:2], in1=halfc.ap[:, 0:1],
                                                                      op0=ALU.add, op1=ALU.mult), wr=[bis], rd=[halfc])
            kb.ew(kb.dve, lambda: nc.vector.tensor_scalar(out=nmask.ap[:, 0:N], in0=score.ap[:, 0:N], scalar1=bis.ap[:, 0:1], scalar2=-BIG, op0=ALU.is_lt, op1=ALU.mult),
                  wr=[nmask], rd=[score, bis])
            kb.store(kb.sp, nm_d[j, :, 0:N], nmask, nmask.ap[:, 0:N], os2, final=(ekb is None))
        if ekb is None:
            kb.finish()
        else:
            kb.end_phase()
    return nc


def build_C(env=None, prefix=""):
    nc, ekb, dti, dto = _env(env, prefix)
    qd_d = dti("qd", [16, 64, NTOK], BF16)
    cnT_d = dti("cnT", [256, S], BF16)
    wuk_d = dti("wuk", [256, 1024])
    wuv_d = dti("wuv", [256, 1024])
    nm_d = dti("negmask", [16, 128, S], BF16)
    relb_d = dti("relb", [32, 32])
    ohg_d = dti("ohg_d", [33, 3 * NU])
    ident_d = dti("ident", [128, 128])
    rev_d = dti("rev", [128, 128])
    odT_d = dto("odT", [1024, NTOK])
    gsc_t = nc.dram_tensor(prefix + "gscratch_d", [32, 3 * NU], F32, kind="Internal")
    with ExitStack() as st:
        if ekb is None:
            kb = KB(nc, st)
        else:
            kb = ekb
            kb.begin_phase(prefix, st)
        at = Attn(kb, ident_d, rev_d)
        at.build_bias(relb_d, ohg_d, 3 * NU, gsc_t.ap())
        ld, ldp, qs, ms, os1 = kb.slot(), kb.slot(), kb.slot(), kb.slot(), kb.slot()
        cnT = Buf(kb.sbuf("sc_cnT", [128, 2, S], BF16)[:])
        wuk = Buf(kb.sbuf("sc_wuk", [128, 2, 1024], BF16)[:])
        wuv = Buf(kb.sbuf("sc_wuv", [128, 2, 1024], BF16)[:])
        bd = Buf(kb.sbuf("sc_bd", [128, 3, 16, 128], BF16)[:])
        kdT = Buf(kb.sbuf("sc_kdT", [64, 4, S], BF16)[:])
        vdA = Buf(kb.sbuf("sc_vdA", [128, 32, 4, 65], BF16)[:])
        qd = Buf(kb.sbuf("sc_qd", [64, 4, 128], BF16)[:])
        nmk = Ring([Buf(kb.sbuf("sc_nm%d" % i, [128, S], BF16)[:]) for i in range(2)])
        odb = Ring([Buf(kb.sbuf("sc_od%d" % i, [128, 4, 64], F32)[:]) for i in range(2)])
        odst = Ring([Buf(kb.sbuf("sc_odst%d" % i, [128, 256], F32)[:]) for i in range(2)])
        for c in range(2):
            kb.load(kb.sp, cnT, cnT.ap[:, c, :], cnT_d[c * 128:(c + 1) * 128, :], ld)
        kb.load(kb.pool, wuk, wuk.ap, wuk_d.rearrange("(k p) c -> p k c", p=128), ldp)
        kb.load(kb.pool, wuv, wuv.ap, wuv_d.rearrange("(k p) c -> p k c", p=128), ldp)
        for r in range(3):
            at.load_skew(bd, bd.ap[:, r, :, :], gsc_t, r * NU + 127, 16, 16, 1, ldp)
        kb.ew(kb.pool, lambda: nc.gpsimd.memset(vdA.ap, 1.0), wr=[vdA])
        flip = [0]
        for hg in range(4):
            for h in range(4):
                hh = hg * 4 + h
                for c8 in range(S // 512):
                    ps = at.ps_s.next()
                    for c in range(2):
                        kb.mm(ps, ps.ap[0:64, :], lhsT=wuk.ap[:, c, hh * 64:(hh + 1) * 64], rhs=cnT.ap[:, c, c8 * 512:(c8 + 1) * 512],
                              start=(c == 0), stop=(c == 1), rd=[wuk, cnT])
                    flip[0] ^= 1
                    if flip[0]:
                        kb.ew(kb.act, lambda ps=ps, h=h, c8=c8: nc.scalar.copy(out=kdT.ap[:, h, c8 * 512:(c8 + 1) * 512], in_=ps.ap[0:64, :]), wr=[kdT], rd=[ps])
                    else:
                        kb.ew(kb.dve, lambda ps=ps, h=h, c8=c8: nc.vector.tensor_copy(out=kdT.ap[:, h, c8 * 512:(c8 + 1) * 512], in_=ps.ap[0:64, :]), wr=[kdT], rd=[ps])
            for kt in range(32):
                ps = at.ps_s.next()
                for c in range(2):
                    kb.mm(ps, ps.ap[:, 0:256], lhsT=cnT.ap[:, c, kt * 128:(kt + 1) * 128], rhs=wuv.ap[:, c, hg * 256:(hg + 1) * 256],
                          start=(c == 0), stop=(c == 1), rd=[wuv, cnT])
                kb.ew(kb.dve, lambda ps=ps, kt=kt: nc.vector.tensor_copy(out=vdA.ap[:, kt, :, 0:64], in_=ps.ap[:, 0:256].rearrange("p (h d) -> p h d", d=64)),
                      wr=[vdA], rd=[ps])
            for j in range(16):
                tsl = slice(j * 128, (j + 1) * 128)
                nkt = 2 * j + 2
                N = nkt * 128
                kb.load(kb.sp, qd, qd.ap, qd_d[hg * 4:(hg + 1) * 4, :, tsl].rearrange("h d t -> d h t"), qs)
                nm = nmk.next()
                kb.load(kb.sp, nm, nm.ap[:, 0:N], nm_d[j, :, 0:N], ms)
                o_d = at.ps_o.next()

                def sc_d(kt, ps, nm=nm, j=j):
                    r = kt - 2 * j
                    for h in range(4):
                        kb.mm(ps, ps.ap[:, h * 128:(h + 1) * 128], lhsT=kdT.ap[:, h, kt * 128:(kt + 1) * 128], rhs=qd.ap[:, h, :], start=(h == 0), stop=False,
                              rd=[kdT, qd])
                    near = r >= -1
                    kb.mm(ps, ps.ap, lhsT=nm.ap[:, kt * 128:(kt + 1) * 128], rhs=at.i4.ap, start=False, stop=not near, rd=[nm, at.i4])
                    if near:
                        kb.mm(ps, ps.ap, lhsT=at.rev.ap, rhs=bd.ap[:, r + 1, hg * 4:(hg + 1) * 4, :], start=False, stop=True, rd=[at.rev, bd])

                at.attend(list(range(nkt)), sc_d, o_d, lambda kt, h: (vdA.ap[:, kt, h, :], [vdA]))
                sm = at.small.next()
                ob = odb.next()
                ov_ = o_d.ap[:, 0:260].rearrange("p (h c) -> p h c", c=65)
                kb.ew(kb.dve, lambda: nc.vector.reciprocal(out=sm.ap[:, 0:4], in_=ov_[:, :, 64]), wr=[sm], rd=[o_d])
                for h in range(4):
                    kb.ew(kb.dve, lambda h=h: nc.vector.tensor_scalar(out=ob.ap[:, h, :], in0=ov_[:, h, 0:64], scalar1=sm.ap[:, h:h + 1], scalar2=None, op0=ALU.mult),
                          wr=[ob], rd=[o_d, sm])
                pst = at.ps_s.next()
                for cc in range(2):
                    kb.mm_t(pst, pst.ap[:, cc * 128:(cc + 1) * 128], ob.ap[:, 2 * cc:2 * cc + 2, :].rearrange("p h d -> p (h d)"), at.identf.ap, first=(cc == 0),
                            rd=[ob, at.identf])
                tb = odst.next()
                kb.ew(kb.act, lambda pst=pst, tb=tb: nc.scalar.copy(out=tb.ap, in_=pst.ap[:, 0:256]), wr=[tb], rd=[pst])
                kb.store(kb.sp, odT_d[hg * 256:(hg + 1) * 256, tsl].rearrange("(c p) t -> p c t", p=128), tb, tb.ap.rearrange("p (c t) -> p c t", t=128), os1,
                         final=(ekb is None))
        if ekb is None:
            kb.finish()
        else:
            kb.end_phase()
    return nc


def build_D(last=False, env=None, prefix=""):
    nc, ekb, dti, dto = _env(env, prefix)
    x1T = dti("x1T", [D, NTOK])
    onT = dti("onT", [1024, NTOK])
    odT = dti("odT", [1024, NTOK])
    amT = dti("amT", [4096, NTOK])
    mod_d = dti("modt_in", [128, 144])
    A_d = dti("A_in", [128, 48])
    GT_d = dti("GT_in", [128, 48])
    wun = dti("wun", [1024, D])
    wud = dti("wud", [1024, D])
    wo = dti("wo", [D, D])
    wfi = dti("wfi", [D, 2 * FF])
    wfo = dti("wfo", [FF, D])
    if last:
        gfT = dti("gfT", [128, KC])
    x3T = dto("x3T", [D, NTOK])
    with ExitStack() as st:
        if ekb is None:
            kb = KB(nc, st)
        else:
            kb = ekb
            kb.begin_phase(prefix, st)
        dn = Dense(kb)
        kb.load(kb.sp, dn.mod, dn.mod.ap, mod_d, dn.cslot)
        kb.load(kb.sp, dn.A, dn.A.ap, A_d, dn.cslot)
        kb.load(kb.sp, dn.GT, dn.GT.ap, GT_d, dn.cslot)
        a_st = Ring([Buf(kb.sbuf("a_st%d" % i, [128, 512], F32)[:]) for i in range(4)])
        a_sl = [kb.slot() for _ in range(4)]
        osl = kb.slot()
        if last:
            gf = Buf(kb.sbuf("gf", [128, KC], F32)[:])
            kb.load(kb.sp, gf, gf.ap, gfT, dn.cslot)
        for tt in range(NTT):
            n0 = tt * TT
            sl_tok = slice(n0, n0 + TT)
            load_xtile(kb, dn, x1T, n0)
            for k in range(8):
                kb.load(kb.pool, dn.hid[k], dn.hid[k].ap, onT[k * 128:(k + 1) * 128, sl_tok], osl)
                kb.load(kb.pool, dn.hid[8 + k], dn.hid[8 + k].ap, odT[k * 128:(k + 1) * 128, sl_tok], osl)
            hold = {}

            def ep_up(pi, st_, psb):
                which, m = pi % 2, pi // 2
                if which == 0:
                    hold[m] = psb
                    return
                psn = hold.pop(m)
                a0, a1 = a_st.next(), a_st.next()
                i0 = (a_st.i - 2) % 4
                i1 = (a_st.i - 1) % 4
                kb.load(kb.sp, a0, a0.ap, amT[m * 128:(m + 1) * 128, sl_tok], a_sl[i0])
                kb.load(kb.sp, a1, a1.ap, amT[2048 + m * 128:2048 + (m + 1) * 128, sl_tok], a_sl[i1])
                kb.ew(kb.dve, lambda: nc.vector.tensor_tensor(out=a0.ap, in0=a0.ap, in1=psn.ap, op=ALU.mult), wr=[a0], rd=[psn])
                kb.ew(kb.dve, lambda: nc.vector.tensor_tensor(out=a1.ap, in0=a1.ap, in1=psb.ap, op=ALU.mult), wr=[a1], rd=[psb])
                kb.ew(kb.dve, lambda: nc.vector.tensor_tensor(out=dn.h[m].ap, in0=a0.ap, in1=a1.ap, op=ALU.add), wr=[dn.h[m]], rd=[a0, a1])

            for m0 in range(0, KC, 4):
                srcn = lambda kc, st_: (dn.hid[kc].ap, [dn.hid[kc]])
                srcd = lambda kc, st_: (dn.hid[8 + kc].ap, [dn.hid[8 + kc]])
                emit_gemm_fm(kb, dn, wun, 8, [((m0 + i) * 128, 128) for i in range(4)], srcn, lambda pi, st_, psb, m0=m0: ep_up(2 * (m0 + pi), st_, psb))
                emit_gemm_fm(kb, dn, wud, 8, [((m0 + i) * 128, 128) for i in range(4)], srcd, lambda pi, st_, psb, m0=m0: ep_up(2 * (m0 + pi) + 1, st_, psb))

            def ep_o(pi, st_, psb):
                xb = dn.xres[pi]
                kb.ew(kb.dve, lambda: nc.vector.scalar_tensor_tensor(out=xb.ap, in0=psb.ap, scalar=dn.GT.ap[:, 16 + pi:17 + pi], in1=xb.ap, op0=ALU.mult, op1=ALU.add),
                      wr=[xb], rd=[psb, dn.GT])

            ysrc = lambda kc, st_: (dn.h[kc].ap, [dn.h[kc]])
            emit_gemm_fm(kb, dn, wo, KC, [(m * 128, 128) for m in range(KC)], ysrc, ep_o)
            emit_ffn(kb, dn, 2, wfi, wfo, None if last else x3T, n0, final_out=(ekb is None))
            if last:
                for kc in range(KC):
                    sq = dn.sq.next()
                    kb.ew(kb.act, lambda kc=kc, sq=sq: nc.scalar.activation(out=sq.ap, in_=dn.xres[kc].ap, func=AF.Square), wr=[sq], rd=[dn.xres[kc]])
                    kb.mm(dn.ps_stat, dn.ps_stat.ap, lhsT=dn.ones.ap, rhs=sq.ap, start=(kc == 0), stop=(kc == KC - 1), rd=[dn.ones, sq])
                t = dn.tmp.next()
                kb.ew(kb.act, lambda: nc.scalar.activation(out=t.ap, in_=dn.ps_stat.ap, func=AF.Sqrt, scale=1.0 / D, bias=dn.epsb.ap[:, 0:1]), wr=[t], rd=[dn.ps_stat, dn.epsb])
                kb.ew(kb.dve, lambda: nc.vector.reciprocal(out=dn.rstd.ap, in_=t.ap), wr=[dn.rstd], rd=[t])
                for kc in range(KC):
                    xb = dn.xres[kc]
                    kb.ew(kb.dve, lambda kc=kc, xb=xb: nc.vector.scalar_tensor_tensor(out=xb.ap, in0=xb.ap, scalar=gf.ap[:, kc:kc + 1], in1=dn.rstd.ap,
                                                                                      op0=ALU.mult, op1=ALU.mult), wr=[xb], rd=[gf, dn.rstd])
                    kb.store(kb.sp, x3T[kc * 128:(kc + 1) * 128, sl_tok], xb, xb.ap, dn.oslot, final=True)
        if ekb is None:
            kb.finish()
        else:
            kb.end_phase()
    return nc


DEPTH = 2


def build_fused(depth=DEPTH):
    nc = bass.Bass("TRN2", target_bir_lowering=False)
    gi = lambda name, shape, d=F32: nc.dram_tensor(name, shape, d, kind="ExternalInput").ap()
    sc = lambda name, shape, d=F32: nc.dram_tensor(name, shape, d, kind="Internal").ap()
    I = {}
    for h in range(2):
        I["xT%d" % h] = gi("xT%d" % h, [D, NTOK])
        for nm, shp in (("ohg", [33, NPAT_N * NU + GC_W]), ("selc", [17, 16 * 2 * 128]), ("visb", [128, 16, 64]), ("bidx", [128, 2, 128]), ("ohg_d", [33, 3 * NU])):
            I["%s%d" % (nm, h)] = gi("%s%d" % (nm, h), shp)
    I["ov"] = gi("ov", [128, 2, 64])
    I["ident"] = gi("ident", [128, 128])
    I["rev"] = gi("rev", [128, 128])
    I["cT"] = gi("cT", [128, KC])
    I["relb"] = gi("relb", [32, 32])
    I["gfT"] = gi("gfT", [128, KC])
    for nm, shp in (("w_ada", [D, 9 * D]), ("badaT", [128, 144]), ("gnT", [128, 48]), ("w_in", [D, N_IN]), ("gkvT", [128, 2]),
                    ("wk1", [2048, 256]), ("wv1", [2048, 256]), ("wk2", [256, 64]), ("wv2", [256, 64]), ("pekT", [64, 32]), ("pevT", [64, 32]),
                    ("wuk", [256, 1024]), ("wuv", [256, 1024]), ("wun", [1024, D]), ("wud", [1024, D]), ("wo", [D, D])):
        I[nm] = gi(nm, [depth] + shp)
    I["w_ffn_in"] = gi("w_ffn_in", [depth, 2, D, 2 * FF])
    I["w_ffn_out"] = gi("w_ffn_out", [depth, 2, FF, D])
    outs = [nc.dram_tensor("out%d" % h, [D, NTOK], F32, kind="ExternalOutput").ap() for h in range(2)]
    Sx = {}
    for h in range(2):
        for nm, shp, d in (("x1T", [D, NTOK], F32), ("xn", [D, NTOK], F32), ("qn", [16, 64, NTOK], BF16), ("qi", [16, 32, NTOK], BF16),
                           ("qd", [16, 64, NTOK], BF16), ("gn", [NTOK, 48], F32), ("wi", [NTOK, 16], F32), ("am", [4096, NTOK], F32),
                           ("onT", [1024, NTOK], F32), ("odT", [1024, NTOK], F32), ("negmask", [16, 128, S], BF16)):
            Sx["%s%d" % (nm, h)] = sc("s_%s%d" % (nm, h), shp, d)
    for nm, shp, d in (("ks", [4, 64, S], BF16), ("kw", [4, 64, S], BF16), ("kc", [4, 64, S], BF16), ("vc", [4, 64, S], BF16),
                       ("vs", [S, 256], BF16), ("vw", [S, 256], BF16), ("ki", [32, S], BF16), ("cn", [256, S], BF16),
                       ("mod", [128, 144], F32), ("A", [128, 48], F32), ("GT", [128, 48], F32)):
        Sx[nm] = sc("s_" + nm, shp, d)
    with ExitStack() as st:
        kb = KB(nc, st)
        kb.setup_fused()
        for l in range(depth):
            last = (l == depth - 1)
            for h in range(2):
                T = {"xT": I["xT%d" % h] if l == 0 else Sx["xn%d" % h], "cT": I["cT"], "wada": I["w_ada"][l], "badaT": I["badaT"][l], "gnT": I["gnT"][l],
                     "wfi": I["w_ffn_in"][l, 0], "wfo": I["w_ffn_out"][l, 0], "win": I["w_in"][l], "gkvT": I["gkvT"][l], "x1T": Sx["x1T%d" % h],
                     "o_qn": Sx["qn%d" % h], "o_qi": Sx["qi%d" % h], "o_qd": Sx["qd%d" % h], "o_gn": Sx["gn%d" % h], "o_wi": Sx["wi%d" % h], "o_am": Sx["am%d" % h],
                     "o_kc": Sx["kc"], "o_vc": Sx["vc"], "o_ks": Sx["ks"], "o_kw": Sx["kw"], "o_ki": Sx["ki"], "o_cn": Sx["cn"], "o_vs": Sx["vs"], "o_vw": Sx["vw"],
                     "o_mod": Sx["mod"], "o_A": Sx["A"], "o_GT": Sx["GT"]}
                build_A(env={"nc": nc, "kb": kb, "T": T}, prefix="L%dA%d_" % (l, h), ghalf=h, load_tabs=(h == 1))
            for h in range(2):
                T = {"qn": Sx["qn%d" % h], "qi": Sx["qi%d" % h], "gn": Sx["gn%d" % h], "wi": Sx["wi%d" % h], "ksT": Sx["ks"], "kwT": Sx["kw"], "kcT": Sx["kc"],
                     "vcT": Sx["vc"], "vs": Sx["vs"], "vw": Sx["vw"], "kiT": Sx["ki"], "wk1": I["wk1"][l], "wv1": I["wv1"][l], "wk2": I["wk2"][l], "wv2": I["wv2"][l],
                     "pekT": I["pekT"][l], "pevT": I["pevT"][l], "relb": I["relb"], "ohg": I["ohg%d" % h], "selc": I["selc%d" % h], "ov": I["ov"],
                     "visb": I["visb%d" % h], "bidx": I["bidx%d" % h], "ident": I["ident"], "rev": I["rev"], "onT": Sx["onT%d" % h], "negmask": Sx["negmask%d" % h]}
                build_B(env={"nc": nc, "kb": kb, "T": T}, prefix="L%dB%d_" % (l, h))
            for h in range(2):
                T = {"qd": Sx["qd%d" % h], "cnT": Sx["cn"], "wuk": I["wuk"][l], "wuv": I["wuv"][l], "negmask": Sx["negmask%d" % h], "relb": I["relb"],
                     "ohg_d": I["ohg_d%d" % h], "ident": I["ident"], "rev": I["rev"], "odT": Sx["odT%d" % h]}
                build_C(env={"nc": nc, "kb": kb, "T": T}, prefix="L%dC%d_" % (l, h))
            for h in range(2):
                T = {"x1T": Sx["x1T%d" % h], "onT": Sx["onT%d" % h], "odT": Sx["odT%d" % h], "amT": Sx["am%d" % h], "modt_in": Sx["mod"], "A_in": Sx["A"],
                     "GT_in": Sx["GT"], "wun": I["wun"][l], "wud": I["wud"][l], "wo": I["wo"][l], "wfi": I["w_ffn_in"][l, 1], "wfo": I["w_ffn_out"][l, 1],
                     "gfT": I["gfT"], "x3T": outs[h] if last else Sx["xn%d" % h]}
                build_D(last, env={"nc": nc, "kb": kb, "T": T}, prefix="L%dD%d_" % (l, h))
        kb.finish()
    return nc


_PROG = []


def _core_tokens(a, b, half):
    t = a[b].reshape(32, 128, *a.shape[2:])[half::2]
    return t.reshape(NTOK, *a.shape[2:])


def kernel(x, c, w_ada, b_ada, g_norm, w_ffn_in, w_ffn_out, w_in, nsa_pe_k, nsa_pe_v, nsa_cmp_k1, nsa_cmp_k2,
           nsa_cmp_v1, nsa_cmp_v2, dsa_g_kv, dsa_w_uk, dsa_w_uv, w_up_nsa, w_up_dsa, w_out, rel_bias, g_final):
    f32 = np.float32
    A = lambda v: np.ascontiguousarray(np.asarray(v, f32))
    x = A(x)
    c = A(c)
    depth = np.asarray(w_ada).shape[0]
    if not _PROG:
        _PROG.append(build_fused(depth))
    nc = _PROG[0]
    ident = np.eye(128, dtype=f32)
    shared = {"ident": ident, "rev": np.ascontiguousarray(ident[::-1]), "relb": A(rel_bias), "gfT": A(np.asarray(g_final).reshape(16, 128).T),
              "w_ada": A(w_ada), "badaT": A(np.asarray(b_ada).reshape(depth, 144, 128).transpose(0, 2, 1)),
              "gnT": A(np.asarray(g_norm).reshape(depth, 48, 128).transpose(0, 2, 1)), "w_in": A(w_in),
              "gkvT": A(np.asarray(dsa_g_kv).reshape(depth, 2, 128).transpose(0, 2, 1)),
              "wk1": A(nsa_cmp_k1), "wv1": A(nsa_cmp_v1), "wk2": A(nsa_cmp_k2), "wv2": A(nsa_cmp_v2),
              "pekT": A(np.asarray(nsa_pe_k).transpose(0, 2, 1)), "pevT": A(np.asarray(nsa_pe_v).transpose(0, 2, 1)),
              "wuk": A(np.asarray(dsa_w_uk).reshape(depth, 256, 1024)), "wuv": A(np.asarray(dsa_w_uv).reshape(depth, 256, 1024)),
              "wun": A(w_up_nsa), "wud": A(w_up_dsa), "wo": A(w_out), "w_ffn_in": A(w_ffn_in), "w_ffn_out": A(w_ffn_out)}
    for h in range(2):
        cn_, cd_ = host_consts_nsa(h), host_consts_dsa(h)
        shared["ohg%d" % h] = cn_["ohg"]
        shared["selc%d" % h] = cn_["selc"]
        shared["visb%d" % h] = cn_["visb"]
        shared["bidx%d" % h] = cn_["bidx"]
        shared["ohg_d%d" % h] = cd_["ohg_d"]
        shared["ov"] = cn_["ov"]
    in_maps = []
    for k in range(NCORES):
        b = k % 4
        m = dict(shared)
        m["xT0"] = np.ascontiguousarray(_core_tokens(x, b, 0).T)
        m["xT1"] = np.ascontiguousarray(_core_tokens(x, b, 1).T)
        m["cT"] = np.ascontiguousarray(c[b].reshape(16, 128).T)
        in_maps.append(m)
    res = run_bass_kernel_spmd(nc, in_maps, core_ids=list(range(NCORES))).results
    out = np.zeros(x.shape, f32)
    for b in range(4):
        for h in range(2):
            out[b].reshape(32, 128, D)[h::2] = np.ascontiguousarray(res[b]["out%d" % h].T).reshape(16, 128, D)
    return out
```

```python
import numpy as np
from contextlib import ExitStack
import concourse.bass as bass
import concourse.mybir as mybir
from concourse.bass_utils import run_bass_kernel_spmd

F32 = mybir.dt.float32
BF16 = mybir.dt.bfloat16
AF = mybir.ActivationFunctionType
ALU = mybir.AluOpType
AX = mybir.AxisListType

D = 2048
KC = 16
FF = 5632
FC = 44
NTOK = 2048
TT = 512
NTT = NTOK // TT
S = 4096
N_IN = 8544
EPS = 1e-6
NCORES = 8
INDEX_SCALE = 512 ** -0.5
WST_ELEMS = 11264

C_QN, C_KC, C_VC, C_KS, C_VS, C_KW, C_VW, C_GN = 0, 1024, 1280, 1536, 1792, 2048, 2304, 2560
C_QD, C_CKV, C_QI, C_KI, C_WI, C_AM = 2608, 3632, 3888, 4400, 4432, 4448


class Eng:
    def __init__(self, name, eng, sem):
        self.name, self.eng, self.sem = name, eng, sem
        self.cnt = 0
        self.seen = {}


class Buf:
    def __init__(self, ap):
        self.ap = ap
        self.wr = None
        self.rd = {}

    def wdeps(self):
        d = list(self.rd.values())
        if self.wr is not None:
            d.append(self.wr)
        return d

    def rdeps(self):
        return [self.wr] if self.wr is not None else []

    def wrote(self, h):
        self.wr = h
        self.rd = {}

    def read(self, h):
        k = h[0]
        if k not in self.rd or self.rd[k][2] < h[2]:
            self.rd[k] = h


class KB:
    def __init__(self, nc, st):
        self.nc, self.st = nc, st
        self.pe = Eng("pe", nc.tensor, st.enter_context(nc.semaphore("s_pe")))
        self.act = Eng("act", nc.scalar, st.enter_context(nc.semaphore("s_act")))
        self.dve = Eng("dve", nc.vector, st.enter_context(nc.semaphore("s_dve")))
        self.pool = Eng("pool", nc.gpsimd, st.enter_context(nc.semaphore("s_pool")))
        self.sp = Eng("sp", nc.sync, st.enter_context(nc.semaphore("s_sp")))
        self.nslot = 0
        self.slots = []
        self.prefix = ""
        self.pools = None
        self.banks = None
        self.nbank = 0
        self.out_handles = []
        self.uid = 0

    def sbuf(self, name, shape, dt):
        return self.st.enter_context(self.nc.sbuf_tensor(self.prefix + name, shape, dt))

    def psum(self, name, shape, dt=F32):
        if self.banks is not None:
            b = self.banks[self.nbank % 8]
            self.nbank += 1
            return b
        return self.st.enter_context(self.nc.psum_tensor(self.prefix + name, shape, dt))

    def setup_fused(self, n_sp=30, n_pool=12):
        self.banks = [self.st.enter_context(self.nc.psum_tensor("bank%d" % i, [128, 512], F32)) for i in range(8)]
        self.pools = {"sp": [], "pool": []}
        for kind, n in (("sp", n_sp), ("pool", n_pool)):
            for i in range(n):
                sem = self.st.enter_context(self.nc.semaphore("s_%s%d" % (kind, i)))
                self.pools[kind].append({"key": "%s%d" % (kind, i), "sem": sem, "cnt": 0, "kind": kind})

    def begin_phase(self, prefix, stack):
        self.prefix = prefix
        self.main_st = self.st
        self.st = stack
        self.nbank = 0
        self.phase_recs = []

    def end_phase(self):
        self.barrier()
        for r in self.phase_recs:
            self.pools[r["kind"]].append(r)
        self.phase_recs = []
        self.slots = []
        self.st = self.main_st
        self.prefix = ""

    def slot(self):
        if self.pools is not None:
            return {"lazy": True}
        self.nslot += 1
        sem = self.st.enter_context(self.nc.semaphore("s_dma%d" % self.nslot))
        sl = {"key": "dma%d" % self.nslot, "sem": sem, "cnt": 0}
        self.slots.append(sl)
        return sl

    def barrier(self):
        engs = [self.pe, self.act, self.dve, self.pool, self.sp]
        for e in engs:
            for e2 in engs:
                if e2.cnt > 0:
                    self.wait(e, (e2.name, e2.sem, e2.cnt))
            for sl in self.slots:
                if sl["cnt"] > 0:
                    self.wait(e, (sl["key"], sl["sem"], sl["cnt"], sl))

    def wait(self, e, h):
        if h is None:
            return
        key, sem, cnt = h[0], h[1], h[2]
        if len(h) > 3:
            cnt = max(cnt, h[3]["cnt"])
            h[3]["waited"] = max(h[3].get("waited", 0), cnt)
        if e.seen.get(key, 0) >= cnt:
            return
        e.eng.wait_ge(sem, cnt)
        e.seen[key] = cnt

    def op(self, e, fn, deps=()):
        for d in deps:
            self.wait(e, d)
        ins = fn()
        e.cnt += 1
        ins.then_inc(e.sem, 1)
        return (e.name, e.sem, e.cnt)

    def dma(self, q, out, in_, slot, deps=()):
        if slot.get("lazy"):
            if "rec" not in slot:
                slot["rec"] = self.pools["pool" if q is self.pool else "sp"].pop()
                self.phase_recs.append(slot["rec"])
                self.slots.append(slot["rec"])
            assert slot["rec"]["kind"] == ("pool" if q is self.pool else "sp"), "slot used from both DMA queue kinds"
            slot = slot["rec"]
        for d in deps:
            self.wait(q, d)
        W = slot.get("waited", 0)
        if W > q.seen.get(slot["key"], 0):
            q.eng.wait_ge(slot["sem"], W)
            q.seen[slot["key"]] = W
        q.eng.dma_start(out=out, in_=in_).then_inc(slot["sem"], 16)
        slot["cnt"] += 16
        return (slot["key"], slot["sem"], slot["cnt"], slot)

    def mm(self, outb, out_ap, lhsT, rhs, start, stop, rd=(), extra=()):
        deps = list(extra)
        if start:
            deps += outb.wdeps()
        for b in rd:
            deps += b.rdeps()
        deps = [d for d in deps if d is not None and d[0] != "pe"]
        h = self.op(self.pe, lambda: self.nc.tensor.matmul(out_ap, lhsT=lhsT, rhs=rhs, start=start, stop=stop,
                                                           skip_group_check=True), deps)
        for b in rd:
            b.read(h)
        outb.wr = h
        if start:
            outb.rd = {}
        return h

    def mm_t(self, outb, out_ap, in_ap, ident_ap, first, rd=()):
        deps = []
        if first:
            deps += outb.wdeps()
        for b in rd:
            deps += b.rdeps()
        deps = [d for d in deps if d is not None and d[0] != "pe"]
        h = self.op(self.pe, lambda: self.nc.tensor.transpose(out_ap, in_ap, ident_ap), deps)
        for b in rd:
            b.read(h)
        outb.wr = h
        if first:
            outb.rd = {}
        return h

    def ew(self, e, fn, wr=(), rd=(), extra=()):
        deps = list(extra)
        for b in wr:
            deps += b.wdeps()
        for b in rd:
            deps += b.rdeps()
        h = self.op(e, fn, deps)
        for b in rd:
            b.read(h)
        for b in wr:
            b.wrote(h)
        return h

    def load(self, q, buf, out_ap, in_ap, slot, extra=()):
        h = self.dma(q, out_ap, in_ap, slot, list(extra) + buf.wdeps())
        buf.wrote(h)
        return h

    def store(self, q, dram_ap, buf, in_ap, slot, extra=(), final=False):
        h = self.dma(q, dram_ap, in_ap, slot, list(extra) + buf.rdeps())
        buf.read(h)
        if final:
            self.out_handles.append(h)
        return h

    def finish(self):
        last = {}
        for h in self.out_handles:
            if h[0] not in last or last[h[0]][2] < h[2]:
                last[h[0]] = h
        for h in last.values():
            self.wait(self.sp, h)


class Ring:
    def __init__(self, bufs):
        self.bufs = bufs
        self.i = 0

    def next(self):
        b = self.bufs[self.i % len(self.bufs)]
        self.i += 1
        return b


class Dense:
    def __init__(self, kb):
        self.kb = kb
        nc = kb.nc
        xres_t = kb.sbuf("xres", [128, KC, TT], F32)
        h_t = kb.sbuf("hmod", [128, KC, TT], BF16)
        hid_t = kb.sbuf("hid", [128, FC, TT], BF16)
        self.xres = [Buf(xres_t[:, k, :]) for k in range(KC)]
        self.h = [Buf(h_t[:, k, :]) for k in range(KC)]
        self.hid = [Buf(hid_t[:, k, :]) for k in range(FC)]
        self.hid_t = hid_t
        self.wst = Ring([Buf(kb.sbuf("wst%d" % i, [128, WST_ELEMS], BF16)[:]) for i in range(3)])
        self.wslots = [kb.slot() for _ in range(3)]
        self.ones = Buf(kb.sbuf("ones32", [128, 128], F32)[:])
        self.epsb = Buf(kb.sbuf("epsb", [128, 1], F32)[:])
        self.sq = Ring([Buf(kb.sbuf("sq%d" % i, [128, 512], F32)[:]) for i in range(2)])
        self.rstd = Buf(kb.sbuf("rstd", [128, TT], F32)[:])
        self.tmp = Ring([Buf(kb.sbuf("tmpf%d" % i, [128, 512], F32)[:]) for i in range(3)])
        self.ps = Ring([Buf(kb.psum("psg%d" % i, [128, 512])[:]) for i in range(6)])
        self.ps_stat = Buf(kb.psum("ps_stat", [128, 512])[:])
        self.ps_misc = Buf(kb.psum("ps_misc", [128, 512])[:])
        self.xslot = kb.slot()
        self.cslot = kb.slot()
        self.cslot2 = kb.slot()
        self.oslot = kb.slot()
        self.mod = Buf(kb.sbuf("modt", [128, 144], F32)[:])
        self.gn = Buf(kb.sbuf("gn", [128, 48], F32)[:])
        self.A = Buf(kb.sbuf("coefA", [128, 48], F32)[:])
        self.GT = Buf(kb.sbuf("coefG", [128, 48], F32)[:])
        kb.ew(kb.dve, lambda: nc.vector.memset(self.ones.ap, 1.0), wr=[self.ones])
        kb.ew(kb.dve, lambda: nc.vector.memset(self.epsb.ap, EPS), wr=[self.epsb])

    def wstage(self):
        i = self.wst.i % 3
        return self.wst.next(), self.wslots[i]


def emit_modprep(kb, dn, cT_d, wada_d, badaT_d, gnT_d, col_lo, col_hi):
    nc = kb.nc
    if True:
        cact = Buf(kb.sbuf("cact", [128, KC, 2], F32)[:])
        craw = Buf(kb.sbuf("craw", [128, KC], F32)[:])
        bada = Buf(kb.sbuf("bada", [128, 144], F32)[:])
        mod, gn = dn.mod, dn.gn
        kb.load(kb.sp, craw, craw.ap, cT_d, dn.cslot)
        kb.load(kb.sp, bada, bada.ap, badaT_d, dn.cslot)
        kb.load(kb.sp, gn, gn.ap, gnT_d, dn.cslot)
        for r in range(2):
            kb.ew(kb.act, lambda r=r: nc.scalar.activation(out=cact.ap[:, :, r], in_=craw.ap, func=AF.Silu),
                  wr=[cact], rd=[craw])
        psm = dn.ps_misc
        stg = [Buf(kb.sbuf("wada_stg%d" % i, [128, KC, 128], F32)[:]) for i in range(2)]
        sslots = [kb.slot(), kb.slot()]
        cbs = list(range(col_lo, col_hi))

        def issue(i):
            cb = cbs[i]
            b = stg[i % 2]
            src = wada_d[:, cb * 128:(cb + 1) * 128].rearrange("(k p) c -> p k c", p=128)
            kb.load(kb.sp, b, b.ap[:, 0:8, :], src[:, 0:8, :], sslots[i % 2])
            b.wr = kb.dma(kb.sp, b.ap[:, 8:16, :], src[:, 8:16, :], sslots[i % 2])

        issue(0)
        first = True
        for i, col in enumerate(cbs):
            if i + 1 < len(cbs):
                issue(i + 1)
            b = stg[i % 2]
            for kc in range(KC):
                kb.mm(psm, psm.ap[:, 2 * col:2 * col + 2], lhsT=b.ap[:, kc, :], rhs=cact.ap[:, kc, :],
                      start=first, stop=(kc == KC - 1), rd=[b, cact])
                first = False
        pv = psm.ap[:, 2 * col_lo:2 * col_hi].rearrange("p (c two) -> p c two", two=2)[:, :, 0]
        kb.ew(kb.dve, lambda: nc.vector.tensor_tensor(out=mod.ap[:, col_lo:col_hi], in0=pv, in1=bada.ap[:, col_lo:col_hi], op=ALU.add),
              wr=[mod], rd=[psm, bada])
        A, GT = dn.A, dn.GT
        for sub in range(3):
            if col_lo <= sub * 48 + 16 and sub * 48 + 32 <= col_hi:
                kb.ew(kb.dve, lambda sub=sub: nc.vector.scalar_tensor_tensor(
                    out=A.ap[:, sub * 16:(sub + 1) * 16], in0=mod.ap[:, sub * 48 + 16:sub * 48 + 32], scalar=1.0,
                    in1=gn.ap[:, sub * 16:(sub + 1) * 16], op0=ALU.add, op1=ALU.mult), wr=[A], rd=[mod, gn])
            if col_lo <= sub * 48 + 32 and sub * 48 + 48 <= col_hi:
                kb.ew(kb.dve, lambda sub=sub: nc.vector.tensor_scalar(
                    out=GT.ap[:, sub * 16:(sub + 1) * 16], in0=mod.ap[:, sub * 48 + 32:sub * 48 + 48],
                    scalar1=(1.0 if sub == 1 else 0.5), scalar2=None, op0=ALU.mult), wr=[GT], rd=[mod])


def emit_norm_mod(kb, dn, sub):
    nc = kb.nc
    xres, h = dn.xres, dn.h
    for st_ in range(TT // 512):
        sl = slice(st_ * 512, (st_ + 1) * 512)
        for kc in range(KC):
            sq = dn.sq.next()
            kb.ew(kb.act, lambda kc=kc, sq=sq: nc.scalar.activation(out=sq.ap, in_=xres[kc].ap[:, sl], func=AF.Square),
                  wr=[sq], rd=[xres[kc]])
            kb.mm(dn.ps_stat, dn.ps_stat.ap, lhsT=dn.ones.ap, rhs=sq.ap, start=(kc == 0), stop=(kc == KC - 1),
                  rd=[dn.ones, sq])
        t = dn.tmp.next()
        kb.ew(kb.act, lambda t=t: nc.scalar.activation(out=t.ap, in_=dn.ps_stat.ap, func=AF.Sqrt, scale=1.0 / D, bias=dn.epsb.ap[:, 0:1]),
              wr=[t], rd=[dn.ps_stat, dn.epsb])
        kb.ew(kb.dve, lambda t=t, sl=sl: nc.vector.reciprocal(out=dn.rstd.ap[:, sl], in_=t.ap), wr=[dn.rstd], rd=[t])
    for kc in range(KC):
        for st_ in range(TT // 512):
            sl = slice(st_ * 512, (st_ + 1) * 512)
            t = dn.tmp.next()
            kb.ew(kb.dve, lambda kc=kc, t=t, sl=sl: nc.vector.scalar_tensor_tensor(
                out=t.ap, in0=xres[kc].ap[:, sl], scalar=dn.A.ap[:, sub * 16 + kc:sub * 16 + kc + 1], in1=dn.rstd.ap[:, sl],
                op0=ALU.mult, op1=ALU.mult), wr=[t], rd=[xres[kc], dn.A, dn.rstd])
            kb.ew(kb.act, lambda kc=kc, t=t, sl=sl: nc.scalar.activation(
                out=h[kc].ap[:, sl], in_=t.ap, func=AF.Identity, bias=dn.mod.ap[:, sub * 48 + kc:sub * 48 + kc + 1], scale=1.0),
                wr=[h[kc]], rd=[t, dn.mod])


def emit_gemm_fm(kb, dn, w_d, kchunks, pieces, src, epilogue, cwcap=None):
    nc = kb.nc
    cwmax = (WST_ELEMS // kchunks) // 64 * 64
    if cwcap:
        cwmax = min(cwmax, cwcap)
    groups, cur = [], []
    for pi, (c0, wd) in enumerate(pieces):
        if cur and (cur[-1][1] + cur[-1][2] == c0) and (c0 + wd - cur[0][1] <= cwmax):
            cur.append((pi, c0, wd))
        else:
            if cur:
                groups.append(cur)
            cur = [(pi, c0, wd)]
    if cur:
        groups.append(cur)
    loaded = {}

    def issue(gi):
        g = groups[gi]
        c0 = g[0][1]
        wtot = g[-1][1] + g[-1][2] - c0
        wb, ws = dn.wstage()
        wv = wb.ap[:, 0:kchunks * wtot].rearrange("p (k c) -> p k c", c=wtot)
        kb.load(kb.pool, wb, wv, w_d[:, c0:c0 + wtot].rearrange("(k p) c -> p k c", p=128), ws)
        loaded[gi] = (wb, wv, c0)

    issue(0)
    if len(groups) > 1:
        issue(1)
    for gi, g in enumerate(groups):
        if gi + 2 < len(groups):
            issue(gi + 2)
        wb, wv, gc0 = loaded.pop(gi)
        for (pi, c0, wd) in g:
            for st_ in range(TT // 512):
                psb = dn.ps.next()
                for kc in range(kchunks):
                    sap, sbufs = src(kc, st_)
                    kb.mm(psb, psb.ap[0:wd, :], lhsT=wv[:, kc, c0 - gc0:c0 - gc0 + wd], rhs=sap,
                          start=(kc == 0), stop=(kc == kchunks - 1), rd=[wb] + sbufs)
                epilogue(pi, st_, psb)


def emit_ffn(kb, dn, sub, win_d, wout_d, xT_out_d, n0, final_out=False):
    nc = kb.nc
    emit_norm_mod(kb, dn, sub)
    hsrc = lambda kc, st_: (dn.h[kc].ap[:, st_ * 512:(st_ + 1) * 512], [dn.h[kc]])
    pieces = []
    for J in range(FF // 256):
        pieces += [(J * 256, 128), (J * 256 + 128, 128), (FF + J * 256, 128), (FF + J * 256 + 128, 128)]
    hold = {}

    def ep_in(pi, st_, psb):
        J, r = pi // 4, pi % 4
        if r < 2:
            hold[(2 * J + r, st_)] = psb
            return
        j = 2 * J + r - 2
        psg = hold.pop((j, st_))
        t = dn.tmp.next()
        sl = slice(st_ * 512, (st_ + 1) * 512)
        kb.ew(kb.act, lambda: nc.scalar.activation(out=t.ap, in_=psg.ap, func=AF.Silu), wr=[t], rd=[psg])
        kb.ew(kb.dve, lambda: nc.vector.tensor_tensor(out=dn.hid[j].ap[:, sl], in0=t.ap, in1=psb.ap, op=ALU.mult),
              wr=[dn.hid[j]], rd=[t, psb])

    emit_gemm_fm(kb, dn, win_d, KC, pieces, hsrc, ep_in, cwcap=256)

    def ep_out(pi, st_, psb):
        sl = slice(st_ * 512, (st_ + 1) * 512)
        xb = dn.xres[pi]
        kb.ew(kb.dve, lambda: nc.vector.scalar_tensor_tensor(
            out=xb.ap[:, sl], in0=psb.ap, scalar=dn.GT.ap[:, sub * 16 + pi:sub * 16 + pi + 1], in1=xb.ap[:, sl],
            op0=ALU.mult, op1=ALU.add), wr=[xb], rd=[psb, dn.GT])
        if xT_out_d is not None:
            kb.store(kb.sp, xT_out_d[pi * 128:(pi + 1) * 128, n0 + st_ * 512:n0 + (st_ + 1) * 512], xb, xb.ap[:, sl],
                     dn.oslot, final=final_out)

    hidsrc = lambda kc, st_: (dn.hid[kc].ap[:, st_ * 512:(st_ + 1) * 512], [dn.hid[kc]])
    emit_gemm_fm(kb, dn, wout_d, FC, [(m * 128, 128) for m in range(KC)], hidsrc, ep_out)


def load_xtile(kb, dn, xT_d, n0):
    for kc in range(KC):
        kb.load(kb.sp, dn.xres[kc], dn.xres[kc].ap, xT_d[kc * 128:(kc + 1) * 128, n0:n0 + TT], dn.xslot)


def _env(env, prefix):
    if env is None:
        nc = bass.Bass("TRN2", target_bir_lowering=False)
        gi = lambda name, shape, d=F32: nc.dram_tensor(name, shape, d, kind="ExternalInput").ap()
        go = lambda name, shape, d=F32: nc.dram_tensor(name, shape, d, kind="ExternalOutput").ap()
        return nc, None, gi, go
    nc = env["nc"]
    g = lambda name, shape, d=F32: env["T"][name]
    return nc, env["kb"], g, g

class OutStage:
    def __init__(self, kb, n=3):
        self.kb = kb
        self.f = [Buf(kb.sbuf("ostgf%d" % i, [128, 512], F32)[:]) for i in range(n)]
        self.b = [Buf(kb.sbuf("ostgb%d" % i, [128, 512], BF16)[:]) for i in range(n)]
        self.fs = [kb.slot() for _ in range(n)]
        self.bs = [kb.slot() for _ in range(n)]
        self.fi = 0
        self.bi = 0
        self.flip = 0

    def next(self, dt=F32):
        if dt == F32:
            k = self.fi % len(self.f)
            self.fi += 1
            return self.f[k], self.fs[k]
        k = self.bi % len(self.b)
        self.bi += 1
        return self.b[k], self.bs[k]


def emit_gemm_tm(kb, dn, w_d, kchunks, segs, src_h, epilogue):
    nc = kb.nc
    for si, (c0, wd) in enumerate(segs):
        wb, ws = dn.wstage()
        wv = wb.ap[:, 0:kchunks * wd].rearrange("p (k c) -> p k c", c=wd)
        kb.load(kb.pool, wb, wv, w_d[:, c0:c0 + wd].rearrange("(k p) c -> p k c", p=128), ws)
        for ts in range(TT // 128):
            psb = dn.ps.next()
            for kc in range(kchunks):
                sap, sbufs = src_h(kc, ts)
                kb.mm(psb, psb.ap[:, 0:wd], lhsT=sap, rhs=wv[:, kc, :], start=(kc == 0), stop=(kc == kchunks - 1), rd=[wb] + sbufs)
            epilogue(si, ts, psb)


def emit_proj(kb, dn, os_, win_d, gkv, o, n0, ghalf=None):
    nc = kb.nc
    emit_norm_mod(kb, dn, 1)
    hsrc = lambda kc, st_: (dn.h[kc].ap[:, st_ * 512:(st_ + 1) * 512], [dn.h[kc]])
    pieces, kinds = [], []

    def add(c0, wd, kind, idx):
        pieces.append((c0, wd))
        kinds.append((kind, idx))

    for c in range(8):
        add(C_QN + c * 128, 128, "qn", c)
    for nm, cc in (("kc", C_KC), ("vc", C_VC), ("ks", C_KS), ("kw", C_KW)):
        for c in range(2):
            add(cc + c * 128, 128, nm, c)
    for c in range(8):
        add(C_QD + c * 128, 128, "qd", c)
    for i in range(2):
        add(C_CKV + i * 128, 128, "ckv", i)
    for c in range(4):
        add(C_QI + c * 128, 128, "qi", c)
    add(C_KI, 128, "ki", 0)
    for i in range(32):
        add(C_AM + i * 128, 128, "am", i)
    sl_tok = slice(n0, n0 + TT)
    tt_ = n0 // TT

    def kdst(ap2):
        if ghalf is None:
            return ap2[:, sl_tok], None
        return ap2.rearrange("p (j hf t) -> p j hf t", hf=2, t=128)[:, 4 * tt_:4 * tt_ + 4, ghalf, :], "p (j t) -> p j t"

    cbuf = [Buf(dn.cbuf_t[:, i, :]) for i in range(2)]

    def evac_copy(psb, wd, dst_ap, dt, rr=None):
        ob, oslot = os_.next(dt)
        oap = ob.ap[0:wd, :]
        os_.flip ^= 1
        if os_.flip:
            kb.ew(kb.act, lambda: nc.scalar.copy(out=oap, in_=psb.ap[0:wd, :]), wr=[ob], rd=[psb])
        else:
            kb.ew(kb.dve, lambda: nc.vector.tensor_copy(out=oap, in_=psb.ap[0:wd, :]), wr=[ob], rd=[psb])
        kb.store(kb.sp, dst_ap, ob, oap if rr is None else oap.rearrange(rr, t=128), oslot, final=(ghalf is None))

    def ep(pi, st_, psb):
        kind, idx = kinds[pi]
        wd = pieces[pi][1]
        if kind in ("qn", "qd"):
            evac_copy(psb, 128, o[kind][2 * idx:2 * idx + 2, :, sl_tok].rearrange("h p n -> (h p) n"), BF16)
        elif kind == "qi":
            evac_copy(psb, 128, o[kind][4 * idx:4 * idx + 4, :, sl_tok].rearrange("h p n -> (h p) n"), BF16)
        elif kind in ("kc", "vc", "ks", "kw"):
            dst, rr = kdst(o[kind][2 * idx:2 * idx + 2].rearrange("h p n -> (h p) n"))
            evac_copy(psb, 128, dst, BF16, rr)
        elif kind == "ki":
            dst, rr = kdst(o["ki"])
            evac_copy(psb, 32, dst, BF16, rr)
        elif kind == "am":
            ob, oslot = os_.next()
            kb.ew(kb.act, lambda: nc.scalar.activation(out=ob.ap, in_=psb.ap, func=AF.Sigmoid), wr=[ob], rd=[psb])
            kb.store(kb.sp, o["am"][idx * 128:(idx + 1) * 128, sl_tok], ob, ob.ap, oslot, final=(ghalf is None))
        elif kind == "ckv":
            kb.ew(kb.dve, lambda: nc.vector.tensor_copy(out=cbuf[idx].ap, in_=psb.ap), wr=[cbuf[idx]], rd=[psb])
            if idx == 1:
                for i in range(2):
                    sq = dn.sq.next()
                    kb.ew(kb.act, lambda i=i, sq=sq: nc.scalar.activation(out=sq.ap, in_=cbuf[i].ap, func=AF.Square), wr=[sq], rd=[cbuf[i]])
                    kb.mm(dn.ps_stat, dn.ps_stat.ap, lhsT=dn.ones.ap, rhs=sq.ap, start=(i == 0), stop=(i == 1), rd=[dn.ones, sq])
                t = dn.tmp.next()
                kb.ew(kb.act, lambda: nc.scalar.activation(out=t.ap, in_=dn.ps_stat.ap, func=AF.Sqrt, scale=1.0 / 256, bias=dn.epsb.ap[:, 0:1]),
                      wr=[t], rd=[dn.ps_stat, dn.epsb])
                t2 = dn.tmp.next()
                kb.ew(kb.dve, lambda: nc.vector.reciprocal(out=t2.ap, in_=t.ap), wr=[t2], rd=[t])
                for i in range(2):
                    ob, oslot = os_.next(BF16)
                    oap = ob.ap
                    kb.ew(kb.dve, lambda i=i, oap=oap: nc.vector.scalar_tensor_tensor(
                        out=oap, in0=cbuf[i].ap, scalar=gkv.ap[:, i:i + 1], in1=t2.ap, op0=ALU.mult, op1=ALU.mult),
                        wr=[ob], rd=[cbuf[i], gkv, t2])
                    dst, rr = kdst(o["cn"][i * 128:(i + 1) * 128, :])
                    kb.store(kb.sp, dst, ob, oap if rr is None else oap.rearrange(rr, t=128), oslot, final=(ghalf is None))

    if dn.pflags & 1:
        emit_gemm_fm(kb, dn, win_d, KC, pieces, hsrc, ep, cwcap=512)

    segs = [(C_VS, 256), (C_VW, 256), (C_GN, 128), (C_KI, 128)]
    hsrc_t = lambda kc, ts: (dn.h[kc].ap[:, ts * 128:(ts + 1) * 128], [dn.h[kc]])

    def ep_t(si, ts, psb):
        tsl = slice(n0 + ts * 128, n0 + (ts + 1) * 128)
        gt_ = 2 * (4 * tt_ + ts) + (ghalf or 0)
        gsl = tsl if ghalf is None else slice(gt_ * 128, (gt_ + 1) * 128)
        if si == 0:
            ob, oslot = os_.next(BF16)
            oap = ob.ap[:, 0:256]
            kb.ew(kb.dve, lambda: nc.vector.tensor_copy(out=oap, in_=psb.ap[:, 0:256]), wr=[ob], rd=[psb])
            kb.store(kb.sp, o["vs"][gsl, :], ob, oap, oslot, final=(ghalf is None))
        elif si == 1:
            ob, oslot = os_.next(BF16)
            oap = ob.ap[:, 0:256]
            kb.ew(kb.dve, lambda: nc.vector.tensor_copy(out=oap, in_=psb.ap[:, 0:256]), wr=[ob], rd=[psb])
            kb.store(kb.sp, o["vw"][gsl, :], ob, oap, oslot, final=(ghalf is None))
        elif si == 2:
            ob2, oslot2 = os_.next()
            kb.ew(kb.act, lambda: nc.scalar.activation(out=ob2.ap[:, 0:48], in_=psb.ap[:, 0:48], func=AF.Sigmoid), wr=[ob2], rd=[psb])
            kb.store(kb.sp, o["gn"][tsl, :], ob2, ob2.ap[:, 0:48], oslot2, final=(ghalf is None))
        else:
            ob, oslot = os_.next()
            kb.ew(kb.dve, lambda: nc.vector.tensor_scalar(out=ob.ap[:, 0:16], in0=psb.ap[:, 32:48], scalar1=INDEX_SCALE, scalar2=None, op0=ALU.mult),
                  wr=[ob], rd=[psb])
            kb.store(kb.sp, o["wi"][tsl, :], ob, ob.ap[:, 0:16], oslot, final=(ghalf is None))

    if dn.pflags & 2:
        if dn.pflags > 3:
            keep = [i for i in range(4) if dn.pflags & (4 << i)]
            segs2 = [segs[i] for i in keep]
            emit_gemm_tm(kb, dn, win_d, KC, segs2, hsrc_t, lambda si, ts, psb: ep_t(keep[si], ts, psb))
        else:
            emit_gemm_tm(kb, dn, win_d, KC, segs, hsrc_t, ep_t)


PROJ_OUTS = [("qn", [16, 64, NTOK], BF16), ("kc", [4, 64, NTOK], BF16), ("vc", [4, 64, NTOK], BF16),
             ("ks", [4, 64, NTOK], BF16), ("kw", [4, 64, NTOK], BF16), ("qd", [16, 64, NTOK], BF16),
             ("cn", [256, NTOK], BF16), ("qi", [16, 32, NTOK], BF16), ("ki", [32, NTOK], BF16),
             ("am", [4096, NTOK], F32), ("vs", [NTOK, 256], BF16), ("vw", [NTOK, 256], BF16),
             ("gn", [NTOK, 48], F32), ("wi", [NTOK, 16], F32)]


def build_A(ntt=NTT, do_proj=True, do_ffn=True, do_mod=True, pflags=3, env=None, prefix="", ghalf=None, load_tabs=False):
    nc, ekb, gi, go = _env(env, prefix)
    dt = lambda name, shape, d=F32, kind="ExternalInput": (gi if kind == "ExternalInput" else go)(name, shape, d)
    xT = dt("xT", [D, NTOK])
    cT = dt("cT", [128, KC])
    if do_mod and not load_tabs:
        wada = dt("wada", [D, 9 * D])
        badaT = dt("badaT", [128, 144])
        gnT = dt("gnT", [128, 48])
    if do_ffn:
        wfi = dt("wfi", [D, 2 * FF])
        wfo = dt("wfo", [FF, D])
    win = dt("win", [D, N_IN])
    gkvT = dt("gkvT", [128, 2])
    x1T = dt("x1T", [D, NTOK], kind="ExternalOutput")
    o = {nm: dt("o_" + nm, shp, d, kind="ExternalOutput") for nm, shp, d in PROJ_OUTS}
    tabs = {"o_mod": dt("o_mod", [128, 144], kind="ExternalOutput"), "o_A": dt("o_A", [128, 48], kind="ExternalOutput"),
            "o_GT": dt("o_GT", [128, 48], kind="ExternalOutput")}
    with ExitStack() as st:
        if ekb is None:
            kb = KB(nc, st)
        else:
            kb = ekb
            kb.begin_phase(prefix, st)
        dn = Dense(kb)
        os_ = OutStage(kb)
        dn.cbuf_t = kb.sbuf("cbuf", [128, 2, 512], F32)
        gkv = Buf(kb.sbuf("gkv", [128, 2], F32)[:])
        kb.load(kb.sp, gkv, gkv.ap, gkvT, dn.cslot)
        if load_tabs:
            kb.load(kb.sp, dn.mod, dn.mod.ap, tabs["o_mod"], dn.cslot)
            kb.load(kb.sp, dn.A, dn.A.ap, tabs["o_A"], dn.cslot)
            kb.load(kb.sp, dn.GT, dn.GT.ap, tabs["o_GT"], dn.cslot)
        elif do_mod:
            emit_modprep(kb, dn, cT, wada, badaT, gnT, 0, 144)
            for nm_, bf_ in (("o_mod", dn.mod), ("o_A", dn.A), ("o_GT", dn.GT)):
                kb.store(kb.sp, tabs[nm_], bf_, bf_.ap, dn.cslot2, final=(ekb is None))
        else:
            kb.ew(kb.dve, lambda: nc.vector.memset(dn.mod.ap, 0.0), wr=[dn.mod])
            kb.ew(kb.dve, lambda: nc.vector.memset(dn.A.ap, 1.0), wr=[dn.A])
            kb.ew(kb.dve, lambda: nc.vector.memset(dn.GT.ap, 1.0), wr=[dn.GT])
        dn.pflags = pflags
        for tt in range(ntt):
            n0 = tt * TT
            load_xtile(kb, dn, xT, n0)
            if do_ffn:
                emit_ffn(kb, dn, 0, wfi, wfo, x1T, n0, final_out=(ekb is None))
            if do_proj:
                emit_proj(kb, dn, os_, win, gkv, o, n0, ghalf=ghalf)
        if ekb is None:
            kb.finish()
        else:
            kb.end_phase()
    return nc


BIG = 30000.0
SCALE = 0.125
NU = 256
NPAT_N = 10
GC_W = 576


def _t5_bucket_np(n):
    n = np.maximum(n, 0)
    nf = np.maximum(n, 1).astype(np.float32)
    large = 16 + (np.log(nf / 16) / np.log(128 / 16) * 16).astype(np.int32)
    large = np.minimum(large, 31)
    return np.where(n < 16, n, large)


def _pattern_onehot(dist, valid):
    n = dist.shape[0]
    oh = np.zeros((33, n), np.float32)
    b = _t5_bucket_np(dist)
    idx = np.arange(n)
    v = valid.astype(bool)
    oh[b[v], idx[v]] += 8.0
    oh[31, idx[v]] -= 8.0
    oh[32, idx[~v]] = 1.0
    return oh


def _dtiles(v, s, r):
    J = 8
    return (2 * J + (v ^ s)) - ((2 * J + r) ^ s)


def host_consts_nsa(v, s=0):
    u = np.arange(NU) - 127
    pats = []
    for r in (-2, -1, 0, 1):
        d = _dtiles(v, s, r) * 128 + u
        pats.append(_pattern_onehot(d, d >= 0))
    for r in (-4, -3, -2, -1, 0, 1):
        d = _dtiles(v, s, r) * 128 + u
        pats.append(_pattern_onehot(d, (d >= 0) & (d < 512)))
    dc = np.arange(GC_W) - 300
    pats.append(_pattern_onehot(dc, dc >= 0))
    ohg = np.concatenate(pats, axis=1)
    half = v ^ s
    selc = np.zeros((17, 16, 2, 128), np.float32)
    for j in range(16):
        i = 2 * j + half
        for nt in range(2):
            n = nt * 128 + np.arange(128)
            m = n - 8 * i + 9
            m = np.where((m >= 16) | (n >= 255), 16, m)
            ok = m >= 0
            selc[16 - m[ok], j, nt, np.arange(128)[ok]] = 1.0
    cs = np.arange(256) * 16
    ss = np.arange(64) * 64
    ov = ((cs[:, None] < ss[None, :] + 64) & (cs[:, None] + 32 > ss[None, :])).astype(np.float32)
    ov[255] = 0
    ov = ov.reshape(2, 128, 64).transpose(1, 0, 2)
    visb = np.zeros((128, 16, 64), np.float32)
    blk = np.arange(64)
    for j in range(16):
        t = (2 * j + half) * 128 + np.arange(128)
        cur = t // 64
        forced = (blk[None, :] == 0) | (blk[None, :] == cur[:, None]) | (blk[None, :] == cur[:, None] - 1)
        vis = blk[None, :] * 64 <= t[:, None]
        visb[:, j, :] = np.where(vis, np.where(forced, 1e4, 0.0), -1e30)
    bidx = np.zeros((128, 2, 128), np.float32)
    for ri, r in enumerate((0, 1)):
        d = _dtiles(v, s, r) * 128 + np.arange(128)[:, None] - np.arange(128)[None, :]
        bidx[:, ri, :] = np.where(d >= 0, 0.0, -1e30)
    return {"ohg": ohg, "selc": selc.reshape(17, 16 * 2 * 128), "ov": ov, "visb": visb, "bidx": bidx}


def host_consts_dsa(v, s=0):
    u = np.arange(NU) - 127
    pats = []
    for r in (-2, -1, 0, 1):
        d = _dtiles(v, s, r) * 128 + u
        pats.append(_pattern_onehot(d, d >= 0))
    return {"ohg_d": np.concatenate(pats, axis=1)}


class Attn:
    def __init__(self, kb, ident_d, rev_d):
        self.kb = kb
        nc = kb.nc
        self.ident = Buf(kb.sbuf("identb", [128, 128], BF16)[:])
        self.i4 = Buf(kb.sbuf("i4", [128, 4, 128], BF16)[:])
        identf = Buf(kb.sbuf("identf", [128, 128], F32)[:])
        self.identf = identf
        isl = kb.slot()
        kb.load(kb.sp, identf, identf.ap, ident_d, isl)
        kb.ew(kb.dve, lambda: nc.vector.tensor_copy(out=self.ident.ap, in_=identf.ap), wr=[self.ident], rd=[identf])
        revf = Buf(kb.sbuf("revf", [128, 128], F32)[:])
        self.rev = Buf(kb.sbuf("revb", [128, 128], BF16)[:])
        kb.load(kb.sp, revf, revf.ap, rev_d, isl)
        kb.ew(kb.dve, lambda: nc.vector.tensor_copy(out=self.rev.ap, in_=revf.ap), wr=[self.rev], rd=[revf])
        for h in range(4):
            kb.ew(kb.dve, lambda h=h: nc.vector.tensor_copy(out=self.i4.ap[:, h, :], in_=identf.ap), wr=[self.i4], rd=[identf])
        self.ps_s = Ring([Buf(kb.psum("ps_s%d" % i, [128, 512])[:]) for i in range(2)])
        self.ps_o = Ring([Buf(kb.psum("ps_o%d" % i, [128, 512])[:]) for i in range(3)])
        self.E = Ring([Buf(kb.sbuf("Ebuf%d" % i, [128, 512], BF16)[:]) for i in range(3)])
        self.small = Ring([Buf(kb.sbuf("small%d" % i, [128, 16], F32)[:]) for i in range(6)])

    def build_bias(self, relb_d, ohg_d, width, scratch_d):
        kb, nc = self.kb, self.kb.nc
        tab = Buf(kb.sbuf("tabaug", [33, 32], F32)[:])
        sl = kb.slot()
        kb.ew(kb.dve, lambda: nc.vector.memset(tab.ap, -BIG), wr=[tab])
        kb.load(kb.sp, tab, tab.ap[0:32, :], relb_d, sl)
        CH = 512
        stg = Buf(kb.sbuf("ohg_stg", [33, CH], F32)[:])
        gout = Buf(kb.sbuf("g_out", [32, CH], F32)[:])
        sl2, sl3 = kb.slot(), kb.slot()
        psb = self.ps_s.bufs[0]
        for c0 in range(0, width, CH):
            w = min(CH, width - c0)
            kb.load(kb.sp, stg, stg.ap[:, 0:w], ohg_d[:, c0:c0 + w], sl2)
            kb.mm(psb, psb.ap[0:32, 0:w], lhsT=tab.ap, rhs=stg.ap[:, 0:w], start=True, stop=True, rd=[tab, stg])
            kb.ew(kb.dve, lambda w=w: nc.vector.tensor_copy(out=gout.ap[:, 0:w], in_=psb.ap[0:32, 0:w]), wr=[gout], rd=[psb])
            self.g_store = kb.store(kb.sp, scratch_d[:, c0:c0 + w], gout, gout.ap[:, 0:w], sl3)

    def load_skew(self, dst_buf, dst_ap, scratch_t, col0, h0, nh, pstride, slot):
        kb = self.kb
        npart = dst_ap.shape[0]
        src = bass.AP(scratch_t, h0 * scratch_t.shape[1] + col0 - pstride * (npart - 1), ((pstride, npart), (scratch_t.shape[1], nh), (1, 128)))
        return kb.load(kb.pool, dst_buf, dst_ap, src, slot, extra=[self.g_store])

    def attend(self, kt_list, score_mms, o_buf, v_of, nheads=4, vw=65, extra_o=None):
        kb, nc = self.kb, self.kb.nc
        st_ = {"first": True}

        def finish(kt, ps):
            e = self.E.next()
            kb.ew(kb.act, lambda ps=ps, e=e: nc.scalar.activation(out=e.ap[:, 0:nheads * 128], in_=ps.ap[:, 0:nheads * 128], func=AF.Exp, scale=SCALE),
                  wr=[e], rd=[ps])
            for h in range(nheads):
                vap, vb = v_of(kt, h)
                kb.mm(o_buf, o_buf.ap[:, h * vw:(h + 1) * vw], lhsT=e.ap[:, h * 128:(h + 1) * 128], rhs=vap,
                      start=st_["first"], stop=False, rd=[e] + vb)
                st_["first"] = False
                if extra_o is not None:
                    extra_o(kt, h, e, False)

        pending = []
        for kt in kt_list:
            ps = self.ps_s.next()
            score_mms(kt, ps)
            pending.append((kt, ps))
            if len(pending) == 2:
                finish(*pending.pop(0))
        while pending:
            finish(*pending.pop(0))
        return o_buf


def build_B(env=None, prefix=""):
    nc, ekb, dti, dto = _env(env, prefix)
    qn_d = dti("qn", [16, 64, NTOK], BF16)
    qi_d = dti("qi", [16, 32, NTOK], BF16)
    gn_d = dti("gn", [NTOK, 48])
    wi_d = dti("wi", [NTOK, 16])
    ksT_d = dti("ksT", [4, 64, S], BF16)
    kwT_d = dti("kwT", [4, 64, S], BF16)
    kcT_d = dti("kcT", [4, 64, S], BF16)
    vcT_d = dti("vcT", [4, 64, S], BF16)
    vs_d = dti("vs", [S, 256], BF16)
    vw_d = dti("vw", [S, 256], BF16)
    kiT_d = dti("kiT", [32, S], BF16)
    wk1_d, wv1_d = dti("wk1", [2048, 256]), dti("wv1", [2048, 256])
    wk2_d, wv2_d = dti("wk2", [256, 64]), dti("wv2", [256, 64])
    pek_d, pev_d = dti("pekT", [64, 32]), dti("pevT", [64, 32])
    relb_d = dti("relb", [32, 32])
    ohg_d = dti("ohg", [33, NPAT_N * NU + GC_W])
    selc_d = dti("selc", [17, 16 * 2 * 128])
    ov_d = dti("ov", [128, 2, 64])
    visb_d = dti("visb", [128, 16, 64])
    bidx_d = dti("bidx", [128, 2, 128])
    ident_d = dti("ident", [128, 128])
    rev_d = dti("rev", [128, 128])
    sw_d = dti("sw", [128, 2])
    onT_d = dto("onT", [1024, NTOK])
    nm_d = dto("negmask", [16, 128, S], BF16)
    gsc_t = nc.dram_tensor(prefix + "gscratch", [32, NPAT_N * NU + GC_W], F32, kind="Internal")
    with ExitStack() as st:
        if ekb is None:
            kb = KB(nc, st)
        else:
            kb = ekb
            kb.begin_phase(prefix, st)
        at = Attn(kb, ident_d, rev_d)
        ld = kb.slot()
        at.build_bias(relb_d, ohg_d, NPAT_N * NU + GC_W, gsc_t.ap())
        ksT = Buf(kb.sbuf("sb_ksT", [64, 4, S], BF16)[:])
        vsA = Buf(kb.sbuf("sb_vsA", [128, 32, 4, 65], BF16)[:])
        kiT = Buf(kb.sbuf("sb_kiT", [32, S], BF16)[:])
        kcmpT = Buf(kb.sbuf("sb_kcmpT", [64, 4, 256], BF16)[:])
        vcA = Buf(kb.sbuf("sb_vcA", [128, 2, 4, 65], BF16)[:])
        ovb = Buf(kb.sbuf("sb_ovb", [128, 2, 64], BF16)[:])
        bsel = Buf(kb.sbuf("sb_bsel", [128, 4, 16, 128], BF16)[:])
        bwin = Buf(kb.sbuf("sb_bwin", [128, 6, 16, 128], BF16)[:])
        pc = Buf(kb.sbuf("sb_pc", [17, 16, 128], BF16)[:])
        selc = Buf(kb.sbuf("sb_selc", [17, 16 * 2 * 128], BF16)[:])
        visb = Buf(kb.sbuf("sb_visb", [128, 16, 64], F32)[:])
        bidx = Buf(kb.sbuf("sb_bidx", [128, 2, 128], F32)[:])
        halfc = Buf(kb.sbuf("sb_halfc", [128, 1], F32)[:])
        swb = Buf(kb.sbuf("sb_sw", [128, 2], F32)[:])
        kb.load(kb.sp, swb, swb.ap, sw_d, ld)
        kb.ew(kb.dve, lambda: nc.vector.memset(halfc.ap, 0.5), wr=[halfc])
        for g in range(4):
            kb.load(kb.sp, ksT, ksT.ap[:, g, :], ksT_d[g], ld)
        kb.ew(kb.pool, lambda: nc.gpsimd.memset(vsA.ap, 1.0), wr=[vsA])
        kb.ew(kb.pool, lambda: nc.gpsimd.memset(vcA.ap, 0.0), wr=[vcA])
        kb.ew(kb.pool, lambda: nc.gpsimd.memset(vcA.ap[:, :, :, 64:65], 1.0), wr=[vcA])
        for kt in range(32):
            kb.load(kb.sp, vsA, vsA.ap[:, kt, :, 0:64], vs_d[kt * 128:(kt + 1) * 128, :].rearrange("p (g d) -> p g d", d=64), ld)
        kb.load(kb.sp, kiT, kiT.ap, kiT_d, ld)
        kb.load(kb.sp, visb, visb.ap, visb_d, ld)
        kb.load(kb.sp, bidx, bidx.ap, bidx_d, ld)
        ldp = kb.slot()
        kb.load(kb.pool, selc, selc.ap, selc_d, ldp)
        kb.load(kb.pool, ovb, ovb.ap, ov_d, ldp)
        for r in range(4):
            at.load_skew(bsel, bsel.ap[:, r, :, :], gsc_t, r * NU + 127, 0, 16, 1, ldp)
        for r in range(6):
            at.load_skew(bwin, bwin.ap[:, r, :, :], gsc_t, (4 + r) * NU + 127, 0, 16, 1, ldp)
        at.load_skew(pc, pc.ap, gsc_t, NPAT_N * NU + 413, 0, 16, 16, ldp)

        with ExitStack() as cst:
            w1 = Buf(cst.enter_context(nc.sbuf_tensor(kb.prefix + "sb_w1c", [64, 32, 256], BF16))[:])
            w2 = Buf(cst.enter_context(nc.sbuf_tensor(kb.prefix + "sb_w2c", [128, 2, 64], BF16))[:])
            pe2 = Buf(cst.enter_context(nc.sbuf_tensor(kb.prefix + "sb_pe2", [64, 32, 2], BF16))[:])
            pe1 = Buf(cst.enter_context(nc.sbuf_tensor(kb.prefix + "sb_pe1", [64, 32], F32))[:])
            src = Buf(cst.enter_context(nc.sbuf_tensor(kb.prefix + "sb_csrc", [64, S], BF16))[:])
            src2 = Buf(cst.enter_context(nc.sbuf_tensor(kb.prefix + "sb_csrc2", [64, S], BF16))[:])
            hb = Buf(cst.enter_context(nc.sbuf_tensor(kb.prefix + "sb_hbias", [128, 2], F32))[:])
            hid = Buf(cst.enter_context(nc.sbuf_tensor(kb.prefix + "sb_chid", [128, 2, 256], BF16))[:])
            lw = kb.slot()
            lsrc = kb.slot()
            psh = at.ps_o.bufs[0]
            psb2 = at.ps_o.bufs[1]
            for which, (w1_d, w2_d, pe_d, srcT_d) in enumerate(((wk1_d, wk2_d, pek_d, kcT_d), (wv1_d, wv2_d, pev_d, vcT_d))):
                kb.load(kb.pool, w1, w1.ap, w1_d.rearrange("(l d) c -> d l c", d=64), lw)
                kb.load(kb.pool, w2, w2.ap, w2_d.rearrange("(k p) c -> p k c", p=128), lw)
                kb.load(kb.sp, pe1, pe1.ap, pe_d, lsrc)
                for r_ in range(2):
                    kb.ew(kb.dve, lambda r_=r_: nc.vector.tensor_copy(out=pe2.ap[:, :, r_], in_=pe1.ap), wr=[pe2], rd=[pe1])
                for c in range(2):
                    for l in range(32):
                        kb.mm(psb2, psb2.ap[:, 2 * c:2 * c + 2], lhsT=w1.ap[:, l, c * 128:(c + 1) * 128], rhs=pe2.ap[:, l, :],
                              start=(c == 0 and l == 0), stop=(l == 31), rd=[w1, pe2])
                kb.ew(kb.dve, lambda: nc.vector.tensor_copy(out=hb.ap, in_=psb2.ap[:, 0:4].rearrange("p (c two) -> p c two", two=2)[:, :, 0]),
                      wr=[hb], rd=[psb2])
                for g in range(4):
                    kb.load(kb.sp, src, src.ap, srcT_d[g], lsrc)
                    s4 = srcT_d[g].rearrange("d (p w t) -> d p w t", w=2, t=128)
                    d4 = src2.ap.rearrange("d (p w t) -> d p w t", w=2, t=128)
                    for w_ in range(2):
                        kb.load(kb.sp, src2, d4[:, :, w_, :], s4[:, :, 1 - w_, :], lsrc)
                    kb.ew(kb.dve, lambda: nc.vector.tensor_scalar(out=src.ap, in0=src.ap, scalar1=swb.ap[0:64, 0:1], scalar2=None, op0=ALU.mult), wr=[src], rd=[swb])
                    kb.ew(kb.dve, lambda: nc.vector.scalar_tensor_tensor(out=src.ap, in0=src2.ap, scalar=swb.ap[0:64, 1:2], in1=src.ap, op0=ALU.mult, op1=ALU.add),
                          wr=[src], rd=[src2, swb])
                    for c in range(2):
                        for l in range(32):
                            kb.mm(psh, psh.ap[:, 0:255], lhsT=w1.ap[:, l, c * 128:(c + 1) * 128], rhs=src.ap[:, l:l + 16 * 254 + 1:16],
                                  start=(l == 0), stop=(l == 31), rd=[w1, src])
                        kb.ew(kb.act, lambda c=c: nc.scalar.activation(out=hid.ap[:, c, 0:255], in_=psh.ap[:, 0:255], func=AF.Silu, bias=hb.ap[:, c:c + 1]),
                              wr=[hid], rd=[psh, hb])
                    if which == 0:
                        for c in range(2):
                            kb.mm(psb2, psb2.ap[0:64, 0:255], lhsT=w2.ap[:, c, :], rhs=hid.ap[:, c, 0:255], start=(c == 0), stop=(c == 1), rd=[w2, hid])
                        kb.ew(kb.dve, lambda g=g: nc.vector.memset(kcmpT.ap[:, g, 255:256], 0.0), wr=[kcmpT])
                        kb.ew(kb.dve, lambda g=g: nc.vector.tensor_copy(out=kcmpT.ap[:, g, 0:255], in_=psb2.ap[0:64, 0:255]), wr=[kcmpT], rd=[psb2])
                    else:
                        for nt in range(2):
                            m = 128 if nt == 0 else 127
                            for c in range(2):
                                kb.mm(psb2, psb2.ap[0:m, 0:64], lhsT=hid.ap[:, c, nt * 128:nt * 128 + m], rhs=w2.ap[:, c, :], start=(c == 0), stop=(c == 1),
                                      rd=[w2, hid])
                            kb.ew(kb.dve, lambda g=g, nt=nt, m=m: nc.vector.tensor_copy(out=vcA.ap[0:m, nt, g, 0:64], in_=psb2.ap[0:m, 0:64]), wr=[vcA], rd=[psb2])
            kb.barrier()
        guard = []

        qn = Buf(kb.sbuf("sb_qn_sb", [64, 16, 128], BF16)[:])
        qi = Buf(kb.sbuf("sb_qi_sb", [32, 16, 128], BF16)[:])
        gnb = Buf(kb.sbuf("sb_gn_sb", [128, 48], F32)[:])
        wib = Buf(kb.sbuf("sb_wi_sb", [128, 16], F32)[:])
        kwT = Buf(kb.sbuf("sb_kw_sb", [64, 4, 768], BF16)[:])
        vwA = Buf(kb.sbuf("sb_vw_sb", [128, 6, 4, 65], BF16)[:])
        onacc = Buf(kb.sbuf("sb_onacc", [128, 16, 64], F32)[:])
        imp = Buf(kb.sbuf("sb_imp", [128, 4, 64], F32)[:])
        imp2 = Buf(kb.sbuf("sb_imp2", [128, 4, 64], F32)[:])
        negsel = Buf(kb.sbuf("sb_negsel", [128, 4, 64], BF16)[:])
        nsx = Ring([Buf(kb.sbuf("sb_nsx%d" % i, [128, S], BF16)[:]) for i in range(1)])
        negselv = Buf(kb.sbuf("sb_negselv", [128, 4, 64], BF16)[:])
        score = Buf(kb.sbuf("sb_score", [128, S], F32)[:])
        nmask = Buf(kb.sbuf("sb_nmask", [128, S], BF16)[:])
        junk = nmask
        rt = Ring([Buf(kb.sbuf("sb_rt%d" % i, [128, 512], F32)[:]) for i in range(2)])
        m8 = Buf(kb.sbuf("sb_m8", [128, 4, 8], F32)[:])
        m8b = Buf(kb.sbuf("sb_m8b", [128, 4, 8], F32)[:])
        bis = Buf(kb.sbuf("sb_bis", [128, 8], F32)[:])
        kb.ew(kb.pool, lambda: nc.gpsimd.memset(vwA.ap, 1.0), wr=[vwA], extra=guard)
        qs, ws_, os1, os2 = kb.slot(), kb.slot(), kb.slot(), kb.slot()
        onst = Ring([Buf(kb.sbuf("sb_onst%d" % i, [128, 512], F32)[:]) for i in range(2)])

        def combine(o_buf, br, g, first_branch):
            sm = at.small.next()
            ov_ = o_buf.ap[:, 0:260].rearrange("p (h c) -> p h c", c=65)
            kb.ew(kb.dve, lambda: nc.vector.tensor_scalar(out=sm.ap[:, 8:12], in0=ov_[:, :, 64], scalar1=1e-30, scalar2=None, op0=ALU.max), wr=[sm], rd=[o_buf])
            kb.ew(kb.dve, lambda: nc.vector.reciprocal(out=sm.ap[:, 0:4], in_=sm.ap[:, 8:12]), wr=[sm])
            kb.ew(kb.dve, lambda: nc.vector.tensor_tensor(out=sm.ap[:, 4:8], in0=sm.ap[:, 0:4], in1=gnb.ap[:, 12 * g + br:12 * g + br + 10:3], op=ALU.mult),
                  wr=[sm], rd=[gnb])
            for h in range(4):
                if first_branch:
                    kb.ew(kb.dve, lambda h=h: nc.vector.tensor_scalar(out=onacc.ap[:, 4 * g + h, :], in0=ov_[:, h, 0:64], scalar1=sm.ap[:, 4 + h:5 + h], scalar2=None,
                                                                       op0=ALU.mult), wr=[onacc], rd=[o_buf, sm])
                else:
                    kb.ew(kb.dve, lambda h=h: nc.vector.scalar_tensor_tensor(out=onacc.ap[:, 4 * g + h, :], in0=ov_[:, h, 0:64], scalar=sm.ap[:, 4 + h:5 + h],
                                                                              in1=onacc.ap[:, 4 * g + h, :], op0=ALU.mult, op1=ALU.add), wr=[onacc], rd=[o_buf, sm])
            return sm

        for j in range(16):
            tsl = slice(j * 128, (j + 1) * 128)
            nkt = 2 * j + 2
            N = nkt * 128
            kb.load(kb.sp, qn, qn.ap, qn_d[:, :, tsl].rearrange("h d t -> d h t"), qs)
            kb.load(kb.sp, qi, qi.ap, qi_d[:, :, tsl].rearrange("h d t -> d h t"), qs)
            kb.load(kb.sp, gnb, gnb.ap, gn_d[tsl, :], qs)
            kb.load(kb.sp, wib, wib.ap, wi_d[tsl, :], qs)
            wk0 = max(0, 2 * j - 4)
            nwk = 2 * j + 2 - wk0
            woff = 6 - nwk
            for g in range(4):
                kb.load(kb.sp, kwT, kwT.ap[:, g, woff * 128:768], kwT_d[g][:, wk0 * 128:(2 * j + 2) * 128], ws_)
            for kl in range(woff, 6):
                kt_ = wk0 + kl - woff
                kb.load(kb.sp, vwA, vwA.ap[:, kl, :, 0:64], vw_d[kt_ * 128:(kt_ + 1) * 128, :].rearrange("p (g d) -> p g d", d=64), ws_)

            for g in range(4):
                o_c = at.ps_o.next()
                o_i = at.ps_o.next()
                first_i = [True]

                def sc_cmp(nt, ps, g=g):
                    kb.mm(ps, ps.ap, lhsT=kcmpT.ap[:, g, nt * 128:(nt + 1) * 128], rhs=qn.ap[:, 4 * g:4 * g + 4, :], start=True, stop=False, rd=[kcmpT, qn])
                    kb.mm(ps, ps.ap, lhsT=selc.ap[:, (j * 2 + nt) * 128:(j * 2 + nt + 1) * 128], rhs=pc.ap[:, 4 * g:4 * g + 4, :], start=False, stop=True,
                          rd=[selc, pc])

                def imp_mm(nt, h, e, _f, o_i=o_i, first_i=first_i):
                    kb.mm(o_i, o_i.ap[:, h * 64:(h + 1) * 64], lhsT=e.ap[:, h * 128:(h + 1) * 128], rhs=ovb.ap[:, nt, :], start=first_i[0], stop=False, rd=[e, ovb])
                    first_i[0] = False

                at.attend([0, 1], sc_cmp, o_c, lambda nt, h, g=g: (vcA.ap[:, nt, g, :], [vcA]), extra_o=imp_mm)
                sm = combine(o_c, 0, g, True)
                for h in range(4):
                    if h == 0:
                        kb.ew(kb.dve, lambda h=h: nc.vector.tensor_scalar(out=imp.ap[:, g, :], in0=o_i.ap[:, 0:64], scalar1=sm.ap[:, 0:1], scalar2=None, op0=ALU.mult),
                              wr=[imp], rd=[o_i, sm])
                    else:
                        kb.ew(kb.dve, lambda h=h: nc.vector.scalar_tensor_tensor(out=imp.ap[:, g, :], in0=o_i.ap[:, h * 64:(h + 1) * 64], scalar=sm.ap[:, h:h + 1],
                                                                                  in1=imp.ap[:, g, :], op0=ALU.mult, op1=ALU.add), wr=[imp], rd=[o_i, sm])
            for g in range(4):
                kb.ew(kb.dve, lambda g=g: nc.vector.tensor_tensor(out=imp.ap[:, g, :], in0=imp.ap[:, g, :], in1=visb.ap[:, j, :], op=ALU.add), wr=[imp], rd=[visb])
                kb.ew(kb.dve, lambda g=g: nc.vector.max(out=m8.ap[:, g, :], in_=imp.ap[:, g, :]), wr=[m8], rd=[imp])
                kb.ew(kb.dve, lambda g=g: nc.vector.match_replace(out=imp2.ap[:, g, :], in_to_replace=m8.ap[:, g, :], in_values=imp.ap[:, g, :], imm_value=-3e38),
                      wr=[imp2], rd=[imp, m8])
                kb.ew(kb.dve, lambda g=g: nc.vector.max(out=m8b.ap[:, g, :], in_=imp2.ap[:, g, :]), wr=[m8b], rd=[imp2])
                kb.ew(kb.dve, lambda g=g: nc.vector.tensor_scalar(out=negsel.ap[:, g, :], in0=imp.ap[:, g, :], scalar1=m8b.ap[:, g, 7:8], scalar2=-BIG,
                                                                   op0=ALU.is_lt, op1=ALU.mult), wr=[negsel], rd=[imp, m8b])

            for g in range(4):
                nsv = negsel.ap[:, g, :].rearrange("q (p w b) -> q p w b", w=2, b=2)
                nvv = negselv.ap[:, g, :].rearrange("q (p w b) -> q p w b", w=2, b=2)
                for w_ in range(2):
                    kb.ew(kb.dve, lambda nsv=nsv, nvv=nvv, w_=w_: nc.vector.tensor_scalar(out=nvv[:, :, w_, :], in0=nsv[:, :, w_, :], scalar1=swb.ap[:, 0:1], scalar2=None,
                                                                                      op0=ALU.mult), wr=[negselv], rd=[negsel, swb])
                    kb.ew(kb.dve, lambda nsv=nsv, nvv=nvv, w_=w_: nc.vector.scalar_tensor_tensor(out=nvv[:, :, w_, :], in0=nsv[:, :, 1 - w_, :], scalar=swb.ap[:, 1:2],
                                                                                             in1=nvv[:, :, w_, :], op0=ALU.mult, op1=ALU.add), wr=[negselv], rd=[negsel, swb])
            for g in range(4):
                nx = nsx.next()
                kb.ew(kb.pool, lambda g=g, nx=nx: nc.gpsimd.tensor_copy(
                    out=nx.ap[:, 0:N].rearrange("p (b k) -> p b k", k=64),
                    in_=negselv.ap[:, g, 0:N // 64].unsqueeze(2).broadcast_to([128, N // 64, 64])), wr=[nx], rd=[negselv])
                o_s = at.ps_o.next()

                def sc_sel(kt, ps, g=g, nx=nx):
                    r = kt - 2 * j
                    kb.mm(ps, ps.ap, lhsT=ksT.ap[:, g, kt * 128:(kt + 1) * 128], rhs=qn.ap[:, 4 * g:4 * g + 4, :], start=True, stop=False, rd=[ksT, qn])
                    near = r >= -2
                    kb.mm(ps, ps.ap, lhsT=nx.ap[:, kt * 128:(kt + 1) * 128], rhs=at.i4.ap, start=False, stop=not near, rd=[nx, at.i4])
                    if near:
                        kb.mm(ps, ps.ap, lhsT=at.rev.ap, rhs=bsel.ap[:, r + 2, 4 * g:4 * g + 4, :], start=False, stop=True, rd=[at.rev, bsel])

                at.attend(list(range(nkt)), sc_sel, o_s, lambda kt, h, g=g: (vsA.ap[:, kt, g, :], [vsA]))
                combine(o_s, 1, g, False)
                o_w = at.ps_o.next()

                def sc_win(kl, ps, g=g):
                    kb.mm(ps, ps.ap, lhsT=kwT.ap[:, g, kl * 128:(kl + 1) * 128], rhs=qn.ap[:, 4 * g:4 * g + 4, :], start=True, stop=False, rd=[kwT, qn])
                    kb.mm(ps, ps.ap, lhsT=at.rev.ap, rhs=bwin.ap[:, kl, 4 * g:4 * g + 4, :], start=False, stop=True, rd=[at.rev, bwin])

                at.attend(list(range(woff, 6)), sc_win, o_w, lambda kl, h, g=g: (vwA.ap[:, kl, g, :], [vwA]))
                combine(o_w, 2, g, False)
            for c4 in range(2):
                pst = at.ps_s.next()
                for cc in range(4):
                    c = c4 * 4 + cc
                    kb.mm_t(pst, pst.ap[:, cc * 128:(cc + 1) * 128], onacc.ap[:, 2 * c:2 * c + 2, :].rearrange("p h d -> p (h d)"), at.identf.ap, first=(cc == 0),
                            rd=[onacc, at.identf])
                tb = onst.next()
                kb.ew(kb.act, lambda pst=pst, tb=tb: nc.scalar.copy(out=tb.ap, in_=pst.ap), wr=[tb], rd=[pst])
                kb.store(kb.sp, onT_d[c4 * 512:(c4 + 1) * 512, tsl].rearrange("(c p) t -> p c t", p=128), tb, tb.ap.rearrange("p (c t) -> p c t", t=128), os1,
                         final=(ekb is None))

            nch = (N + 511) // 512
            for h in range(16):
                for c in range(nch):
                    w = min(512, N - c * 512)
                    ps = at.ps_s.next()
                    kb.mm(ps, ps.ap[:, 0:w], lhsT=qi.ap[:, h, :], rhs=kiT.ap[:, c * 512:c * 512 + w], start=True, stop=True, rd=[qi, kiT])
                    t = rt.next()
                    kb.ew(kb.act, lambda ps=ps, t=t, w=w: nc.scalar.activation(out=t.ap[:, 0:w], in_=ps.ap[:, 0:w], func=AF.Relu), wr=[t], rd=[ps])
                    if h == 0:
                        kb.ew(kb.dve, lambda t=t, c=c, w=w: nc.vector.tensor_scalar(out=score.ap[:, c * 512:c * 512 + w], in0=t.ap[:, 0:w], scalar1=wib.ap[:, 0:1],
                                                                                    scalar2=None, op0=ALU.mult), wr=[score], rd=[t, wib])
                    else:
                        kb.ew(kb.dve, lambda t=t, c=c, w=w, h=h: nc.vector.scalar_tensor_tensor(
                            out=score.ap[:, c * 512:c * 512 + w], in0=t.ap[:, 0:w], scalar=wib.ap[:, h:h + 1], in1=score.ap[:, c * 512:c * 512 + w],
                            op0=ALU.mult, op1=ALU.add), wr=[score], rd=[t, wib])
            kb.ew(kb.dve, lambda: nc.vector.tensor_reduce(out=bis.ap[:, 1:2], in_=score.ap[:, 0:N], axis=AX.X, op=ALU.max, apply_absolute_value=True),
                  wr=[bis], rd=[score])
            kb.ew(kb.dve, lambda: nc.vector.tensor_scalar(out=bis.ap[:, 1:2], in0=bis.ap[:, 1:2], scalar1=1.001, scalar2=1e-6, op0=ALU.mult, op1=ALU.add), wr=[bis])
            kb.ew(kb.dve, lambda: nc.vector.tensor_scalar(out=bis.ap[:, 0:1], in0=bis.ap[:, 1:2], scalar1=-1.0, scalar2=None, op0=ALU.mult), wr=[bis])
            kb.ew(kb.dve, lambda: nc.vector.memset(bis.ap[:, 2:3], 0.0), wr=[bis])
            kb.ew(kb.dve, lambda: nc.vector.tensor_tensor(out=score.ap[:, N - 256:N], in0=score.ap[:, N - 256:N], in1=bidx.ap.rearrange("p r k -> p (r k)"), op=ALU.add),
                  wr=[score], rd=[bidx])
            for it in range(17):
                kb.ew(kb.dve, lambda: nc.vector.tensor_scalar(out=junk.ap[:, 0:N], in0=score.ap[:, 0:N], scalar1=bis.ap[:, 2:3], scalar2=None, op0=ALU.is_ge,
                                                               op1=ALU.add, accum_out=bis.ap[:, 3:4]), wr=[junk, bis], rd=[score])
                kb.ew(kb.dve, lambda: nc.vector.tensor_scalar(out=bis.ap[:, 4:5], in0=bis.ap[:, 3:4], scalar1=255.5, scalar2=None, op0=ALU.is_ge), wr=[bis])
                kb.ew(kb.dve, lambda: nc.vector.tensor_tensor(out=bis.ap[:, 5:6], in0=bis.ap[:, 2:3], in1=bis.ap[:, 0:1], op=ALU.subtract), wr=[bis])
                kb.ew(kb.dve, lambda: nc.vector.tensor_tensor(out=bis.ap[:, 6:7], in0=bis.ap[:, 1:2], in1=bis.ap[:, 2:3], op=ALU.subtract), wr=[bis])
                kb.ew(kb.dve, lambda: nc.vector.scalar_tensor_tensor(out=bis.ap[:, 0:1], in0=bis.ap[:, 5:6], scalar=bis.ap[:, 4:5], in1=bis.ap[:, 0:1],
                                                                      op0=ALU.mult, op1=ALU.add), wr=[bis])
                kb.ew(kb.dve, lambda: nc.vector.scalar_tensor_tensor(out=bis.ap[:, 1:2], in0=bis.ap[:, 6:7], scalar=bis.ap[:, 4:5], in1=bis.ap[:, 2:3],
                                                                      op0=ALU.mult, op1=ALU.add), wr=[bis])
                kb.ew(kb.dve, lambda: nc.vector.scalar_tensor_tensor(out=bis.ap[:, 2:3], in0=bis.ap[:, 0:1], scalar=bis.ap[:, 1:2], in1=halfc.ap[:, 0:1],
                                                                      op0=ALU.add, op1=ALU.mult), wr=[bis], rd=[halfc])
            kb.ew(kb.dve, lambda: nc.vector.tensor_scalar(out=nmask.ap[:, 0:N], in0=score.ap[:, 0:N], scalar1=bis.ap[:, 0:1], scalar2=-BIG, op0=ALU.is_lt, op1=ALU.mult),
                  wr=[nmask], rd=[score, bis])
            kb.store(kb.sp, nm_d[j, :, 0:N], nmask, nmask.ap[:, 0:N], os2, final=(ekb is None))
        if ekb is None:
            kb.finish()
        else:
            kb.end_phase()
    return nc


def build_C(env=None, prefix=""):
    nc, ekb, dti, dto = _env(env, prefix)
    qd_d = dti("qd", [16, 64, NTOK], BF16)
    cnT_d = dti("cnT", [256, S], BF16)
    wuk_d = dti("wuk", [256, 1024])
    wuv_d = dti("wuv", [256, 1024])
    nm_d = dti("negmask", [16, 128, S], BF16)
    relb_d = dti("relb", [32, 32])
    ohg_d = dti("ohg_d", [33, 4 * NU])
    ident_d = dti("ident", [128, 128])
    rev_d = dti("rev", [128, 128])
    odT_d = dto("odT", [1024, NTOK])
    gsc_t = nc.dram_tensor(prefix + "gscratch_d", [32, 4 * NU], F32, kind="Internal")
    with ExitStack() as st:
        if ekb is None:
            kb = KB(nc, st)
        else:
            kb = ekb
            kb.begin_phase(prefix, st)
        at = Attn(kb, ident_d, rev_d)
        at.build_bias(relb_d, ohg_d, 4 * NU, gsc_t.ap())
        ld, ldp, qs, ms, os1 = kb.slot(), kb.slot(), kb.slot(), kb.slot(), kb.slot()
        cnT = Buf(kb.sbuf("sc_cnT", [128, 2, S], BF16)[:])
        wuk = Buf(kb.sbuf("sc_wuk", [128, 2, 1024], BF16)[:])
        wuv = Buf(kb.sbuf("sc_wuv", [128, 2, 1024], BF16)[:])
        bd = Buf(kb.sbuf("sc_bd", [128, 4, 16, 128], BF16)[:])
        kdT = Buf(kb.sbuf("sc_kdT", [64, 4, S], BF16)[:])
        vdA = Buf(kb.sbuf("sc_vdA", [128, 32, 4, 65], BF16)[:])
        qd = Buf(kb.sbuf("sc_qd", [64, 4, 128], BF16)[:])
        nmk = Ring([Buf(kb.sbuf("sc_nm%d" % i, [128, S], BF16)[:]) for i in range(2)])
        odb = Ring([Buf(kb.sbuf("sc_od%d" % i, [128, 4, 64], F32)[:]) for i in range(2)])
        odst = Ring([Buf(kb.sbuf("sc_odst%d" % i, [128, 256], F32)[:]) for i in range(2)])
        for c in range(2):
            kb.load(kb.sp, cnT, cnT.ap[:, c, :], cnT_d[c * 128:(c + 1) * 128, :], ld)
        kb.load(kb.pool, wuk, wuk.ap, wuk_d.rearrange("(k p) c -> p k c", p=128), ldp)
        kb.load(kb.pool, wuv, wuv.ap, wuv_d.rearrange("(k p) c -> p k c", p=128), ldp)
        for r in range(4):
            at.load_skew(bd, bd.ap[:, r, :, :], gsc_t, r * NU + 127, 16, 16, 1, ldp)
        kb.ew(kb.pool, lambda: nc.gpsimd.memset(vdA.ap, 1.0), wr=[vdA])
        flip = [0]
        for hg in range(4):
            for h in range(4):
                hh = hg * 4 + h
                for c8 in range(S // 512):
                    ps = at.ps_s.next()
                    for c in range(2):
                        kb.mm(ps, ps.ap[0:64, :], lhsT=wuk.ap[:, c, hh * 64:(hh + 1) * 64], rhs=cnT.ap[:, c, c8 * 512:(c8 + 1) * 512],
                              start=(c == 0), stop=(c == 1), rd=[wuk, cnT])
                    flip[0] ^= 1
                    if flip[0]:
                        kb.ew(kb.act, lambda ps=ps, h=h, c8=c8: nc.scalar.copy(out=kdT.ap[:, h, c8 * 512:(c8 + 1) * 512], in_=ps.ap[0:64, :]), wr=[kdT], rd=[ps])
                    else:
                        kb.ew(kb.dve, lambda ps=ps, h=h, c8=c8: nc.vector.tensor_copy(out=kdT.ap[:, h, c8 * 512:(c8 + 1) * 512], in_=ps.ap[0:64, :]), wr=[kdT], rd=[ps])
            for kt in range(32):
                ps = at.ps_s.next()
                for c in range(2):
                    kb.mm(ps, ps.ap[:, 0:256], lhsT=cnT.ap[:, c, kt * 128:(kt + 1) * 128], rhs=wuv.ap[:, c, hg * 256:(hg + 1) * 256],
                          start=(c == 0), stop=(c == 1), rd=[wuv, cnT])
                kb.ew(kb.dve, lambda ps=ps, kt=kt: nc.vector.tensor_copy(out=vdA.ap[:, kt, :, 0:64], in_=ps.ap[:, 0:256].rearrange("p (h d) -> p h d", d=64)),
                      wr=[vdA], rd=[ps])
            for j in range(16):
                tsl = slice(j * 128, (j + 1) * 128)
                nkt = 2 * j + 2
                N = nkt * 128
                kb.load(kb.sp, qd, qd.ap, qd_d[hg * 4:(hg + 1) * 4, :, tsl].rearrange("h d t -> d h t"), qs)
                nm = nmk.next()
                kb.load(kb.sp, nm, nm.ap[:, 0:N], nm_d[j, :, 0:N], ms)
                o_d = at.ps_o.next()

                def sc_d(kt, ps, nm=nm, j=j):
                    r = kt - 2 * j
                    for h in range(4):
                        kb.mm(ps, ps.ap[:, h * 128:(h + 1) * 128], lhsT=kdT.ap[:, h, kt * 128:(kt + 1) * 128], rhs=qd.ap[:, h, :], start=(h == 0), stop=False,
                              rd=[kdT, qd])
                    near = r >= -2
                    kb.mm(ps, ps.ap, lhsT=nm.ap[:, kt * 128:(kt + 1) * 128], rhs=at.i4.ap, start=False, stop=not near, rd=[nm, at.i4])
                    if near:
                        kb.mm(ps, ps.ap, lhsT=at.rev.ap, rhs=bd.ap[:, r + 2, hg * 4:(hg + 1) * 4, :], start=False, stop=True, rd=[at.rev, bd])

                at.attend(list(range(nkt)), sc_d, o_d, lambda kt, h: (vdA.ap[:, kt, h, :], [vdA]))
                sm = at.small.next()
                ob = odb.next()
                ov_ = o_d.ap[:, 0:260].rearrange("p (h c) -> p h c", c=65)
                kb.ew(kb.dve, lambda: nc.vector.reciprocal(out=sm.ap[:, 0:4], in_=ov_[:, :, 64]), wr=[sm], rd=[o_d])
                for h in range(4):
                    kb.ew(kb.dve, lambda h=h: nc.vector.tensor_scalar(out=ob.ap[:, h, :], in0=ov_[:, h, 0:64], scalar1=sm.ap[:, h:h + 1], scalar2=None, op0=ALU.mult),
                          wr=[ob], rd=[o_d, sm])
                pst = at.ps_s.next()
                for cc in range(2):
                    kb.mm_t(pst, pst.ap[:, cc * 128:(cc + 1) * 128], ob.ap[:, 2 * cc:2 * cc + 2, :].rearrange("p h d -> p (h d)"), at.identf.ap, first=(cc == 0),
                            rd=[ob, at.identf])
                tb = odst.next()
                kb.ew(kb.act, lambda pst=pst, tb=tb: nc.scalar.copy(out=tb.ap, in_=pst.ap[:, 0:256]), wr=[tb], rd=[pst])
                kb.store(kb.sp, odT_d[hg * 256:(hg + 1) * 256, tsl].rearrange("(c p) t -> p c t", p=128), tb, tb.ap.rearrange("p (c t) -> p c t", t=128), os1,
                         final=(ekb is None))
        if ekb is None:
            kb.finish()
        else:
            kb.end_phase()
    return nc


def build_D(last=False, env=None, prefix=""):
    nc, ekb, dti, dto = _env(env, prefix)
    x1T = dti("x1T", [D, NTOK])
    onT = dti("onT", [1024, NTOK])
    odT = dti("odT", [1024, NTOK])
    amT = dti("amT", [4096, NTOK])
    mod_d = dti("modt_in", [128, 144])
    A_d = dti("A_in", [128, 48])
    GT_d = dti("GT_in", [128, 48])
    wun = dti("wun", [1024, D])
    wud = dti("wud", [1024, D])
    wo = dti("wo", [D, D])
    wfi = dti("wfi", [D, 2 * FF])
    wfo = dti("wfo", [FF, D])
    if last:
        gfT = dti("gfT", [128, KC])
    x3T = dto("x3T", [D, NTOK])
    with ExitStack() as st:
        if ekb is None:
            kb = KB(nc, st)
        else:
            kb = ekb
            kb.begin_phase(prefix, st)
        dn = Dense(kb)
        kb.load(kb.sp, dn.mod, dn.mod.ap, mod_d, dn.cslot)
        kb.load(kb.sp, dn.A, dn.A.ap, A_d, dn.cslot)
        kb.load(kb.sp, dn.GT, dn.GT.ap, GT_d, dn.cslot)
        a_st = Ring([Buf(kb.sbuf("a_st%d" % i, [128, 512], F32)[:]) for i in range(4)])
        a_sl = [kb.slot() for _ in range(4)]
        osl = kb.slot()
        if last:
            gf = Buf(kb.sbuf("gf", [128, KC], F32)[:])
            kb.load(kb.sp, gf, gf.ap, gfT, dn.cslot)
        for tt in range(NTT):
            n0 = tt * TT
            sl_tok = slice(n0, n0 + TT)
            load_xtile(kb, dn, x1T, n0)
            for k in range(8):
                kb.load(kb.pool, dn.hid[k], dn.hid[k].ap, onT[k * 128:(k + 1) * 128, sl_tok], osl)
                kb.load(kb.pool, dn.hid[8 + k], dn.hid[8 + k].ap, odT[k * 128:(k + 1) * 128, sl_tok], osl)
            hold = {}

            def ep_up(pi, st_, psb):
                which, m = pi % 2, pi // 2
                if which == 0:
                    hold[m] = psb
                    return
                psn = hold.pop(m)
                a0, a1 = a_st.next(), a_st.next()
                i0 = (a_st.i - 2) % 4
                i1 = (a_st.i - 1) % 4
                kb.load(kb.sp, a0, a0.ap, amT[m * 128:(m + 1) * 128, sl_tok], a_sl[i0])
                kb.load(kb.sp, a1, a1.ap, amT[2048 + m * 128:2048 + (m + 1) * 128, sl_tok], a_sl[i1])
                kb.ew(kb.dve, lambda: nc.vector.tensor_tensor(out=a0.ap, in0=a0.ap, in1=psn.ap, op=ALU.mult), wr=[a0], rd=[psn])
                kb.ew(kb.dve, lambda: nc.vector.tensor_tensor(out=a1.ap, in0=a1.ap, in1=psb.ap, op=ALU.mult), wr=[a1], rd=[psb])
                kb.ew(kb.dve, lambda: nc.vector.tensor_tensor(out=dn.h[m].ap, in0=a0.ap, in1=a1.ap, op=ALU.add), wr=[dn.h[m]], rd=[a0, a1])

            for m0 in range(0, KC, 4):
                srcn = lambda kc, st_: (dn.hid[kc].ap, [dn.hid[kc]])
                srcd = lambda kc, st_: (dn.hid[8 + kc].ap, [dn.hid[8 + kc]])
                emit_gemm_fm(kb, dn, wun, 8, [((m0 + i) * 128, 128) for i in range(4)], srcn, lambda pi, st_, psb, m0=m0: ep_up(2 * (m0 + pi), st_, psb))
                emit_gemm_fm(kb, dn, wud, 8, [((m0 + i) * 128, 128) for i in range(4)], srcd, lambda pi, st_, psb, m0=m0: ep_up(2 * (m0 + pi) + 1, st_, psb))

            def ep_o(pi, st_, psb):
                xb = dn.xres[pi]
                kb.ew(kb.dve, lambda: nc.vector.scalar_tensor_tensor(out=xb.ap, in0=psb.ap, scalar=dn.GT.ap[:, 16 + pi:17 + pi], in1=xb.ap, op0=ALU.mult, op1=ALU.add),
                      wr=[xb], rd=[psb, dn.GT])

            ysrc = lambda kc, st_: (dn.h[kc].ap, [dn.h[kc]])
            emit_gemm_fm(kb, dn, wo, KC, [(m * 128, 128) for m in range(KC)], ysrc, ep_o)
            emit_ffn(kb, dn, 2, wfi, wfo, None if last else x3T, n0, final_out=(ekb is None))
            if last:
                for kc in range(KC):
                    sq = dn.sq.next()
                    kb.ew(kb.act, lambda kc=kc, sq=sq: nc.scalar.activation(out=sq.ap, in_=dn.xres[kc].ap, func=AF.Square), wr=[sq], rd=[dn.xres[kc]])
                    kb.mm(dn.ps_stat, dn.ps_stat.ap, lhsT=dn.ones.ap, rhs=sq.ap, start=(kc == 0), stop=(kc == KC - 1), rd=[dn.ones, sq])
                t = dn.tmp.next()
                kb.ew(kb.act, lambda: nc.scalar.activation(out=t.ap, in_=dn.ps_stat.ap, func=AF.Sqrt, scale=1.0 / D, bias=dn.epsb.ap[:, 0:1]), wr=[t], rd=[dn.ps_stat, dn.epsb])
                kb.ew(kb.dve, lambda: nc.vector.reciprocal(out=dn.rstd.ap, in_=t.ap), wr=[dn.rstd], rd=[t])
                for kc in range(KC):
                    xb = dn.xres[kc]
                    kb.ew(kb.dve, lambda kc=kc, xb=xb: nc.vector.scalar_tensor_tensor(out=xb.ap, in0=xb.ap, scalar=gf.ap[:, kc:kc + 1], in1=dn.rstd.ap,
                                                                                      op0=ALU.mult, op1=ALU.mult), wr=[xb], rd=[gf, dn.rstd])
                    kb.store(kb.sp, x3T[kc * 128:(kc + 1) * 128, sl_tok], xb, xb.ap, dn.oslot, final=True)
        if ekb is None:
            kb.finish()
        else:
            kb.end_phase()
    return nc


DEPTH = 2


def build_fused(depth=DEPTH):
    nc = bass.Bass("TRN2", target_bir_lowering=False)
    gi = lambda name, shape, d=F32: nc.dram_tensor(name, shape, d, kind="ExternalInput").ap()
    sc = lambda name, shape, d=F32: nc.dram_tensor(name, shape, d, kind="Internal").ap()
    I = {}
    for h in range(2):
        I["xT%d" % h] = gi("xT%d" % h, [D, NTOK])
        for nm, shp in (("ohg", [33, NPAT_N * NU + GC_W]), ("selc", [17, 16 * 2 * 128]), ("visb", [128, 16, 64]), ("bidx", [128, 2, 128]), ("ohg_d", [33, 4 * NU])):
            I["%s%d" % (nm, h)] = gi("%s%d" % (nm, h), shp)
    I["ov"] = gi("ov", [128, 2, 64])
    I["ident"] = gi("ident", [128, 128])
    I["rev"] = gi("rev", [128, 128])
    I["sw"] = gi("sw", [128, 2])
    I["cT"] = gi("cT", [128, KC])
    I["relb"] = gi("relb", [32, 32])
    I["gfT"] = gi("gfT", [128, KC])
    for nm, shp in (("w_ada", [D, 9 * D]), ("badaT", [128, 144]), ("gnT", [128, 48]), ("w_in", [D, N_IN]), ("gkvT", [128, 2]),
                    ("wk1", [2048, 256]), ("wv1", [2048, 256]), ("wk2", [256, 64]), ("wv2", [256, 64]), ("pekT", [64, 32]), ("pevT", [64, 32]),
                    ("wuk", [256, 1024]), ("wuv", [256, 1024]), ("wun", [1024, D]), ("wud", [1024, D]), ("wo", [D, D])):
        I[nm] = gi(nm, [depth] + shp)
    I["w_ffn_in"] = gi("w_ffn_in", [depth, 2, D, 2 * FF])
    I["w_ffn_out"] = gi("w_ffn_out", [depth, 2, FF, D])
    outs = [nc.dram_tensor("out0", [D, NTOK], F32, kind="ExternalOutput").ap()]
    Sx = {}
    for h in range(2):
        for nm, shp, d in (("x1T", [D, NTOK], F32), ("xn", [D, NTOK], F32), ("qn", [16, 64, NTOK], BF16), ("qi", [16, 32, NTOK], BF16),
                           ("qd", [16, 64, NTOK], BF16), ("gn", [NTOK, 48], F32), ("wi", [NTOK, 16], F32), ("am", [4096, NTOK], F32),
                           ("onT", [1024, NTOK], F32), ("odT", [1024, NTOK], F32), ("negmask", [16, 128, S], BF16)):
            Sx["%s%d" % (nm, h)] = sc("s_%s%d" % (nm, h), shp, d)
    for nm, shp, d in (("ks", [4, 64, S], BF16), ("kw", [4, 64, S], BF16), ("kc", [4, 64, S], BF16), ("vc", [4, 64, S], BF16),
                       ("vs", [S, 256], BF16), ("vw", [S, 256], BF16), ("ki", [32, S], BF16), ("cn", [256, S], BF16),
                       ("mod", [128, 144], F32), ("A", [128, 48], F32), ("GT", [128, 48], F32)):
        Sx[nm] = sc("s_" + nm, shp, d)
    with ExitStack() as st:
        kb = KB(nc, st)
        kb.setup_fused()
        for l in range(depth):
            last = (l == depth - 1)
            for h in range(2):
                T = {"xT": I["xT%d" % h] if l == 0 else Sx["xn%d" % h], "cT": I["cT"], "wada": I["w_ada"][l], "badaT": I["badaT"][l], "gnT": I["gnT"][l],
                     "wfi": I["w_ffn_in"][l, 0], "wfo": I["w_ffn_out"][l, 0], "win": I["w_in"][l], "gkvT": I["gkvT"][l], "x1T": Sx["x1T%d" % h],
                     "o_qn": Sx["qn%d" % h], "o_qi": Sx["qi%d" % h], "o_qd": Sx["qd%d" % h], "o_gn": Sx["gn%d" % h], "o_wi": Sx["wi%d" % h], "o_am": Sx["am%d" % h],
                     "o_kc": Sx["kc"], "o_vc": Sx["vc"], "o_ks": Sx["ks"], "o_kw": Sx["kw"], "o_ki": Sx["ki"], "o_cn": Sx["cn"], "o_vs": Sx["vs"], "o_vw": Sx["vw"],
                     "o_mod": Sx["mod"], "o_A": Sx["A"], "o_GT": Sx["GT"]}
                build_A(env={"nc": nc, "kb": kb, "T": T}, prefix="L%dA%d_" % (l, h), ghalf=h, load_tabs=(h == 1))
            vh = (0,) if last else (0, 1)
            for h in vh:
                T = {"qn": Sx["qn%d" % h], "qi": Sx["qi%d" % h], "gn": Sx["gn%d" % h], "wi": Sx["wi%d" % h], "ksT": Sx["ks"], "kwT": Sx["kw"], "kcT": Sx["kc"],
                     "vcT": Sx["vc"], "vs": Sx["vs"], "vw": Sx["vw"], "kiT": Sx["ki"], "wk1": I["wk1"][l], "wv1": I["wv1"][l], "wk2": I["wk2"][l], "wv2": I["wv2"][l],
                     "pekT": I["pekT"][l], "pevT": I["pevT"][l], "relb": I["relb"], "ohg": I["ohg%d" % h], "selc": I["selc%d" % h], "ov": I["ov"],
                     "visb": I["visb%d" % h], "bidx": I["bidx%d" % h], "ident": I["ident"], "rev": I["rev"], "sw": I["sw"], "onT": Sx["onT%d" % h], "negmask": Sx["negmask%d" % h]}
                build_B(env={"nc": nc, "kb": kb, "T": T}, prefix="L%dB%d_" % (l, h))
            for h in vh:
                T = {"qd": Sx["qd%d" % h], "cnT": Sx["cn"], "wuk": I["wuk"][l], "wuv": I["wuv"][l], "negmask": Sx["negmask%d" % h], "relb": I["relb"],
                     "ohg_d": I["ohg_d%d" % h], "ident": I["ident"], "rev": I["rev"], "odT": Sx["odT%d" % h]}
                build_C(env={"nc": nc, "kb": kb, "T": T}, prefix="L%dC%d_" % (l, h))
            for h in vh:
                T = {"x1T": Sx["x1T%d" % h], "onT": Sx["onT%d" % h], "odT": Sx["odT%d" % h], "amT": Sx["am%d" % h], "modt_in": Sx["mod"], "A_in": Sx["A"],
                     "GT_in": Sx["GT"], "wun": I["wun"][l], "wud": I["wud"][l], "wo": I["wo"][l], "wfi": I["w_ffn_in"][l, 1], "wfo": I["w_ffn_out"][l, 1],
                     "gfT": I["gfT"], "x3T": outs[h] if last else Sx["xn%d" % h]}
                build_D(last, env={"nc": nc, "kb": kb, "T": T}, prefix="L%dD%d_" % (l, h))
        kb.finish()
    return nc


_PROG = []


def _core_tokens(a, b, half):
    t = a[b].reshape(32, 128, *a.shape[2:])[half::2]
    return t.reshape(NTOK, *a.shape[2:])


def kernel(x, c, w_ada, b_ada, g_norm, w_ffn_in, w_ffn_out, w_in, nsa_pe_k, nsa_pe_v, nsa_cmp_k1, nsa_cmp_k2,
           nsa_cmp_v1, nsa_cmp_v2, dsa_g_kv, dsa_w_uk, dsa_w_uv, w_up_nsa, w_up_dsa, w_out, rel_bias, g_final):
    f32 = np.float32
    A = lambda v: np.ascontiguousarray(np.asarray(v, f32))
    x = A(x)
    c = A(c)
    depth = np.asarray(w_ada).shape[0]
    if not _PROG:
        _PROG.append(build_fused(depth))
    nc = _PROG[0]
    ident = np.eye(128, dtype=f32)
    shared = {"ident": ident, "rev": np.ascontiguousarray(ident[::-1]), "relb": A(rel_bias), "gfT": A(np.asarray(g_final).reshape(16, 128).T),
              "w_ada": A(w_ada), "badaT": A(np.asarray(b_ada).reshape(depth, 144, 128).transpose(0, 2, 1)),
              "gnT": A(np.asarray(g_norm).reshape(depth, 48, 128).transpose(0, 2, 1)), "w_in": A(w_in),
              "gkvT": A(np.asarray(dsa_g_kv).reshape(depth, 2, 128).transpose(0, 2, 1)),
              "wk1": A(nsa_cmp_k1), "wv1": A(nsa_cmp_v1), "wk2": A(nsa_cmp_k2), "wv2": A(nsa_cmp_v2),
              "pekT": A(np.asarray(nsa_pe_k).transpose(0, 2, 1)), "pevT": A(np.asarray(nsa_pe_v).transpose(0, 2, 1)),
              "wuk": A(np.asarray(dsa_w_uk).reshape(depth, 256, 1024)), "wuv": A(np.asarray(dsa_w_uv).reshape(depth, 256, 1024)),
              "wun": A(w_up_nsa), "wud": A(w_up_dsa), "wo": A(w_out), "w_ffn_in": A(w_ffn_in), "w_ffn_out": A(w_ffn_out)}
    percore = []
    for s in range(2):
        pc_ = {"sw": np.ascontiguousarray(np.tile(np.array([[1.0 - s, float(s)]], f32), (128, 1)))}
        for v in range(2):
            cn_, cd_ = host_consts_nsa(v, s), host_consts_dsa(v, s)
            pc_["ohg%d" % v] = cn_["ohg"]
            pc_["selc%d" % v] = cn_["selc"]
            pc_["visb%d" % v] = cn_["visb"]
            pc_["bidx%d" % v] = cn_["bidx"]
            pc_["ohg_d%d" % v] = cd_["ohg_d"]
            pc_["ov"] = cn_["ov"]
        percore.append(pc_)
    in_maps = []
    for k in range(NCORES):
        b, s = k % 4, k // 4
        m = dict(shared)
        m.update(percore[s])
        m["xT0"] = np.ascontiguousarray(_core_tokens(x, b, s).T)
        m["xT1"] = np.ascontiguousarray(_core_tokens(x, b, 1 - s).T)
        m["cT"] = np.ascontiguousarray(c[b].reshape(16, 128).T)
        in_maps.append(m)
    res = run_bass_kernel_spmd(nc, in_maps, core_ids=list(range(NCORES))).results
    out = np.zeros(x.shape, f32)
    for b in range(4):
        for h in range(2):
            out[b].reshape(32, 128, D)[h::2] = np.ascontiguousarray(res[b + 4 * h]["out0"].T).reshape(16, 128, D)
    return out
```

```python
import numpy as np
from contextlib import ExitStack
import concourse.bass as bass
import concourse.mybir as mybir
from concourse.bass_utils import run_bass_kernel_spmd

F32 = mybir.dt.float32
BF16 = mybir.dt.bfloat16
AF = mybir.ActivationFunctionType
ALU = mybir.AluOpType
AX = mybir.AxisListType

D = 2048
KC = 16
FF = 5632
FC = 44
NTOK = 2048
TT = 512
NTT = NTOK // TT
S = 4096
N_IN = 8544
EPS = 1e-6
NCORES = 8
INDEX_SCALE = 512 ** -0.5
WST_ELEMS = 11264

C_QN, C_KC, C_VC, C_KS, C_VS, C_KW, C_VW, C_GN = 0, 1024, 1280, 1536, 1792, 2048, 2304, 2560
C_QD, C_CKV, C_QI, C_KI, C_WI, C_AM = 2608, 3632, 3888, 4400, 4432, 4448


class Eng:
    def __init__(self, name, eng, sem):
        self.name, self.eng, self.sem = name, eng, sem
        self.cnt = 0
        self.seen = {}


class Buf:
    def __init__(self, ap):
        self.ap = ap
        self.wr = None
        self.rd = {}

    def wdeps(self):
        d = list(self.rd.values())
        if self.wr is not None:
            d.append(self.wr)
        return d

    def rdeps(self):
        return [self.wr] if self.wr is not None else []

    def wrote(self, h):
        self.wr = h
        self.rd = {}

    def read(self, h):
        k = h[0]
        if k not in self.rd or self.rd[k][2] < h[2]:
            self.rd[k] = h


class KB:
    def __init__(self, nc, st):
        self.nc, self.st = nc, st
        self.pe = Eng("pe", nc.tensor, st.enter_context(nc.semaphore("s_pe")))
        self.act = Eng("act", nc.scalar, st.enter_context(nc.semaphore("s_act")))
        self.dve = Eng("dve", nc.vector, st.enter_context(nc.semaphore("s_dve")))
        self.pool = Eng("pool", nc.gpsimd, st.enter_context(nc.semaphore("s_pool")))
        self.sp = Eng("sp", nc.sync, st.enter_context(nc.semaphore("s_sp")))
        self.nslot = 0
        self.slots = []
        self.prefix = ""
        self.pools = None
        self.banks = None
        self.nbank = 0
        self.out_handles = []
        self.uid = 0

    def sbuf(self, name, shape, dt):
        return self.st.enter_context(self.nc.sbuf_tensor(self.prefix + name, shape, dt))

    def psum(self, name, shape, dt=F32):
        if self.banks is not None:
            b = self.banks[self.nbank % 8]
            self.nbank += 1
            return b
        return self.st.enter_context(self.nc.psum_tensor(self.prefix + name, shape, dt))

    def setup_fused(self, n_sp=30, n_pool=12):
        self.banks = [self.st.enter_context(self.nc.psum_tensor("bank%d" % i, [128, 512], F32)) for i in range(8)]
        self.pools = {"sp": [], "pool": []}
        for kind, n in (("sp", n_sp), ("pool", n_pool)):
            for i in range(n):
                sem = self.st.enter_context(self.nc.semaphore("s_%s%d" % (kind, i)))
                self.pools[kind].append({"key": "%s%d" % (kind, i), "sem": sem, "cnt": 0, "kind": kind})

    def begin_phase(self, prefix, stack):
        self.prefix = prefix
        self.main_st = self.st
        self.st = stack
        self.nbank = 0
        self.phase_recs = []

    def end_phase(self):
        self.barrier()
        for r in self.phase_recs:
            self.pools[r["kind"]].append(r)
        self.phase_recs = []
        self.slots = []
        self.st = self.main_st
        self.prefix = ""

    def slot(self):
        if self.pools is not None:
            return {"lazy": True}
        self.nslot += 1
        sem = self.st.enter_context(self.nc.semaphore("s_dma%d" % self.nslot))
        sl = {"key": "dma%d" % self.nslot, "sem": sem, "cnt": 0}
        self.slots.append(sl)
        return sl

    def barrier(self):
        engs = [self.pe, self.act, self.dve, self.pool, self.sp]
        for e in engs:
            for e2 in engs:
                if e2.cnt > 0:
                    self.wait(e, (e2.name, e2.sem, e2.cnt))
            for sl in self.slots:
                if sl["cnt"] > 0:
                    self.wait(e, (sl["key"], sl["sem"], sl["cnt"], sl))

    def wait(self, e, h):
        if h is None:
            return
        key, sem, cnt = h[0], h[1], h[2]
        if len(h) > 3:
            cnt = max(cnt, h[3]["cnt"])
            h[3]["waited"] = max(h[3].get("waited", 0), cnt)
        if e.seen.get(key, 0) >= cnt:
            return
        e.eng.wait_ge(sem, cnt)
        e.seen[key] = cnt

    def op(self, e, fn, deps=()):
        for d in deps:
            self.wait(e, d)
        ins = fn()
        e.cnt += 1
        ins.then_inc(e.sem, 1)
        return (e.name, e.sem, e.cnt)

    def dma(self, q, out, in_, slot, deps=()):
        if slot.get("lazy"):
            if "rec" not in slot:
                slot["rec"] = self.pools["pool" if q is self.pool else "sp"].pop()
                self.phase_recs.append(slot["rec"])
                self.slots.append(slot["rec"])
            assert slot["rec"]["kind"] == ("pool" if q is self.pool else "sp"), "slot used from both DMA queue kinds"
            slot = slot["rec"]
        for d in deps:
            self.wait(q, d)
        W = slot.get("waited", 0)
        if W > q.seen.get(slot["key"], 0):
            q.eng.wait_ge(slot["sem"], W)
            q.seen[slot["key"]] = W
        q.eng.dma_start(out=out, in_=in_).then_inc(slot["sem"], 16)
        slot["cnt"] += 16
        return (slot["key"], slot["sem"], slot["cnt"], slot)

    def mm(self, outb, out_ap, lhsT, rhs, start, stop, rd=(), extra=()):
        deps = list(extra)
        if start:
            deps += outb.wdeps()
        for b in rd:
            deps += b.rdeps()
        deps = [d for d in deps if d is not None and d[0] != "pe"]
        h = self.op(self.pe, lambda: self.nc.tensor.matmul(out_ap, lhsT=lhsT, rhs=rhs, start=start, stop=stop,
                                                           skip_group_check=True), deps)
        for b in rd:
            b.read(h)
        outb.wr = h
        if start:
            outb.rd = {}
        return h

    def mm_t(self, outb, out_ap, in_ap, ident_ap, first, rd=()):
        deps = []
        if first:
            deps += outb.wdeps()
        for b in rd:
            deps += b.rdeps()
        deps = [d for d in deps if d is not None and d[0] != "pe"]
        h = self.op(self.pe, lambda: self.nc.tensor.transpose(out_ap, in_ap, ident_ap), deps)
        for b in rd:
            b.read(h)
        outb.wr = h
        if first:
            outb.rd = {}
        return h

    def ew(self, e, fn, wr=(), rd=(), extra=()):
        deps = list(extra)
        for b in wr:
            deps += b.wdeps()
        for b in rd:
            deps += b.rdeps()
        h = self.op(e, fn, deps)
        for b in rd:
            b.read(h)
        for b in wr:
            b.wrote(h)
        return h

    def load(self, q, buf, out_ap, in_ap, slot, extra=()):
        h = self.dma(q, out_ap, in_ap, slot, list(extra) + buf.wdeps())
        buf.wrote(h)
        return h

    def store(self, q, dram_ap, buf, in_ap, slot, extra=(), final=False):
        h = self.dma(q, dram_ap, in_ap, slot, list(extra) + buf.rdeps())
        buf.read(h)
        if final:
            self.out_handles.append(h)
        return h

    def finish(self):
        last = {}
        for h in self.out_handles:
            if h[0] not in last or last[h[0]][2] < h[2]:
                last[h[0]] = h
        for h in last.values():
            self.wait(self.sp, h)


class Ring:
    def __init__(self, bufs):
        self.bufs = bufs
        self.i = 0

    def next(self):
        b = self.bufs[self.i % len(self.bufs)]
        self.i += 1
        return b


class Dense:
    def __init__(self, kb):
        self.kb = kb
        nc = kb.nc
        xres_t = kb.sbuf("xres", [128, KC, TT], F32)
        h_t = kb.sbuf("hmod", [128, KC, TT], BF16)
        hid_t = kb.sbuf("hid", [128, FC, TT], BF16)
        self.xres = [Buf(xres_t[:, k, :]) for k in range(KC)]
        self.h = [Buf(h_t[:, k, :]) for k in range(KC)]
        self.hid = [Buf(hid_t[:, k, :]) for k in range(FC)]
        self.hid_t = hid_t
        self.wst = Ring([Buf(kb.sbuf("wst%d" % i, [128, WST_ELEMS], BF16)[:]) for i in range(3)])
        self.wslots = [kb.slot() for _ in range(3)]
        self.ones = Buf(kb.sbuf("ones32", [128, 128], F32)[:])
        self.epsb = Buf(kb.sbuf("epsb", [128, 1], F32)[:])
        self.sq = Ring([Buf(kb.sbuf("sq%d" % i, [128, 512], F32)[:]) for i in range(2)])
        self.rstd = Buf(kb.sbuf("rstd", [128, TT], F32)[:])
        self.tmp = Ring([Buf(kb.sbuf("tmpf%d" % i, [128, 512], F32)[:]) for i in range(3)])
        self.ps = Ring([Buf(kb.psum("psg%d" % i, [128, 512])[:]) for i in range(6)])
        self.ps_stat = Buf(kb.psum("ps_stat", [128, 512])[:])
        self.ps_misc = Buf(kb.psum("ps_misc", [128, 512])[:])
        self.xslot = kb.slot()
        self.cslot = kb.slot()
        self.cslot2 = kb.slot()
        self.oslot = kb.slot()
        self.mod = Buf(kb.sbuf("modt", [128, 144], F32)[:])
        self.gn = Buf(kb.sbuf("gn", [128, 48], F32)[:])
        self.A = Buf(kb.sbuf("coefA", [128, 48], F32)[:])
        self.GT = Buf(kb.sbuf("coefG", [128, 48], F32)[:])
        kb.ew(kb.dve, lambda: nc.vector.memset(self.ones.ap, 1.0), wr=[self.ones])
        kb.ew(kb.dve, lambda: nc.vector.memset(self.epsb.ap, EPS), wr=[self.epsb])

    def wstage(self):
        i = self.wst.i % 3
        return self.wst.next(), self.wslots[i]


def emit_modprep(kb, dn, cT_d, wada_d, badaT_d, gnT_d, col_lo, col_hi):
    nc = kb.nc
    if True:
        cact = Buf(kb.sbuf("cact", [128, KC, 2], F32)[:])
        craw = Buf(kb.sbuf("craw", [128, KC], F32)[:])
        bada = Buf(kb.sbuf("bada", [128, 144], F32)[:])
        mod, gn = dn.mod, dn.gn
        kb.load(kb.sp, craw, craw.ap, cT_d, dn.cslot)
        kb.load(kb.sp, bada, bada.ap, badaT_d, dn.cslot)
        kb.load(kb.sp, gn, gn.ap, gnT_d, dn.cslot)
        for r in range(2):
            kb.ew(kb.act, lambda r=r: nc.scalar.activation(out=cact.ap[:, :, r], in_=craw.ap, func=AF.Silu),
                  wr=[cact], rd=[craw])
        psm = dn.ps_misc
        stg = [Buf(kb.sbuf("wada_stg%d" % i, [128, KC, 128], F32)[:]) for i in range(2)]
        sslots = [kb.slot(), kb.slot()]
        cbs = list(range(col_lo, col_hi))

        def issue(i):
            cb = cbs[i]
            b = stg[i % 2]
            src = wada_d[:, cb * 128:(cb + 1) * 128].rearrange("(k p) c -> p k c", p=128)
            kb.load(kb.sp, b, b.ap[:, 0:8, :], src[:, 0:8, :], sslots[i % 2])
            b.wr = kb.dma(kb.sp, b.ap[:, 8:16, :], src[:, 8:16, :], sslots[i % 2])

        issue(0)
        first = True
        for i, col in enumerate(cbs):
            if i + 1 < len(cbs):
                issue(i + 1)
            b = stg[i % 2]
            for kc in range(KC):
                kb.mm(psm, psm.ap[:, 2 * col:2 * col + 2], lhsT=b.ap[:, kc, :], rhs=cact.ap[:, kc, :],
                      start=first, stop=(kc == KC - 1), rd=[b, cact])
                first = False
        pv = psm.ap[:, 2 * col_lo:2 * col_hi].rearrange("p (c two) -> p c two", two=2)[:, :, 0]
        kb.ew(kb.dve, lambda: nc.vector.tensor_tensor(out=mod.ap[:, col_lo:col_hi], in0=pv, in1=bada.ap[:, col_lo:col_hi], op=ALU.add),
              wr=[mod], rd=[psm, bada])
        A, GT = dn.A, dn.GT
        for sub in range(3):
            if col_lo <= sub * 48 + 16 and sub * 48 + 32 <= col_hi:
                kb.ew(kb.dve, lambda sub=sub: nc.vector.scalar_tensor_tensor(
                    out=A.ap[:, sub * 16:(sub + 1) * 16], in0=mod.ap[:, sub * 48 + 16:sub * 48 + 32], scalar=1.0,
                    in1=gn.ap[:, sub * 16:(sub + 1) * 16], op0=ALU.add, op1=ALU.mult), wr=[A], rd=[mod, gn])
            if col_lo <= sub * 48 + 32 and sub * 48 + 48 <= col_hi:
                kb.ew(kb.dve, lambda sub=sub: nc.vector.tensor_scalar(
                    out=GT.ap[:, sub * 16:(sub + 1) * 16], in0=mod.ap[:, sub * 48 + 32:sub * 48 + 48],
                    scalar1=(1.0 if sub == 1 else 0.5), scalar2=None, op0=ALU.mult), wr=[GT], rd=[mod])


def emit_norm_mod(kb, dn, sub):
    nc = kb.nc
    xres, h = dn.xres, dn.h
    for st_ in range(TT // 512):
        sl = slice(st_ * 512, (st_ + 1) * 512)
        for kc in range(KC):
            sq = dn.sq.next()
            kb.ew(kb.act, lambda kc=kc, sq=sq: nc.scalar.activation(out=sq.ap, in_=xres[kc].ap[:, sl], func=AF.Square),
                  wr=[sq], rd=[xres[kc]])
            kb.mm(dn.ps_stat, dn.ps_stat.ap, lhsT=dn.ones.ap, rhs=sq.ap, start=(kc == 0), stop=(kc == KC - 1),
                  rd=[dn.ones, sq])
        t = dn.tmp.next()
        kb.ew(kb.act, lambda t=t: nc.scalar.activation(out=t.ap, in_=dn.ps_stat.ap, func=AF.Sqrt, scale=1.0 / D, bias=dn.epsb.ap[:, 0:1]),
              wr=[t], rd=[dn.ps_stat, dn.epsb])
        kb.ew(kb.dve, lambda t=t, sl=sl: nc.vector.reciprocal(out=dn.rstd.ap[:, sl], in_=t.ap), wr=[dn.rstd], rd=[t])
    for kc in range(KC):
        for st_ in range(TT // 512):
            sl = slice(st_ * 512, (st_ + 1) * 512)
            t = dn.tmp.next()
            kb.ew(kb.dve, lambda kc=kc, t=t, sl=sl: nc.vector.scalar_tensor_tensor(
                out=t.ap, in0=xres[kc].ap[:, sl], scalar=dn.A.ap[:, sub * 16 + kc:sub * 16 + kc + 1], in1=dn.rstd.ap[:, sl],
                op0=ALU.mult, op1=ALU.mult), wr=[t], rd=[xres[kc], dn.A, dn.rstd])
            kb.ew(kb.act, lambda kc=kc, t=t, sl=sl: nc.scalar.activation(
                out=h[kc].ap[:, sl], in_=t.ap, func=AF.Identity, bias=dn.mod.ap[:, sub * 48 + kc:sub * 48 + kc + 1], scale=1.0),
                wr=[h[kc]], rd=[t, dn.mod])


def emit_gemm_fm(kb, dn, w_d, kchunks, pieces, src, epilogue, cwcap=None):
    nc = kb.nc
    cwmax = (WST_ELEMS // kchunks) // 64 * 64
    if cwcap:
        cwmax = min(cwmax, cwcap)
    groups, cur = [], []
    for pi, (c0, wd) in enumerate(pieces):
        if cur and (cur[-1][1] + cur[-1][2] == c0) and (c0 + wd - cur[0][1] <= cwmax):
            cur.append((pi, c0, wd))
        else:
            if cur:
                groups.append(cur)
            cur = [(pi, c0, wd)]
    if cur:
        groups.append(cur)
    loaded = {}

    def issue(gi):
        g = groups[gi]
        c0 = g[0][1]
        wtot = g[-1][1] + g[-1][2] - c0
        wb, ws = dn.wstage()
        wv = wb.ap[:, 0:kchunks * wtot].rearrange("p (k c) -> p k c", c=wtot)
        kb.load(kb.pool, wb, wv, w_d[:, c0:c0 + wtot].rearrange("(k p) c -> p k c", p=128), ws)
        loaded[gi] = (wb, wv, c0)

    issue(0)
    if len(groups) > 1:
        issue(1)
    for gi, g in enumerate(groups):
        if gi + 2 < len(groups):
            issue(gi + 2)
        wb, wv, gc0 = loaded.pop(gi)
        for (pi, c0, wd) in g:
            for st_ in range(TT // 512):
                psb = dn.ps.next()
                for kc in range(kchunks):
                    sap, sbufs = src(kc, st_)
                    kb.mm(psb, psb.ap[0:wd, :], lhsT=wv[:, kc, c0 - gc0:c0 - gc0 + wd], rhs=sap,
                          start=(kc == 0), stop=(kc == kchunks - 1), rd=[wb] + sbufs)
                epilogue(pi, st_, psb)


def emit_ffn(kb, dn, sub, win_d, wout_d, xT_out_d, n0, final_out=False):
    nc = kb.nc
    emit_norm_mod(kb, dn, sub)
    hsrc = lambda kc, st_: (dn.h[kc].ap[:, st_ * 512:(st_ + 1) * 512], [dn.h[kc]])
    pieces = []
    for J in range(FF // 256):
        pieces += [(J * 256, 128), (J * 256 + 128, 128), (FF + J * 256, 128), (FF + J * 256 + 128, 128)]
    hold = {}

    def ep_in(pi, st_, psb):
        J, r = pi // 4, pi % 4
        if r < 2:
            hold[(2 * J + r, st_)] = psb
            return
        j = 2 * J + r - 2
        psg = hold.pop((j, st_))
        t = dn.tmp.next()
        sl = slice(st_ * 512, (st_ + 1) * 512)
        kb.ew(kb.act, lambda: nc.scalar.activation(out=t.ap, in_=psg.ap, func=AF.Silu), wr=[t], rd=[psg])
        kb.ew(kb.dve, lambda: nc.vector.tensor_tensor(out=dn.hid[j].ap[:, sl], in0=t.ap, in1=psb.ap, op=ALU.mult),
              wr=[dn.hid[j]], rd=[t, psb])

    emit_gemm_fm(kb, dn, win_d, KC, pieces, hsrc, ep_in, cwcap=256)

    def ep_out(pi, st_, psb):
        sl = slice(st_ * 512, (st_ + 1) * 512)
        xb = dn.xres[pi]
        kb.ew(kb.dve, lambda: nc.vector.scalar_tensor_tensor(
            out=xb.ap[:, sl], in0=psb.ap, scalar=dn.GT.ap[:, sub * 16 + pi:sub * 16 + pi + 1], in1=xb.ap[:, sl],
            op0=ALU.mult, op1=ALU.add), wr=[xb], rd=[psb, dn.GT])
        if xT_out_d is not None:
            kb.store(kb.sp, xT_out_d[pi * 128:(pi + 1) * 128, n0 + st_ * 512:n0 + (st_ + 1) * 512], xb, xb.ap[:, sl],
                     dn.oslot, final=final_out)

    hidsrc = lambda kc, st_: (dn.hid[kc].ap[:, st_ * 512:(st_ + 1) * 512], [dn.hid[kc]])
    emit_gemm_fm(kb, dn, wout_d, FC, [(m * 128, 128) for m in range(KC)], hidsrc, ep_out)


def load_xtile(kb, dn, xT_d, n0):
    for kc in range(KC):
        kb.load(kb.sp, dn.xres[kc], dn.xres[kc].ap, xT_d[kc * 128:(kc + 1) * 128, n0:n0 + TT], dn.xslot)


def _env(env, prefix):
    if env is None:
        nc = bass.Bass("TRN2", target_bir_lowering=False)
        gi = lambda name, shape, d=F32: nc.dram_tensor(name, shape, d, kind="ExternalInput").ap()
        go = lambda name, shape, d=F32: nc.dram_tensor(name, shape, d, kind="ExternalOutput").ap()
        return nc, None, gi, go
    nc = env["nc"]
    g = lambda name, shape, d=F32: env["T"][name]
    return nc, env["kb"], g, g

class OutStage:
    def __init__(self, kb, n=3):
        self.kb = kb
        self.f = [Buf(kb.sbuf("ostgf%d" % i, [128, 512], F32)[:]) for i in range(n)]
        self.b = [Buf(kb.sbuf("ostgb%d" % i, [128, 512], BF16)[:]) for i in range(n)]
        self.fs = [kb.slot() for _ in range(n)]
        self.bs = [kb.slot() for _ in range(n)]
        self.fi = 0
        self.bi = 0
        self.flip = 0

    def next(self, dt=F32):
        if dt == F32:
            k = self.fi % len(self.f)
            self.fi += 1
            return self.f[k], self.fs[k]
        k = self.bi % len(self.b)
        self.bi += 1
        return self.b[k], self.bs[k]


def emit_gemm_tm(kb, dn, w_d, kchunks, segs, src_h, epilogue):
    nc = kb.nc
    for si, (c0, wd) in enumerate(segs):
        wb, ws = dn.wstage()
        wv = wb.ap[:, 0:kchunks * wd].rearrange("p (k c) -> p k c", c=wd)
        kb.load(kb.pool, wb, wv, w_d[:, c0:c0 + wd].rearrange("(k p) c -> p k c", p=128), ws)
        for ts in range(TT // 128):
            psb = dn.ps.next()
            for kc in range(kchunks):
                sap, sbufs = src_h(kc, ts)
                kb.mm(psb, psb.ap[:, 0:wd], lhsT=sap, rhs=wv[:, kc, :], start=(kc == 0), stop=(kc == kchunks - 1), rd=[wb] + sbufs)
            epilogue(si, ts, psb)


def emit_proj(kb, dn, os_, win_d, gkv, o, n0, ghalf=None, kv_only=False):
    nc = kb.nc
    emit_norm_mod(kb, dn, 1)
    hsrc = lambda kc, st_: (dn.h[kc].ap[:, st_ * 512:(st_ + 1) * 512], [dn.h[kc]])
    pieces, kinds = [], []

    def add(c0, wd, kind, idx):
        if kv_only and kind in ("qn", "qd", "qi", "am"):
            return
        pieces.append((c0, wd))
        kinds.append((kind, idx))

    for c in range(8):
        add(C_QN + c * 128, 128, "qn", c)
    for nm, cc in (("kc", C_KC), ("vc", C_VC), ("ks", C_KS), ("kw", C_KW)):
        for c in range(2):
            add(cc + c * 128, 128, nm, c)
    for c in range(8):
        add(C_QD + c * 128, 128, "qd", c)
    for i in range(2):
        add(C_CKV + i * 128, 128, "ckv", i)
    for c in range(4):
        add(C_QI + c * 128, 128, "qi", c)
    add(C_KI, 128, "ki", 0)
    for i in range(32):
        add(C_AM + i * 128, 128, "am", i)
    sl_tok = slice(n0, n0 + TT)
    tt_ = n0 // TT

    def kdst(ap2):
        if ghalf is None:
            return ap2[:, sl_tok], None
        return ap2.rearrange("p (j hf t) -> p j hf t", hf=2, t=128)[:, 4 * tt_:4 * tt_ + 4, ghalf, :], "p (j t) -> p j t"

    cbuf = [Buf(dn.cbuf_t[:, i, :]) for i in range(2)]

    def evac_copy(psb, wd, dst_ap, dt, rr=None):
        ob, oslot = os_.next(dt)
        oap = ob.ap[0:wd, :]
        os_.flip ^= 1
        if os_.flip:
            kb.ew(kb.act, lambda: nc.scalar.copy(out=oap, in_=psb.ap[0:wd, :]), wr=[ob], rd=[psb])
        else:
            kb.ew(kb.dve, lambda: nc.vector.tensor_copy(out=oap, in_=psb.ap[0:wd, :]), wr=[ob], rd=[psb])
        kb.store(kb.sp, dst_ap, ob, oap if rr is None else oap.rearrange(rr, t=128), oslot, final=(ghalf is None))

    def ep(pi, st_, psb):
        kind, idx = kinds[pi]
        wd = pieces[pi][1]
        if kind in ("qn", "qd"):
            evac_copy(psb, 128, o[kind][2 * idx:2 * idx + 2, :, sl_tok].rearrange("h p n -> (h p) n"), BF16)
        elif kind == "qi":
            evac_copy(psb, 128, o[kind][4 * idx:4 * idx + 4, :, sl_tok].rearrange("h p n -> (h p) n"), BF16)
        elif kind in ("kc", "vc", "ks", "kw"):
            dst, rr = kdst(o[kind][2 * idx:2 * idx + 2].rearrange("h p n -> (h p) n"))
            evac_copy(psb, 128, dst, BF16, rr)
        elif kind == "ki":
            dst, rr = kdst(o["ki"])
            evac_copy(psb, 32, dst, BF16, rr)
        elif kind == "am":
            ob, oslot = os_.next()
            kb.ew(kb.act, lambda: nc.scalar.activation(out=ob.ap, in_=psb.ap, func=AF.Sigmoid), wr=[ob], rd=[psb])
            kb.store(kb.sp, o["am"][idx * 128:(idx + 1) * 128, sl_tok], ob, ob.ap, oslot, final=(ghalf is None))
        elif kind == "ckv":
            kb.ew(kb.dve, lambda: nc.vector.tensor_copy(out=cbuf[idx].ap, in_=psb.ap), wr=[cbuf[idx]], rd=[psb])
            if idx == 1:
                for i in range(2):
                    sq = dn.sq.next()
                    kb.ew(kb.act, lambda i=i, sq=sq: nc.scalar.activation(out=sq.ap, in_=cbuf[i].ap, func=AF.Square), wr=[sq], rd=[cbuf[i]])
                    kb.mm(dn.ps_stat, dn.ps_stat.ap, lhsT=dn.ones.ap, rhs=sq.ap, start=(i == 0), stop=(i == 1), rd=[dn.ones, sq])
                t = dn.tmp.next()
                kb.ew(kb.act, lambda: nc.scalar.activation(out=t.ap, in_=dn.ps_stat.ap, func=AF.Sqrt, scale=1.0 / 256, bias=dn.epsb.ap[:, 0:1]),
                      wr=[t], rd=[dn.ps_stat, dn.epsb])
                t2 = dn.tmp.next()
                kb.ew(kb.dve, lambda: nc.vector.reciprocal(out=t2.ap, in_=t.ap), wr=[t2], rd=[t])
                for i in range(2):
                    ob, oslot = os_.next(BF16)
                    oap = ob.ap
                    kb.ew(kb.dve, lambda i=i, oap=oap: nc.vector.scalar_tensor_tensor(
                        out=oap, in0=cbuf[i].ap, scalar=gkv.ap[:, i:i + 1], in1=t2.ap, op0=ALU.mult, op1=ALU.mult),
                        wr=[ob], rd=[cbuf[i], gkv, t2])
                    dst, rr = kdst(o["cn"][i * 128:(i + 1) * 128, :])
                    kb.store(kb.sp, dst, ob, oap if rr is None else oap.rearrange(rr, t=128), oslot, final=(ghalf is None))

    if dn.pflags & 1:
        emit_gemm_fm(kb, dn, win_d, KC, pieces, hsrc, ep, cwcap=512)

    segs = [(C_VS, 256), (C_VW, 256)] if kv_only else [(C_VS, 256), (C_VW, 256), (C_GN, 128), (C_KI, 128)]
    hsrc_t = lambda kc, ts: (dn.h[kc].ap[:, ts * 128:(ts + 1) * 128], [dn.h[kc]])

    def ep_t(si, ts, psb):
        tsl = slice(n0 + ts * 128, n0 + (ts + 1) * 128)
        gt_ = 2 * (4 * tt_ + ts) + (ghalf or 0)
        gsl = tsl if ghalf is None else slice(gt_ * 128, (gt_ + 1) * 128)
        if si == 0:
            ob, oslot = os_.next(BF16)
            oap = ob.ap[:, 0:256]
            kb.ew(kb.dve, lambda: nc.vector.tensor_copy(out=oap, in_=psb.ap[:, 0:256]), wr=[ob], rd=[psb])
            kb.store(kb.sp, o["vs"][gsl, :], ob, oap, oslot, final=(ghalf is None))
        elif si == 1:
            ob, oslot = os_.next(BF16)
            oap = ob.ap[:, 0:256]
            kb.ew(kb.dve, lambda: nc.vector.tensor_copy(out=oap, in_=psb.ap[:, 0:256]), wr=[ob], rd=[psb])
            kb.store(kb.sp, o["vw"][gsl, :], ob, oap, oslot, final=(ghalf is None))
        elif si == 2:
            ob2, oslot2 = os_.next()
            kb.ew(kb.act, lambda: nc.scalar.activation(out=ob2.ap[:, 0:48], in_=psb.ap[:, 0:48], func=AF.Sigmoid), wr=[ob2], rd=[psb])
            kb.store(kb.sp, o["gn"][tsl, :], ob2, ob2.ap[:, 0:48], oslot2, final=(ghalf is None))
        else:
            ob, oslot = os_.next()
            kb.ew(kb.dve, lambda: nc.vector.tensor_scalar(out=ob.ap[:, 0:16], in0=psb.ap[:, 32:48], scalar1=INDEX_SCALE, scalar2=None, op0=ALU.mult),
                  wr=[ob], rd=[psb])
            kb.store(kb.sp, o["wi"][tsl, :], ob, ob.ap[:, 0:16], oslot, final=(ghalf is None))

    if dn.pflags & 2:
        if dn.pflags > 3:
            keep = [i for i in range(4) if dn.pflags & (4 << i)]
            segs2 = [segs[i] for i in keep]
            emit_gemm_tm(kb, dn, win_d, KC, segs2, hsrc_t, lambda si, ts, psb: ep_t(keep[si], ts, psb))
        else:
            emit_gemm_tm(kb, dn, win_d, KC, segs, hsrc_t, ep_t)


PROJ_OUTS = [("qn", [16, 64, NTOK], BF16), ("kc", [4, 64, NTOK], BF16), ("vc", [4, 64, NTOK], BF16),
             ("ks", [4, 64, NTOK], BF16), ("kw", [4, 64, NTOK], BF16), ("qd", [16, 64, NTOK], BF16),
             ("cn", [256, NTOK], BF16), ("qi", [16, 32, NTOK], BF16), ("ki", [32, NTOK], BF16),
             ("am", [4096, NTOK], F32), ("vs", [NTOK, 256], BF16), ("vw", [NTOK, 256], BF16),
             ("gn", [NTOK, 48], F32), ("wi", [NTOK, 16], F32)]


def build_A(ntt=NTT, do_proj=True, do_ffn=True, do_mod=True, pflags=3, env=None, prefix="", ghalf=None, load_tabs=False, kv_only=False):
    nc, ekb, gi, go = _env(env, prefix)
    dt = lambda name, shape, d=F32, kind="ExternalInput": (gi if kind == "ExternalInput" else go)(name, shape, d)
    xT = dt("xT", [D, NTOK])
    cT = dt("cT", [128, KC])
    if do_mod and not load_tabs:
        wada = dt("wada", [D, 9 * D])
        badaT = dt("badaT", [128, 144])
        gnT = dt("gnT", [128, 48])
    if do_ffn:
        wfi = dt("wfi", [D, 2 * FF])
        wfo = dt("wfo", [FF, D])
    win = dt("win", [D, N_IN])
    gkvT = dt("gkvT", [128, 2])
    x1T = dt("x1T", [D, NTOK], kind="ExternalOutput")
    o = {nm: dt("o_" + nm, shp, d, kind="ExternalOutput") for nm, shp, d in PROJ_OUTS}
    tabs = {"o_mod": dt("o_mod", [128, 144], kind="ExternalOutput"), "o_A": dt("o_A", [128, 48], kind="ExternalOutput"),
            "o_GT": dt("o_GT", [128, 48], kind="ExternalOutput")}
    with ExitStack() as st:
        if ekb is None:
            kb = KB(nc, st)
        else:
            kb = ekb
            kb.begin_phase(prefix, st)
        dn = Dense(kb)
        os_ = OutStage(kb)
        dn.cbuf_t = kb.sbuf("cbuf", [128, 2, 512], F32)
        gkv = Buf(kb.sbuf("gkv", [128, 2], F32)[:])
        kb.load(kb.sp, gkv, gkv.ap, gkvT, dn.cslot)
        if load_tabs:
            kb.load(kb.sp, dn.mod, dn.mod.ap, tabs["o_mod"], dn.cslot)
            kb.load(kb.sp, dn.A, dn.A.ap, tabs["o_A"], dn.cslot)
            kb.load(kb.sp, dn.GT, dn.GT.ap, tabs["o_GT"], dn.cslot)
        elif do_mod:
            emit_modprep(kb, dn, cT, wada, badaT, gnT, 0, 144)
            for nm_, bf_ in (("o_mod", dn.mod), ("o_A", dn.A), ("o_GT", dn.GT)):
                kb.store(kb.sp, tabs[nm_], bf_, bf_.ap, dn.cslot2, final=(ekb is None))
        else:
            kb.ew(kb.dve, lambda: nc.vector.memset(dn.mod.ap, 0.0), wr=[dn.mod])
            kb.ew(kb.dve, lambda: nc.vector.memset(dn.A.ap, 1.0), wr=[dn.A])
            kb.ew(kb.dve, lambda: nc.vector.memset(dn.GT.ap, 1.0), wr=[dn.GT])
        dn.pflags = pflags
        for tt in range(ntt):
            n0 = tt * TT
            load_xtile(kb, dn, xT, n0)
            if do_ffn:
                emit_ffn(kb, dn, 0, wfi, wfo, x1T, n0, final_out=(ekb is None))
            if do_proj:
                emit_proj(kb, dn, os_, win, gkv, o, n0, ghalf=ghalf, kv_only=kv_only)
        if ekb is None:
            kb.finish()
        else:
            kb.end_phase()
    return nc


BIG = 30000.0
SCALE = 0.125
NU = 256
NPAT_N = 10
GC_W = 576


def _t5_bucket_np(n):
    n = np.maximum(n, 0)
    nf = np.maximum(n, 1).astype(np.float32)
    large = 16 + (np.log(nf / 16) / np.log(128 / 16) * 16).astype(np.int32)
    large = np.minimum(large, 31)
    return np.where(n < 16, n, large)


def _pattern_onehot(dist, valid):
    n = dist.shape[0]
    oh = np.zeros((33, n), np.float32)
    b = _t5_bucket_np(dist)
    idx = np.arange(n)
    v = valid.astype(bool)
    oh[b[v], idx[v]] += 8.0
    oh[31, idx[v]] -= 8.0
    oh[32, idx[~v]] = 1.0
    return oh


def _dtiles(v, s, r):
    J = 8
    return (2 * J + (v ^ s)) - ((2 * J + r) ^ s)


def host_consts_nsa(v, s=0):
    u = np.arange(NU) - 127
    pats = []
    for r in (-2, -1, 0, 1):
        d = _dtiles(v, s, r) * 128 + u
        pats.append(_pattern_onehot(d, d >= 0))
    for r in (-4, -3, -2, -1, 0, 1):
        d = _dtiles(v, s, r) * 128 + u
        pats.append(_pattern_onehot(d, (d >= 0) & (d < 512)))
    dc = np.arange(GC_W) - 300
    pats.append(_pattern_onehot(dc, dc >= 0))
    ohg = np.concatenate(pats, axis=1)
    half = v ^ s
    selc = np.zeros((17, 16, 2, 128), np.float32)
    for j in range(16):
        i = 2 * j + half
        for nt in range(2):
            n = nt * 128 + np.arange(128)
            m = n - 8 * i + 9
            m = np.where((m >= 16) | (n >= 255), 16, m)
            ok = m >= 0
            selc[16 - m[ok], j, nt, np.arange(128)[ok]] = 1.0
    cs = np.arange(256) * 16
    ss = np.arange(64) * 64
    ov = ((cs[:, None] < ss[None, :] + 64) & (cs[:, None] + 32 > ss[None, :])).astype(np.float32)
    ov[255] = 0
    ov = ov.reshape(2, 128, 64).transpose(1, 0, 2)
    visb = np.zeros((128, 16, 64), np.float32)
    blk = np.arange(64)
    for j in range(16):
        t = (2 * j + half) * 128 + np.arange(128)
        cur = t // 64
        forced = (blk[None, :] == 0) | (blk[None, :] == cur[:, None]) | (blk[None, :] == cur[:, None] - 1)
        vis = blk[None, :] * 64 <= t[:, None]
        visb[:, j, :] = np.where(vis, np.where(forced, 1e4, 0.0), -1e30)
    bidx = np.zeros((128, 2, 128), np.float32)
    for ri, r in enumerate((0, 1)):
        d = _dtiles(v, s, r) * 128 + np.arange(128)[:, None] - np.arange(128)[None, :]
        bidx[:, ri, :] = np.where(d >= 0, 0.0, -1e30)
    return {"ohg": ohg, "selc": selc.reshape(17, 16 * 2 * 128), "ov": ov, "visb": visb, "bidx": bidx}


def host_consts_dsa(v, s=0):
    u = np.arange(NU) - 127
    pats = []
    for r in (-2, -1, 0, 1):
        d = _dtiles(v, s, r) * 128 + u
        pats.append(_pattern_onehot(d, d >= 0))
    return {"ohg_d": np.concatenate(pats, axis=1)}


class Attn:
    def __init__(self, kb, ident_d, rev_d):
        self.kb = kb
        nc = kb.nc
        self.ident = Buf(kb.sbuf("identb", [128, 128], BF16)[:])
        self.i4 = Buf(kb.sbuf("i4", [128, 4, 128], BF16)[:])
        identf = Buf(kb.sbuf("identf", [128, 128], F32)[:])
        self.identf = identf
        isl = kb.slot()
        kb.load(kb.sp, identf, identf.ap, ident_d, isl)
        kb.ew(kb.dve, lambda: nc.vector.tensor_copy(out=self.ident.ap, in_=identf.ap), wr=[self.ident], rd=[identf])
        revf = Buf(kb.sbuf("revf", [128, 128], F32)[:])
        self.rev = Buf(kb.sbuf("revb", [128, 128], BF16)[:])
        kb.load(kb.sp, revf, revf.ap, rev_d, isl)
        kb.ew(kb.dve, lambda: nc.vector.tensor_copy(out=self.rev.ap, in_=revf.ap), wr=[self.rev], rd=[revf])
        for h in range(4):
            kb.ew(kb.dve, lambda h=h: nc.vector.tensor_copy(out=self.i4.ap[:, h, :], in_=identf.ap), wr=[self.i4], rd=[identf])
        self.ps_s = Ring([Buf(kb.psum("ps_s%d" % i, [128, 512])[:]) for i in range(3)])
        self.ps_o = Ring([Buf(kb.psum("ps_o%d" % i, [128, 512])[:]) for i in range(3)])
        self.E = Ring([Buf(kb.sbuf("Ebuf%d" % i, [128, 512], BF16)[:]) for i in range(4)])
        self.small = Ring([Buf(kb.sbuf("small%d" % i, [128, 16], F32)[:]) for i in range(6)])

    def build_bias(self, relb_d, ohg_d, width, scratch_d):
        kb, nc = self.kb, self.kb.nc
        tab = Buf(kb.sbuf("tabaug", [33, 32], F32)[:])
        sl = kb.slot()
        kb.ew(kb.dve, lambda: nc.vector.memset(tab.ap, -BIG), wr=[tab])
        kb.load(kb.sp, tab, tab.ap[0:32, :], relb_d, sl)
        CH = 512
        stg = Buf(kb.sbuf("ohg_stg", [33, CH], F32)[:])
        gout = Buf(kb.sbuf("g_out", [32, CH], F32)[:])
        sl2, sl3 = kb.slot(), kb.slot()
        psb = self.ps_s.bufs[0]
        for c0 in range(0, width, CH):
            w = min(CH, width - c0)
            kb.load(kb.sp, stg, stg.ap[:, 0:w], ohg_d[:, c0:c0 + w], sl2)
            kb.mm(psb, psb.ap[0:32, 0:w], lhsT=tab.ap, rhs=stg.ap[:, 0:w], start=True, stop=True, rd=[tab, stg])
            kb.ew(kb.dve, lambda w=w: nc.vector.tensor_copy(out=gout.ap[:, 0:w], in_=psb.ap[0:32, 0:w]), wr=[gout], rd=[psb])
            self.g_store = kb.store(kb.sp, scratch_d[:, c0:c0 + w], gout, gout.ap[:, 0:w], sl3)

    def load_skew(self, dst_buf, dst_ap, scratch_t, col0, h0, nh, pstride, slot):
        kb = self.kb
        npart = dst_ap.shape[0]
        src = bass.AP(scratch_t, h0 * scratch_t.shape[1] + col0 - pstride * (npart - 1), ((pstride, npart), (scratch_t.shape[1], nh), (1, 128)))
        return kb.load(kb.pool, dst_buf, dst_ap, src, slot, extra=[self.g_store])

    def attend(self, kt_list, score_mms, o_buf, v_of, nheads=4, vw=65, extra_o=None):
        kb, nc = self.kb, self.kb.nc
        st_ = {"first": True}

        def finish(kt, ps):
            e = self.E.next()
            kb.ew(kb.act, lambda ps=ps, e=e: nc.scalar.activation(out=e.ap[:, 0:nheads * 128], in_=ps.ap[:, 0:nheads * 128], func=AF.Exp, scale=SCALE),
                  wr=[e], rd=[ps])
            for h in range(nheads):
                vap, vb = v_of(kt, h)
                kb.mm(o_buf, o_buf.ap[:, h * vw:(h + 1) * vw], lhsT=e.ap[:, h * 128:(h + 1) * 128], rhs=vap,
                      start=st_["first"], stop=False, rd=[e] + vb)
                st_["first"] = False
                if extra_o is not None:
                    extra_o(kt, h, e, False)

        pending = []
        for kt in kt_list:
            ps = self.ps_s.next()
            score_mms(kt, ps)
            pending.append((kt, ps))
            if len(pending) == 3:
                finish(*pending.pop(0))
        while pending:
            finish(*pending.pop(0))
        return o_buf


def build_B(env=None, prefix=""):
    nc, ekb, dti, dto = _env(env, prefix)
    qn_d = dti("qn", [16, 64, NTOK], BF16)
    qi_d = dti("qi", [16, 32, NTOK], BF16)
    gn_d = dti("gn", [NTOK, 48])
    wi_d = dti("wi", [NTOK, 16])
    ksT_d = dti("ksT", [4, 64, S], BF16)
    kwT_d = dti("kwT", [4, 64, S], BF16)
    kcT_d = dti("kcT", [4, 64, S], BF16)
    vcT_d = dti("vcT", [4, 64, S], BF16)
    vs_d = dti("vs", [S, 256], BF16)
    vw_d = dti("vw", [S, 256], BF16)
    kiT_d = dti("kiT", [32, S], BF16)
    wk1_d, wv1_d = dti("wk1", [2048, 256]), dti("wv1", [2048, 256])
    wk2_d, wv2_d = dti("wk2", [256, 64]), dti("wv2", [256, 64])
    pek_d, pev_d = dti("pekT", [64, 32]), dti("pevT", [64, 32])
    relb_d = dti("relb", [32, 32])
    ohg_d = dti("ohg", [33, NPAT_N * NU + GC_W])
    selc_d = dti("selc", [17, 16 * 2 * 128])
    ov_d = dti("ov", [128, 2, 64])
    visb_d = dti("visb", [128, 16, 64])
    bidx_d = dti("bidx", [128, 2, 128])
    ident_d = dti("ident", [128, 128])
    rev_d = dti("rev", [128, 128])
    sw_d = dti("sw", [128, 2])
    onT_d = dto("onT", [1024, NTOK])
    nm_d = dto("negmask", [16, 128, S], BF16)
    gsc_t = nc.dram_tensor(prefix + "gscratch", [32, NPAT_N * NU + GC_W], F32, kind="Internal")
    with ExitStack() as st:
        if ekb is None:
            kb = KB(nc, st)
        else:
            kb = ekb
            kb.begin_phase(prefix, st)
        at = Attn(kb, ident_d, rev_d)
        ld = kb.slot()
        at.build_bias(relb_d, ohg_d, NPAT_N * NU + GC_W, gsc_t.ap())
        ksT = Buf(kb.sbuf("sb_ksT", [64, 4, S], BF16)[:])
        vsA = Buf(kb.sbuf("sb_vsA", [128, 32, 4, 65], BF16)[:])
        kiT = Buf(kb.sbuf("sb_kiT", [32, S], BF16)[:])
        kcmpT = Buf(kb.sbuf("sb_kcmpT", [64, 4, 256], BF16)[:])
        vcA = Buf(kb.sbuf("sb_vcA", [128, 2, 4, 65], BF16)[:])
        ovb = Buf(kb.sbuf("sb_ovb", [128, 2, 64], BF16)[:])
        bsel = Buf(kb.sbuf("sb_bsel", [128, 4, 16, 128], BF16)[:])
        bwin = Buf(kb.sbuf("sb_bwin", [128, 6, 16, 128], BF16)[:])
        pc = Buf(kb.sbuf("sb_pc", [17, 16, 128], BF16)[:])
        selc = Buf(kb.sbuf("sb_selc", [17, 16 * 2 * 128], BF16)[:])
        visb = Buf(kb.sbuf("sb_visb", [128, 16, 64], F32)[:])
        bidx = Buf(kb.sbuf("sb_bidx", [128, 2, 128], F32)[:])
        halfc = Buf(kb.sbuf("sb_halfc", [128, 1], F32)[:])
        swb = Buf(kb.sbuf("sb_sw", [128, 2], F32)[:])
        kb.load(kb.sp, swb, swb.ap, sw_d, ld)
        kb.ew(kb.dve, lambda: nc.vector.memset(halfc.ap, 0.5), wr=[halfc])
        for g in range(4):
            kb.load(kb.sp, ksT, ksT.ap[:, g, :], ksT_d[g], ld)
        kb.ew(kb.pool, lambda: nc.gpsimd.memset(vsA.ap, 1.0), wr=[vsA])
        kb.ew(kb.pool, lambda: nc.gpsimd.memset(vcA.ap, 0.0), wr=[vcA])
        kb.ew(kb.pool, lambda: nc.gpsimd.memset(vcA.ap[:, :, :, 64:65], 1.0), wr=[vcA])
        for kt in range(32):
            kb.load(kb.sp, vsA, vsA.ap[:, kt, :, 0:64], vs_d[kt * 128:(kt + 1) * 128, :].rearrange("p (g d) -> p g d", d=64), ld)
        kb.load(kb.sp, kiT, kiT.ap, kiT_d, ld)
        kb.load(kb.sp, visb, visb.ap, visb_d, ld)
        kb.load(kb.sp, bidx, bidx.ap, bidx_d, ld)
        ldp = kb.slot()
        kb.load(kb.pool, selc, selc.ap, selc_d, ldp)
        kb.load(kb.pool, ovb, ovb.ap, ov_d, ldp)
        for r in range(4):
            at.load_skew(bsel, bsel.ap[:, r, :, :], gsc_t, r * NU + 127, 0, 16, 1, ldp)
        for r in range(6):
            at.load_skew(bwin, bwin.ap[:, r, :, :], gsc_t, (4 + r) * NU + 127, 0, 16, 1, ldp)
        at.load_skew(pc, pc.ap, gsc_t, NPAT_N * NU + 413, 0, 16, 16, ldp)

        with ExitStack() as cst:
            w1 = Buf(cst.enter_context(nc.sbuf_tensor(kb.prefix + "sb_w1c", [64, 32, 256], BF16))[:])
            w2 = Buf(cst.enter_context(nc.sbuf_tensor(kb.prefix + "sb_w2c", [128, 2, 64], BF16))[:])
            pe2 = Buf(cst.enter_context(nc.sbuf_tensor(kb.prefix + "sb_pe2", [64, 32, 2], BF16))[:])
            pe1 = Buf(cst.enter_context(nc.sbuf_tensor(kb.prefix + "sb_pe1", [64, 32], F32))[:])
            src = Buf(cst.enter_context(nc.sbuf_tensor(kb.prefix + "sb_csrc", [64, S], BF16))[:])
            src2 = Buf(cst.enter_context(nc.sbuf_tensor(kb.prefix + "sb_csrc2", [64, S], BF16))[:])
            hb = Buf(cst.enter_context(nc.sbuf_tensor(kb.prefix + "sb_hbias", [128, 2], F32))[:])
            hid = Buf(cst.enter_context(nc.sbuf_tensor(kb.prefix + "sb_chid", [128, 2, 256], BF16))[:])
            lw = kb.slot()
            lsrc = kb.slot()
            psh = at.ps_o.bufs[0]
            psb2 = at.ps_o.bufs[1]
            for which, (w1_d, w2_d, pe_d, srcT_d) in enumerate(((wk1_d, wk2_d, pek_d, kcT_d), (wv1_d, wv2_d, pev_d, vcT_d))):
                kb.load(kb.pool, w1, w1.ap, w1_d.rearrange("(l d) c -> d l c", d=64), lw)
                kb.load(kb.pool, w2, w2.ap, w2_d.rearrange("(k p) c -> p k c", p=128), lw)
                kb.load(kb.sp, pe1, pe1.ap, pe_d, lsrc)
                for r_ in range(2):
                    kb.ew(kb.dve, lambda r_=r_: nc.vector.tensor_copy(out=pe2.ap[:, :, r_], in_=pe1.ap), wr=[pe2], rd=[pe1])
                for c in range(2):
                    for l in range(32):
                        kb.mm(psb2, psb2.ap[:, 2 * c:2 * c + 2], lhsT=w1.ap[:, l, c * 128:(c + 1) * 128], rhs=pe2.ap[:, l, :],
                              start=(c == 0 and l == 0), stop=(l == 31), rd=[w1, pe2])
                kb.ew(kb.dve, lambda: nc.vector.tensor_copy(out=hb.ap, in_=psb2.ap[:, 0:4].rearrange("p (c two) -> p c two", two=2)[:, :, 0]),
                      wr=[hb], rd=[psb2])
                for g in range(4):
                    kb.load(kb.sp, src, src.ap, srcT_d[g], lsrc)
                    s4 = srcT_d[g].rearrange("d (p w t) -> d p w t", w=2, t=128)
                    d4 = src2.ap.rearrange("d (p w t) -> d p w t", w=2, t=128)
                    for w_ in range(2):
                        kb.load(kb.sp, src2, d4[:, :, w_, :], s4[:, :, 1 - w_, :], lsrc)
                    kb.ew(kb.dve, lambda: nc.vector.tensor_scalar(out=src.ap, in0=src.ap, scalar1=swb.ap[0:64, 0:1], scalar2=None, op0=ALU.mult), wr=[src], rd=[swb])
                    kb.ew(kb.dve, lambda: nc.vector.scalar_tensor_tensor(out=src.ap, in0=src2.ap, scalar=swb.ap[0:64, 1:2], in1=src.ap, op0=ALU.mult, op1=ALU.add),
                          wr=[src], rd=[src2, swb])
                    for c in range(2):
                        for l in range(32):
                            kb.mm(psh, psh.ap[:, 0:255], lhsT=w1.ap[:, l, c * 128:(c + 1) * 128], rhs=src.ap[:, l:l + 16 * 254 + 1:16],
                                  start=(l == 0), stop=(l == 31), rd=[w1, src])
                        kb.ew(kb.act, lambda c=c: nc.scalar.activation(out=hid.ap[:, c, 0:255], in_=psh.ap[:, 0:255], func=AF.Silu, bias=hb.ap[:, c:c + 1]),
                              wr=[hid], rd=[psh, hb])
                    if which == 0:
                        for c in range(2):
                            kb.mm(psb2, psb2.ap[0:64, 0:255], lhsT=w2.ap[:, c, :], rhs=hid.ap[:, c, 0:255], start=(c == 0), stop=(c == 1), rd=[w2, hid])
                        kb.ew(kb.dve, lambda g=g: nc.vector.memset(kcmpT.ap[:, g, 255:256], 0.0), wr=[kcmpT])
                        kb.ew(kb.dve, lambda g=g: nc.vector.tensor_copy(out=kcmpT.ap[:, g, 0:255], in_=psb2.ap[0:64, 0:255]), wr=[kcmpT], rd=[psb2])
                    else:
                        for nt in range(2):
                            m = 128 if nt == 0 else 127
                            for c in range(2):
                                kb.mm(psb2, psb2.ap[0:m, 0:64], lhsT=hid.ap[:, c, nt * 128:nt * 128 + m], rhs=w2.ap[:, c, :], start=(c == 0), stop=(c == 1),
                                      rd=[w2, hid])
                            kb.ew(kb.dve, lambda g=g, nt=nt, m=m: nc.vector.tensor_copy(out=vcA.ap[0:m, nt, g, 0:64], in_=psb2.ap[0:m, 0:64]), wr=[vcA], rd=[psb2])
            kb.barrier()
        guard = []

        qn = Buf(kb.sbuf("sb_qn_sb", [64, 16, 128], BF16)[:])
        qi = Buf(kb.sbuf("sb_qi_sb", [32, 16, 128], BF16)[:])
        gnb = Buf(kb.sbuf("sb_gn_sb", [128, 48], F32)[:])
        wib = Buf(kb.sbuf("sb_wi_sb", [128, 16], F32)[:])
        kwT = Buf(kb.sbuf("sb_kw_sb", [64, 4, 768], BF16)[:])
        vwA = Buf(kb.sbuf("sb_vw_sb", [128, 6, 4, 65], BF16)[:])
        onacc = Buf(kb.sbuf("sb_onacc", [128, 16, 64], F32)[:])
        imp = Buf(kb.sbuf("sb_imp", [128, 4, 64], F32)[:])
        imp2 = Buf(kb.sbuf("sb_imp2", [128, 4, 64], F32)[:])
        negsel = Buf(kb.sbuf("sb_negsel", [128, 4, 64], BF16)[:])
        nsx = Ring([Buf(kb.sbuf("sb_nsx%d" % i, [128, S], BF16)[:]) for i in range(1)])
        negselv = Buf(kb.sbuf("sb_negselv", [128, 4, 64], BF16)[:])
        score = Buf(kb.sbuf("sb_score", [128, S], F32)[:])
        nmask = Buf(kb.sbuf("sb_nmask", [128, S], BF16)[:])
        junk = nmask
        rt = Ring([Buf(kb.sbuf("sb_rt%d" % i, [128, 512], F32)[:]) for i in range(2)])
        m8 = Buf(kb.sbuf("sb_m8", [128, 4, 8], F32)[:])
        m8b = Buf(kb.sbuf("sb_m8b", [128, 4, 8], F32)[:])
        bis = Buf(kb.sbuf("sb_bis", [128, 8], F32)[:])
        stp = Buf(kb.sbuf("sb_stp", [128, 24], F32)[:])
        p2 = Buf(kb.sbuf("sb_p2", [128, 24], F32)[:])
        for k_ in range(24):
            kb.ew(kb.dve, lambda k_=k_: nc.vector.memset(p2.ap[:, k_:k_ + 1], 2.0 ** -k_), wr=[p2])
        kb.ew(kb.pool, lambda: nc.gpsimd.memset(vwA.ap, 1.0), wr=[vwA], extra=guard)
        qs, ws_, os1, os2 = kb.slot(), kb.slot(), kb.slot(), kb.slot()
        onst = Ring([Buf(kb.sbuf("sb_onst%d" % i, [128, 512], F32)[:]) for i in range(2)])

        def combine(o_buf, br, g, first_branch):
            sm = at.small.next()
            ov_ = o_buf.ap[:, 0:260].rearrange("p (h c) -> p h c", c=65)
            kb.ew(kb.dve, lambda: nc.vector.tensor_scalar(out=sm.ap[:, 8:12], in0=ov_[:, :, 64], scalar1=1e-30, scalar2=None, op0=ALU.max), wr=[sm], rd=[o_buf])
            kb.ew(kb.dve, lambda: nc.vector.reciprocal(out=sm.ap[:, 0:4], in_=sm.ap[:, 8:12]), wr=[sm])
            kb.ew(kb.dve, lambda: nc.vector.tensor_tensor(out=sm.ap[:, 4:8], in0=sm.ap[:, 0:4], in1=gnb.ap[:, 12 * g + br:12 * g + br + 10:3], op=ALU.mult),
                  wr=[sm], rd=[gnb])
            for h in range(4):
                if first_branch:
                    kb.ew(kb.dve, lambda h=h: nc.vector.tensor_scalar(out=onacc.ap[:, 4 * g + h, :], in0=ov_[:, h, 0:64], scalar1=sm.ap[:, 4 + h:5 + h], scalar2=None,
                                                                       op0=ALU.mult), wr=[onacc], rd=[o_buf, sm])
                else:
                    kb.ew(kb.dve, lambda h=h: nc.vector.scalar_tensor_tensor(out=onacc.ap[:, 4 * g + h, :], in0=ov_[:, h, 0:64], scalar=sm.ap[:, 4 + h:5 + h],
                                                                              in1=onacc.ap[:, 4 * g + h, :], op0=ALU.mult, op1=ALU.add), wr=[onacc], rd=[o_buf, sm])
            return sm

        for j in range(16):
            tsl = slice(j * 128, (j + 1) * 128)
            nkt = 2 * j + 2
            N = nkt * 128
            kb.load(kb.sp, qn, qn.ap, qn_d[:, :, tsl].rearrange("h d t -> d h t"), qs)
            kb.load(kb.sp, qi, qi.ap, qi_d[:, :, tsl].rearrange("h d t -> d h t"), qs)
            kb.load(kb.sp, gnb, gnb.ap, gn_d[tsl, :], qs)
            kb.load(kb.sp, wib, wib.ap, wi_d[tsl, :], qs)
            wk0 = max(0, 2 * j - 4)
            nwk = 2 * j + 2 - wk0
            woff = 6 - nwk
            for g in range(4):
                kb.load(kb.sp, kwT, kwT.ap[:, g, woff * 128:768], kwT_d[g][:, wk0 * 128:(2 * j + 2) * 128], ws_)
            for kl in range(woff, 6):
                kt_ = wk0 + kl - woff
                kb.load(kb.sp, vwA, vwA.ap[:, kl, :, 0:64], vw_d[kt_ * 128:(kt_ + 1) * 128, :].rearrange("p (g d) -> p g d", d=64), ws_)

            for g in range(4):
                o_c = at.ps_o.next()
                o_i = at.ps_o.next()
                first_i = [True]

                def sc_cmp(nt, ps, g=g):
                    kb.mm(ps, ps.ap, lhsT=kcmpT.ap[:, g, nt * 128:(nt + 1) * 128], rhs=qn.ap[:, 4 * g:4 * g + 4, :], start=True, stop=False, rd=[kcmpT, qn])
                    kb.mm(ps, ps.ap, lhsT=selc.ap[:, (j * 2 + nt) * 128:(j * 2 + nt + 1) * 128], rhs=pc.ap[:, 4 * g:4 * g + 4, :], start=False, stop=True,
                          rd=[selc, pc])

                def imp_mm(nt, h, e, _f, o_i=o_i, first_i=first_i):
                    kb.mm(o_i, o_i.ap[:, h * 64:(h + 1) * 64], lhsT=e.ap[:, h * 128:(h + 1) * 128], rhs=ovb.ap[:, nt, :], start=first_i[0], stop=False, rd=[e, ovb])
                    first_i[0] = False

                at.attend([0, 1], sc_cmp, o_c, lambda nt, h, g=g: (vcA.ap[:, nt, g, :], [vcA]), extra_o=imp_mm)
                sm = combine(o_c, 0, g, True)
                for h in range(4):
                    if h == 0:
                        kb.ew(kb.dve, lambda h=h: nc.vector.tensor_scalar(out=imp.ap[:, g, :], in0=o_i.ap[:, 0:64], scalar1=sm.ap[:, 0:1], scalar2=None, op0=ALU.mult),
                              wr=[imp], rd=[o_i, sm])
                    else:
                        kb.ew(kb.dve, lambda h=h: nc.vector.scalar_tensor_tensor(out=imp.ap[:, g, :], in0=o_i.ap[:, h * 64:(h + 1) * 64], scalar=sm.ap[:, h:h + 1],
                                                                                  in1=imp.ap[:, g, :], op0=ALU.mult, op1=ALU.add), wr=[imp], rd=[o_i, sm])
            for g in range(4):
                kb.ew(kb.dve, lambda g=g: nc.vector.tensor_tensor(out=imp.ap[:, g, :], in0=imp.ap[:, g, :], in1=visb.ap[:, j, :], op=ALU.add), wr=[imp], rd=[visb])
                kb.ew(kb.dve, lambda g=g: nc.vector.max(out=m8.ap[:, g, :], in_=imp.ap[:, g, :]), wr=[m8], rd=[imp])
                kb.ew(kb.dve, lambda g=g: nc.vector.match_replace(out=imp2.ap[:, g, :], in_to_replace=m8.ap[:, g, :], in_values=imp.ap[:, g, :], imm_value=-3e38),
                      wr=[imp2], rd=[imp, m8])
                kb.ew(kb.dve, lambda g=g: nc.vector.max(out=m8b.ap[:, g, :], in_=imp2.ap[:, g, :]), wr=[m8b], rd=[imp2])
                kb.ew(kb.dve, lambda g=g: nc.vector.tensor_scalar(out=negsel.ap[:, g, :], in0=imp.ap[:, g, :], scalar1=m8b.ap[:, g, 7:8], scalar2=-BIG,
                                                                   op0=ALU.is_lt, op1=ALU.mult), wr=[negsel], rd=[imp, m8b])

            for g in range(4):
                nsv = negsel.ap[:, g, :].rearrange("q (p w b) -> q p w b", w=2, b=2)
                nvv = negselv.ap[:, g, :].rearrange("q (p w b) -> q p w b", w=2, b=2)
                for w_ in range(2):
                    kb.ew(kb.dve, lambda nsv=nsv, nvv=nvv, w_=w_: nc.vector.tensor_scalar(out=nvv[:, :, w_, :], in0=nsv[:, :, w_, :], scalar1=swb.ap[:, 0:1], scalar2=None,
                                                                                      op0=ALU.mult), wr=[negselv], rd=[negsel, swb])
                    kb.ew(kb.dve, lambda nsv=nsv, nvv=nvv, w_=w_: nc.vector.scalar_tensor_tensor(out=nvv[:, :, w_, :], in0=nsv[:, :, 1 - w_, :], scalar=swb.ap[:, 1:2],
                                                                                             in1=nvv[:, :, w_, :], op0=ALU.mult, op1=ALU.add), wr=[negselv], rd=[negsel, swb])
            for g in range(4):
                nx = nsx.next()
                kb.ew(kb.pool, lambda g=g, nx=nx: nc.gpsimd.tensor_copy(
                    out=nx.ap[:, 0:N].rearrange("p (b k) -> p b k", k=64),
                    in_=negselv.ap[:, g, 0:N // 64].unsqueeze(2).broadcast_to([128, N // 64, 64])), wr=[nx], rd=[negselv])
                o_s = at.ps_o.next()

                def sc_sel(kt, ps, g=g, nx=nx):
                    r = kt - 2 * j
                    kb.mm(ps, ps.ap, lhsT=ksT.ap[:, g, kt * 128:(kt + 1) * 128], rhs=qn.ap[:, 4 * g:4 * g + 4, :], start=True, stop=False, rd=[ksT, qn])
                    near = r >= -2
                    kb.mm(ps, ps.ap, lhsT=nx.ap[:, kt * 128:(kt + 1) * 128], rhs=at.i4.ap, start=False, stop=not near, rd=[nx, at.i4])
                    if near:
                        kb.mm(ps, ps.ap, lhsT=at.rev.ap, rhs=bsel.ap[:, r + 2, 4 * g:4 * g + 4, :], start=False, stop=True, rd=[at.rev, bsel])

                at.attend(list(range(nkt)), sc_sel, o_s, lambda kt, h, g=g: (vsA.ap[:, kt, g, :], [vsA]))
                combine(o_s, 1, g, False)
                o_w = at.ps_o.next()

                def sc_win(kl, ps, g=g):
                    kb.mm(ps, ps.ap, lhsT=kwT.ap[:, g, kl * 128:(kl + 1) * 128], rhs=qn.ap[:, 4 * g:4 * g + 4, :], start=True, stop=False, rd=[kwT, qn])
                    kb.mm(ps, ps.ap, lhsT=at.rev.ap, rhs=bwin.ap[:, kl, 4 * g:4 * g + 4, :], start=False, stop=True, rd=[at.rev, bwin])

                at.attend(list(range(woff, 6)), sc_win, o_w, lambda kl, h, g=g: (vwA.ap[:, kl, g, :], [vwA]))
                combine(o_w, 2, g, False)
            for c4 in range(2):
                pst = at.ps_s.next()
                for cc in range(4):
                    c = c4 * 4 + cc
                    kb.mm_t(pst, pst.ap[:, cc * 128:(cc + 1) * 128], onacc.ap[:, 2 * c:2 * c + 2, :].rearrange("p h d -> p (h d)"), at.identf.ap, first=(cc == 0),
                            rd=[onacc, at.identf])
                tb = onst.next()
                kb.ew(kb.act, lambda pst=pst, tb=tb: nc.scalar.copy(out=tb.ap, in_=pst.ap), wr=[tb], rd=[pst])
                kb.store(kb.sp, onT_d[c4 * 512:(c4 + 1) * 512, tsl].rearrange("(c p) t -> p c t", p=128), tb, tb.ap.rearrange("p (c t) -> p c t", t=128), os1,
                         final=(ekb is None))

            nch = (N + 511) // 512
            for h in range(16):
                for c in range(nch):
                    w = min(512, N - c * 512)
                    ps = at.ps_s.next()
                    kb.mm(ps, ps.ap[:, 0:w], lhsT=qi.ap[:, h, :], rhs=kiT.ap[:, c * 512:c * 512 + w], start=True, stop=True, rd=[qi, kiT])
                    t = rt.next()
                    kb.ew(kb.act, lambda ps=ps, t=t, w=w: nc.scalar.activation(out=t.ap[:, 0:w], in_=ps.ap[:, 0:w], func=AF.Relu), wr=[t], rd=[ps])
                    if h == 0:
                        kb.ew(kb.dve, lambda t=t, c=c, w=w: nc.vector.tensor_scalar(out=score.ap[:, c * 512:c * 512 + w], in0=t.ap[:, 0:w], scalar1=wib.ap[:, 0:1],
                                                                                    scalar2=None, op0=ALU.mult), wr=[score], rd=[t, wib])
                    else:
                        kb.ew(kb.dve, lambda t=t, c=c, w=w, h=h: nc.vector.scalar_tensor_tensor(
                            out=score.ap[:, c * 512:c * 512 + w], in0=t.ap[:, 0:w], scalar=wib.ap[:, h:h + 1], in1=score.ap[:, c * 512:c * 512 + w],
                            op0=ALU.mult, op1=ALU.add), wr=[score], rd=[t, wib])
            kb.ew(kb.dve, lambda: nc.vector.tensor_reduce(out=bis.ap[:, 1:2], in_=score.ap[:, 0:N], axis=AX.X, op=ALU.max, apply_absolute_value=True),
                  wr=[bis], rd=[score])
            kb.ew(kb.dve, lambda: nc.vector.tensor_scalar(out=bis.ap[:, 1:2], in0=bis.ap[:, 1:2], scalar1=1.001, scalar2=1e-6, op0=ALU.mult, op1=ALU.add), wr=[bis])
            NIT = 17
            kb.ew(kb.dve, lambda: nc.vector.tensor_scalar(out=stp.ap[:, 0:NIT + 1], in0=p2.ap[:, 0:NIT + 1], scalar1=bis.ap[:, 1:2], scalar2=None, op0=ALU.mult),
                  wr=[stp], rd=[p2, bis])
            kb.ew(kb.dve, lambda: nc.vector.memset(bis.ap[:, 2:3], 0.0), wr=[bis])
            kb.ew(kb.dve, lambda: nc.vector.tensor_tensor(out=score.ap[:, N - 256:N], in0=score.ap[:, N - 256:N], in1=bidx.ap.rearrange("p r k -> p (r k)"), op=ALU.add),
                  wr=[score], rd=[bidx])
            for it in range(NIT):
                kb.ew(kb.dve, lambda: nc.vector.tensor_scalar(out=junk.ap[:, 0:N], in0=score.ap[:, 0:N], scalar1=bis.ap[:, 2:3], scalar2=None, op0=ALU.is_ge,
                                                               op1=ALU.add, accum_out=bis.ap[:, 3:4]), wr=[junk, bis], rd=[score])
                kb.ew(kb.dve, lambda: nc.vector.tensor_scalar(out=bis.ap[:, 4:5], in0=bis.ap[:, 3:4], scalar1=255.5, scalar2=0.5, op0=ALU.is_ge, op1=ALU.subtract), wr=[bis])
                kb.ew(kb.dve, lambda it=it: nc.vector.scalar_tensor_tensor(out=bis.ap[:, 2:3], in0=bis.ap[:, 4:5], scalar=stp.ap[:, it:it + 1], in1=bis.ap[:, 2:3],
                                                                         op0=ALU.mult, op1=ALU.add), wr=[bis], rd=[stp])
            kb.ew(kb.dve, lambda: nc.vector.tensor_tensor(out=bis.ap[:, 0:1], in0=bis.ap[:, 2:3], in1=stp.ap[:, NIT:NIT + 1], op=ALU.subtract), wr=[bis], rd=[stp])
            kb.ew(kb.dve, lambda: nc.vector.tensor_scalar(out=nmask.ap[:, 0:N], in0=score.ap[:, 0:N], scalar1=bis.ap[:, 0:1], scalar2=-BIG, op0=ALU.is_lt, op1=ALU.mult),
                  wr=[nmask], rd=[score, bis])
            kb.store(kb.sp, nm_d[j, :, 0:N], nmask, nmask.ap[:, 0:N], os2, final=(ekb is None))
        if ekb is None:
            kb.finish()
        else:
            kb.end_phase()
    return nc


def build_C(env=None, prefix=""):
    nc, ekb, dti, dto = _env(env, prefix)
    qd_d = dti("qd", [16, 64, NTOK], BF16)
    cnT_d = dti("cnT", [256, S], BF16)
    wuk_d = dti("wuk", [256, 1024])
    wuv_d = dti("wuv", [256, 1024])
    nm_d = dti("negmask", [16, 128, S], BF16)
    relb_d = dti("relb", [32, 32])
    ohg_d = dti("ohg_d", [33, 4 * NU])
    ident_d = dti("ident", [128, 128])
    rev_d = dti("rev", [128, 128])
    odT_d = dto("odT", [1024, NTOK])
    gsc_t = nc.dram_tensor(prefix + "gscratch_d", [32, 4 * NU], F32, kind="Internal")
    with ExitStack() as st:
        if ekb is None:
            kb = KB(nc, st)
        else:
            kb = ekb
            kb.begin_phase(prefix, st)
        at = Attn(kb, ident_d, rev_d)
        at.build_bias(relb_d, ohg_d, 4 * NU, gsc_t.ap())
        ld, ldp, qs, ms, os1 = kb.slot(), kb.slot(), kb.slot(), kb.slot(), kb.slot()
        cnT = Buf(kb.sbuf("sc_cnT", [128, 2, S], BF16)[:])
        wuk = Buf(kb.sbuf("sc_wuk", [128, 2, 1024], BF16)[:])
        wuv = Buf(kb.sbuf("sc_wuv", [128, 2, 1024], BF16)[:])
        bd = Buf(kb.sbuf("sc_bd", [128, 4, 16, 128], BF16)[:])
        kdT = Buf(kb.sbuf("sc_kdT", [64, 4, S], BF16)[:])
        vdA = Buf(kb.sbuf("sc_vdA", [128, 32, 4, 65], BF16)[:])
        qd = Buf(kb.sbuf("sc_qd", [64, 4, 128], BF16)[:])
        nmk = Ring([Buf(kb.sbuf("sc_nm%d" % i, [128, S], BF16)[:]) for i in range(2)])
        odb = Ring([Buf(kb.sbuf("sc_od%d" % i, [128, 4, 64], F32)[:]) for i in range(2)])
        odst = Ring([Buf(kb.sbuf("sc_odst%d" % i, [128, 256], F32)[:]) for i in range(2)])
        for c in range(2):
            kb.load(kb.sp, cnT, cnT.ap[:, c, :], cnT_d[c * 128:(c + 1) * 128, :], ld)
        kb.load(kb.pool, wuk, wuk.ap, wuk_d.rearrange("(k p) c -> p k c", p=128), ldp)
        kb.load(kb.pool, wuv, wuv.ap, wuv_d.rearrange("(k p) c -> p k c", p=128), ldp)
        for r in range(4):
            at.load_skew(bd, bd.ap[:, r, :, :], gsc_t, r * NU + 127, 16, 16, 1, ldp)
        kb.ew(kb.pool, lambda: nc.gpsimd.memset(vdA.ap, 1.0), wr=[vdA])
        flip = [0]
        for hg in range(4):
            for h in range(4):
                hh = hg * 4 + h
                for c8 in range(S // 512):
                    ps = at.ps_s.next()
                    for c in range(2):
                        kb.mm(ps, ps.ap[0:64, :], lhsT=wuk.ap[:, c, hh * 64:(hh + 1) * 64], rhs=cnT.ap[:, c, c8 * 512:(c8 + 1) * 512],
                              start=(c == 0), stop=(c == 1), rd=[wuk, cnT])
                    flip[0] ^= 1
                    if flip[0]:
                        kb.ew(kb.act, lambda ps=ps, h=h, c8=c8: nc.scalar.copy(out=kdT.ap[:, h, c8 * 512:(c8 + 1) * 512], in_=ps.ap[0:64, :]), wr=[kdT], rd=[ps])
                    else:
                        kb.ew(kb.dve, lambda ps=ps, h=h, c8=c8: nc.vector.tensor_copy(out=kdT.ap[:, h, c8 * 512:(c8 + 1) * 512], in_=ps.ap[0:64, :]), wr=[kdT], rd=[ps])
            for kt in range(32):
                ps = at.ps_s.next()
                for c in range(2):
                    kb.mm(ps, ps.ap[:, 0:256], lhsT=cnT.ap[:, c, kt * 128:(kt + 1) * 128], rhs=wuv.ap[:, c, hg * 256:(hg + 1) * 256],
                          start=(c == 0), stop=(c == 1), rd=[wuv, cnT])
                kb.ew(kb.dve, lambda ps=ps, kt=kt: nc.vector.tensor_copy(out=vdA.ap[:, kt, :, 0:64], in_=ps.ap[:, 0:256].rearrange("p (h d) -> p h d", d=64)),
                      wr=[vdA], rd=[ps])
            for j in range(16):
                tsl = slice(j * 128, (j + 1) * 128)
                nkt = 2 * j + 2
                N = nkt * 128
                kb.load(kb.sp, qd, qd.ap, qd_d[hg * 4:(hg + 1) * 4, :, tsl].rearrange("h d t -> d h t"), qs)
                nm = nmk.next()
                kb.load(kb.sp, nm, nm.ap[:, 0:N], nm_d[j, :, 0:N], ms)
                o_d = at.ps_o.next()

                def sc_d(kt, ps, nm=nm, j=j):
                    r = kt - 2 * j
                    for h in range(4):
                        kb.mm(ps, ps.ap[:, h * 128:(h + 1) * 128], lhsT=kdT.ap[:, h, kt * 128:(kt + 1) * 128], rhs=qd.ap[:, h, :], start=(h == 0), stop=False,
                              rd=[kdT, qd])
                    near = r >= -2
                    kb.mm(ps, ps.ap, lhsT=nm.ap[:, kt * 128:(kt + 1) * 128], rhs=at.i4.ap, start=False, stop=not near, rd=[nm, at.i4])
                    if near:
                        kb.mm(ps, ps.ap, lhsT=at.rev.ap, rhs=bd.ap[:, r + 2, hg * 4:(hg + 1) * 4, :], start=False, stop=True, rd=[at.rev, bd])

                at.attend(list(range(nkt)), sc_d, o_d, lambda kt, h: (vdA.ap[:, kt, h, :], [vdA]))
                sm = at.small.next()
                ob = odb.next()
                ov_ = o_d.ap[:, 0:260].rearrange("p (h c) -> p h c", c=65)
                kb.ew(kb.dve, lambda: nc.vector.reciprocal(out=sm.ap[:, 0:4], in_=ov_[:, :, 64]), wr=[sm], rd=[o_d])
                for h in range(4):
                    kb.ew(kb.dve, lambda h=h: nc.vector.tensor_scalar(out=ob.ap[:, h, :], in0=ov_[:, h, 0:64], scalar1=sm.ap[:, h:h + 1], scalar2=None, op0=ALU.mult),
                          wr=[ob], rd=[o_d, sm])
                pst = at.ps_s.next()
                for cc in range(2):
                    kb.mm_t(pst, pst.ap[:, cc * 128:(cc + 1) * 128], ob.ap[:, 2 * cc:2 * cc + 2, :].rearrange("p h d -> p (h d)"), at.identf.ap, first=(cc == 0),
                            rd=[ob, at.identf])
                tb = odst.next()
                kb.ew(kb.act, lambda pst=pst, tb=tb: nc.scalar.copy(out=tb.ap, in_=pst.ap[:, 0:256]), wr=[tb], rd=[pst])
                kb.store(kb.sp, odT_d[hg * 256:(hg + 1) * 256, tsl].rearrange("(c p) t -> p c t", p=128), tb, tb.ap.rearrange("p (c t) -> p c t", t=128), os1,
                         final=(ekb is None))
        if ekb is None:
            kb.finish()
        else:
            kb.end_phase()
    return nc


def build_D(last=False, env=None, prefix=""):
    nc, ekb, dti, dto = _env(env, prefix)
    x1T = dti("x1T", [D, NTOK])
    onT = dti("onT", [1024, NTOK])
    odT = dti("odT", [1024, NTOK])
    amT = dti("amT", [4096, NTOK])
    mod_d = dti("modt_in", [128, 144])
    A_d = dti("A_in", [128, 48])
    GT_d = dti("GT_in", [128, 48])
    wun = dti("wun", [1024, D])
    wud = dti("wud", [1024, D])
    wo = dti("wo", [D, D])
    wfi = dti("wfi", [D, 2 * FF])
    wfo = dti("wfo", [FF, D])
    if last:
        gfT = dti("gfT", [128, KC])
    x3T = dto("x3T", [D, NTOK])
    with ExitStack() as st:
        if ekb is None:
            kb = KB(nc, st)
        else:
            kb = ekb
            kb.begin_phase(prefix, st)
        dn = Dense(kb)
        kb.load(kb.sp, dn.mod, dn.mod.ap, mod_d, dn.cslot)
        kb.load(kb.sp, dn.A, dn.A.ap, A_d, dn.cslot)
        kb.load(kb.sp, dn.GT, dn.GT.ap, GT_d, dn.cslot)
        a_st = Ring([Buf(kb.sbuf("a_st%d" % i, [128, 512], F32)[:]) for i in range(4)])
        a_sl = [kb.slot() for _ in range(4)]
        osl = kb.slot()
        if last:
            gf = Buf(kb.sbuf("gf", [128, KC], F32)[:])
            kb.load(kb.sp, gf, gf.ap, gfT, dn.cslot)
        for tt in range(NTT):
            n0 = tt * TT
            sl_tok = slice(n0, n0 + TT)
            load_xtile(kb, dn, x1T, n0)
            for k in range(8):
                kb.load(kb.pool, dn.hid[k], dn.hid[k].ap, onT[k * 128:(k + 1) * 128, sl_tok], osl)
                kb.load(kb.pool, dn.hid[8 + k], dn.hid[8 + k].ap, odT[k * 128:(k + 1) * 128, sl_tok], osl)
            hold = {}

            def ep_up(pi, st_, psb):
                which, m = pi % 2, pi // 2
                if which == 0:
                    hold[m] = psb
                    return
                psn = hold.pop(m)
                a0, a1 = a_st.next(), a_st.next()
                i0 = (a_st.i - 2) % 4
                i1 = (a_st.i - 1) % 4
                kb.load(kb.sp, a0, a0.ap, amT[m * 128:(m + 1) * 128, sl_tok], a_sl[i0])
                kb.load(kb.sp, a1, a1.ap, amT[2048 + m * 128:2048 + (m + 1) * 128, sl_tok], a_sl[i1])
                kb.ew(kb.dve, lambda: nc.vector.tensor_tensor(out=a0.ap, in0=a0.ap, in1=psn.ap, op=ALU.mult), wr=[a0], rd=[psn])
                kb.ew(kb.dve, lambda: nc.vector.tensor_tensor(out=a1.ap, in0=a1.ap, in1=psb.ap, op=ALU.mult), wr=[a1], rd=[psb])
                kb.ew(kb.dve, lambda: nc.vector.tensor_tensor(out=dn.h[m].ap, in0=a0.ap, in1=a1.ap, op=ALU.add), wr=[dn.h[m]], rd=[a0, a1])

            for m0 in range(0, KC, 4):
                srcn = lambda kc, st_: (dn.hid[kc].ap, [dn.hid[kc]])
                srcd = lambda kc, st_: (dn.hid[8 + kc].ap, [dn.hid[8 + kc]])
                emit_gemm_fm(kb, dn, wun, 8, [((m0 + i) * 128, 128) for i in range(4)], srcn, lambda pi, st_, psb, m0=m0: ep_up(2 * (m0 + pi), st_, psb))
                emit_gemm_fm(kb, dn, wud, 8, [((m0 + i) * 128, 128) for i in range(4)], srcd, lambda pi, st_, psb, m0=m0: ep_up(2 * (m0 + pi) + 1, st_, psb))

            def ep_o(pi, st_, psb):
                xb = dn.xres[pi]
                kb.ew(kb.dve, lambda: nc.vector.scalar_tensor_tensor(out=xb.ap, in0=psb.ap, scalar=dn.GT.ap[:, 16 + pi:17 + pi], in1=xb.ap, op0=ALU.mult, op1=ALU.add),
                      wr=[xb], rd=[psb, dn.GT])

            ysrc = lambda kc, st_: (dn.h[kc].ap, [dn.h[kc]])
            emit_gemm_fm(kb, dn, wo, KC, [(m * 128, 128) for m in range(KC)], ysrc, ep_o)
            emit_ffn(kb, dn, 2, wfi, wfo, None if last else x3T, n0, final_out=(ekb is None))
            if last:
                for kc in range(KC):
                    sq = dn.sq.next()
                    kb.ew(kb.act, lambda kc=kc, sq=sq: nc.scalar.activation(out=sq.ap, in_=dn.xres[kc].ap, func=AF.Square), wr=[sq], rd=[dn.xres[kc]])
                    kb.mm(dn.ps_stat, dn.ps_stat.ap, lhsT=dn.ones.ap, rhs=sq.ap, start=(kc == 0), stop=(kc == KC - 1), rd=[dn.ones, sq])
                t = dn.tmp.next()
                kb.ew(kb.act, lambda: nc.scalar.activation(out=t.ap, in_=dn.ps_stat.ap, func=AF.Sqrt, scale=1.0 / D, bias=dn.epsb.ap[:, 0:1]), wr=[t], rd=[dn.ps_stat, dn.epsb])
                kb.ew(kb.dve, lambda: nc.vector.reciprocal(out=dn.rstd.ap, in_=t.ap), wr=[dn.rstd], rd=[t])
                for kc in range(KC):
                    xb = dn.xres[kc]
                    kb.ew(kb.dve, lambda kc=kc, xb=xb: nc.vector.scalar_tensor_tensor(out=xb.ap, in0=xb.ap, scalar=gf.ap[:, kc:kc + 1], in1=dn.rstd.ap,
                                                                                      op0=ALU.mult, op1=ALU.mult), wr=[xb], rd=[gf, dn.rstd])
                    kb.store(kb.sp, x3T[kc * 128:(kc + 1) * 128, sl_tok], xb, xb.ap, dn.oslot, final=True)
        if ekb is None:
            kb.finish()
        else:
            kb.end_phase()
    return nc


DEPTH = 2


def build_fused(depth=DEPTH):
    nc = bass.Bass("TRN2", target_bir_lowering=False)
    gi = lambda name, shape, d=F32: nc.dram_tensor(name, shape, d, kind="ExternalInput").ap()
    sc = lambda name, shape, d=F32: nc.dram_tensor(name, shape, d, kind="Internal").ap()
    I = {}
    for h in range(2):
        I["xT%d" % h] = gi("xT%d" % h, [D, NTOK])
        for nm, shp in (("ohg", [33, NPAT_N * NU + GC_W]), ("selc", [17, 16 * 2 * 128]), ("visb", [128, 16, 64]), ("bidx", [128, 2, 128]), ("ohg_d", [33, 4 * NU])):
            I["%s%d" % (nm, h)] = gi("%s%d" % (nm, h), shp)
    I["ov"] = gi("ov", [128, 2, 64])
    I["ident"] = gi("ident", [128, 128])
    I["rev"] = gi("rev", [128, 128])
    I["sw"] = gi("sw", [128, 2])
    I["cT"] = gi("cT", [128, KC])
    I["relb"] = gi("relb", [32, 32])
    I["gfT"] = gi("gfT", [128, KC])
    for nm, shp in (("w_ada", [D, 9 * D]), ("badaT", [128, 144]), ("gnT", [128, 48]), ("w_in", [D, N_IN]), ("gkvT", [128, 2]),
                    ("wk1", [2048, 256]), ("wv1", [2048, 256]), ("wk2", [256, 64]), ("wv2", [256, 64]), ("pekT", [64, 32]), ("pevT", [64, 32]),
                    ("wuk", [256, 1024]), ("wuv", [256, 1024]), ("wun", [1024, D]), ("wud", [1024, D]), ("wo", [D, D])):
        I[nm] = gi(nm, [depth] + shp)
    I["w_ffn_in"] = gi("w_ffn_in", [depth, 2, D, 2 * FF])
    I["w_ffn_out"] = gi("w_ffn_out", [depth, 2, FF, D])
    outs = [nc.dram_tensor("out0", [D, NTOK], F32, kind="ExternalOutput").ap()]
    Sx = {}
    for h in range(2):
        for nm, shp, d in (("x1T", [D, NTOK], F32), ("xn", [D, NTOK], F32), ("qn", [16, 64, NTOK], BF16), ("qi", [16, 32, NTOK], BF16),
                           ("qd", [16, 64, NTOK], BF16), ("gn", [NTOK, 48], F32), ("wi", [NTOK, 16], F32), ("am", [4096, NTOK], F32),
                           ("onT", [1024, NTOK], F32), ("odT", [1024, NTOK], F32), ("negmask", [16, 128, S], BF16)):
            Sx["%s%d" % (nm, h)] = sc("s_%s%d" % (nm, h), shp, d)
    for nm, shp, d in (("ks", [4, 64, S], BF16), ("kw", [4, 64, S], BF16), ("kc", [4, 64, S], BF16), ("vc", [4, 64, S], BF16),
                       ("vs", [S, 256], BF16), ("vw", [S, 256], BF16), ("ki", [32, S], BF16), ("cn", [256, S], BF16),
                       ("mod", [128, 144], F32), ("A", [128, 48], F32), ("GT", [128, 48], F32)):
        Sx[nm] = sc("s_" + nm, shp, d)
    with ExitStack() as st:
        kb = KB(nc, st)
        kb.setup_fused()
        for l in range(depth):
            last = (l == depth - 1)
            for h in range(2):
                T = {"xT": I["xT%d" % h] if l == 0 else Sx["xn%d" % h], "cT": I["cT"], "wada": I["w_ada"][l], "badaT": I["badaT"][l], "gnT": I["gnT"][l],
                     "wfi": I["w_ffn_in"][l, 0], "wfo": I["w_ffn_out"][l, 0], "win": I["w_in"][l], "gkvT": I["gkvT"][l], "x1T": Sx["x1T%d" % h],
                     "o_qn": Sx["qn%d" % h], "o_qi": Sx["qi%d" % h], "o_qd": Sx["qd%d" % h], "o_gn": Sx["gn%d" % h], "o_wi": Sx["wi%d" % h], "o_am": Sx["am%d" % h],
                     "o_kc": Sx["kc"], "o_vc": Sx["vc"], "o_ks": Sx["ks"], "o_kw": Sx["kw"], "o_ki": Sx["ki"], "o_cn": Sx["cn"], "o_vs": Sx["vs"], "o_vw": Sx["vw"],
                     "o_mod": Sx["mod"], "o_A": Sx["A"], "o_GT": Sx["GT"]}
                build_A(env={"nc": nc, "kb": kb, "T": T}, prefix="L%dA%d_" % (l, h), ghalf=h, load_tabs=(h == 1), kv_only=(last and h == 1))
            vh = (0,) if last else (0, 1)
            for h in vh:
                T = {"qn": Sx["qn%d" % h], "qi": Sx["qi%d" % h], "gn": Sx["gn%d" % h], "wi": Sx["wi%d" % h], "ksT": Sx["ks"], "kwT": Sx["kw"], "kcT": Sx["kc"],
                     "vcT": Sx["vc"], "vs": Sx["vs"], "vw": Sx["vw"], "kiT": Sx["ki"], "wk1": I["wk1"][l], "wv1": I["wv1"][l], "wk2": I["wk2"][l], "wv2": I["wv2"][l],
                     "pekT": I["pekT"][l], "pevT": I["pevT"][l], "relb": I["relb"], "ohg": I["ohg%d" % h], "selc": I["selc%d" % h], "ov": I["ov"],
                     "visb": I["visb%d" % h], "bidx": I["bidx%d" % h], "ident": I["ident"], "rev": I["rev"], "sw": I["sw"], "onT": Sx["onT%d" % h], "negmask": Sx["negmask%d" % h]}
                build_B(env={"nc": nc, "kb": kb, "T": T}, prefix="L%dB%d_" % (l, h))
            for h in vh:
                T = {"qd": Sx["qd%d" % h], "cnT": Sx["cn"], "wuk": I["wuk"][l], "wuv": I["wuv"][l], "negmask": Sx["negmask%d" % h], "relb": I["relb"],
                     "ohg_d": I["ohg_d%d" % h], "ident": I["ident"], "rev": I["rev"], "odT": Sx["odT%d" % h]}
                build_C(env={"nc": nc, "kb": kb, "T": T}, prefix="L%dC%d_" % (l, h))
            for h in vh:
                T = {"x1T": Sx["x1T%d" % h], "onT": Sx["onT%d" % h], "odT": Sx["odT%d" % h], "amT": Sx["am%d" % h], "modt_in": Sx["mod"], "A_in": Sx["A"],
                     "GT_in": Sx["GT"], "wun": I["wun"][l], "wud": I["wud"][l], "wo": I["wo"][l], "wfi": I["w_ffn_in"][l, 1], "wfo": I["w_ffn_out"][l, 1],
                     "gfT": I["gfT"], "x3T": outs[h] if last else Sx["xn%d" % h]}
                build_D(last, env={"nc": nc, "kb": kb, "T": T}, prefix="L%dD%d_" % (l, h))
        kb.finish()
    return nc


_PROG = []


def _core_tokens(a, b, half):
    t = a[b].reshape(32, 128, *a.shape[2:])[half::2]
    return t.reshape(NTOK, *a.shape[2:])


def kernel(x, c, w_ada, b_ada, g_norm, w_ffn_in, w_ffn_out, w_in, nsa_pe_k, nsa_pe_v, nsa_cmp_k1, nsa_cmp_k2,
           nsa_cmp_v1, nsa_cmp_v2, dsa_g_kv, dsa_w_uk, dsa_w_uv, w_up_nsa, w_up_dsa, w_out, rel_bias, g_final):
    f32 = np.float32
    A = lambda v: np.ascontiguousarray(np.asarray(v, f32))
    x = A(x)
    c = A(c)
    depth = np.asarray(w_ada).shape[0]
    if not _PROG:
        _PROG.append(build_fused(depth))
    nc = _PROG[0]
    ident = np.eye(128, dtype=f32)
    shared = {"ident": ident, "rev": np.ascontiguousarray(ident[::-1]), "relb": A(rel_bias), "gfT": A(np.asarray(g_final).reshape(16, 128).T),
              "w_ada": A(w_ada), "badaT": A(np.asarray(b_ada).reshape(depth, 144, 128).transpose(0, 2, 1)),
              "gnT": A(np.asarray(g_norm).reshape(depth, 48, 128).transpose(0, 2, 1)), "w_in": A(w_in),
              "gkvT": A(np.asarray(dsa_g_kv).reshape(depth, 2, 128).transpose(0, 2, 1)),
              "wk1": A(nsa_cmp_k1), "wv1": A(nsa_cmp_v1), "wk2": A(nsa_cmp_k2), "wv2": A(nsa_cmp_v2),
              "pekT": A(np.asarray(nsa_pe_k).transpose(0, 2, 1)), "pevT": A(np.asarray(nsa_pe_v).transpose(0, 2, 1)),
              "wuk": A(np.asarray(dsa_w_uk).reshape(depth, 256, 1024)), "wuv": A(np.asarray(dsa_w_uv).reshape(depth, 256, 1024)),
              "wun": A(w_up_nsa), "wud": A(w_up_dsa), "wo": A(w_out), "w_ffn_in": A(w_ffn_in), "w_ffn_out": A(w_ffn_out)}
    percore = []
    for s in range(2):
        pc_ = {"sw": np.ascontiguousarray(np.tile(np.array([[1.0 - s, float(s)]], f32), (128, 1)))}
        for v in range(2):
            cn_, cd_ = host_consts_nsa(v, s), host_consts_dsa(v, s)
            pc_["ohg%d" % v] = cn_["ohg"]
            pc_["selc%d" % v] = cn_["selc"]
            pc_["visb%d" % v] = cn_["visb"]
            pc_["bidx%d" % v] = cn_["bidx"]
            pc_["ohg_d%d" % v] = cd_["ohg_d"]
            pc_["ov"] = cn_["ov"]
        percore.append(pc_)
    in_maps = []
    for k in range(NCORES):
        b, s = k % 4, k // 4
        m = dict(shared)
        m.update(percore[s])
        m["xT0"] = np.ascontiguousarray(_core_tokens(x, b, s).T)
        m["xT1"] = np.ascontiguousarray(_core_tokens(x, b, 1 - s).T)
        m["cT"] = np.ascontiguousarray(c[b].reshape(16, 128).T)
        in_maps.append(m)
    res = run_bass_kernel_spmd(nc, in_maps, core_ids=list(range(NCORES))).results
    out = np.zeros(x.shape, f32)
    for b in range(4):
        for h in range(2):
            out[b].reshape(32, 128, D)[h::2] = np.ascontiguousarray(res[b + 4 * h]["out0"].T).reshape(16, 128, D)
    return out
```

```python
import numpy as np
from contextlib import ExitStack
import concourse.bass as bass
import concourse.mybir as mybir
from concourse.bass_utils import run_bass_kernel_spmd

F32 = mybir.dt.float32
BF16 = mybir.dt.bfloat16
AF = mybir.ActivationFunctionType
ALU = mybir.AluOpType
AX = mybir.AxisListType

D = 2048
KC = 16
FF = 5632
FC = 44
NTOK = 2048
TT = 512
NTT = NTOK // TT
S = 4096
N_IN = 8544
EPS = 1e-6
NCORES = 8
INDEX_SCALE = 512 ** -0.5
WST_ELEMS = 11264

C_QN, C_KC, C_VC, C_KS, C_VS, C_KW, C_VW, C_GN = 0, 1024, 1280, 1536, 1792, 2048, 2304, 2560
C_QD, C_CKV, C_QI, C_KI, C_WI, C_AM = 2608, 3632, 3888, 4400, 4432, 4448


class Eng:
    def __init__(self, name, eng, sem):
        self.name, self.eng, self.sem = name, eng, sem
        self.cnt = 0
        self.seen = {}


class Buf:
    def __init__(self, ap):
        self.ap = ap
        self.wr = None
        self.rd = {}

    def wdeps(self):
        d = list(self.rd.values())
        if self.wr is not None:
            d.append(self.wr)
        return d

    def rdeps(self):
        return [self.wr] if self.wr is not None else []

    def wrote(self, h):
        self.wr = h
        self.rd = {}

    def read(self, h):
        k = h[0]
        if k not in self.rd or self.rd[k][2] < h[2]:
            self.rd[k] = h


class KB:
    def __init__(self, nc, st):
        self.nc, self.st = nc, st
        self.pe = Eng("pe", nc.tensor, st.enter_context(nc.semaphore("s_pe")))
        self.act = Eng("act", nc.scalar, st.enter_context(nc.semaphore("s_act")))
        self.dve = Eng("dve", nc.vector, st.enter_context(nc.semaphore("s_dve")))
        self.pool = Eng("pool", nc.gpsimd, st.enter_context(nc.semaphore("s_pool")))
        self.sp = Eng("sp", nc.sync, st.enter_context(nc.semaphore("s_sp")))
        self.nslot = 0
        self.slots = []
        self.prefix = ""
        self.pools = None
        self.banks = None
        self.nbank = 0
        self.out_handles = []
        self.uid = 0

    def sbuf(self, name, shape, dt):
        return self.st.enter_context(self.nc.sbuf_tensor(self.prefix + name, shape, dt))

    def psum(self, name, shape, dt=F32):
        if self.banks is not None:
            b = self.banks[self.nbank % 8]
            self.nbank += 1
            return b
        return self.st.enter_context(self.nc.psum_tensor(self.prefix + name, shape, dt))

    def setup_fused(self, n_sp=30, n_pool=12):
        self.banks = [self.st.enter_context(self.nc.psum_tensor("bank%d" % i, [128, 512], F32)) for i in range(8)]
        self.pools = {"sp": [], "pool": []}
        for kind, n in (("sp", n_sp), ("pool", n_pool)):
            for i in range(n):
                sem = self.st.enter_context(self.nc.semaphore("s_%s%d" % (kind, i)))
                self.pools[kind].append({"key": "%s%d" % (kind, i), "sem": sem, "cnt": 0, "kind": kind})

    def begin_phase(self, prefix, stack):
        self.prefix = prefix
        self.main_st = self.st
        self.st = stack
        self.nbank = 0
        self.phase_recs = []

    def end_phase(self):
        self.barrier()
        for r in self.phase_recs:
            self.pools[r["kind"]].append(r)
        self.phase_recs = []
        self.slots = []
        self.st = self.main_st
        self.prefix = ""

    def slot(self):
        if self.pools is not None:
            return {"lazy": True}
        self.nslot += 1
        sem = self.st.enter_context(self.nc.semaphore("s_dma%d" % self.nslot))
        sl = {"key": "dma%d" % self.nslot, "sem": sem, "cnt": 0}
        self.slots.append(sl)
        return sl

    def barrier(self):
        engs = [self.pe, self.act, self.dve, self.pool, self.sp]
        for e in engs:
            for e2 in engs:
                if e2.cnt > 0:
                    self.wait(e, (e2.name, e2.sem, e2.cnt))
            for sl in self.slots:
                if sl["cnt"] > 0:
                    self.wait(e, (sl["key"], sl["sem"], sl["cnt"], sl))

    def wait(self, e, h):
        if h is None:
            return
        key, sem, cnt = h[0], h[1], h[2]
        if len(h) > 3:
            cnt = max(cnt, h[3]["cnt"])
            h[3]["waited"] = max(h[3].get("waited", 0), cnt)
        if e.seen.get(key, 0) >= cnt:
            return
        e.eng.wait_ge(sem, cnt)
        e.seen[key] = cnt

    def op(self, e, fn, deps=()):
        for d in deps:
            self.wait(e, d)
        ins = fn()
        e.cnt += 1
        ins.then_inc(e.sem, 1)
        return (e.name, e.sem, e.cnt)

    def dma(self, q, out, in_, slot, deps=()):
        if slot.get("lazy"):
            if "rec" not in slot:
                slot["rec"] = self.pools["pool" if q is self.pool else "sp"].pop()
                self.phase_recs.append(slot["rec"])
                self.slots.append(slot["rec"])
            assert slot["rec"]["kind"] == ("pool" if q is self.pool else "sp"), "slot used from both DMA queue kinds"
            slot = slot["rec"]
        for d in deps:
            self.wait(q, d)
        W = slot.get("waited", 0)
        if W > q.seen.get(slot["key"], 0):
            q.eng.wait_ge(slot["sem"], W)
            q.seen[slot["key"]] = W
        q.eng.dma_start(out=out, in_=in_).then_inc(slot["sem"], 16)
        slot["cnt"] += 16
        return (slot["key"], slot["sem"], slot["cnt"], slot)

    def mm(self, outb, out_ap, lhsT, rhs, start, stop, rd=(), extra=()):
        deps = list(extra)
        if start:
            deps += outb.wdeps()
        for b in rd:
            deps += b.rdeps()
        deps = [d for d in deps if d is not None and d[0] != "pe"]
        h = self.op(self.pe, lambda: self.nc.tensor.matmul(out_ap, lhsT=lhsT, rhs=rhs, start=start, stop=stop,
                                                           skip_group_check=True), deps)
        for b in rd:
            b.read(h)
        outb.wr = h
        if start:
            outb.rd = {}
        return h

    def mm_t(self, outb, out_ap, in_ap, ident_ap, first, rd=()):
        deps = []
        if first:
            deps += outb.wdeps()
        for b in rd:
            deps += b.rdeps()
        deps = [d for d in deps if d is not None and d[0] != "pe"]
        h = self.op(self.pe, lambda: self.nc.tensor.transpose(out_ap, in_ap, ident_ap), deps)
        for b in rd:
            b.read(h)
        outb.wr = h
        if first:
            outb.rd = {}
        return h

    def ew(self, e, fn, wr=(), rd=(), extra=()):
        deps = list(extra)
        for b in wr:
            deps += b.wdeps()
        for b in rd:
            deps += b.rdeps()
        h = self.op(e, fn, deps)
        for b in rd:
            b.read(h)
        for b in wr:
            b.wrote(h)
        return h

    def load(self, q, buf, out_ap, in_ap, slot, extra=()):
        h = self.dma(q, out_ap, in_ap, slot, list(extra) + buf.wdeps())
        buf.wrote(h)
        return h

    def store(self, q, dram_ap, buf, in_ap, slot, extra=(), final=False):
        h = self.dma(q, dram_ap, in_ap, slot, list(extra) + buf.rdeps())
        buf.read(h)
        if final:
            self.out_handles.append(h)
        return h

    def finish(self):
        last = {}
        for h in self.out_handles:
            if h[0] not in last or last[h[0]][2] < h[2]:
                last[h[0]] = h
        for h in last.values():
            self.wait(self.sp, h)


class Ring:
    def __init__(self, bufs):
        self.bufs = bufs
        self.i = 0

    def next(self):
        b = self.bufs[self.i % len(self.bufs)]
        self.i += 1
        return b


class Dense:
    def __init__(self, kb):
        self.kb = kb
        nc = kb.nc
        xres_t = kb.sbuf("xres", [128, KC, TT], F32)
        h_t = kb.sbuf("hmod", [128, KC, TT], BF16)
        hid_t = kb.sbuf("hid", [128, FC, TT], BF16)
        self.xres = [Buf(xres_t[:, k, :]) for k in range(KC)]
        self.h = [Buf(h_t[:, k, :]) for k in range(KC)]
        self.hid = [Buf(hid_t[:, k, :]) for k in range(FC)]
        self.hid_t = hid_t
        self.wst = Ring([Buf(kb.sbuf("wst%d" % i, [128, WST_ELEMS], BF16)[:]) for i in range(3)])
        self.wslots = [kb.slot() for _ in range(3)]
        self.ones = Buf(kb.sbuf("ones32", [128, 128], F32)[:])
        self.epsb = Buf(kb.sbuf("epsb", [128, 1], F32)[:])
        self.sq = Ring([Buf(kb.sbuf("sq%d" % i, [128, 512], F32)[:]) for i in range(2)])
        self.rstd = Buf(kb.sbuf("rstd", [128, TT], F32)[:])
        self.tmp = Ring([Buf(kb.sbuf("tmpf%d" % i, [128, 512], F32)[:]) for i in range(3)])
        self.ps = Ring([Buf(kb.psum("psg%d" % i, [128, 512])[:]) for i in range(6)])
        self.ps_stat = Buf(kb.psum("ps_stat", [128, 512])[:])
        self.ps_misc = Buf(kb.psum("ps_misc", [128, 512])[:])
        self.xslot = kb.slot()
        self.cslot = kb.slot()
        self.cslot2 = kb.slot()
        self.oslot = kb.slot()
        self.mod = Buf(kb.sbuf("modt", [128, 144], F32)[:])
        self.gn = Buf(kb.sbuf("gn", [128, 48], F32)[:])
        self.A = Buf(kb.sbuf("coefA", [128, 48], F32)[:])
        self.GT = Buf(kb.sbuf("coefG", [128, 48], F32)[:])
        kb.ew(kb.dve, lambda: nc.vector.memset(self.ones.ap, 1.0), wr=[self.ones])
        kb.ew(kb.dve, lambda: nc.vector.memset(self.epsb.ap, EPS), wr=[self.epsb])

    def wstage(self):
        i = self.wst.i % 3
        return self.wst.next(), self.wslots[i]


def emit_modprep(kb, dn, cT_d, wada_d, badaT_d, gnT_d, col_lo, col_hi):
    nc = kb.nc
    if True:
        cact = Buf(kb.sbuf("cact", [128, KC, 2], F32)[:])
        craw = Buf(kb.sbuf("craw", [128, KC], F32)[:])
        bada = Buf(kb.sbuf("bada", [128, 144], F32)[:])
        mod, gn = dn.mod, dn.gn
        kb.load(kb.sp, craw, craw.ap, cT_d, dn.cslot)
        kb.load(kb.sp, bada, bada.ap, badaT_d, dn.cslot)
        kb.load(kb.sp, gn, gn.ap, gnT_d, dn.cslot)
        for r in range(2):
            kb.ew(kb.act, lambda r=r: nc.scalar.activation(out=cact.ap[:, :, r], in_=craw.ap, func=AF.Silu),
                  wr=[cact], rd=[craw])
        psm = dn.ps_misc
        stg = [Buf(kb.sbuf("wada_stg%d" % i, [128, KC, 128], F32)[:]) for i in range(2)]
        sslots = [kb.slot(), kb.slot()]
        cbs = list(range(col_lo, col_hi))

        def issue(i):
            cb = cbs[i]
            b = stg[i % 2]
            src = wada_d[:, cb * 128:(cb + 1) * 128].rearrange("(k p) c -> p k c", p=128)
            q_ = kb.sp if i % 2 == 0 else kb.pool
            kb.load(q_, b, b.ap[:, 0:8, :], src[:, 0:8, :], sslots[i % 2])
            b.wr = kb.dma(q_, b.ap[:, 8:16, :], src[:, 8:16, :], sslots[i % 2])

        issue(0)
        first = True
        for i, col in enumerate(cbs):
            if i + 1 < len(cbs):
                issue(i + 1)
            b = stg[i % 2]
            for kc in range(KC):
                kb.mm(psm, psm.ap[:, 2 * col:2 * col + 2], lhsT=b.ap[:, kc, :], rhs=cact.ap[:, kc, :],
                      start=first, stop=(kc == KC - 1), rd=[b, cact])
                first = False
        pv = psm.ap[:, 2 * col_lo:2 * col_hi].rearrange("p (c two) -> p c two", two=2)[:, :, 0]
        kb.ew(kb.dve, lambda: nc.vector.tensor_tensor(out=mod.ap[:, col_lo:col_hi], in0=pv, in1=bada.ap[:, col_lo:col_hi], op=ALU.add),
              wr=[mod], rd=[psm, bada])
        A, GT = dn.A, dn.GT
        for sub in range(3):
            if col_lo <= sub * 48 + 16 and sub * 48 + 32 <= col_hi:
                kb.ew(kb.dve, lambda sub=sub: nc.vector.scalar_tensor_tensor(
                    out=A.ap[:, sub * 16:(sub + 1) * 16], in0=mod.ap[:, sub * 48 + 16:sub * 48 + 32], scalar=1.0,
                    in1=gn.ap[:, sub * 16:(sub + 1) * 16], op0=ALU.add, op1=ALU.mult), wr=[A], rd=[mod, gn])
            if col_lo <= sub * 48 + 32 and sub * 48 + 48 <= col_hi:
                kb.ew(kb.dve, lambda sub=sub: nc.vector.tensor_scalar(
                    out=GT.ap[:, sub * 16:(sub + 1) * 16], in0=mod.ap[:, sub * 48 + 32:sub * 48 + 48],
                    scalar1=(1.0 if sub == 1 else 0.5), scalar2=None, op0=ALU.mult), wr=[GT], rd=[mod])


def emit_norm_mod(kb, dn, sub):
    nc = kb.nc
    xres, h = dn.xres, dn.h
    for st_ in range(TT // 512):
        sl = slice(st_ * 512, (st_ + 1) * 512)
        for kc in range(KC):
            sq = dn.sq.next()
            kb.ew(kb.act, lambda kc=kc, sq=sq: nc.scalar.activation(out=sq.ap, in_=xres[kc].ap[:, sl], func=AF.Square),
                  wr=[sq], rd=[xres[kc]])
            kb.mm(dn.ps_stat, dn.ps_stat.ap, lhsT=dn.ones.ap, rhs=sq.ap, start=(kc == 0), stop=(kc == KC - 1),
                  rd=[dn.ones, sq])
        t = dn.tmp.next()
        kb.ew(kb.act, lambda t=t: nc.scalar.activation(out=t.ap, in_=dn.ps_stat.ap, func=AF.Sqrt, scale=1.0 / D, bias=dn.epsb.ap[:, 0:1]),
              wr=[t], rd=[dn.ps_stat, dn.epsb])
        kb.ew(kb.dve, lambda t=t, sl=sl: nc.vector.reciprocal(out=dn.rstd.ap[:, sl], in_=t.ap), wr=[dn.rstd], rd=[t])
    for kc in range(KC):
        for st_ in range(TT // 512):
            sl = slice(st_ * 512, (st_ + 1) * 512)
            t = dn.tmp.next()
            kb.ew(kb.dve, lambda kc=kc, t=t, sl=sl: nc.vector.scalar_tensor_tensor(
                out=t.ap, in0=xres[kc].ap[:, sl], scalar=dn.A.ap[:, sub * 16 + kc:sub * 16 + kc + 1], in1=dn.rstd.ap[:, sl],
                op0=ALU.mult, op1=ALU.mult), wr=[t], rd=[xres[kc], dn.A, dn.rstd])
            kb.ew(kb.act, lambda kc=kc, t=t, sl=sl: nc.scalar.activation(
                out=h[kc].ap[:, sl], in_=t.ap, func=AF.Identity, bias=dn.mod.ap[:, sub * 48 + kc:sub * 48 + kc + 1], scale=1.0),
                wr=[h[kc]], rd=[t, dn.mod])


def emit_gemm_fm(kb, dn, w_d, kchunks, pieces, src, epilogue, cwcap=None):
    nc = kb.nc
    cwmax = (WST_ELEMS // kchunks) // 64 * 64
    if cwcap:
        cwmax = min(cwmax, cwcap)
    groups, cur = [], []
    for pi, (c0, wd) in enumerate(pieces):
        if cur and (cur[-1][1] + cur[-1][2] == c0) and (c0 + wd - cur[0][1] <= cwmax):
            cur.append((pi, c0, wd))
        else:
            if cur:
                groups.append(cur)
            cur = [(pi, c0, wd)]
    if cur:
        groups.append(cur)
    loaded = {}

    def issue(gi):
        g = groups[gi]
        c0 = g[0][1]
        wtot = g[-1][1] + g[-1][2] - c0
        wb, ws = dn.wstage()
        wv = wb.ap[:, 0:kchunks * wtot].rearrange("p (k c) -> p k c", c=wtot)
        kb.load(kb.pool, wb, wv, w_d[:, c0:c0 + wtot].rearrange("(k p) c -> p k c", p=128), ws)
        loaded[gi] = (wb, wv, c0)

    issue(0)
    if len(groups) > 1:
        issue(1)
    for gi, g in enumerate(groups):
        if gi + 2 < len(groups):
            issue(gi + 2)
        wb, wv, gc0 = loaded.pop(gi)
        for (pi, c0, wd) in g:
            for st_ in range(TT // 512):
                psb = dn.ps.next()
                for kc in range(kchunks):
                    sap, sbufs = src(kc, st_)
                    kb.mm(psb, psb.ap[0:wd, :], lhsT=wv[:, kc, c0 - gc0:c0 - gc0 + wd], rhs=sap,
                          start=(kc == 0), stop=(kc == kchunks - 1), rd=[wb] + sbufs)
                epilogue(pi, st_, psb)


def emit_ffn(kb, dn, sub, win_d, wout_d, xT_out_d, n0, final_out=False):
    nc = kb.nc
    emit_norm_mod(kb, dn, sub)
    hsrc = lambda kc, st_: (dn.h[kc].ap[:, st_ * 512:(st_ + 1) * 512], [dn.h[kc]])
    pieces = []
    for J in range(FF // 256):
        pieces += [(J * 256, 128), (J * 256 + 128, 128), (FF + J * 256, 128), (FF + J * 256 + 128, 128)]
    hold = {}

    def ep_in(pi, st_, psb):
        J, r = pi // 4, pi % 4
        if r < 2:
            hold[(2 * J + r, st_)] = psb
            return
        j = 2 * J + r - 2
        psg = hold.pop((j, st_))
        t = dn.tmp.next()
        sl = slice(st_ * 512, (st_ + 1) * 512)
        kb.ew(kb.act, lambda: nc.scalar.activation(out=t.ap, in_=psg.ap, func=AF.Silu), wr=[t], rd=[psg])
        kb.ew(kb.dve, lambda: nc.vector.tensor_tensor(out=dn.hid[j].ap[:, sl], in0=t.ap, in1=psb.ap, op=ALU.mult),
              wr=[dn.hid[j]], rd=[t, psb])

    emit_gemm_fm(kb, dn, win_d, KC, pieces, hsrc, ep_in, cwcap=256)

    def ep_out(pi, st_, psb):
        sl = slice(st_ * 512, (st_ + 1) * 512)
        xb = dn.xres[pi]
        kb.ew(kb.dve, lambda: nc.vector.scalar_tensor_tensor(
            out=xb.ap[:, sl], in0=psb.ap, scalar=dn.GT.ap[:, sub * 16 + pi:sub * 16 + pi + 1], in1=xb.ap[:, sl],
            op0=ALU.mult, op1=ALU.add), wr=[xb], rd=[psb, dn.GT])
        if xT_out_d is not None:
            kb.store(kb.sp, xT_out_d[pi * 128:(pi + 1) * 128, n0 + st_ * 512:n0 + (st_ + 1) * 512], xb, xb.ap[:, sl],
                     dn.oslot, final=final_out)

    hidsrc = lambda kc, st_: (dn.hid[kc].ap[:, st_ * 512:(st_ + 1) * 512], [dn.hid[kc]])
    emit_gemm_fm(kb, dn, wout_d, FC, [(m * 128, 128) for m in range(KC)], hidsrc, ep_out)


def load_xtile(kb, dn, xT_d, n0):
    for kc in range(KC):
        kb.load(kb.sp, dn.xres[kc], dn.xres[kc].ap, xT_d[kc * 128:(kc + 1) * 128, n0:n0 + TT], dn.xslot)


def _env(env, prefix):
    if env is None:
        nc = bass.Bass("TRN2", target_bir_lowering=False)
        gi = lambda name, shape, d=F32: nc.dram_tensor(name, shape, d, kind="ExternalInput").ap()
        go = lambda name, shape, d=F32: nc.dram_tensor(name, shape, d, kind="ExternalOutput").ap()
        return nc, None, gi, go
    nc = env["nc"]
    g = lambda name, shape, d=F32: env["T"][name]
    return nc, env["kb"], g, g

class OutStage:
    def __init__(self, kb, n=3):
        self.kb = kb
        self.f = [Buf(kb.sbuf("ostgf%d" % i, [128, 512], F32)[:]) for i in range(n)]
        self.b = [Buf(kb.sbuf("ostgb%d" % i, [128, 512], BF16)[:]) for i in range(n)]
        self.fs = [kb.slot() for _ in range(n)]
        self.bs = [kb.slot() for _ in range(n)]
        self.fi = 0
        self.bi = 0
        self.flip = 0

    def next(self, dt=F32):
        if dt == F32:
            k = self.fi % len(self.f)
            self.fi += 1
            return self.f[k], self.fs[k]
        k = self.bi % len(self.b)
        self.bi += 1
        return self.b[k], self.bs[k]


def emit_gemm_tm(kb, dn, w_d, kchunks, segs, src_h, epilogue):
    nc = kb.nc
    loaded = {}

    def issue(si):
        c0, wd = segs[si]
        wb, ws = dn.wstage()
        wv = wb.ap[:, 0:kchunks * wd].rearrange("p (k c) -> p k c", c=wd)
        kb.load(kb.pool, wb, wv, w_d[:, c0:c0 + wd].rearrange("(k p) c -> p k c", p=128), ws)
        loaded[si] = (wb, wv)

    issue(0)
    for si, (c0, wd) in enumerate(segs):
        if si + 1 < len(segs):
            issue(si + 1)
        wb, wv = loaded.pop(si)
        for ts in range(TT // 128):
            psb = dn.ps.next()
            for kc in range(kchunks):
                sap, sbufs = src_h(kc, ts)
                kb.mm(psb, psb.ap[:, 0:wd], lhsT=sap, rhs=wv[:, kc, :], start=(kc == 0), stop=(kc == kchunks - 1), rd=[wb] + sbufs)
            epilogue(si, ts, psb)


def emit_proj(kb, dn, os_, win_d, gkv, o, n0, ghalf=None, kv_only=False):
    nc = kb.nc
    emit_norm_mod(kb, dn, 1)
    hsrc = lambda kc, st_: (dn.h[kc].ap[:, st_ * 512:(st_ + 1) * 512], [dn.h[kc]])
    pieces, kinds = [], []

    def add(c0, wd, kind, idx):
        if kv_only and kind in ("qn", "qd", "qi", "am"):
            return
        pieces.append((c0, wd))
        kinds.append((kind, idx))

    for c in range(8):
        add(C_QN + c * 128, 128, "qn", c)
    for nm, cc in (("kc", C_KC), ("vc", C_VC), ("ks", C_KS), ("kw", C_KW)):
        for c in range(2):
            add(cc + c * 128, 128, nm, c)
    for c in range(8):
        add(C_QD + c * 128, 128, "qd", c)
    for i in range(2):
        add(C_CKV + i * 128, 128, "ckv", i)
    for c in range(4):
        add(C_QI + c * 128, 128, "qi", c)
    add(C_KI, 128, "ki", 0)
    for i in range(32):
        add(C_AM + i * 128, 128, "am", i)
    sl_tok = slice(n0, n0 + TT)
    tt_ = n0 // TT

    def kdst(ap2):
        if ghalf is None:
            return ap2[:, sl_tok], None
        return ap2.rearrange("p (j hf t) -> p j hf t", hf=2, t=128)[:, 4 * tt_:4 * tt_ + 4, ghalf, :], "p (j t) -> p j t"

    cbuf = [Buf(dn.cbuf_t[:, i, :]) for i in range(2)]

    def evac_copy(psb, wd, dst_ap, dt, rr=None):
        ob, oslot = os_.next(dt)
        oap = ob.ap[0:wd, :]
        os_.flip ^= 1
        if os_.flip:
            kb.ew(kb.act, lambda: nc.scalar.copy(out=oap, in_=psb.ap[0:wd, :]), wr=[ob], rd=[psb])
        else:
            kb.ew(kb.dve, lambda: nc.vector.tensor_copy(out=oap, in_=psb.ap[0:wd, :]), wr=[ob], rd=[psb])
        kb.store(kb.sp, dst_ap, ob, oap if rr is None else oap.rearrange(rr, t=128), oslot, final=(ghalf is None))

    def ep(pi, st_, psb):
        kind, idx = kinds[pi]
        wd = pieces[pi][1]
        if kind in ("qn", "qd"):
            evac_copy(psb, 128, o[kind][2 * idx:2 * idx + 2, :, sl_tok].rearrange("h p n -> (h p) n"), BF16)
        elif kind == "qi":
            evac_copy(psb, 128, o[kind][4 * idx:4 * idx + 4, :, sl_tok].rearrange("h p n -> (h p) n"), BF16)
        elif kind in ("kc", "vc", "ks", "kw"):
            dst, rr = kdst(o[kind][2 * idx:2 * idx + 2].rearrange("h p n -> (h p) n"))
            evac_copy(psb, 128, dst, BF16, rr)
        elif kind == "ki":
            dst, rr = kdst(o["ki"])
            evac_copy(psb, 32, dst, BF16, rr)
        elif kind == "am":
            ob, oslot = os_.next()
            kb.ew(kb.act, lambda: nc.scalar.activation(out=ob.ap, in_=psb.ap, func=AF.Sigmoid), wr=[ob], rd=[psb])
            kb.store(kb.sp, o["am"][idx * 128:(idx + 1) * 128, sl_tok], ob, ob.ap, oslot, final=(ghalf is None))
        elif kind == "ckv":
            kb.ew(kb.dve, lambda: nc.vector.tensor_copy(out=cbuf[idx].ap, in_=psb.ap), wr=[cbuf[idx]], rd=[psb])
            if idx == 1:
                for i in range(2):
                    sq = dn.sq.next()
                    kb.ew(kb.act, lambda i=i, sq=sq: nc.scalar.activation(out=sq.ap, in_=cbuf[i].ap, func=AF.Square), wr=[sq], rd=[cbuf[i]])
                    kb.mm(dn.ps_stat, dn.ps_stat.ap, lhsT=dn.ones.ap, rhs=sq.ap, start=(i == 0), stop=(i == 1), rd=[dn.ones, sq])
                t = dn.tmp.next()
                kb.ew(kb.act, lambda: nc.scalar.activation(out=t.ap, in_=dn.ps_stat.ap, func=AF.Sqrt, scale=1.0 / 256, bias=dn.epsb.ap[:, 0:1]),
                      wr=[t], rd=[dn.ps_stat, dn.epsb])
                t2 = dn.tmp.next()
                kb.ew(kb.dve, lambda: nc.vector.reciprocal(out=t2.ap, in_=t.ap), wr=[t2], rd=[t])
                for i in range(2):
                    ob, oslot = os_.next(BF16)
                    oap = ob.ap
                    kb.ew(kb.dve, lambda i=i, oap=oap: nc.vector.scalar_tensor_tensor(
                        out=oap, in0=cbuf[i].ap, scalar=gkv.ap[:, i:i + 1], in1=t2.ap, op0=ALU.mult, op1=ALU.mult),
                        wr=[ob], rd=[cbuf[i], gkv, t2])
                    dst, rr = kdst(o["cn"][i * 128:(i + 1) * 128, :])
                    kb.store(kb.sp, dst, ob, oap if rr is None else oap.rearrange(rr, t=128), oslot, final=(ghalf is None))

    if dn.pflags & 1:
        emit_gemm_fm(kb, dn, win_d, KC, pieces, hsrc, ep, cwcap=512)

    segs = [(C_VS, 256), (C_VW, 256)] if kv_only else [(C_VS, 256), (C_VW, 256), (C_GN, 128), (C_KI, 128)]
    hsrc_t = lambda kc, ts: (dn.h[kc].ap[:, ts * 128:(ts + 1) * 128], [dn.h[kc]])

    def ep_t(si, ts, psb):
        tsl = slice(n0 + ts * 128, n0 + (ts + 1) * 128)
        gt_ = 2 * (4 * tt_ + ts) + (ghalf or 0)
        gsl = tsl if ghalf is None else slice(gt_ * 128, (gt_ + 1) * 128)
        if si == 0:
            ob, oslot = os_.next(BF16)
            oap = ob.ap[:, 0:256]
            kb.ew(kb.dve, lambda: nc.vector.tensor_copy(out=oap, in_=psb.ap[:, 0:256]), wr=[ob], rd=[psb])
            kb.store(kb.sp, o["vs"][gsl, :], ob, oap, oslot, final=(ghalf is None))
        elif si == 1:
            ob, oslot = os_.next(BF16)
            oap = ob.ap[:, 0:256]
            kb.ew(kb.dve, lambda: nc.vector.tensor_copy(out=oap, in_=psb.ap[:, 0:256]), wr=[ob], rd=[psb])
            kb.store(kb.sp, o["vw"][gsl, :], ob, oap, oslot, final=(ghalf is None))
        elif si == 2:
            ob2, oslot2 = os_.next()
            kb.ew(kb.act, lambda: nc.scalar.activation(out=ob2.ap[:, 0:48], in_=psb.ap[:, 0:48], func=AF.Sigmoid), wr=[ob2], rd=[psb])
            kb.store(kb.sp, o["gn"][tsl, :], ob2, ob2.ap[:, 0:48], oslot2, final=(ghalf is None))
        else:
            ob, oslot = os_.next()
            kb.ew(kb.dve, lambda: nc.vector.tensor_scalar(out=ob.ap[:, 0:16], in0=psb.ap[:, 32:48], scalar1=INDEX_SCALE, scalar2=None, op0=ALU.mult),
                  wr=[ob], rd=[psb])
            kb.store(kb.sp, o["wi"][tsl, :], ob, ob.ap[:, 0:16], oslot, final=(ghalf is None))

    if dn.pflags & 2:
        if dn.pflags > 3:
            keep = [i for i in range(4) if dn.pflags & (4 << i)]
            segs2 = [segs[i] for i in keep]
            emit_gemm_tm(kb, dn, win_d, KC, segs2, hsrc_t, lambda si, ts, psb: ep_t(keep[si], ts, psb))
        else:
            emit_gemm_tm(kb, dn, win_d, KC, segs, hsrc_t, ep_t)


PROJ_OUTS = [("qn", [16, 64, NTOK], BF16), ("kc", [4, 64, NTOK], BF16), ("vc", [4, 64, NTOK], BF16),
             ("ks", [4, 64, NTOK], BF16), ("kw", [4, 64, NTOK], BF16), ("qd", [16, 64, NTOK], BF16),
             ("cn", [256, NTOK], BF16), ("qi", [16, 32, NTOK], BF16), ("ki", [32, NTOK], BF16),
             ("am", [4096, NTOK], F32), ("vs", [NTOK, 256], BF16), ("vw", [NTOK, 256], BF16),
             ("gn", [NTOK, 48], F32), ("wi", [NTOK, 16], F32)]


def build_A(ntt=NTT, do_proj=True, do_ffn=True, do_mod=True, pflags=3, env=None, prefix="", ghalf=None, load_tabs=False, kv_only=False):
    nc, ekb, gi, go = _env(env, prefix)
    dt = lambda name, shape, d=F32, kind="ExternalInput": (gi if kind == "ExternalInput" else go)(name, shape, d)
    xT = dt("xT", [D, NTOK])
    cT = dt("cT", [128, KC])
    if do_mod and not load_tabs:
        wada = dt("wada", [D, 9 * D])
        badaT = dt("badaT", [128, 144])
        gnT = dt("gnT", [128, 48])
    if do_ffn:
        wfi = dt("wfi", [D, 2 * FF])
        wfo = dt("wfo", [FF, D])
    win = dt("win", [D, N_IN])
    gkvT = dt("gkvT", [128, 2])
    x1T = dt("x1T", [D, NTOK], kind="ExternalOutput")
    o = {nm: dt("o_" + nm, shp, d, kind="ExternalOutput") for nm, shp, d in PROJ_OUTS}
    tabs = {"o_mod": dt("o_mod", [128, 144], kind="ExternalOutput"), "o_A": dt("o_A", [128, 48], kind="ExternalOutput"),
            "o_GT": dt("o_GT", [128, 48], kind="ExternalOutput")}
    with ExitStack() as st:
        if ekb is None:
            kb = KB(nc, st)
        else:
            kb = ekb
            kb.begin_phase(prefix, st)
        dn = Dense(kb)
        os_ = OutStage(kb)
        dn.cbuf_t = kb.sbuf("cbuf", [128, 2, 512], F32)
        gkv = Buf(kb.sbuf("gkv", [128, 2], F32)[:])
        kb.load(kb.sp, gkv, gkv.ap, gkvT, dn.cslot)
        if load_tabs:
            kb.load(kb.sp, dn.mod, dn.mod.ap, tabs["o_mod"], dn.cslot)
            kb.load(kb.sp, dn.A, dn.A.ap, tabs["o_A"], dn.cslot)
            kb.load(kb.sp, dn.GT, dn.GT.ap, tabs["o_GT"], dn.cslot)
        elif do_mod:
            emit_modprep(kb, dn, cT, wada, badaT, gnT, 0, 144)
            for nm_, bf_ in (("o_mod", dn.mod), ("o_A", dn.A), ("o_GT", dn.GT)):
                kb.store(kb.sp, tabs[nm_], bf_, bf_.ap, dn.cslot2, final=(ekb is None))
        else:
            kb.ew(kb.dve, lambda: nc.vector.memset(dn.mod.ap, 0.0), wr=[dn.mod])
            kb.ew(kb.dve, lambda: nc.vector.memset(dn.A.ap, 1.0), wr=[dn.A])
            kb.ew(kb.dve, lambda: nc.vector.memset(dn.GT.ap, 1.0), wr=[dn.GT])
        dn.pflags = pflags
        for tt in range(ntt):
            n0 = tt * TT
            load_xtile(kb, dn, xT, n0)
            if do_ffn:
                emit_ffn(kb, dn, 0, wfi, wfo, x1T, n0, final_out=(ekb is None))
            if do_proj:
                emit_proj(kb, dn, os_, win, gkv, o, n0, ghalf=ghalf, kv_only=kv_only)
        if ekb is None:
            kb.finish()
        else:
            kb.end_phase()
    return nc


BIG = 30000.0
SCALE = 0.125
NU = 256
NPAT_N = 10
GC_W = 576


def _t5_bucket_np(n):
    n = np.maximum(n, 0)
    nf = np.maximum(n, 1).astype(np.float32)
    large = 16 + (np.log(nf / 16) / np.log(128 / 16) * 16).astype(np.int32)
    large = np.minimum(large, 31)
    return np.where(n < 16, n, large)


def _pattern_onehot(dist, valid):
    n = dist.shape[0]
    oh = np.zeros((33, n), np.float32)
    b = _t5_bucket_np(dist)
    idx = np.arange(n)
    v = valid.astype(bool)
    oh[b[v], idx[v]] += 8.0
    oh[31, idx[v]] -= 8.0
    oh[32, idx[~v]] = 1.0
    return oh


def _dtiles(v, s, r):
    J = 8
    return (2 * J + (v ^ s)) - ((2 * J + r) ^ s)


def host_consts_nsa(v, s=0):
    u = np.arange(NU) - 127
    pats = []
    for r in (-2, -1, 0, 1):
        d = _dtiles(v, s, r) * 128 + u
        pats.append(_pattern_onehot(d, d >= 0))
    for r in (-4, -3, -2, -1, 0, 1):
        d = _dtiles(v, s, r) * 128 + u
        pats.append(_pattern_onehot(d, (d >= 0) & (d < 512)))
    dc = np.arange(GC_W) - 300
    pats.append(_pattern_onehot(dc, dc >= 0))
    ohg = np.concatenate(pats, axis=1)
    half = v ^ s
    selc = np.zeros((17, 16, 2, 128), np.float32)
    for j in range(16):
        i = 2 * j + half
        for nt in range(2):
            n = nt * 128 + np.arange(128)
            m = n - 8 * i + 9
            m = np.where((m >= 16) | (n >= 255), 16, m)
            ok = m >= 0
            selc[16 - m[ok], j, nt, np.arange(128)[ok]] = 1.0
    cs = np.arange(256) * 16
    ss = np.arange(64) * 64
    ov = ((cs[:, None] < ss[None, :] + 64) & (cs[:, None] + 32 > ss[None, :])).astype(np.float32)
    ov[255] = 0
    ov = ov.reshape(2, 128, 64).transpose(1, 0, 2)
    visb = np.zeros((128, 16, 64), np.float32)
    blk = np.arange(64)
    for j in range(16):
        t = (2 * j + half) * 128 + np.arange(128)
        cur = t // 64
        forced = (blk[None, :] == 0) | (blk[None, :] == cur[:, None]) | (blk[None, :] == cur[:, None] - 1)
        vis = blk[None, :] * 64 <= t[:, None]
        visb[:, j, :] = np.where(vis, np.where(forced, 1e4, 0.0), -1e30)
    bidx = np.zeros((128, 2, 128), np.float32)
    for ri, r in enumerate((0, 1)):
        d = _dtiles(v, s, r) * 128 + np.arange(128)[:, None] - np.arange(128)[None, :]
        bidx[:, ri, :] = np.where(d >= 0, 0.0, -1e30)
    return {"ohg": ohg, "selc": selc.reshape(17, 16 * 2 * 128), "ov": ov, "visb": visb, "bidx": bidx}


def host_consts_dsa(v, s=0):
    u = np.arange(NU) - 127
    pats = []
    for r in (-2, -1, 0, 1):
        d = _dtiles(v, s, r) * 128 + u
        pats.append(_pattern_onehot(d, d >= 0))
    return {"ohg_d": np.concatenate(pats, axis=1)}


class Attn:
    def __init__(self, kb, ident_d, rev_d):
        self.kb = kb
        nc = kb.nc
        self.ident = Buf(kb.sbuf("identb", [128, 128], BF16)[:])
        self.i4 = Buf(kb.sbuf("i4", [128, 4, 128], BF16)[:])
        identf = Buf(kb.sbuf("identf", [128, 128], F32)[:])
        self.identf = identf
        isl = kb.slot()
        kb.load(kb.sp, identf, identf.ap, ident_d, isl)
        kb.ew(kb.dve, lambda: nc.vector.tensor_copy(out=self.ident.ap, in_=identf.ap), wr=[self.ident], rd=[identf])
        revf = Buf(kb.sbuf("revf", [128, 128], F32)[:])
        self.rev = Buf(kb.sbuf("revb", [128, 128], BF16)[:])
        kb.load(kb.sp, revf, revf.ap, rev_d, isl)
        kb.ew(kb.dve, lambda: nc.vector.tensor_copy(out=self.rev.ap, in_=revf.ap), wr=[self.rev], rd=[revf])
        for h in range(4):
            kb.ew(kb.dve, lambda h=h: nc.vector.tensor_copy(out=self.i4.ap[:, h, :], in_=identf.ap), wr=[self.i4], rd=[identf])
        self.ps_s = Ring([Buf(kb.psum("ps_s%d" % i, [128, 512])[:]) for i in range(3)])
        self.ps_o = Ring([Buf(kb.psum("ps_o%d" % i, [128, 512])[:]) for i in range(3)])
        self.E = Ring([Buf(kb.sbuf("Ebuf%d" % i, [128, 512], BF16)[:]) for i in range(4)])
        self.small = Ring([Buf(kb.sbuf("small%d" % i, [128, 16], F32)[:]) for i in range(6)])

    def build_bias(self, relb_d, ohg_d, width, scratch_d):
        kb, nc = self.kb, self.kb.nc
        tab = Buf(kb.sbuf("tabaug", [33, 32], F32)[:])
        sl = kb.slot()
        kb.ew(kb.dve, lambda: nc.vector.memset(tab.ap, -BIG), wr=[tab])
        kb.load(kb.sp, tab, tab.ap[0:32, :], relb_d, sl)
        CH = 512
        stg = Buf(kb.sbuf("ohg_stg", [33, CH], F32)[:])
        gout = Buf(kb.sbuf("g_out", [32, CH], F32)[:])
        sl2, sl3 = kb.slot(), kb.slot()
        psb = self.ps_s.bufs[0]
        for c0 in range(0, width, CH):
            w = min(CH, width - c0)
            kb.load(kb.sp, stg, stg.ap[:, 0:w], ohg_d[:, c0:c0 + w], sl2)
            kb.mm(psb, psb.ap[0:32, 0:w], lhsT=tab.ap, rhs=stg.ap[:, 0:w], start=True, stop=True, rd=[tab, stg])
            kb.ew(kb.dve, lambda w=w: nc.vector.tensor_copy(out=gout.ap[:, 0:w], in_=psb.ap[0:32, 0:w]), wr=[gout], rd=[psb])
            self.g_store = kb.store(kb.sp, scratch_d[:, c0:c0 + w], gout, gout.ap[:, 0:w], sl3)

    def load_skew(self, dst_buf, dst_ap, scratch_t, col0, h0, nh, pstride, slot):
        kb = self.kb
        npart = dst_ap.shape[0]
        src = bass.AP(scratch_t, h0 * scratch_t.shape[1] + col0 - pstride * (npart - 1), ((pstride, npart), (scratch_t.shape[1], nh), (1, 128)))
        return kb.load(kb.pool, dst_buf, dst_ap, src, slot, extra=[self.g_store])

    def attend(self, kt_list, score_mms, o_buf, v_of, nheads=4, vw=65, extra_o=None):
        kb, nc = self.kb, self.kb.nc
        st_ = {"first": True}

        def finish(kt, ps):
            e = self.E.next()
            kb.ew(kb.act, lambda ps=ps, e=e: nc.scalar.activation(out=e.ap[:, 0:nheads * 128], in_=ps.ap[:, 0:nheads * 128], func=AF.Exp, scale=SCALE),
                  wr=[e], rd=[ps])
            for h in range(nheads):
                vap, vb = v_of(kt, h)
                kb.mm(o_buf, o_buf.ap[:, h * vw:(h + 1) * vw], lhsT=e.ap[:, h * 128:(h + 1) * 128], rhs=vap,
                      start=st_["first"], stop=False, rd=[e] + vb)
                st_["first"] = False
                if extra_o is not None:
                    extra_o(kt, h, e, False)

        pending = []
        for kt in kt_list:
            ps = self.ps_s.next()
            score_mms(kt, ps)
            pending.append((kt, ps))
            if len(pending) == 3:
                finish(*pending.pop(0))
        while pending:
            finish(*pending.pop(0))
        return o_buf


def build_B(env=None, prefix=""):
    nc, ekb, dti, dto = _env(env, prefix)
    qn_d = dti("qn", [16, 64, NTOK], BF16)
    qi_d = dti("qi", [16, 32, NTOK], BF16)
    gn_d = dti("gn", [NTOK, 48])
    wi_d = dti("wi", [NTOK, 16])
    ksT_d = dti("ksT", [4, 64, S], BF16)
    kwT_d = dti("kwT", [4, 64, S], BF16)
    kcT_d = dti("kcT", [4, 64, S], BF16)
    vcT_d = dti("vcT", [4, 64, S], BF16)
    vs_d = dti("vs", [S, 256], BF16)
    vw_d = dti("vw", [S, 256], BF16)
    kiT_d = dti("kiT", [32, S], BF16)
    wk1_d, wv1_d = dti("wk1", [2048, 256]), dti("wv1", [2048, 256])
    wk2_d, wv2_d = dti("wk2", [256, 64]), dti("wv2", [256, 64])
    pek_d, pev_d = dti("pekT", [64, 32]), dti("pevT", [64, 32])
    relb_d = dti("relb", [32, 32])
    ohg_d = dti("ohg", [33, NPAT_N * NU + GC_W])
    selc_d = dti("selc", [17, 16 * 2 * 128])
    ov_d = dti("ov", [128, 2, 64])
    visb_d = dti("visb", [128, 16, 64])
    bidx_d = dti("bidx", [128, 2, 128])
    ident_d = dti("ident", [128, 128])
    rev_d = dti("rev", [128, 128])
    sw_d = dti("sw", [128, 2])
    onT_d = dto("onT", [1024, NTOK])
    nm_d = dto("negmask", [16, 128, S], BF16)
    gsc_t = nc.dram_tensor(prefix + "gscratch", [32, NPAT_N * NU + GC_W], F32, kind="Internal")
    with ExitStack() as st:
        if ekb is None:
            kb = KB(nc, st)
        else:
            kb = ekb
            kb.begin_phase(prefix, st)
        at = Attn(kb, ident_d, rev_d)
        ld = kb.slot()
        at.build_bias(relb_d, ohg_d, NPAT_N * NU + GC_W, gsc_t.ap())
        ksT = Buf(kb.sbuf("sb_ksT", [64, 4, S], BF16)[:])
        vsA = Buf(kb.sbuf("sb_vsA", [128, 32, 4, 65], BF16)[:])
        kiT = Buf(kb.sbuf("sb_kiT", [32, S], BF16)[:])
        kcmpT = Buf(kb.sbuf("sb_kcmpT", [64, 4, 256], BF16)[:])
        vcA = Buf(kb.sbuf("sb_vcA", [128, 2, 4, 65], BF16)[:])
        ovb = Buf(kb.sbuf("sb_ovb", [128, 2, 64], BF16)[:])
        bsel = Buf(kb.sbuf("sb_bsel", [128, 4, 16, 128], BF16)[:])
        bwin = Buf(kb.sbuf("sb_bwin", [128, 6, 16, 128], BF16)[:])
        pc = Buf(kb.sbuf("sb_pc", [17, 16, 128], BF16)[:])
        selc = Buf(kb.sbuf("sb_selc", [17, 16 * 2 * 128], BF16)[:])
        visb = Buf(kb.sbuf("sb_visb", [128, 16, 64], F32)[:])
        bidx = Buf(kb.sbuf("sb_bidx", [128, 2, 128], F32)[:])
        halfc = Buf(kb.sbuf("sb_halfc", [128, 1], F32)[:])
        swb = Buf(kb.sbuf("sb_sw", [128, 2], F32)[:])
        kb.load(kb.sp, swb, swb.ap, sw_d, ld)
        kb.ew(kb.dve, lambda: nc.vector.memset(halfc.ap, 0.5), wr=[halfc])
        for g in range(4):
            kb.load(kb.sp, ksT, ksT.ap[:, g, :], ksT_d[g], ld)
        kb.ew(kb.pool, lambda: nc.gpsimd.memset(vsA.ap, 1.0), wr=[vsA])
        kb.ew(kb.pool, lambda: nc.gpsimd.memset(vcA.ap, 0.0), wr=[vcA])
        kb.ew(kb.pool, lambda: nc.gpsimd.memset(vcA.ap[:, :, :, 64:65], 1.0), wr=[vcA])
        for kt in range(32):
            kb.load(kb.sp, vsA, vsA.ap[:, kt, :, 0:64], vs_d[kt * 128:(kt + 1) * 128, :].rearrange("p (g d) -> p g d", d=64), ld)
        kb.load(kb.sp, kiT, kiT.ap, kiT_d, ld)
        kb.load(kb.sp, visb, visb.ap, visb_d, ld)
        kb.load(kb.sp, bidx, bidx.ap, bidx_d, ld)
        ldp = kb.slot()
        kb.load(kb.pool, selc, selc.ap, selc_d, ldp)
        kb.load(kb.pool, ovb, ovb.ap, ov_d, ldp)
        for r in range(4):
            at.load_skew(bsel, bsel.ap[:, r, :, :], gsc_t, r * NU + 127, 0, 16, 1, ldp)
        for r in range(6):
            at.load_skew(bwin, bwin.ap[:, r, :, :], gsc_t, (4 + r) * NU + 127, 0, 16, 1, ldp)
        at.load_skew(pc, pc.ap, gsc_t, NPAT_N * NU + 413, 0, 16, 16, ldp)

        with ExitStack() as cst:
            w1 = Buf(cst.enter_context(nc.sbuf_tensor(kb.prefix + "sb_w1c", [64, 32, 256], BF16))[:])
            w2 = Buf(cst.enter_context(nc.sbuf_tensor(kb.prefix + "sb_w2c", [128, 2, 64], BF16))[:])
            pe2 = Buf(cst.enter_context(nc.sbuf_tensor(kb.prefix + "sb_pe2", [64, 32, 2], BF16))[:])
            pe1 = Buf(cst.enter_context(nc.sbuf_tensor(kb.prefix + "sb_pe1", [64, 32], F32))[:])
            src = Buf(cst.enter_context(nc.sbuf_tensor(kb.prefix + "sb_csrc", [64, S], BF16))[:])
            src2 = Buf(cst.enter_context(nc.sbuf_tensor(kb.prefix + "sb_csrc2", [64, S], BF16))[:])
            hb = Buf(cst.enter_context(nc.sbuf_tensor(kb.prefix + "sb_hbias", [128, 2], F32))[:])
            hid = Buf(cst.enter_context(nc.sbuf_tensor(kb.prefix + "sb_chid", [128, 2, 256], BF16))[:])
            lw = kb.slot()
            lsrc = kb.slot()
            psh = at.ps_o.bufs[0]
            psb2 = at.ps_o.bufs[1]
            for which, (w1_d, w2_d, pe_d, srcT_d) in enumerate(((wk1_d, wk2_d, pek_d, kcT_d), (wv1_d, wv2_d, pev_d, vcT_d))):
                kb.load(kb.pool, w1, w1.ap, w1_d.rearrange("(l d) c -> d l c", d=64), lw)
                kb.load(kb.pool, w2, w2.ap, w2_d.rearrange("(k p) c -> p k c", p=128), lw)
                kb.load(kb.sp, pe1, pe1.ap, pe_d, lsrc)
                for r_ in range(2):
                    kb.ew(kb.dve, lambda r_=r_: nc.vector.tensor_copy(out=pe2.ap[:, :, r_], in_=pe1.ap), wr=[pe2], rd=[pe1])
                for c in range(2):
                    for l in range(32):
                        kb.mm(psb2, psb2.ap[:, 2 * c:2 * c + 2], lhsT=w1.ap[:, l, c * 128:(c + 1) * 128], rhs=pe2.ap[:, l, :],
                              start=(c == 0 and l == 0), stop=(l == 31), rd=[w1, pe2])
                kb.ew(kb.dve, lambda: nc.vector.tensor_copy(out=hb.ap, in_=psb2.ap[:, 0:4].rearrange("p (c two) -> p c two", two=2)[:, :, 0]),
                      wr=[hb], rd=[psb2])
                for g in range(4):
                    kb.load(kb.sp, src, src.ap, srcT_d[g], lsrc)
                    s4 = srcT_d[g].rearrange("d (p w t) -> d p w t", w=2, t=128)
                    d4 = src2.ap.rearrange("d (p w t) -> d p w t", w=2, t=128)
                    for w_ in range(2):
                        kb.load(kb.sp, src2, d4[:, :, w_, :], s4[:, :, 1 - w_, :], lsrc)
                    kb.ew(kb.dve, lambda: nc.vector.tensor_scalar(out=src.ap, in0=src.ap, scalar1=swb.ap[0:64, 0:1], scalar2=None, op0=ALU.mult), wr=[src], rd=[swb])
                    kb.ew(kb.dve, lambda: nc.vector.scalar_tensor_tensor(out=src.ap, in0=src2.ap, scalar=swb.ap[0:64, 1:2], in1=src.ap, op0=ALU.mult, op1=ALU.add),
                          wr=[src], rd=[src2, swb])
                    for c in range(2):
                        for l in range(32):
                            kb.mm(psh, psh.ap[:, 0:255], lhsT=w1.ap[:, l, c * 128:(c + 1) * 128], rhs=src.ap[:, l:l + 16 * 254 + 1:16],
                                  start=(l == 0), stop=(l == 31), rd=[w1, src])
                        kb.ew(kb.act, lambda c=c: nc.scalar.activation(out=hid.ap[:, c, 0:255], in_=psh.ap[:, 0:255], func=AF.Silu, bias=hb.ap[:, c:c + 1]),
                              wr=[hid], rd=[psh, hb])
                    if which == 0:
                        for c in range(2):
                            kb.mm(psb2, psb2.ap[0:64, 0:255], lhsT=w2.ap[:, c, :], rhs=hid.ap[:, c, 0:255], start=(c == 0), stop=(c == 1), rd=[w2, hid])
                        kb.ew(kb.dve, lambda g=g: nc.vector.memset(kcmpT.ap[:, g, 255:256], 0.0), wr=[kcmpT])
                        kb.ew(kb.dve, lambda g=g: nc.vector.tensor_copy(out=kcmpT.ap[:, g, 0:255], in_=psb2.ap[0:64, 0:255]), wr=[kcmpT], rd=[psb2])
                    else:
                        for nt in range(2):
                            m = 128 if nt == 0 else 127
                            for c in range(2):
                                kb.mm(psb2, psb2.ap[0:m, 0:64], lhsT=hid.ap[:, c, nt * 128:nt * 128 + m], rhs=w2.ap[:, c, :], start=(c == 0), stop=(c == 1),
                                      rd=[w2, hid])
                            kb.ew(kb.dve, lambda g=g, nt=nt, m=m: nc.vector.tensor_copy(out=vcA.ap[0:m, nt, g, 0:64], in_=psb2.ap[0:m, 0:64]), wr=[vcA], rd=[psb2])
            kb.barrier()
        guard = []

        qn = Buf(kb.sbuf("sb_qn_sb", [64, 16, 128], BF16)[:])
        qi = Buf(kb.sbuf("sb_qi_sb", [32, 16, 128], BF16)[:])
        gnb = Buf(kb.sbuf("sb_gn_sb", [128, 48], F32)[:])
        wib = Buf(kb.sbuf("sb_wi_sb", [128, 16], F32)[:])
        kwT = Buf(kb.sbuf("sb_kw_sb", [64, 4, 768], BF16)[:])
        vwA = Buf(kb.sbuf("sb_vw_sb", [128, 6, 4, 65], BF16)[:])
        onacc = Buf(kb.sbuf("sb_onacc", [128, 16, 64], F32)[:])
        imp = Buf(kb.sbuf("sb_imp", [128, 4, 64], F32)[:])
        imp2 = Buf(kb.sbuf("sb_imp2", [128, 4, 64], F32)[:])
        negsel = Buf(kb.sbuf("sb_negsel", [128, 4, 64], BF16)[:])
        nsx = Ring([Buf(kb.sbuf("sb_nsx%d" % i, [128, S], BF16)[:]) for i in range(1)])
        negselv = Buf(kb.sbuf("sb_negselv", [128, 4, 64], BF16)[:])
        score = Buf(kb.sbuf("sb_score", [128, S], F32)[:])
        nmask = Buf(kb.sbuf("sb_nmask", [128, S], BF16)[:])
        junk = nmask
        rt = Ring([Buf(kb.sbuf("sb_rt%d" % i, [128, 512], F32)[:]) for i in range(2)])
        m8 = Buf(kb.sbuf("sb_m8", [128, 4, 8], F32)[:])
        m8b = Buf(kb.sbuf("sb_m8b", [128, 4, 8], F32)[:])
        bis = Buf(kb.sbuf("sb_bis", [128, 8], F32)[:])
        stp = Buf(kb.sbuf("sb_stp", [128, 24], F32)[:])
        p2 = Buf(kb.sbuf("sb_p2", [128, 24], F32)[:])
        for k_ in range(24):
            kb.ew(kb.dve, lambda k_=k_: nc.vector.memset(p2.ap[:, k_:k_ + 1], 2.0 ** -k_), wr=[p2])
        kb.ew(kb.pool, lambda: nc.gpsimd.memset(vwA.ap, 1.0), wr=[vwA], extra=guard)
        qs, ws_, os1, os2 = kb.slot(), kb.slot(), kb.slot(), kb.slot()
        onst = Ring([Buf(kb.sbuf("sb_onst%d" % i, [128, 512], F32)[:]) for i in range(2)])

        def combine(o_buf, br, g, first_branch):
            sm = at.small.next()
            ov_ = o_buf.ap[:, 0:260].rearrange("p (h c) -> p h c", c=65)
            kb.ew(kb.dve, lambda: nc.vector.tensor_scalar(out=sm.ap[:, 8:12], in0=ov_[:, :, 64], scalar1=1e-30, scalar2=None, op0=ALU.max), wr=[sm], rd=[o_buf])
            kb.ew(kb.dve, lambda: nc.vector.reciprocal(out=sm.ap[:, 0:4], in_=sm.ap[:, 8:12]), wr=[sm])
            kb.ew(kb.dve, lambda: nc.vector.tensor_tensor(out=sm.ap[:, 4:8], in0=sm.ap[:, 0:4], in1=gnb.ap[:, 12 * g + br:12 * g + br + 10:3], op=ALU.mult),
                  wr=[sm], rd=[gnb])
            for h in range(4):
                if first_branch:
                    kb.ew(kb.dve, lambda h=h: nc.vector.tensor_scalar(out=onacc.ap[:, 4 * g + h, :], in0=ov_[:, h, 0:64], scalar1=sm.ap[:, 4 + h:5 + h], scalar2=None,
                                                                       op0=ALU.mult), wr=[onacc], rd=[o_buf, sm])
                else:
                    kb.ew(kb.dve, lambda h=h: nc.vector.scalar_tensor_tensor(out=onacc.ap[:, 4 * g + h, :], in0=ov_[:, h, 0:64], scalar=sm.ap[:, 4 + h:5 + h],
                                                                              in1=onacc.ap[:, 4 * g + h, :], op0=ALU.mult, op1=ALU.add), wr=[onacc], rd=[o_buf, sm])
            return sm

        for j in range(16):
            tsl = slice(j * 128, (j + 1) * 128)
            nkt = 2 * j + 2
            N = nkt * 128
            kb.load(kb.sp, qn, qn.ap, qn_d[:, :, tsl].rearrange("h d t -> d h t"), qs)
            kb.load(kb.sp, qi, qi.ap, qi_d[:, :, tsl].rearrange("h d t -> d h t"), qs)
            kb.load(kb.sp, gnb, gnb.ap, gn_d[tsl, :], qs)
            kb.load(kb.sp, wib, wib.ap, wi_d[tsl, :], qs)
            wk0 = max(0, 2 * j - 4)
            nwk = 2 * j + 2 - wk0
            woff = 6 - nwk
            for g in range(4):
                kb.load(kb.sp, kwT, kwT.ap[:, g, woff * 128:768], kwT_d[g][:, wk0 * 128:(2 * j + 2) * 128], ws_)
            for kl in range(woff, 6):
                kt_ = wk0 + kl - woff
                kb.load(kb.sp, vwA, vwA.ap[:, kl, :, 0:64], vw_d[kt_ * 128:(kt_ + 1) * 128, :].rearrange("p (g d) -> p g d", d=64), ws_)

            for g in range(4):
                o_c = at.ps_o.next()
                o_i = at.ps_o.next()
                first_i = [True]

                def sc_cmp(nt, ps, g=g):
                    kb.mm(ps, ps.ap, lhsT=kcmpT.ap[:, g, nt * 128:(nt + 1) * 128], rhs=qn.ap[:, 4 * g:4 * g + 4, :], start=True, stop=False, rd=[kcmpT, qn])
                    kb.mm(ps, ps.ap, lhsT=selc.ap[:, (j * 2 + nt) * 128:(j * 2 + nt + 1) * 128], rhs=pc.ap[:, 4 * g:4 * g + 4, :], start=False, stop=True,
                          rd=[selc, pc])

                def imp_mm(nt, h, e, _f, o_i=o_i, first_i=first_i):
                    kb.mm(o_i, o_i.ap[:, h * 64:(h + 1) * 64], lhsT=e.ap[:, h * 128:(h + 1) * 128], rhs=ovb.ap[:, nt, :], start=first_i[0], stop=False, rd=[e, ovb])
                    first_i[0] = False

                at.attend([0, 1], sc_cmp, o_c, lambda nt, h, g=g: (vcA.ap[:, nt, g, :], [vcA]), extra_o=imp_mm)
                sm = combine(o_c, 0, g, True)
                for h in range(4):
                    if h == 0:
                        kb.ew(kb.dve, lambda h=h: nc.vector.tensor_scalar(out=imp.ap[:, g, :], in0=o_i.ap[:, 0:64], scalar1=sm.ap[:, 0:1], scalar2=None, op0=ALU.mult),
                              wr=[imp], rd=[o_i, sm])
                    else:
                        kb.ew(kb.dve, lambda h=h: nc.vector.scalar_tensor_tensor(out=imp.ap[:, g, :], in0=o_i.ap[:, h * 64:(h + 1) * 64], scalar=sm.ap[:, h:h + 1],
                                                                                  in1=imp.ap[:, g, :], op0=ALU.mult, op1=ALU.add), wr=[imp], rd=[o_i, sm])
            for g in range(4):
                kb.ew(kb.dve, lambda g=g: nc.vector.tensor_tensor(out=imp.ap[:, g, :], in0=imp.ap[:, g, :], in1=visb.ap[:, j, :], op=ALU.add), wr=[imp], rd=[visb])
                kb.ew(kb.dve, lambda g=g: nc.vector.max(out=m8.ap[:, g, :], in_=imp.ap[:, g, :]), wr=[m8], rd=[imp])
                kb.ew(kb.dve, lambda g=g: nc.vector.match_replace(out=imp2.ap[:, g, :], in_to_replace=m8.ap[:, g, :], in_values=imp.ap[:, g, :], imm_value=-3e38),
                      wr=[imp2], rd=[imp, m8])
                kb.ew(kb.dve, lambda g=g: nc.vector.max(out=m8b.ap[:, g, :], in_=imp2.ap[:, g, :]), wr=[m8b], rd=[imp2])
                kb.ew(kb.dve, lambda g=g: nc.vector.tensor_scalar(out=negsel.ap[:, g, :], in0=imp.ap[:, g, :], scalar1=m8b.ap[:, g, 7:8], scalar2=-BIG,
                                                                   op0=ALU.is_lt, op1=ALU.mult), wr=[negsel], rd=[imp, m8b])

            for g in range(4):
                nsv = negsel.ap[:, g, :].rearrange("q (p w b) -> q p w b", w=2, b=2)
                nvv = negselv.ap[:, g, :].rearrange("q (p w b) -> q p w b", w=2, b=2)
                for w_ in range(2):
                    kb.ew(kb.dve, lambda nsv=nsv, nvv=nvv, w_=w_: nc.vector.tensor_scalar(out=nvv[:, :, w_, :], in0=nsv[:, :, w_, :], scalar1=swb.ap[:, 0:1], scalar2=None,
                                                                                      op0=ALU.mult), wr=[negselv], rd=[negsel, swb])
                    kb.ew(kb.dve, lambda nsv=nsv, nvv=nvv, w_=w_: nc.vector.scalar_tensor_tensor(out=nvv[:, :, w_, :], in0=nsv[:, :, 1 - w_, :], scalar=swb.ap[:, 1:2],
                                                                                             in1=nvv[:, :, w_, :], op0=ALU.mult, op1=ALU.add), wr=[negselv], rd=[negsel, swb])
            for g in range(4):
                nx = nsx.next()
                kb.ew(kb.pool, lambda g=g, nx=nx: nc.gpsimd.tensor_copy(
                    out=nx.ap[:, 0:N].rearrange("p (b k) -> p b k", k=64),
                    in_=negselv.ap[:, g, 0:N // 64].unsqueeze(2).broadcast_to([128, N // 64, 64])), wr=[nx], rd=[negselv])
                o_s = at.ps_o.next()

                def sc_sel(kt, ps, g=g, nx=nx):
                    r = kt - 2 * j
                    kb.mm(ps, ps.ap, lhsT=ksT.ap[:, g, kt * 128:(kt + 1) * 128], rhs=qn.ap[:, 4 * g:4 * g + 4, :], start=True, stop=False, rd=[ksT, qn])
                    near = r >= -2
                    kb.mm(ps, ps.ap, lhsT=nx.ap[:, kt * 128:(kt + 1) * 128], rhs=at.i4.ap, start=False, stop=not near, rd=[nx, at.i4])
                    if near:
                        kb.mm(ps, ps.ap, lhsT=at.rev.ap, rhs=bsel.ap[:, r + 2, 4 * g:4 * g + 4, :], start=False, stop=True, rd=[at.rev, bsel])

                at.attend(list(range(nkt)), sc_sel, o_s, lambda kt, h, g=g: (vsA.ap[:, kt, g, :], [vsA]))
                combine(o_s, 1, g, False)
                o_w = at.ps_o.next()

                def sc_win(kl, ps, g=g):
                    kb.mm(ps, ps.ap, lhsT=kwT.ap[:, g, kl * 128:(kl + 1) * 128], rhs=qn.ap[:, 4 * g:4 * g + 4, :], start=True, stop=False, rd=[kwT, qn])
                    kb.mm(ps, ps.ap, lhsT=at.rev.ap, rhs=bwin.ap[:, kl, 4 * g:4 * g + 4, :], start=False, stop=True, rd=[at.rev, bwin])

                at.attend(list(range(woff, 6)), sc_win, o_w, lambda kl, h, g=g: (vwA.ap[:, kl, g, :], [vwA]))
                combine(o_w, 2, g, False)
            for c4 in range(2):
                pst = at.ps_s.next()
                for cc in range(4):
                    c = c4 * 4 + cc
                    kb.mm_t(pst, pst.ap[:, cc * 128:(cc + 1) * 128], onacc.ap[:, 2 * c:2 * c + 2, :].rearrange("p h d -> p (h d)"), at.identf.ap, first=(cc == 0),
                            rd=[onacc, at.identf])
                tb = onst.next()
                kb.ew(kb.act, lambda pst=pst, tb=tb: nc.scalar.copy(out=tb.ap, in_=pst.ap), wr=[tb], rd=[pst])
                kb.store(kb.sp, onT_d[c4 * 512:(c4 + 1) * 512, tsl].rearrange("(c p) t -> p c t", p=128), tb, tb.ap.rearrange("p (c t) -> p c t", t=128), os1,
                         final=(ekb is None))

            nch = (N + 511) // 512
            for h in range(16):
                for c in range(nch):
                    w = min(512, N - c * 512)
                    ps = at.ps_s.next()
                    kb.mm(ps, ps.ap[:, 0:w], lhsT=qi.ap[:, h, :], rhs=kiT.ap[:, c * 512:c * 512 + w], start=True, stop=True, rd=[qi, kiT])
                    t = rt.next()
                    kb.ew(kb.act, lambda ps=ps, t=t, w=w: nc.scalar.activation(out=t.ap[:, 0:w], in_=ps.ap[:, 0:w], func=AF.Relu), wr=[t], rd=[ps])
                    if h == 0:
                        kb.ew(kb.dve, lambda t=t, c=c, w=w: nc.vector.tensor_scalar(out=score.ap[:, c * 512:c * 512 + w], in0=t.ap[:, 0:w], scalar1=wib.ap[:, 0:1],
                                                                                    scalar2=None, op0=ALU.mult), wr=[score], rd=[t, wib])
                    else:
                        kb.ew(kb.dve, lambda t=t, c=c, w=w, h=h: nc.vector.scalar_tensor_tensor(
                            out=score.ap[:, c * 512:c * 512 + w], in0=t.ap[:, 0:w], scalar=wib.ap[:, h:h + 1], in1=score.ap[:, c * 512:c * 512 + w],
                            op0=ALU.mult, op1=ALU.add), wr=[score], rd=[t, wib])
            kb.ew(kb.dve, lambda: nc.vector.tensor_reduce(out=bis.ap[:, 1:2], in_=score.ap[:, 0:N], axis=AX.X, op=ALU.max, apply_absolute_value=True),
                  wr=[bis], rd=[score])
            kb.ew(kb.dve, lambda: nc.vector.tensor_scalar(out=bis.ap[:, 1:2], in0=bis.ap[:, 1:2], scalar1=1.001, scalar2=1e-6, op0=ALU.mult, op1=ALU.add), wr=[bis])
            NIT = 17
            kb.ew(kb.dve, lambda: nc.vector.tensor_scalar(out=stp.ap[:, 0:NIT + 1], in0=p2.ap[:, 0:NIT + 1], scalar1=bis.ap[:, 1:2], scalar2=None, op0=ALU.mult),
                  wr=[stp], rd=[p2, bis])
            kb.ew(kb.dve, lambda: nc.vector.memset(bis.ap[:, 2:3], 0.0), wr=[bis])
            kb.ew(kb.dve, lambda: nc.vector.tensor_tensor(out=score.ap[:, N - 256:N], in0=score.ap[:, N - 256:N], in1=bidx.ap.rearrange("p r k -> p (r k)"), op=ALU.add),
                  wr=[score], rd=[bidx])
            for it in range(NIT):
                kb.ew(kb.dve, lambda: nc.vector.tensor_scalar(out=junk.ap[:, 0:N], in0=score.ap[:, 0:N], scalar1=bis.ap[:, 2:3], scalar2=None, op0=ALU.is_ge,
                                                               op1=ALU.add, accum_out=bis.ap[:, 3:4]), wr=[junk, bis], rd=[score])
                kb.ew(kb.dve, lambda: nc.vector.tensor_scalar(out=bis.ap[:, 4:5], in0=bis.ap[:, 3:4], scalar1=255.5, scalar2=0.5, op0=ALU.is_ge, op1=ALU.subtract), wr=[bis])
                kb.ew(kb.dve, lambda it=it: nc.vector.scalar_tensor_tensor(out=bis.ap[:, 2:3], in0=bis.ap[:, 4:5], scalar=stp.ap[:, it:it + 1], in1=bis.ap[:, 2:3],
                                                                         op0=ALU.mult, op1=ALU.add), wr=[bis], rd=[stp])
            kb.ew(kb.dve, lambda: nc.vector.tensor_tensor(out=bis.ap[:, 0:1], in0=bis.ap[:, 2:3], in1=stp.ap[:, NIT:NIT + 1], op=ALU.subtract), wr=[bis], rd=[stp])
            kb.ew(kb.dve, lambda: nc.vector.tensor_scalar(out=nmask.ap[:, 0:N], in0=score.ap[:, 0:N], scalar1=bis.ap[:, 0:1], scalar2=-BIG, op0=ALU.is_lt, op1=ALU.mult),
                  wr=[nmask], rd=[score, bis])
            kb.store(kb.sp, nm_d[j, :, 0:N], nmask, nmask.ap[:, 0:N], os2, final=(ekb is None))
        if ekb is None:
            kb.finish()
        else:
            kb.end_phase()
    return nc


def build_C(env=None, prefix=""):
    nc, ekb, dti, dto = _env(env, prefix)
    qd_d = dti("qd", [16, 64, NTOK], BF16)
    cnT_d = dti("cnT", [256, S], BF16)
    wuk_d = dti("wuk", [256, 1024])
    wuv_d = dti("wuv", [256, 1024])
    nm_d = dti("negmask", [16, 128, S], BF16)
    relb_d = dti("relb", [32, 32])
    ohg_d = dti("ohg_d", [33, 4 * NU])
    ident_d = dti("ident", [128, 128])
    rev_d = dti("rev", [128, 128])
    odT_d = dto("odT", [1024, NTOK])
    gsc_t = nc.dram_tensor(prefix + "gscratch_d", [32, 4 * NU], F32, kind="Internal")
    with ExitStack() as st:
        if ekb is None:
            kb = KB(nc, st)
        else:
            kb = ekb
            kb.begin_phase(prefix, st)
        at = Attn(kb, ident_d, rev_d)
        at.build_bias(relb_d, ohg_d, 4 * NU, gsc_t.ap())
        ld, ldp, qs, ms, os1 = kb.slot(), kb.slot(), kb.slot(), kb.slot(), kb.slot()
        cnT = Buf(kb.sbuf("sc_cnT", [128, 2, S], BF16)[:])
        wuk = Buf(kb.sbuf("sc_wuk", [128, 2, 1024], BF16)[:])
        wuv = Buf(kb.sbuf("sc_wuv", [128, 2, 1024], BF16)[:])
        bd = Buf(kb.sbuf("sc_bd", [128, 4, 16, 128], BF16)[:])
        kdT = Buf(kb.sbuf("sc_kdT", [64, 4, S], BF16)[:])
        vdA = Buf(kb.sbuf("sc_vdA", [128, 32, 4, 65], BF16)[:])
        qd = Buf(kb.sbuf("sc_qd", [64, 4, 128], BF16)[:])
        nmk = Ring([Buf(kb.sbuf("sc_nm%d" % i, [128, S], BF16)[:]) for i in range(2)])
        odb = Ring([Buf(kb.sbuf("sc_od%d" % i, [128, 4, 64], F32)[:]) for i in range(2)])
        odst = Ring([Buf(kb.sbuf("sc_odst%d" % i, [128, 256], F32)[:]) for i in range(2)])
        for c in range(2):
            kb.load(kb.sp, cnT, cnT.ap[:, c, :], cnT_d[c * 128:(c + 1) * 128, :], ld)
        kb.load(kb.pool, wuk, wuk.ap, wuk_d.rearrange("(k p) c -> p k c", p=128), ldp)
        kb.load(kb.pool, wuv, wuv.ap, wuv_d.rearrange("(k p) c -> p k c", p=128), ldp)
        for r in range(4):
            at.load_skew(bd, bd.ap[:, r, :, :], gsc_t, r * NU + 127, 16, 16, 1, ldp)
        kb.ew(kb.pool, lambda: nc.gpsimd.memset(vdA.ap, 1.0), wr=[vdA])
        flip = [0]
        for hg in range(4):
            for h in range(4):
                hh = hg * 4 + h
                for c8 in range(S // 512):
                    ps = at.ps_s.next()
                    for c in range(2):
                        kb.mm(ps, ps.ap[0:64, :], lhsT=wuk.ap[:, c, hh * 64:(hh + 1) * 64], rhs=cnT.ap[:, c, c8 * 512:(c8 + 1) * 512],
                              start=(c == 0), stop=(c == 1), rd=[wuk, cnT])
                    flip[0] ^= 1
                    if flip[0]:
                        kb.ew(kb.act, lambda ps=ps, h=h, c8=c8: nc.scalar.copy(out=kdT.ap[:, h, c8 * 512:(c8 + 1) * 512], in_=ps.ap[0:64, :]), wr=[kdT], rd=[ps])
                    else:
                        kb.ew(kb.dve, lambda ps=ps, h=h, c8=c8: nc.vector.tensor_copy(out=kdT.ap[:, h, c8 * 512:(c8 + 1) * 512], in_=ps.ap[0:64, :]), wr=[kdT], rd=[ps])
            for kt in range(32):
                ps = at.ps_s.next()
                for c in range(2):
                    kb.mm(ps, ps.ap[:, 0:256], lhsT=cnT.ap[:, c, kt * 128:(kt + 1) * 128], rhs=wuv.ap[:, c, hg * 256:(hg + 1) * 256],
                          start=(c == 0), stop=(c == 1), rd=[wuv, cnT])
                kb.ew(kb.dve, lambda ps=ps, kt=kt: nc.vector.tensor_copy(out=vdA.ap[:, kt, :, 0:64], in_=ps.ap[:, 0:256].rearrange("p (h d) -> p h d", d=64)),
                      wr=[vdA], rd=[ps])
            for j in range(16):
                tsl = slice(j * 128, (j + 1) * 128)
                nkt = 2 * j + 2
                N = nkt * 128
                kb.load(kb.sp, qd, qd.ap, qd_d[hg * 4:(hg + 1) * 4, :, tsl].rearrange("h d t -> d h t"), qs)
                nm = nmk.next()
                kb.load(kb.sp, nm, nm.ap[:, 0:N], nm_d[j, :, 0:N], ms)
                o_d = at.ps_o.next()

                def sc_d(kt, ps, nm=nm, j=j):
                    r = kt - 2 * j
                    for h in range(4):
                        kb.mm(ps, ps.ap[:, h * 128:(h + 1) * 128], lhsT=kdT.ap[:, h, kt * 128:(kt + 1) * 128], rhs=qd.ap[:, h, :], start=(h == 0), stop=False,
                              rd=[kdT, qd])
                    near = r >= -2
                    kb.mm(ps, ps.ap, lhsT=nm.ap[:, kt * 128:(kt + 1) * 128], rhs=at.i4.ap, start=False, stop=not near, rd=[nm, at.i4])
                    if near:
                        kb.mm(ps, ps.ap, lhsT=at.rev.ap, rhs=bd.ap[:, r + 2, hg * 4:(hg + 1) * 4, :], start=False, stop=True, rd=[at.rev, bd])

                at.attend(list(range(nkt)), sc_d, o_d, lambda kt, h: (vdA.ap[:, kt, h, :], [vdA]))
                sm = at.small.next()
                ob = odb.next()
                ov_ = o_d.ap[:, 0:260].rearrange("p (h c) -> p h c", c=65)
                kb.ew(kb.dve, lambda: nc.vector.reciprocal(out=sm.ap[:, 0:4], in_=ov_[:, :, 64]), wr=[sm], rd=[o_d])
                for h in range(4):
                    kb.ew(kb.dve, lambda h=h: nc.vector.tensor_scalar(out=ob.ap[:, h, :], in0=ov_[:, h, 0:64], scalar1=sm.ap[:, h:h + 1], scalar2=None, op0=ALU.mult),
                          wr=[ob], rd=[o_d, sm])
                pst = at.ps_s.next()
                for cc in range(2):
                    kb.mm_t(pst, pst.ap[:, cc * 128:(cc + 1) * 128], ob.ap[:, 2 * cc:2 * cc + 2, :].rearrange("p h d -> p (h d)"), at.identf.ap, first=(cc == 0),
                            rd=[ob, at.identf])
                tb = odst.next()
                kb.ew(kb.act, lambda pst=pst, tb=tb: nc.scalar.copy(out=tb.ap, in_=pst.ap[:, 0:256]), wr=[tb], rd=[pst])
                kb.store(kb.sp, odT_d[hg * 256:(hg + 1) * 256, tsl].rearrange("(c p) t -> p c t", p=128), tb, tb.ap.rearrange("p (c t) -> p c t", t=128), os1,
                         final=(ekb is None))
        if ekb is None:
            kb.finish()
        else:
            kb.end_phase()
    return nc


def build_D(last=False, env=None, prefix=""):
    nc, ekb, dti, dto = _env(env, prefix)
    x1T = dti("x1T", [D, NTOK])
    onT = dti("onT", [1024, NTOK])
    odT = dti("odT", [1024, NTOK])
    amT = dti("amT", [4096, NTOK])
    mod_d = dti("modt_in", [128, 144])
    A_d = dti("A_in", [128, 48])
    GT_d = dti("GT_in", [128, 48])
    wun = dti("wun", [1024, D])
    wud = dti("wud", [1024, D])
    wo = dti("wo", [D, D])
    wfi = dti("wfi", [D, 2 * FF])
    wfo = dti("wfo", [FF, D])
    if last:
        gfT = dti("gfT", [128, KC])
    x3T = dto("x3T", [D, NTOK])
    with ExitStack() as st:
        if ekb is None:
            kb = KB(nc, st)
        else:
            kb = ekb
            kb.begin_phase(prefix, st)
        dn = Dense(kb)
        kb.load(kb.sp, dn.mod, dn.mod.ap, mod_d, dn.cslot)
        kb.load(kb.sp, dn.A, dn.A.ap, A_d, dn.cslot)
        kb.load(kb.sp, dn.GT, dn.GT.ap, GT_d, dn.cslot)
        a_st = Ring([Buf(kb.sbuf("a_st%d" % i, [128, 512], F32)[:]) for i in range(4)])
        a_sl = [kb.slot() for _ in range(4)]
        osl = kb.slot()
        if last:
            gf = Buf(kb.sbuf("gf", [128, KC], F32)[:])
            kb.load(kb.sp, gf, gf.ap, gfT, dn.cslot)
        for tt in range(NTT):
            n0 = tt * TT
            sl_tok = slice(n0, n0 + TT)
            load_xtile(kb, dn, x1T, n0)
            for k in range(8):
                kb.load(kb.pool, dn.hid[k], dn.hid[k].ap, onT[k * 128:(k + 1) * 128, sl_tok], osl)
                kb.load(kb.pool, dn.hid[8 + k], dn.hid[8 + k].ap, odT[k * 128:(k + 1) * 128, sl_tok], osl)
            hold = {}

            def ep_up(pi, st_, psb):
                which, m = pi % 2, pi // 2
                if which == 0:
                    hold[m] = psb
                    return
                psn = hold.pop(m)
                a0, a1 = a_st.next(), a_st.next()
                i0 = (a_st.i - 2) % 4
                i1 = (a_st.i - 1) % 4
                kb.load(kb.sp, a0, a0.ap, amT[m * 128:(m + 1) * 128, sl_tok], a_sl[i0])
                kb.load(kb.sp, a1, a1.ap, amT[2048 + m * 128:2048 + (m + 1) * 128, sl_tok], a_sl[i1])
                kb.ew(kb.dve, lambda: nc.vector.tensor_tensor(out=a0.ap, in0=a0.ap, in1=psn.ap, op=ALU.mult), wr=[a0], rd=[psn])
                kb.ew(kb.dve, lambda: nc.vector.tensor_tensor(out=a1.ap, in0=a1.ap, in1=psb.ap, op=ALU.mult), wr=[a1], rd=[psb])
                kb.ew(kb.dve, lambda: nc.vector.tensor_tensor(out=dn.h[m].ap, in0=a0.ap, in1=a1.ap, op=ALU.add), wr=[dn.h[m]], rd=[a0, a1])

            for m0 in range(0, KC, 4):
                srcn = lambda kc, st_: (dn.hid[kc].ap, [dn.hid[kc]])
                srcd = lambda kc, st_: (dn.hid[8 + kc].ap, [dn.hid[8 + kc]])
                emit_gemm_fm(kb, dn, wun, 8, [((m0 + i) * 128, 128) for i in range(4)], srcn, lambda pi, st_, psb, m0=m0: ep_up(2 * (m0 + pi), st_, psb))
                emit_gemm_fm(kb, dn, wud, 8, [((m0 + i) * 128, 128) for i in range(4)], srcd, lambda pi, st_, psb, m0=m0: ep_up(2 * (m0 + pi) + 1, st_, psb))

            def ep_o(pi, st_, psb):
                xb = dn.xres[pi]
                kb.ew(kb.dve, lambda: nc.vector.scalar_tensor_tensor(out=xb.ap, in0=psb.ap, scalar=dn.GT.ap[:, 16 + pi:17 + pi], in1=xb.ap, op0=ALU.mult, op1=ALU.add),
                      wr=[xb], rd=[psb, dn.GT])

            ysrc = lambda kc, st_: (dn.h[kc].ap, [dn.h[kc]])
            emit_gemm_fm(kb, dn, wo, KC, [(m * 128, 128) for m in range(KC)], ysrc, ep_o)
            emit_ffn(kb, dn, 2, wfi, wfo, None if last else x3T, n0, final_out=(ekb is None))
            if last:
                for kc in range(KC):
                    sq = dn.sq.next()
                    kb.ew(kb.act, lambda kc=kc, sq=sq: nc.scalar.activation(out=sq.ap, in_=dn.xres[kc].ap, func=AF.Square), wr=[sq], rd=[dn.xres[kc]])
                    kb.mm(dn.ps_stat, dn.ps_stat.ap, lhsT=dn.ones.ap, rhs=sq.ap, start=(kc == 0), stop=(kc == KC - 1), rd=[dn.ones, sq])
                t = dn.tmp.next()
                kb.ew(kb.act, lambda: nc.scalar.activation(out=t.ap, in_=dn.ps_stat.ap, func=AF.Sqrt, scale=1.0 / D, bias=dn.epsb.ap[:, 0:1]), wr=[t], rd=[dn.ps_stat, dn.epsb])
                kb.ew(kb.dve, lambda: nc.vector.reciprocal(out=dn.rstd.ap, in_=t.ap), wr=[dn.rstd], rd=[t])
                for kc in range(KC):
                    xb = dn.xres[kc]
                    kb.ew(kb.dve, lambda kc=kc, xb=xb: nc.vector.scalar_tensor_tensor(out=xb.ap, in0=xb.ap, scalar=gf.ap[:, kc:kc + 1], in1=dn.rstd.ap,
                                                                                      op0=ALU.mult, op1=ALU.mult), wr=[xb], rd=[gf, dn.rstd])
                    kb.store(kb.sp, x3T[kc * 128:(kc + 1) * 128, sl_tok], xb, xb.ap, dn.oslot, final=True)
        if ekb is None:
            kb.finish()
        else:
            kb.end_phase()
    return nc


DEPTH = 2


def build_fused(depth=DEPTH):
    nc = bass.Bass("TRN2", target_bir_lowering=False)
    gi = lambda name, shape, d=F32: nc.dram_tensor(name, shape, d, kind="ExternalInput").ap()
    sc = lambda name, shape, d=F32: nc.dram_tensor(name, shape, d, kind="Internal").ap()
    I = {}
    for h in range(2):
        I["xT%d" % h] = gi("xT%d" % h, [D, NTOK])
        for nm, shp in (("ohg", [33, NPAT_N * NU + GC_W]), ("selc", [17, 16 * 2 * 128]), ("visb", [128, 16, 64]), ("bidx", [128, 2, 128]), ("ohg_d", [33, 4 * NU])):
            I["%s%d" % (nm, h)] = gi("%s%d" % (nm, h), shp)
    I["ov"] = gi("ov", [128, 2, 64])
    I["ident"] = gi("ident", [128, 128])
    I["rev"] = gi("rev", [128, 128])
    I["sw"] = gi("sw", [128, 2])
    I["cT"] = gi("cT", [128, KC])
    I["relb"] = gi("relb", [32, 32])
    I["gfT"] = gi("gfT", [128, KC])
    for nm, shp in (("w_ada", [D, 9 * D]), ("badaT", [128, 144]), ("gnT", [128, 48]), ("w_in", [D, N_IN]), ("gkvT", [128, 2]),
                    ("wk1", [2048, 256]), ("wv1", [2048, 256]), ("wk2", [256, 64]), ("wv2", [256, 64]), ("pekT", [64, 32]), ("pevT", [64, 32]),
                    ("wuk", [256, 1024]), ("wuv", [256, 1024]), ("wun", [1024, D]), ("wud", [1024, D]), ("wo", [D, D])):
        I[nm] = gi(nm, [depth] + shp)
    I["w_ffn_in"] = gi("w_ffn_in", [depth, 2, D, 2 * FF])
    I["w_ffn_out"] = gi("w_ffn_out", [depth, 2, FF, D])
    outs = [nc.dram_tensor("out0", [D, NTOK], F32, kind="ExternalOutput").ap()]
    Sx = {}
    for h in range(2):
        for nm, shp, d in (("x1T", [D, NTOK], F32), ("xn", [D, NTOK], F32), ("qn", [16, 64, NTOK], BF16), ("qi", [16, 32, NTOK], BF16),
                           ("qd", [16, 64, NTOK], BF16), ("gn", [NTOK, 48], F32), ("wi", [NTOK, 16], F32), ("am", [4096, NTOK], F32),
                           ("onT", [1024, NTOK], F32), ("odT", [1024, NTOK], F32), ("negmask", [16, 128, S], BF16)):
            Sx["%s%d" % (nm, h)] = sc("s_%s%d" % (nm, h), shp, d)
    for nm, shp, d in (("ks", [4, 64, S], BF16), ("kw", [4, 64, S], BF16), ("kc", [4, 64, S], BF16), ("vc", [4, 64, S], BF16),
                       ("vs", [S, 256], BF16), ("vw", [S, 256], BF16), ("ki", [32, S], BF16), ("cn", [256, S], BF16),
                       ("mod", [128, 144], F32), ("A", [128, 48], F32), ("GT", [128, 48], F32)):
        Sx[nm] = sc("s_" + nm, shp, d)
    with ExitStack() as st:
        kb = KB(nc, st)
        kb.setup_fused()
        for l in range(depth):
            last = (l == depth - 1)
            for h in range(2):
                T = {"xT": I["xT%d" % h] if l == 0 else Sx["xn%d" % h], "cT": I["cT"], "wada": I["w_ada"][l], "badaT": I["badaT"][l], "gnT": I["gnT"][l],
                     "wfi": I["w_ffn_in"][l, 0], "wfo": I["w_ffn_out"][l, 0], "win": I["w_in"][l], "gkvT": I["gkvT"][l], "x1T": Sx["x1T%d" % h],
                     "o_qn": Sx["qn%d" % h], "o_qi": Sx["qi%d" % h], "o_qd": Sx["qd%d" % h], "o_gn": Sx["gn%d" % h], "o_wi": Sx["wi%d" % h], "o_am": Sx["am%d" % h],
                     "o_kc": Sx["kc"], "o_vc": Sx["vc"], "o_ks": Sx["ks"], "o_kw": Sx["kw"], "o_ki": Sx["ki"], "o_cn": Sx["cn"], "o_vs": Sx["vs"], "o_vw": Sx["vw"],
                     "o_mod": Sx["mod"], "o_A": Sx["A"], "o_GT": Sx["GT"]}
                build_A(env={"nc": nc, "kb": kb, "T": T}, prefix="L%dA%d_" % (l, h), ghalf=h, load_tabs=(h == 1), kv_only=(last and h == 1))
            vh = (0,) if last else (0, 1)
            for h in vh:
                T = {"qn": Sx["qn%d" % h], "qi": Sx["qi%d" % h], "gn": Sx["gn%d" % h], "wi": Sx["wi%d" % h], "ksT": Sx["ks"], "kwT": Sx["kw"], "kcT": Sx["kc"],
                     "vcT": Sx["vc"], "vs": Sx["vs"], "vw": Sx["vw"], "kiT": Sx["ki"], "wk1": I["wk1"][l], "wv1": I["wv1"][l], "wk2": I["wk2"][l], "wv2": I["wv2"][l],
                     "pekT": I["pekT"][l], "pevT": I["pevT"][l], "relb": I["relb"], "ohg": I["ohg%d" % h], "selc": I["selc%d" % h], "ov": I["ov"],
                     "visb": I["visb%d" % h], "bidx": I["bidx%d" % h], "ident": I["ident"], "rev": I["rev"], "sw": I["sw"], "onT": Sx["onT%d" % h], "negmask": Sx["negmask%d" % h]}
                build_B(env={"nc": nc, "kb": kb, "T": T}, prefix="L%dB%d_" % (l, h))
            for h in vh:
                T = {"qd": Sx["qd%d" % h], "cnT": Sx["cn"], "wuk": I["wuk"][l], "wuv": I["wuv"][l], "negmask": Sx["negmask%d" % h], "relb": I["relb"],
                     "ohg_d": I["ohg_d%d" % h], "ident": I["ident"], "rev": I["rev"], "odT": Sx["odT%d" % h]}
                build_C(env={"nc": nc, "kb": kb, "T": T}, prefix="L%dC%d_" % (l, h))
            for h in vh:
                T = {"x1T": Sx["x1T%d" % h], "onT": Sx["onT%d" % h], "odT": Sx["odT%d" % h], "amT": Sx["am%d" % h], "modt_in": Sx["mod"], "A_in": Sx["A"],
                     "GT_in": Sx["GT"], "wun": I["wun"][l], "wud": I["wud"][l], "wo": I["wo"][l], "wfi": I["w_ffn_in"][l, 1], "wfo": I["w_ffn_out"][l, 1],
                     "gfT": I["gfT"], "x3T": outs[h] if last else Sx["xn%d" % h]}
                build_D(last, env={"nc": nc, "kb": kb, "T": T}, prefix="L%dD%d_" % (l, h))
        kb.finish()
    return nc


_PROG = []


def _core_tokens(a, b, half):
    t = a[b].reshape(32, 128, *a.shape[2:])[half::2]
    return t.reshape(NTOK, *a.shape[2:])


def kernel(x, c, w_ada, b_ada, g_norm, w_ffn_in, w_ffn_out, w_in, nsa_pe_k, nsa_pe_v, nsa_cmp_k1, nsa_cmp_k2,
           nsa_cmp_v1, nsa_cmp_v2, dsa_g_kv, dsa_w_uk, dsa_w_uv, w_up_nsa, w_up_dsa, w_out, rel_bias, g_final):
    f32 = np.float32
    A = lambda v: np.ascontiguousarray(np.asarray(v, f32))
    x = A(x)
    c = A(c)
    depth = np.asarray(w_ada).shape[0]
    if not _PROG:
        _PROG.append(build_fused(depth))
    nc = _PROG[0]
    ident = np.eye(128, dtype=f32)
    shared = {"ident": ident, "rev": np.ascontiguousarray(ident[::-1]), "relb": A(rel_bias), "gfT": A(np.asarray(g_final).reshape(16, 128).T),
              "w_ada": A(w_ada), "badaT": A(np.asarray(b_ada).reshape(depth, 144, 128).transpose(0, 2, 1)),
              "gnT": A(np.asarray(g_norm).reshape(depth, 48, 128).transpose(0, 2, 1)), "w_in": A(w_in),
              "gkvT": A(np.asarray(dsa_g_kv).reshape(depth, 2, 128).transpose(0, 2, 1)),
              "wk1": A(nsa_cmp_k1), "wv1": A(nsa_cmp_v1), "wk2": A(nsa_cmp_k2), "wv2": A(nsa_cmp_v2),
              "pekT": A(np.asarray(nsa_pe_k).transpose(0, 2, 1)), "pevT": A(np.asarray(nsa_pe_v).transpose(0, 2, 1)),
              "wuk": A(np.asarray(dsa_w_uk).reshape(depth, 256, 1024)), "wuv": A(np.asarray(dsa_w_uv).reshape(depth, 256, 1024)),
              "wun": A(w_up_nsa), "wud": A(w_up_dsa), "wo": A(w_out), "w_ffn_in": A(w_ffn_in), "w_ffn_out": A(w_ffn_out)}
    percore = []
    for s in range(2):
        pc_ = {"sw": np.ascontiguousarray(np.tile(np.array([[1.0 - s, float(s)]], f32), (128, 1)))}
        for v in range(2):
            cn_, cd_ = host_consts_nsa(v, s), host_consts_dsa(v, s)
            pc_["ohg%d" % v] = cn_["ohg"]
            pc_["selc%d" % v] = cn_["selc"]
            pc_["visb%d" % v] = cn_["visb"]
            pc_["bidx%d" % v] = cn_["bidx"]
            pc_["ohg_d%d" % v] = cd_["ohg_d"]
            pc_["ov"] = cn_["ov"]
        percore.append(pc_)
    in_maps = []
    for k in range(NCORES):
        b, s = k % 4, k // 4
        m = dict(shared)
        m.update(percore[s])
        m["xT0"] = np.ascontiguousarray(_core_tokens(x, b, s).T)
        m["xT1"] = np.ascontiguousarray(_core_tokens(x, b, 1 - s).T)
        m["cT"] = np.ascontiguousarray(c[b].reshape(16, 128).T)
        in_maps.append(m)
    res = run_bass_kernel_spmd(nc, in_maps, core_ids=list(range(NCORES))).results
    out = np.zeros(x.shape, f32)
    for b in range(4):
        for h in range(2):
            out[b].reshape(32, 128, D)[h::2] = np.ascontiguousarray(res[b + 4 * h]["out0"].T).reshape(16, 128, D)
    return out
```
